# Optimizing a Trainium2 kernel written in Bass

```python
import math
import jax, jax.numpy as jnp
from jax import lax
import numpy as np

D_MODEL = 1024
BATCH = 16
SEQ = 4096
DEPTH = 4
DEC_BATCH = 16
DEC_SEQ = 32
PAST_LEN = 1024

CHUNK = 64
Q_BLOCK = 128
EPS = 1e-6
MLA_HEADS = 8
MLA_Q_RANK = 384
MLA_KV_RANK = 256
MLA_NOPE = 128
MLA_ROPE = 64
MLA_V = 128
ROPE_BASE = 10000.0
HG_HEADS = 4
HG_DK = 128
HG_DV = 128
FFN_DIM = 2816
CONV_W = 3

MLA_QK = MLA_NOPE + MLA_ROPE
MLA_VW = MLA_HEADS * MLA_V
HG_KW = HG_HEADS * HG_DK
HG_VW = HG_HEADS * HG_DV
IN_SPLITS = (MLA_Q_RANK, MLA_KV_RANK, MLA_ROPE, HG_KW, HG_KW, HG_VW, HG_VW, D_MODEL, D_MODEL)
IN_COLS = sum(IN_SPLITS)
IN_OFFSETS = tuple(int(o) for o in np.cumsum(IN_SPLITS)[:-1])

kernel_name = 'hybrid_mla_hgrn2_convffn_stream'


def rmsnorm(x, g):
    xf = x.astype(jnp.float32)
    y = xf * lax.rsqrt(jnp.mean(xf * xf, axis=-1, keepdims=True) + EPS)
    return (y * g.astype(jnp.float32)).astype(x.dtype)


def rope(x, pos):
    half = MLA_ROPE // 2
    inv = ROPE_BASE ** (-jnp.arange(half, dtype=jnp.float32) / half)
    ang = pos.astype(jnp.float32)[:, None] * inv[None, :]
    shape = (ang.shape[0],) + (1,) * (x.ndim - 3) + (half,)
    cos = jnp.cos(ang).reshape(shape)
    sin = jnp.sin(ang).reshape(shape)
    xf = x.astype(jnp.float32)
    x1, x2 = xf[..., :half], xf[..., half:]
    return jnp.concatenate([x1 * cos - x2 * sin, x2 * cos + x1 * sin], axis=-1).astype(x.dtype)


def mla_expand(latent, w_ukv):
    B, Lk, _ = latent.shape
    kv = jnp.einsum('blr,rc->blc', latent, w_ukv).reshape(B, Lk, MLA_HEADS, MLA_NOPE + MLA_V)
    return kv[..., :MLA_NOPE], kv[..., MLA_NOPE:]


def mla_attend(q_nope, q_pe, k_nope, k_pe, v, mask):
    scale = 1.0 / math.sqrt(MLA_QK)
    s = (jnp.einsum('bqhd,bkhd->bhqk', q_nope, k_nope)
         + jnp.einsum('bqhr,bkr->bhqk', q_pe, k_pe)).astype(jnp.float32) * scale
    if mask is not None:
        s = jnp.where(mask[None, None], s, -jnp.inf)
    p = jax.nn.softmax(s, axis=-1).astype(v.dtype)
    return jnp.einsum('bhqk,bkhd->bqhd', p, v)


def mla_prompt(q_nope, q_pe, k_nope, k_pe, v):
    B, L = q_nope.shape[0], q_nope.shape[1]
    nb = L // Q_BLOCK
    qn_b = q_nope.reshape(B, nb, Q_BLOCK, MLA_HEADS, MLA_NOPE).transpose(1, 0, 2, 3, 4)
    qp_b = q_pe.reshape(B, nb, Q_BLOCK, MLA_HEADS, MLA_ROPE).transpose(1, 0, 2, 3, 4)
    kchunk = jnp.arange(L) // CHUNK

    def blk(args):
        qn, qp, q0 = args
        qchunk = (q0 + jnp.arange(Q_BLOCK)) // CHUNK
        mask = kchunk[None, :] <= qchunk[:, None]
        return mla_attend(qn, qp, k_nope, k_pe, v, mask)

    out = lax.map(blk, (qn_b, qp_b, jnp.arange(nb) * Q_BLOCK))
    return out.transpose(1, 0, 2, 3, 4).reshape(B, L, MLA_VW)


def hgrn_scan(q, k, v, logf, S0):
    B, L, H, _ = q.shape
    c = min(CHUNK, L)
    n = L // c

    def to_chunks(a):
        return a.reshape(B, n, c, H, a.shape[-1]).transpose(1, 0, 3, 2, 4)

    causal = jnp.tril(jnp.ones((c, c), dtype=bool))

    def step(S, inp):
        qc, kc, vc, lfc = inp
        b = jnp.cumsum(lfc, axis=2)
        o_inter = jnp.einsum('bhtk,bhkv->bhtv', qc * jnp.exp(b), S)
        diff = b[:, :, :, None, :] - b[:, :, None, :, :]
        decay = jnp.exp(jnp.where(causal[:, :, None], diff, -jnp.inf))
        A = jnp.einsum('bhtk,bhtsk,bhsk->bhts', qc, decay, kc)
        o = o_inter + jnp.einsum('bhts,bhsv->bhtv', A, vc)
        b_end = b[:, :, -1:, :]
        S_new = (jnp.exp(b_end[:, :, 0, :])[..., None] * S
                 + jnp.einsum('bhsk,bhsv->bhkv', kc * jnp.exp(b_end - b), vc))
        return S_new, o

    S_fin, o = lax.scan(step, S0, (to_chunks(q), to_chunks(k), to_chunks(v), to_chunks(logf)))
    o = o.transpose(1, 0, 3, 2, 4).reshape(B, L, H, HG_DV)
    return o, S_fin


def conv_ffn(h, conv_state, w_up, conv_w, conv_b, w_down):
    L = h.shape[1]
    up = jnp.einsum('bld,df->blf', h, w_up)
    a, val = up[..., :FFN_DIM], up[..., FFN_DIM:]
    ext = jnp.concatenate([conv_state.astype(a.dtype), a], axis=1)
    conv = conv_b + sum(conv_w[j] * ext[:, j:j + L] for j in range(CONV_W))
    out = jnp.einsum('blf,fd->bld', jax.nn.gelu(conv) * val, w_down)
    return out, ext[:, -(CONV_W - 1):]


def trunk_layer(x, pos, past_latent, past_kpe, S0, conv_state, lb,
                norm_mix_l, w_in_l, q_norm_l, w_uq_l, kv_norm_l, w_ukv_l, hgrn_norm_l,
                w_proj_a_l, w_proj_b_l, w_out_l, norm_ffn_l, w_up_l, conv_w_l, conv_b_l, w_down_l):
    B, L, _ = x.shape
    h = rmsnorm(x, norm_mix_l)
    cq, ckv, kr, hq, hf, hi, hg, ga, gb = jnp.split(
        jnp.einsum('bld,dc->blc', h, w_in_l), IN_OFFSETS, axis=-1)

    q = jnp.einsum('blr,rc->blc', rmsnorm(cq, q_norm_l), w_uq_l).reshape(B, L, MLA_HEADS, MLA_QK)
    q_nope, q_pe = q[..., :MLA_NOPE], rope(q[..., MLA_NOPE:], pos)
    lat_new = rmsnorm(ckv, kv_norm_l)
    kpe_new = rope(kr, pos)
    if past_latent is None:
        k_nope, v = mla_expand(lat_new, w_ukv_l)
        o_a = mla_prompt(q_nope, q_pe, k_nope, kpe_new, v)
    else:
        lat_all = jnp.concatenate([past_latent.astype(lat_new.dtype), lat_new], axis=1)
        kpe_all = jnp.concatenate([past_kpe.astype(kpe_new.dtype), kpe_new], axis=1)
        k_nope, v = mla_expand(lat_all, w_ukv_l)
        o_a = mla_attend(q_nope, q_pe, k_nope, kpe_all, v, None).reshape(B, L, MLA_VW)

    f = lb + (1.0 - lb) * jax.nn.sigmoid(hf.astype(jnp.float32))
    q_h = (jax.nn.silu(hq.astype(jnp.float32)) * HG_DK ** -0.5).reshape(B, L, HG_HEADS, HG_DK)
    k_h = (1.0 - f).reshape(B, L, HG_HEADS, HG_DK)
    logf = jnp.log(f).reshape(B, L, HG_HEADS, HG_DK)
    v_h = hi.astype(jnp.float32).reshape(B, L, HG_HEADS, HG_DV)
    o_h, S_new = hgrn_scan(q_h, k_h, v_h, logf, S0.astype(jnp.float32))
    o_b = rmsnorm(o_h, hgrn_norm_l) * jax.nn.silu(hg.astype(jnp.float32)).reshape(B, L, HG_HEADS, HG_DV)
    o_b = o_b.reshape(B, L, HG_VW).astype(x.dtype)

    mix = (jax.nn.sigmoid(ga) * jnp.einsum('blc,cd->bld', o_a, w_proj_a_l)
           + jax.nn.sigmoid(gb) * jnp.einsum('blc,cd->bld', o_b, w_proj_b_l))
    x = x + jnp.einsum('bld,de->ble', mix, w_out_l)

    ffn_out, conv_new = conv_ffn(rmsnorm(x, norm_ffn_l), conv_state, w_up_l, conv_w_l, conv_b_l, w_down_l)
    x = x + ffn_out
    return x, lat_new, kpe_new, S_new.astype(x.dtype), conv_new


def setup_inputs(seed: int = 0) -> dict:
    key = jax.random.key(seed)
    ks = jax.random.split(key, 24)
    nrm = jax.random.normal
    f32 = jnp.float32

    def w(k, shape, fan_in):
        return nrm(k, shape, f32) * fan_in ** -0.5

    def gain(k, shape):
        return 1.0 + 0.02 * nrm(k, shape, f32)

    return {
        'x_prompt': nrm(ks[0], (BATCH, SEQ, D_MODEL), f32),
        'x_sample': nrm(ks[1], (DEC_BATCH, DEC_SEQ, D_MODEL), f32),
        'cache_mla_latent': nrm(ks[2], (DEPTH, DEC_BATCH, PAST_LEN, MLA_KV_RANK), f32),
        'cache_mla_krope': nrm(ks[3], (DEPTH, DEC_BATCH, PAST_LEN, MLA_ROPE), f32),
        'state_hgrn': 0.3 * nrm(ks[4], (DEPTH, DEC_BATCH, HG_HEADS, HG_DK, HG_DV), f32),
        'state_ffn_conv': nrm(ks[5], (DEPTH, DEC_BATCH, CONV_W - 1, FFN_DIM), f32),
        'norm_mix': gain(ks[6], (DEPTH, D_MODEL)),
        'w_in': w(ks[7], (DEPTH, D_MODEL, IN_COLS), D_MODEL),
        'q_norm': gain(ks[8], (DEPTH, MLA_Q_RANK)),
        'w_uq': w(ks[9], (DEPTH, MLA_Q_RANK, MLA_HEADS * MLA_QK), MLA_Q_RANK),
        'kv_norm': gain(ks[10], (DEPTH, MLA_KV_RANK)),
        'w_ukv': w(ks[11], (DEPTH, MLA_KV_RANK, MLA_HEADS * (MLA_NOPE + MLA_V)), MLA_KV_RANK),
        'lb_logits': 0.5 * nrm(ks[12], (DEPTH, HG_KW), f32),
        'hgrn_norm': gain(ks[13], (DEPTH, HG_DV)),
        'w_proj_a': w(ks[14], (DEPTH, MLA_VW, D_MODEL), MLA_VW),
        'w_proj_b': w(ks[15], (DEPTH, HG_VW, D_MODEL), HG_VW),
        'w_out': w(ks[16], (DEPTH, D_MODEL, D_MODEL), D_MODEL),
        'norm_ffn': gain(ks[17], (DEPTH, D_MODEL)),
        'w_up': w(ks[18], (DEPTH, D_MODEL, 2 * FFN_DIM), D_MODEL),
        'conv_w': w(ks[19], (DEPTH, CONV_W, FFN_DIM), CONV_W),
        'conv_b': 0.02 * nrm(ks[20], (DEPTH, FFN_DIM), f32),
        'w_down': w(ks[21], (DEPTH, FFN_DIM, D_MODEL), FFN_DIM),
        'norm_final': gain(ks[22], (D_MODEL,)),
    }


def reference(x_prompt, x_sample, cache_mla_latent, cache_mla_krope, state_hgrn, state_ffn_conv,
              norm_mix, w_in, q_norm, w_uq, kv_norm, w_ukv, lb_logits, hgrn_norm,
              w_proj_a, w_proj_b, w_out, norm_ffn, w_up, conv_w, conv_b, w_down, norm_final):
    lb_soft = jax.nn.softmax(lb_logits.astype(jnp.float32), axis=0)
    lower_bounds = jnp.cumsum(lb_soft, axis=0) - lb_soft[0:1]

    B, Lp = x_prompt.shape[0], x_prompt.shape[1]
    pos_p = jnp.arange(Lp)
    pos_s = PAST_LEN + jnp.arange(x_sample.shape[1])
    S0_p = jnp.zeros((B, HG_HEADS, HG_DK, HG_DV), x_prompt.dtype)
    conv0_p = jnp.zeros((B, CONV_W - 1, FFN_DIM), x_prompt.dtype)

    xp, xs = x_prompt, x_sample
    lat_p, kpe_p, hs_p, cv_p = [], [], [], []
    lat_s, kpe_s, hs_s, cv_s = [], [], [], []
    for l in range(DEPTH):
        weights = (norm_mix[l], w_in[l], q_norm[l], w_uq[l], kv_norm[l], w_ukv[l], hgrn_norm[l],
                   w_proj_a[l], w_proj_b[l], w_out[l], norm_ffn[l], w_up[l], conv_w[l], conv_b[l], w_down[l])
        xp, a1, a2, a3, a4 = trunk_layer(xp, pos_p, None, None, S0_p, conv0_p, lower_bounds[l], *weights)
        xs, b1, b2, b3, b4 = trunk_layer(xs, pos_s, cache_mla_latent[l], cache_mla_krope[l],
                                         state_hgrn[l], state_ffn_conv[l], lower_bounds[l], *weights)
        lat_p.append(a1); kpe_p.append(a2); hs_p.append(a3); cv_p.append(a4)
        lat_s.append(b1); kpe_s.append(b2); hs_s.append(b3); cv_s.append(b4)

    y_prompt = rmsnorm(xp, norm_final)
    y_sample = rmsnorm(xs, norm_final)
    return (y_prompt, y_sample,
            jnp.stack(lat_p), jnp.stack(kpe_p), jnp.stack(hs_p), jnp.stack(cv_p),
            jnp.stack(lat_s), jnp.stack(kpe_s), jnp.stack(hs_s), jnp.stack(cv_s))
```

```python
import math
from contextlib import ExitStack

import numpy as np
import concourse.bass as bass
import concourse.mybir as mybir
from concourse.bass_utils import run_bass_kernel_spmd

F32 = mybir.dt.float32
BF16 = mybir.dt.bfloat16
AF = mybir.ActivationFunctionType
ALU = mybir.AluOpType
AX = mybir.AxisListType

EPS = 1e-6
NCORES = 8


class Res:
    __slots__ = ("w", "rd")

    def __init__(self):
        self.w = None
        self.rd = []


class Op:
    __slots__ = ("eng", "fn", "deps", "sig", "sigidx", "dma", "dcnt", "waits")


class Prog:
    ENGS = ("pe", "act", "dve", "pool", "sp")

    def __init__(self):
        self.ops = {e: [] for e in self.ENGS}
        self.dma_counts = {}

    def add(self, eng, fn, reads=(), writes=(), dma=None, ndma=1):
        op = Op()
        op.eng = eng
        op.fn = fn
        op.sig = False
        op.sigidx = 0
        op.dma = dma
        op.dcnt = 0
        if dma is not None:
            c = self.dma_counts.get(dma, 0) + 16 * ndma
            self.dma_counts[dma] = c
            op.dcnt = c
        deps = {}
        for r in reads:
            if r.w is not None:
                deps[r.w] = True
        for r in writes:
            if r.w is not None:
                deps.setdefault(r.w, False)
            for o in r.rd:
                deps.setdefault(o, False)
        keep = []
        for d, raw in deps.items():
            if d is op:
                continue
            if d.dma is not None:
                keep.append(d)
            elif d.eng == eng:
                if eng in ("pe", "sp"):
                    continue
                d.sig = True
                keep.append(d)
            else:
                d.sig = True
                keep.append(d)
        op.deps = keep
        for r in reads:
            r.rd.append(op)
        for r in writes:
            r.w = op
            r.rd = []
        self.ops[eng].append(op)
        return op

    def finalize(self):
        for e in self.ENGS:
            n = 0
            for op in self.ops[e]:
                if op.sig and op.dma is None:
                    n += 1
                    op.sigidx = n
        for e in self.ENGS:
            for op in self.ops[e]:
                w = {}
                for d in op.deps:
                    if d.dma is not None:
                        k = ("dma", d.dma)
                        v = d.dcnt
                    else:
                        k = ("eng", d.eng)
                        v = d.sigidx
                    if w.get(k, 0) < v:
                        w[k] = v
                op.waits = w

    def emit(self, eng, e, semof):
        waited = {}
        for op in self.ops[eng]:
            for k, v in op.waits.items():
                if waited.get(k, 0) < v:
                    e.wait_ge(semof(k), v)
                    waited[k] = v
            r = op.fn(e)
            if op.dma is not None:
                lst = r if isinstance(r, (list, tuple)) else [r]
                for ins in lst:
                    ins.then_inc(semof(("dma", op.dma)), 16)
            elif op.sig:
                ins = r[-1] if isinstance(r, (list, tuple)) else r
                ins.then_inc(semof(("eng", eng)), 1)
        if eng in ("sp", "pool"):
            for k, c in self.dma_counts.items():
                if self.dma_eng.get(k) == eng:
                    e.wait_ge(semof(("dma", k)), c)


GRAN = 512


class Buf:
    def __init__(self, ar, gran, off, nbytes, dtype, shape):
        self.off = off
        self.nbytes = nbytes
        self.dtype = dtype
        esz = 4 if dtype == F32 else 2
        self.esz = esz
        n = nbytes // esz
        v = ar[:, off // 2:(off + nbytes) // 2]
        if dtype == F32:
            v = v.bitcast(F32)
        self.flat = v
        self.shape = shape
        if len(shape) == 1:
            self.v = v
        elif len(shape) == 2:
            self.v = v.rearrange("p (a b) -> p a b", b=shape[1])
        else:
            self.v = v.rearrange("p (a b c) -> p a b c", b=shape[1], c=shape[2])
        self.gran = gran
        self.n = n

    def r(self, lo=0, hi=None):
        if hi is None:
            hi = self.n
        b0 = (self.off + lo * self.esz) // GRAN
        b1 = (self.off + hi * self.esz - 1) // GRAN
        return self.gran[b0:b1 + 1]

    def rs(self, i, inner):
        return self.r(i * inner, (i + 1) * inner)


class Cfg:
    def __init__(self, LP=4096, DEPTH=4, PAST=1024, NSEQ=2, wslots=2):
        self.LP = LP
        self.DEPTH = DEPTH
        self.PAST = PAST
        self.NSEQ = NSEQ
        self.NT = LP // 512
        self.wslots = wslots
        D = DEPTH
        o = 0
        self.po = {}
        for name, n in (("g1", D * 8), ("g2", D * 8), ("gq", D * 3), ("gkv", D * 2), ("gh", D),
                        ("lbl", 4 * D), ("cw", D * 3 * 22), ("cb", D * 22), ("gf", 8)):
            self.po[name] = (o, n)
            o += n
        self.NPAR = o


def build_program(cfg):
    LP, DEPTH, PAST, NSEQ, NT = cfg.LP, cfg.DEPTH, cfg.PAST, cfg.NSEQ, cfg.NT
    NPG = PAST // 512
    nc = bass.Bass("TRN2", target_bir_lowering=False)
    P = Prog()
    P.dma_eng = {}

    def dram(name, shape, dt=F32, kind="ExternalInput"):
        return nc.dram_tensor(name, list(shape), dt, kind=kind).ap()

    d_xp = dram("xp", [NSEQ, 8, 128, LP])
    d_xs = dram("xs", [8, 128, 64])
    d_latp = dram("latp", [DEPTH, 2, 2, 128, PAST])
    d_krp = dram("krp", [DEPTH, 2, 128, PAST])
    d_sh = dram("sh", [DEPTH, 2, 128, 4, 128])
    d_scv = dram("scv", [DEPTH, 2, 128, 22, 2])
    d_par = dram("par", [128, cfg.NPAR])
    d_const = dram("const", [128, 128 + 128 + 128 + 512 + 64])
    d_csp = dram("csp", [2, 128, LP])
    d_css = dram("css", [2, 128, 64])
    d_win = dram("w_in", [DEPTH, 1024, 4800])
    d_winC = dram("w_inC", [DEPTH, 1024, 896])
    d_wuq = dram("w_uqR", [DEPTH, 384, 2048])
    d_wukv = dram("w_ukvR", [DEPTH, 256, 2048])
    d_wpa = dram("w_pa", [DEPTH, 1024, 1024])
    d_wpb = dram("w_pb", [DEPTH, 512, 1024])
    d_wo = dram("w_o", [DEPTH, 1024, 1024])
    d_wup = dram("w_up", [DEPTH, 1024, 5632])
    d_wdn = dram("w_dn", [DEPTH, 2816, 1024])
    o_yp = dram("o_yp", [NSEQ, 8, 128, LP], kind="ExternalOutput")
    o_ys = dram("o_ys", [8, 128, 64], kind="ExternalOutput")
    o_latp = dram("o_latp", [DEPTH, NSEQ, 2, 128, LP], kind="ExternalOutput")
    o_kpep = dram("o_kpep", [DEPTH, NSEQ, 64, LP], kind="ExternalOutput")
    o_hsp = dram("o_hsp", [DEPTH, NSEQ, 128, 4, 128], kind="ExternalOutput")
    o_cvp = dram("o_cvp", [DEPTH, NSEQ, 128, 22, 2], kind="ExternalOutput")
    o_lats = dram("o_lats", [DEPTH, 2, 128, 64], kind="ExternalOutput")
    o_kpes = dram("o_kpes", [DEPTH, 64, 64], kind="ExternalOutput")
    o_hss = dram("o_hss", [DEPTH, 2, 128, 4, 128], kind="ExternalOutput")
    o_cvs = dram("o_cvs", [DEPTH, 2, 128, 22, 2], kind="ExternalOutput")
    s_in = dram("s_in", [DEPTH, 1024, 4096], BF16, "Internal")
    s_inC = dram("s_inC", [DEPTH, 1024, 896], BF16, "Internal")
    s_uq = dram("s_uq", [DEPTH, 384, 2048], BF16, "Internal")
    s_ukv = dram("s_ukv", [DEPTH, 256, 2048], BF16, "Internal")
    s_pa = dram("s_pa", [DEPTH, 1024, 1024], BF16, "Internal")
    s_pb = dram("s_pb", [DEPTH, 512, 1024], BF16, "Internal")
    s_wo = dram("s_wo", [DEPTH, 1024, 1024], BF16, "Internal")
    s_up = dram("s_up", [DEPTH, 1024, 5632], BF16, "Internal")
    s_dn = dram("s_dn", [DEPTH, 2816, 1024], BF16, "Internal")
    s_kc = dram("s_kc", [NSEQ, DEPTH, 128, NT * 4, 8, 128], BF16, "Internal")
    s_vc = dram("s_vc", [NSEQ, DEPTH, NT * 4, 128, 1024], BF16, "Internal")
    s_pc = dram("s_pc", [NSEQ, DEPTH, 128, LP], BF16, "Internal")

    es = ExitStack()
    ARENA_BYTES = 207 * 1024
    ar = es.enter_context(nc.sbuf_tensor("arena", [128, ARENA_BYTES // 2], BF16))
    gran = [Res() for _ in range(ARENA_BYTES // GRAN + 1)]
    top = [0]

    def alloc(shape, dt, at=None):
        esz = 4 if dt == F32 else 2
        n = 1
        for s in shape:
            n *= s
        nb = n * esz
        nb_al = (nb + GRAN - 1) // GRAN * GRAN
        if at is None:
            off = top[0]
            top[0] += nb_al
        else:
            off = at[0]
            at[0] += nb_al
        assert off + nb_al <= ARENA_BYTES, ("arena overflow", off + nb_al)
        return Buf(ar, gran, off, nb, dt, shape)

    nws = cfg.wslots
    xT = alloc([8, 512], F32)
    hT = alloc([8, 512], BF16)
    WS = [alloc([8192], BF16) for _ in range(nws)]
    rstd = [alloc([512], F32) for _ in range(2)]
    gates = alloc([16, 512], BF16)
    obT = alloc([4, 512], BF16)
    Sst = alloc([DEPTH, 4, 128], F32)
    cs = alloc([2, 512], F32)
    identb = alloc([128], BF16)
    onesb = alloc([128], BF16)
    onesf = alloc([128], F32)
    constf = alloc([128 + 128 + 128 + 512 + 64], F32)
    par = alloc([cfg.NPAR], F32)
    g1s = alloc([DEPTH, 8], F32)
    g2s = alloc([DEPTH, 8], F32)
    gqs = alloc([DEPTH, 3], F32)
    gkvs = alloc([DEPTH, 2], F32)
    ghs = alloc([DEPTH], F32)
    gfs = alloc([8], F32)
    lbe = alloc([4, DEPTH], F32)
    lbm = alloc([4], F32)
    lbv = alloc([4, DEPTH], F32)
    oml = alloc([4, DEPTH], F32)
    ctail = alloc([DEPTH, 22, 2], F32)
    shalo = alloc([2, 22, 2], F32)
    stail = alloc([2, 22, 2], F32)
    S0s = alloc([2, 4, 128], F32)
    dd = alloc([4, 8], F32)
    epsb = alloc([4], F32)
    OV = top[0]
    a = [OV]
    TT = [alloc([512], F32, a) for _ in range(7)]
    qeb = alloc([2, 512], BF16, a)
    keb = alloc([2, 512], BF16, a)
    kdb = alloc([2, 512], BF16, a)
    qbb = alloc([4, 512], BF16, a)
    Vtok = alloc([4, 512], BF16, a)
    kdTokE = alloc([4, 4, 128], BF16, a)
    kdTokO = alloc([4, 4, 128], BF16, a)
    Am = alloc([4, 4, 128], BF16, a)
    Sb = alloc([4, 8, 128], BF16, a)
    sghg = alloc([4, 512], BF16, a)
    sqo = alloc([512], BF16, a)
    to = alloc([512], F32, a)
    endH = a[0]
    a = [OV]
    Qn = alloc([8, 512], BF16, a)
    Qp = alloc([8, 512], BF16, a)
    Kcur = alloc([4, 8, 128], BF16, a)
    Vcur = alloc([4, 1024], BF16, a)
    Pcur = alloc([512], BF16, a)
    M12 = a[0]
    a = [M12]
    cqf = alloc([3, 512], F32, a)
    cqn = alloc([3, 512], BF16, a)
    ckvf = alloc([2, 512], F32, a)
    latb = alloc([2, 512], BF16, a)
    kpef = alloc([512], F32, a)
    tmp1 = alloc([512], F32, a)
    tmp2 = alloc([512], F32, a)
    sqm = alloc([3, 512], BF16, a)
    endM1 = a[0]
    a = [M12]
    KS = []
    for i in range(2):
        KS.append((alloc([4, 4, 128], BF16, a), alloc([4, 512], BF16, a), alloc([512], BF16, a)))
    Pt = [alloc([512], BF16, a) for _ in range(4)]
    oaT = alloc([8, 512], BF16, a)
    Pacc = alloc([4, 512], F32, a)
    rsb = alloc([512], F32, a)
    mta = [alloc([512], F32, a) for _ in range(2)]
    mtb = [alloc([512], F32, a) for _ in range(2)]
    lpf = alloc([2, 512], F32, a)
    lpb = alloc([2, 512], BF16, a)
    kpf = alloc([512], F32, a)
    Vnew = alloc([2, 1024], BF16, a)
    endM2 = a[0]
    a = [OV]
    gT = alloc([22, 512], BF16, a)
    aext = [alloc([520], F32, a) for _ in range(2)]
    cvb = [alloc([512], F32, a) for _ in range(2)]
    ub = [alloc([512], F32, a) for _ in range(2)]
    sbf = [alloc([512], F32, a) for _ in range(2)]
    endF = a[0]
    assert max(endH, endM1, endM2, endF) <= ARENA_BYTES, (endH, endM1, endM2, endF)

    PSB = [es.enter_context(nc.psum_tensor(f"ps{i}", [128, 512], F32)) for i in range(8)]
    PSR = [Res() for _ in range(8)]

    dres = {}

    def DR(key):
        r = dres.get(key)
        if r is None:
            r = dres[key] = Res()
        return r

    def dma(eng, key, fn, reads, writes, n=1):
        P.dma_eng[key] = eng
        return P.add(eng, fn, reads, writes, dma=key, ndma=n)

    def act(fn, reads, writes):
        return P.add("act", fn, reads, writes)

    def dve(fn, reads, writes):
        return P.add("dve", fn, reads, writes)

    def pool(fn, reads, writes):
        return P.add("pool", fn, reads, writes)

    def pe(fn, reads, writes):
        return P.add("pe", fn, reads, writes)

    def mm_group(out_ap, pairs, reads, writes, start=True, stop=True):
        def fn(e, out_ap=out_ap, pairs=pairs, start=start, stop=stop):
            ins = None
            n = len(pairs)
            for i, (l, r) in enumerate(pairs):
                ins = e.matmul(out_ap, l, r, start=(start and i == 0), stop=(stop and i == n - 1))
            return ins
        return pe(fn, reads, writes)

    dma("sp", "par", lambda e: e.dma_start(out=par.v, in_=d_par[:, :]), [], par.r())
    dma("sp", "const", lambda e: e.dma_start(out=constf.v, in_=d_const[:, :]), [], constf.r())
    identf_v = constf.v[:, 0:128]
    maskP_v = constf.v[:, 128:256]
    maskS_v = constf.v[:, 256:384]
    resetP_v = constf.v[:, 384:896]
    resetS_v = constf.v[:, 896:960]
    dve(lambda e: e.tensor_copy(out=identb.v, in_=identf_v), constf.r(), identb.r())
    dve(lambda e: e.memset(onesb.v, 1.0), [], onesb.r())
    dve(lambda e: e.memset(onesf.v, 1.0), [], onesf.r())
    epsc = {}
    for i_, n_ in enumerate((1024.0, 384.0, 256.0, 128.0)):
        dve(lambda e, i_=i_, n_=n_: e.memset(epsb.v[:, i_:i_ + 1], n_ * EPS), [], epsb.r())
        epsc[n_ * EPS] = epsb.v[:, i_:i_ + 1]
    dve(lambda e: e.memset(Sst.flat, 0.0), [], Sst.r())
    dve(lambda e: e.memset(ctail.flat, 0.0), [], ctail.r())

    def pv(name):
        o, n = cfg.po[name]
        return par.flat[:, o:o + n]

    def scale_par(dst, name, s):
        dve(lambda e: e.tensor_scalar(out=dst.flat, in0=pv(name), scalar1=float(s), scalar2=None,
                                      op0=ALU.mult), par.r(), dst.r())

    scale_par(g1s, "g1", math.sqrt(1024.0))
    scale_par(g2s, "g2", math.sqrt(1024.0))
    scale_par(gqs, "gq", math.sqrt(384.0))
    scale_par(gkvs, "gkv", math.sqrt(256.0))
    scale_par(ghs, "gh", math.sqrt(128.0))
    scale_par(gfs, "gf", math.sqrt(1024.0))
    lbl_v = pv("lbl").rearrange("p (h l) -> p h l", l=DEPTH)
    dve(lambda e: e.tensor_reduce(out=lbm.v, in_=lbl_v, axis=AX.X, op=ALU.max), par.r(), lbm.r())
    dve(lambda e: e.tensor_tensor(out=lbe.v, in0=lbl_v,
                                  in1=lbm.v.unsqueeze(2).to_broadcast([128, 4, DEPTH]), op=ALU.subtract),
        par.r() + lbm.r(), lbe.r())
    act(lambda e: e.activation(out=lbe.flat, in_=lbe.flat, func=AF.Exp), lbe.r(), lbe.r())
    dve(lambda e: e.tensor_reduce(out=lbm.v, in_=lbe.v, axis=AX.X, op=ALU.add), lbe.r(), lbm.r())
    dve(lambda e: e.reciprocal(out=lbm.v, in_=lbm.v), lbm.r(), lbm.r())
    dve(lambda e: e.tensor_tensor(out=lbe.v, in0=lbe.v,
                                  in1=lbm.v.unsqueeze(2).to_broadcast([128, 4, DEPTH]), op=ALU.mult),
        lbe.r() + lbm.r(), lbe.r())
    dve(lambda e: e.memset(lbv.flat, 0.0), [], lbv.r())
    for l in range(1, DEPTH):
        dve(lambda e, l=l: e.tensor_tensor(out=lbv.v[:, :, l], in0=lbv.v[:, :, l - 1], in1=lbe.v[:, :, l],
                                           op=ALU.add), lbv.r() + lbe.r(), lbv.r())
    dve(lambda e: e.tensor_scalar(out=oml.flat, in0=lbv.flat, scalar1=-1.0, scalar2=1.0,
                                  op0=ALU.mult, op1=ALU.add), lbv.r(), oml.r())

    def cast_layer(l):
        def fn(e, l=l):
            out = []

            def rows(dst, src, nrows, step=128):
                for r0 in range(0, nrows, step):
                    out.append(e.dma_start(out=dst[r0:r0 + step, :], in_=src[r0:r0 + step, :]))
            rows(s_inC[l], d_winC[l], 1024)
            rows(s_in[l], d_win[l][:, 704:4800], 1024)
            rows(s_uq[l], d_wuq[l], 384)
            rows(s_ukv[l], d_wukv[l], 256)
            rows(s_pa[l], d_wpa[l], 1024)
            rows(s_pb[l], d_wpb[l], 512)
            rows(s_wo[l], d_wo[l], 1024)
            rows(s_up[l], d_wup[l], 1024)
            rows(s_dn[l], d_wdn[l], 2816)
            return out
        n = 8 + 8 + 3 + 2 + 8 + 4 + 8 + 8 + 22
        dma("pool", f"cast{l}", fn, [], [DR(("w", l))], n=n)

    for l in range(DEPTH):
        cast_layer(l)

    wctr = [0]

    def load_piece(l, src_fn, nel):
        i = wctr[0] % nws
        wctr[0] += 1
        slot = WS[i]

        def fn(e, slot=slot):
            return [e.dma_start(out=d, in_=s) for d, s in src_fn(slot.flat)]
        ncalls = len(src_fn(slot.flat))
        dma("sp", f"w{i}", fn, [DR(("w", l))], slot.r(0, nel), n=ncalls)
        return slot

    def wview(src2d, kc, c0, c1):
        return src2d.rearrange("(k p) n -> p k n", p=128)[:, :, c0:c1]

    def piece_simple(l, src2d, kc, c0, c1):
        w = c1 - c0

        def src_fn(flat):
            return [(flat[:, 0:kc * w].rearrange("p (k n) -> p k n", n=w), wview(src2d, kc, c0, c1))]
        slot = load_piece(l, src_fn, kc * w)
        return slot, slot.flat[:, 0:kc * w].rearrange("p (k n) -> p k n", n=w)

    bctr = [0]

    def next_bank(banks):
        b = banks[bctr[0] % len(banks)]
        bctr[0] += 1
        return b

    def rstd_from(buf, bank, c, N):
        act(lambda e: e.activation(out=buf.v[:, 0:N], in_=PSB[bank][:, 0:N], func=AF.Sqrt, bias=epsc[c], scale=1.0),
            [PSR[bank]] + epsb.r(), buf.r())
        dve(lambda e: e.reciprocal(out=buf.v[:, 0:N], in_=buf.v[:, 0:N]), buf.r(), buf.r())

    SCALE = 1.0 / math.sqrt(192.0)
    GC = 2.0 * 0.7978845608028654

    def rmsnorm_x(N, gs_col, out_fn, out_res_fn):
        for c in range(8):
            act(lambda e, c=c: e.activation(out=hT.v[:, c, 0:N], in_=xT.v[:, c, 0:N], func=AF.Square),
                xT.rs(c, 512), hT.rs(c, 512))
        b = 3
        mm_group(PSB[b][:, 0:N], [(onesb.v, hT.v[:, c, 0:N]) for c in range(8)], hT.r() + onesb.r(), [PSR[b]])
        rb = rstd[0]
        rstd_from(rb, b, 1024.0 * EPS, N)
        for c in range(8):
            dve(lambda e, c=c: e.scalar_tensor_tensor(out=out_fn(c), in0=xT.v[:, c, 0:N], scalar=gs_col(c),
                                                      in1=rb.v[:, 0:N], op0=ALU.mult, op1=ALU.mult),
                xT.rs(c, 512) + rb.r(), out_res_fn(c))

    import os as _os
    KSTOP = float(_os.environ.get("KSTOP", "99"))

    def tile_layer(l, N, kind, s=0, j=0):
        prompt = kind == "p"
        C = 64 if prompt else 32
        TB = 128 if prompt else 64
        NB = N // TB
        NCH = N // C
        cos_v = cs.v[:, 0, 0:N]
        sin_v = cs.v[:, 1, 0:N]
        mask_v = maskP_v if prompt else maskS_v[0:64, 0:64]
        reset_v = resetP_v if prompt else resetS_v
        dbanks = [0, 1]

        rmsnorm_x(N, lambda c: g1s.v[:, l, c:c + 1], lambda c: hT.v[:, c, 0:N], lambda c: hT.rs(c, 512))

        if KSTOP <= 1:
            return
        def dense(wv, kc_n, cols, rhs_fn, rhs_res, slot, banks=dbanks):
            b = next_bank(banks)
            mm_group(PSB[b][:, 0:N], [(wv[:, kc, cols[0]:cols[1]], rhs_fn(kc)) for kc in range(kc_n)],
                     rhs_res + slot.r(), [PSR[b]])
            return b

        hrhs = lambda kc: hT.v[:, kc, 0:N]

        slotA, wA = piece_simple(l, s_in[l], 8, 0, 1024)
        slotB, wB = piece_simple(l, s_in[l], 8, 1024, 2048)
        T0, T1, T2, T3, T4, T5, T6 = TT
        pool(lambda e: e.memset(kdTokE.v[C:TB], 0.0), [], kdTokE.r())
        pool(lambda e: e.memset(kdTokO.v[0:C], 0.0), [], kdTokO.r())
        if not prompt:
            dma("sp", "s0s", lambda e: [e.dma_start(out=S0s.v[:, q], in_=d_sh[l, q]) for q in range(2)],
                [], S0s.r(), n=2)
        for h in range(4):
            bq = dense(wA, 8, (h * 128, h * 128 + 128), hrhs, hT.r(), slotA)
            act(lambda e, bq=bq: e.activation(out=T0.v[:, 0:N], in_=PSB[bq][:, 0:N], func=AF.Silu),
                [PSR[bq]], T0.r())
            bf = dense(wA, 8, (512 + h * 128, 512 + h * 128 + 128), hrhs, hT.r(), slotA)
            act(lambda e, bf=bf: e.activation(out=T1.v[:, 0:N], in_=PSB[bf][:, 0:N], func=AF.Sigmoid),
                [PSR[bf]], T1.r())
            dve(lambda e, h=h: e.tensor_scalar(out=T1.v[:, 0:N], in0=T1.v[:, 0:N], scalar1=oml.v[:, h, l:l + 1],
                                               scalar2=lbv.v[:, h, l:l + 1], op0=ALU.mult, op1=ALU.add),
                T1.r() + oml.r() + lbv.r(), T1.r())
            act(lambda e: e.activation(out=T2.v[:, 0:N], in_=T1.v[:, 0:N], func=AF.Ln), T1.r(), T2.r())
            if KSTOP <= 1.1:
                return
            dve(lambda e: e.tensor_tensor_scan(out=T3.v[:, 0:N], data0=reset_v, data1=T2.v[:, 0:N], initial=0.0,
                                               op0=ALU.mult, op1=ALU.add), T2.r() + constf.r(), T3.r())
            dve(lambda e: e.tensor_scalar(out=T1.v[:, 0:N], in0=T1.v[:, 0:N], scalar1=-1.0, scalar2=1.0,
                                          op0=ALU.mult, op1=ALU.add), T1.r(), T1.r())
            if KSTOP <= 1.2:
                return
            b3 = T3.v[:, 0:N].rearrange("p (c t) -> p c t", t=C)
            rmid = b3[:, :, C // 2 - 1:C // 2].to_broadcast([128, NCH, C])
            rend = b3[:, :, C - 1:C].to_broadcast([128, NCH, C])
            dve(lambda e: e.tensor_tensor(out=T2.v[:, 0:N].rearrange("p (c t) -> p c t", t=C), in0=b3, in1=rmid,
                                          op=ALU.subtract), T3.r(), T2.r())
            dve(lambda e: e.tensor_tensor(out=T4.v[:, 0:N].rearrange("p (c t) -> p c t", t=C), in0=b3, in1=rend,
                                          op=ALU.subtract), T3.r(), T4.r())
            act(lambda e, h=h: e.activation(out=dd.v[:, h, 0:NCH].unsqueeze(2), in_=b3[:, :, C - 1:C], func=AF.Exp),
                T3.r(), dd.r())
            if KSTOP <= 1.3:
                return
            hb = h % 2
            act(lambda e: e.activation(out=T5.v[:, 0:N], in_=T3.v[:, 0:N], func=AF.Exp), T3.r(), T5.r())
            dve(lambda e, h=h: e.scalar_tensor_tensor(out=qbb.v[:, h, 0:N], in0=T0.v[:, 0:N], scalar=128.0 ** -0.5,
                                                      in1=T5.v[:, 0:N], op0=ALU.mult, op1=ALU.mult),
                T0.r() + T5.r(), qbb.rs(h, 512))
            act(lambda e: e.activation(out=T6.v[:, 0:N], in_=T2.v[:, 0:N], func=AF.Exp), T2.r(), T6.r())
            dve(lambda e, hb=hb: e.scalar_tensor_tensor(out=qeb.v[:, hb, 0:N], in0=T0.v[:, 0:N], scalar=128.0 ** -0.5,
                                                        in1=T6.v[:, 0:N], op0=ALU.mult, op1=ALU.mult),
                T0.r() + T6.r(), qeb.rs(hb, 512))
            act(lambda e: e.activation(out=T5.v[:, 0:N], in_=T2.v[:, 0:N], func=AF.Exp, scale=-1.0), T2.r(), T5.r())
            dve(lambda e, hb=hb: e.tensor_tensor(out=keb.v[:, hb, 0:N], in0=T1.v[:, 0:N], in1=T5.v[:, 0:N],
                                                 op=ALU.mult), T1.r() + T5.r(), keb.rs(hb, 512))
            act(lambda e: e.activation(out=T6.v[:, 0:N], in_=T4.v[:, 0:N], func=AF.Exp, scale=-1.0), T4.r(), T6.r())
            dve(lambda e, hb=hb: e.tensor_tensor(out=kdb.v[:, hb, 0:N], in0=T1.v[:, 0:N], in1=T6.v[:, 0:N],
                                                 op=ALU.mult), T1.r() + T6.r(), kdb.rs(hb, 512))
            if KSTOP <= 1.4:
                return
            if h == 0:
                for blk in range(NB):
                    b = next_bank(dbanks)
                    mm_group(PSB[b][0:TB, 0:512],
                             [(hT.v[:, kc, blk * TB:(blk + 1) * TB], wB[:, kc, 0:512]) for kc in range(8)],
                             hT.r() + slotB.r(), [PSR[b]])
                    act(lambda e, b=b, blk=blk: e.activation(out=Vtok.v[0:TB, blk, :], in_=PSB[b][0:TB, 0:512],
                                                             func=AF.Copy), [PSR[b]], Vtok.rs(blk, 512))
            if KSTOP <= 1.5:
                return
            psb7 = PSB[7][:, :].bitcast(BF16)
            for blk in range(NB):
                pe(lambda e, blk=blk, hb=hb: e.transpose(psb7[0:TB, blk * 128:(blk + 1) * 128],
                                                         kdb.v[:, hb, blk * TB:(blk + 1) * TB], identb.v),
                   kdb.rs(hb, 512) + identb.r(), [PSR[7]])
            act(lambda e, h=h: e.activation(out=kdTokE.v[0:C, h, 0:NB, :],
                                            in_=psb7[0:C, 0:NB * 128].rearrange("p (b k) -> p b k", k=128),
                                            func=AF.Copy), [PSR[7]], kdTokE.rs(h, 512))
            act(lambda e, h=h: e.activation(out=kdTokO.v[C:TB, h, 0:NB, :],
                                            in_=psb7[C:TB, 0:NB * 128].rearrange("p (b k) -> p b k", k=128),
                                            func=AF.Copy), [PSR[7]], kdTokO.rs(h, 512))
            if KSTOP <= 1.6:
                return
            for blk in range(NB):
                mm_group(PSB[6][0:TB, blk * 128:blk * 128 + TB],
                         [(keb.v[:, hb, blk * TB:(blk + 1) * TB], qeb.v[:, hb, blk * TB:(blk + 1) * TB])],
                         keb.rs(hb, 512) + qeb.rs(hb, 512), [PSR[6]])
            dve(lambda e, h=h: e.tensor_tensor(
                out=Am.v[0:TB, h, 0:NB, 0:TB],
                in0=PSB[6][0:TB, 0:NB * 128].rearrange("p (b t) -> p b t", t=128)[:, :, 0:TB],
                in1=mask_v.unsqueeze(1).to_broadcast([TB, NB, TB]), op=ALU.mult),
                [PSR[6]] + constf.r(), Am.rs(h, 512))
            if KSTOP <= 1.7:
                return
            for c in range(NCH):
                blk = (c * C) // TB
                r0 = (c * C) % TB
                ub_ = 4 + c // 4
                kdX = kdTokE if r0 == 0 else kdTokO
                mm_group(PSB[ub_][:, (c % 4) * 128:(c % 4) * 128 + 128],
                         [(kdX.v[0:TB, h, blk, :], Vtok.v[0:TB, blk, h * 128:(h + 1) * 128])],
                         kdX.rs(h, 512) + Vtok.rs(blk, 512), [PSR[ub_]])
            if KSTOP <= 1.8:
                return
            if prompt:
                Sh = Sst.v[:, l, h, :]
                Shr = Sst.r((l * 4 + h) * 128, (l * 4 + h + 1) * 128)
                if j == 0:
                    dve(lambda e, Sh=Sh: e.memset(Sh, 0.0), [], Shr)
                act(lambda e, h=h, Sh=Sh: e.activation(out=Sb.v[:, h, 0, :], in_=Sh, func=AF.Copy),
                    Shr, Sb.rs(h, 1024))
                for c in range(NCH):
                    ub_ = 4 + c // 4
                    dve(lambda e, h=h, c=c, ub_=ub_, Sh=Sh: e.scalar_tensor_tensor(
                        out=Sh, in0=Sh, scalar=dd.v[:, h, c:c + 1],
                        in1=PSB[ub_][:, (c % 4) * 128:(c % 4) * 128 + 128], op0=ALU.mult, op1=ALU.add),
                        Shr + dd.r() + [PSR[ub_]], Shr)
                    if c + 1 < NCH:
                        act(lambda e, h=h, c=c, Sh=Sh: e.activation(out=Sb.v[:, h, c + 1, :], in_=Sh, func=AF.Copy),
                            Shr, Sb.rs(h, 1024))
                if j == NT - 1:
                    dma("pool", f"o_hsp{l}_{h}", lambda e, h=h, Sh=Sh: e.dma_start(out=o_hsp[l, s][:, h, :], in_=Sh),
                        Shr, [])
            else:
                for q in range(2):
                    Sq = S0s.v[:, q, h, :]
                    Sqr = S0s.r()
                    act(lambda e, h=h, q=q, Sq=Sq: e.activation(out=Sb.v[:, h, q, :], in_=Sq, func=AF.Copy),
                        Sqr, Sb.rs(h, 1024))
                    dve(lambda e, h=h, q=q, Sq=Sq: e.scalar_tensor_tensor(
                        out=Sq, in0=Sq, scalar=dd.v[:, h, q:q + 1], in1=PSB[4][:, q * 128:q * 128 + 128],
                        op0=ALU.mult, op1=ALU.add), Sqr + dd.r() + [PSR[4]], Sqr)
            if KSTOP <= 1.9:
                return
            bg = dense(wB, 8, (512 + h * 128, 512 + h * 128 + 128), hrhs, hT.r(), slotB)
            act(lambda e, bg=bg, h=h: e.activation(out=sghg.v[:, h, 0:N], in_=PSB[bg][:, 0:N], func=AF.Silu),
                [PSR[bg]], sghg.rs(h, 512))
        if KSTOP <= 2:
            return
        if not prompt:
            dma("pool", "o_hss", lambda e: [e.dma_start(out=o_hss[l, q], in_=S0s.v[:, q]) for q in range(2)],
                S0s.r(), [], n=2)

        slotD, wD = piece_simple(l, s_in[l], 8, 2048, 3072)
        for oc in range(8):
            b = dense(wD, 8, (oc * 128, oc * 128 + 128), hrhs, hT.r(), slotD)
            act(lambda e, b=b, oc=oc: e.activation(out=gates.v[:, oc, 0:N], in_=PSB[b][:, 0:N], func=AF.Sigmoid),
                [PSR[b]], gates.rs(oc, 512))
        slotE, wE = piece_simple(l, s_in[l], 8, 3072, 4096)
        for oc in range(8):
            b = dense(wE, 8, (oc * 128, oc * 128 + 128), hrhs, hT.r(), slotE)
            act(lambda e, b=b, oc=oc: e.activation(out=gates.v[:, 8 + oc, 0:N], in_=PSB[b][:, 0:N], func=AF.Sigmoid),
                [PSR[b]], gates.rs(8 + oc, 512))

        if KSTOP <= 3:
            return
        for h in range(4):
            ob_ = 2
            for blk in range(NB):
                def fn(e, h=h, blk=blk):
                    ins = e.matmul(PSB[ob_][:, blk * TB:(blk + 1) * TB], Vtok.v[0:TB, blk, h * 128:(h + 1) * 128],
                                   Am.v[0:TB, h, blk, 0:TB], start=(blk == 0), stop=False, skip_group_check=True)
                    ncb = TB // C
                    for ci in range(ncb):
                        c = blk * ncb + ci
                        ins = e.matmul(PSB[ob_][:, c * C:(c + 1) * C], Sb.v[:, h, c, :], qbb.v[:, h, c * C:(c + 1) * C],
                                       start=False, stop=(blk == NB - 1 and ci == ncb - 1), skip_group_check=True)
                    return ins
                pe(fn, Vtok.rs(blk, 512) + Am.rs(h, 512) + Sb.rs(h, 1024) + qbb.rs(h, 512), [PSR[ob_]])
            act(lambda e: e.activation(out=sqo.v[:, 0:N], in_=PSB[ob_][:, 0:N], func=AF.Square), [PSR[ob_]], sqo.r())
            mm_group(PSB[3][:, 0:N], [(onesb.v, sqo.v[:, 0:N])], sqo.r() + onesb.r(), [PSR[3]])
            rb = rstd[1]
            rstd_from(rb, 3, 128.0 * EPS, N)
            dve(lambda e: e.tensor_tensor(out=to.v[:, 0:N], in0=PSB[ob_][:, 0:N], in1=rb.v[:, 0:N], op=ALU.mult),
                [PSR[ob_]] + rb.r(), to.r())
            dve(lambda e, h=h: e.scalar_tensor_tensor(out=obT.v[:, h, 0:N], in0=to.v[:, 0:N], scalar=ghs.v[:, l:l + 1],
                                                      in1=sghg.v[:, h, 0:N], op0=ALU.mult, op1=ALU.mult),
                to.r() + ghs.r() + sghg.rs(h, 512), obT.rs(h, 512))

        if KSTOP <= 4:
            return
        slotC, wC = piece_simple(l, s_inC[l], 8, 0, 896)
        for c in range(3):
            b = dense(wC, 8, (c * 128, c * 128 + 128), hrhs, hT.r(), slotC)
            act(lambda e, b=b, c=c: e.activation(out=cqf.v[:, c, 0:N], in_=PSB[b][:, 0:N], func=AF.Copy),
                [PSR[b]], cqf.rs(c, 512))
            act(lambda e, b=b, c=c: e.activation(out=sqm.v[:, c, 0:N], in_=PSB[b][:, 0:N], func=AF.Square),
                [PSR[b]], sqm.rs(c, 512))
        mm_group(PSB[3][:, 0:N], [(onesb.v, sqm.v[:, c, 0:N]) for c in range(3)], sqm.r() + onesb.r(), [PSR[3]])
        rq = rstd[0]
        rstd_from(rq, 3, 384.0 * EPS, N)
        for c in range(3):
            dve(lambda e, c=c: e.scalar_tensor_tensor(out=cqn.v[:, c, 0:N], in0=cqf.v[:, c, 0:N],
                                                      scalar=gqs.v[:, l, c:c + 1], in1=rq.v[:, 0:N],
                                                      op0=ALU.mult, op1=ALU.mult),
                cqf.rs(c, 512) + rq.r() + gqs.r(), cqn.rs(c, 512))
        for c in range(2):
            b = dense(wC, 8, (384 + c * 128, 384 + c * 128 + 128), hrhs, hT.r(), slotC)
            act(lambda e, b=b, c=c: e.activation(out=ckvf.v[:, c, 0:N], in_=PSB[b][:, 0:N], func=AF.Copy),
                [PSR[b]], ckvf.rs(c, 512))
            act(lambda e, b=b, c=c: e.activation(out=sqm.v[:, c, 0:N], in_=PSB[b][:, 0:N], func=AF.Square),
                [PSR[b]], sqm.rs(c, 512))
        mm_group(PSB[3][:, 0:N], [(onesb.v, sqm.v[:, c, 0:N]) for c in range(2)], sqm.r() + onesb.r(), [PSR[3]])
        rk = rstd[1]
        rstd_from(rk, 3, 256.0 * EPS, N)
        for c in range(2):
            dve(lambda e, c=c: e.scalar_tensor_tensor(out=ckvf.v[:, c, 0:N], in0=ckvf.v[:, c, 0:N],
                                                      scalar=gkvs.v[:, l, c:c + 1], in1=rk.v[:, 0:N],
                                                      op0=ALU.mult, op1=ALU.mult),
                ckvf.rs(c, 512) + rk.r() + gkvs.r(), ckvf.rs(c, 512))
            act(lambda e, c=c: e.activation(out=latb.v[:, c, 0:N], in_=ckvf.v[:, c, 0:N], func=AF.Copy),
                ckvf.rs(c, 512), latb.rs(c, 512))
        if prompt:
            dma("pool", "o_lat", lambda e: e.dma_start(
                out=o_latp[l, s].rearrange("c p t -> p c t")[:, :, j * 512:(j + 1) * 512], in_=ckvf.v),
                ckvf.r(), [])
        else:
            dma("pool", "o_lat", lambda e: e.dma_start(out=o_lats[l].rearrange("c p t -> p c t"),
                                                        in_=ckvf.v[:, :, 0:64]), ckvf.r(), [])
        bk = dense(wC, 8, (640, 768), hrhs, hT.r(), slotC)
        dve(lambda e: e.tensor_tensor(out=tmp1.v[:, 0:N], in0=PSB[bk][:, 0:N], in1=cos_v, op=ALU.mult),
            [PSR[bk]] + cs.r(), tmp1.r())
        bkp = dense(wC, 8, (768, 896), hrhs, hT.r(), slotC)
        dve(lambda e: e.tensor_tensor(out=tmp2.v[:, 0:N], in0=PSB[bkp][:, 0:N], in1=sin_v, op=ALU.mult),
            [PSR[bkp]] + cs.r(), tmp2.r())
        dve(lambda e: e.tensor_tensor(out=kpef.v[:, 0:N], in0=tmp1.v[:, 0:N], in1=tmp2.v[:, 0:N], op=ALU.add),
            tmp1.r() + tmp2.r(), kpef.r())
        act(lambda e: e.activation(out=Pcur.v[:, 0:N], in_=kpef.v[:, 0:N], func=AF.Copy), kpef.r(), Pcur.r())
        if prompt:
            dma("pool", "o_kpe", lambda e: e.dma_start(out=o_kpep[l, s][:, j * 512:(j + 1) * 512],
                                                        in_=kpef.v[0:64, :]), kpef.r(), [])
        else:
            dma("pool", "o_kpe", lambda e: e.dma_start(out=o_kpes[l], in_=kpef.v[0:64, 0:64]), kpef.r(), [])
        if KSTOP <= 5:
            return
        Qp4 = Qp.v.rearrange("p (a b) n -> p a b n", b=2)
        pool(lambda e: e.memset(Qp4[64:128, :, 0, :], 0.0), [], Qp.r())
        pool(lambda e: e.memset(Qp4[0:64, :, 1, :], 0.0), [], Qp.r())
        slotU, wU = piece_simple(l, s_uq[l], 3, 0, 2048)
        qrhs = lambda kc: cqn.v[:, kc, 0:N]
        for h in range(8):
            b = dense(wU, 3, (h * 128, h * 128 + 128), qrhs, cqn.r(), slotU)
            act(lambda e, b=b, h=h: e.activation(out=Qn.v[:, h, 0:N], in_=PSB[b][:, 0:N], func=AF.Copy),
                [PSR[b]], Qn.rs(h, 512))
        for pr in range(4):
            b1 = dense(wU, 3, (1024 + pr * 128, 1024 + pr * 128 + 128), qrhs, cqn.r(), slotU)
            dve(lambda e, b1=b1: e.tensor_tensor(out=tmp1.v[:, 0:N], in0=PSB[b1][:, 0:N], in1=cos_v, op=ALU.mult),
                [PSR[b1]] + cs.r(), tmp1.r())
            b2 = dense(wU, 3, (1536 + pr * 128, 1536 + pr * 128 + 128), qrhs, cqn.r(), slotU)
            dve(lambda e, b2=b2: e.tensor_tensor(out=tmp2.v[:, 0:N], in0=PSB[b2][:, 0:N], in1=sin_v, op=ALU.mult),
                [PSR[b2]] + cs.r(), tmp2.r())
            dve(lambda e, pr=pr: e.tensor_tensor(out=Qp.v[0:64, 2 * pr, 0:N], in0=tmp1.v[0:64, 0:N],
                                                 in1=tmp2.v[0:64, 0:N], op=ALU.add),
                tmp1.r() + tmp2.r(), Qp.rs(2 * pr, 512))
            dve(lambda e, pr=pr: e.tensor_tensor(out=Qp.v[64:128, 2 * pr + 1, 0:N], in0=tmp1.v[64:128, 0:N],
                                                 in1=tmp2.v[64:128, 0:N], op=ALU.add),
                tmp1.r() + tmp2.r(), Qp.rs(2 * pr + 1, 512))
        if KSTOP <= 6:
            return
        slotK, wK = piece_simple(l, s_ukv[l], 2, 0, 2048)
        lrhs = lambda kc: latb.v[:, kc, 0:N]
        if prompt:
            for h in range(8):
                b = dense(wK, 2, (h * 128, h * 128 + 128), lrhs, latb.r(), slotK)
                act(lambda e, b=b, h=h: e.activation(out=Kcur.v[:, :, h, :],
                                                     in_=PSB[b][:, 0:512].rearrange("p (b k) -> p b k", k=128),
                                                     func=AF.Copy), [PSR[b]], Kcur.r())
            for blk in range(4):
                for half in range(2):
                    b = next_bank(dbanks)
                    mm_group(PSB[b][:, 0:512],
                             [(latb.v[:, kc, blk * 128:(blk + 1) * 128],
                               wK[:, kc, 1024 + half * 512:1024 + half * 512 + 512]) for kc in range(2)],
                             latb.r() + slotK.r(), [PSR[b]])
                    act(lambda e, b=b, blk=blk, half=half: e.activation(
                        out=Vcur.v[:, blk, half * 512:(half + 1) * 512], in_=PSB[b][:, 0:512], func=AF.Copy),
                        [PSR[b]], Vcur.rs(blk, 1024))
            if j < NT - 1:
                dma("pool", "kcw", lambda e: e.dma_start(out=s_kc[s, l][:, j * 4:(j + 1) * 4], in_=Kcur.v),
                    Kcur.r(), [DR(("kc", s, l, j))])
                dma("pool", "vcw", lambda e: e.dma_start(
                    out=s_vc[s, l][j * 4:(j + 1) * 4].rearrange("b k c -> k b c"), in_=Vcur.v),
                    Vcur.r(), [DR(("vc", s, l, j))])
                dma("pool", "pcw", lambda e: e.dma_start(out=s_pc[s, l][:, j * 512:(j + 1) * 512], in_=Pcur.v),
                    Pcur.r(), [DR(("pc", s, l, j))])
            if KSTOP <= 7:
                return
            attention_prompt(l, s, j)
        else:
            attention_sample(l, wK, slotK)

        if KSTOP <= 8:
            return
        slotPA, wPA = piece_simple(l, s_pa[l], 8, 0, 1024)
        slotPB, wPB = piece_simple(l, s_pb[l], 4, 0, 1024)
        mbanks = [[0, 1], [2, 3], [4, 5], [6, 7]]
        for oc in range(8):
            bx, by = mbanks[oc % 4]
            mm_group(PSB[bx][:, 0:N], [(wPA[:, kc, oc * 128:(oc + 1) * 128], oaT.v[:, kc, 0:N]) for kc in range(8)],
                     oaT.r() + slotPA.r(), [PSR[bx]])
            mm_group(PSB[by][:, 0:N], [(wPB[:, kc, oc * 128:(oc + 1) * 128], obT.v[:, kc, 0:N]) for kc in range(4)],
                     obT.r() + slotPB.r(), [PSR[by]])
            ta = mta[oc % 2]
            tb = mtb[oc % 2]
            dve(lambda e, bx=bx, oc=oc, ta=ta: e.tensor_tensor(out=ta.v[:, 0:N], in0=PSB[bx][:, 0:N],
                                                               in1=gates.v[:, oc, 0:N], op=ALU.mult),
                [PSR[bx]] + gates.rs(oc, 512), ta.r())
            dve(lambda e, by=by, oc=oc, tb=tb: e.tensor_tensor(out=tb.v[:, 0:N], in0=PSB[by][:, 0:N],
                                                               in1=gates.v[:, 8 + oc, 0:N], op=ALU.mult),
                [PSR[by]] + gates.rs(8 + oc, 512), tb.r())
            dve(lambda e, oc=oc, ta=ta, tb=tb: e.tensor_tensor(out=hT.v[:, oc, 0:N], in0=ta.v[:, 0:N],
                                                               in1=tb.v[:, 0:N], op=ALU.add),
                ta.r() + tb.r(), hT.rs(oc, 512))
        slotO, wO = piece_simple(l, s_wo[l], 8, 0, 1024)
        for oc in range(8):
            b = dense(wO, 8, (oc * 128, oc * 128 + 128), hrhs, hT.r(), slotO, banks=[0, 1, 2, 3])
            dve(lambda e, b=b, oc=oc: e.tensor_tensor(out=xT.v[:, oc, 0:N], in0=xT.v[:, oc, 0:N], in1=PSB[b][:, 0:N],
                                                      op=ALU.add), [PSR[b]] + xT.rs(oc, 512), xT.rs(oc, 512))

        if KSTOP <= 9:
            return
        rmsnorm_x(N, lambda c: g2s.v[:, l, c:c + 1], lambda c: hT.v[:, c, 0:N], lambda c: hT.rs(c, 512))
        nseg = 1 if prompt else 2
        L = N // nseg
        cwv = pv("cw").rearrange("p (l j c) -> p l j c", j=3, c=22)
        cbv = pv("cb").rearrange("p (l c) -> p l c", c=22)
        fb = [[0, 1], [2, 3], [4, 5], [6, 7]]
        for q in range(6):
            ncol = 512 if q < 5 else 256

            def src_fn(flat, q=q, ncol=ncol):
                v = flat[:, 0:8 * 2 * ncol].rearrange("p (k n) -> p k n", n=2 * ncol)
                return [(v[:, :, 0:ncol], wview(s_up[l], 8, q * 512, q * 512 + ncol)),
                        (v[:, :, ncol:2 * ncol], wview(s_up[l], 8, 2816 + q * 512, 2816 + q * 512 + ncol))]
            slotP = load_piece(l, src_fn, 8 * 2 * ncol)
            wP = slotP.flat[:, 0:8 * 2 * ncol].rearrange("p (k n) -> p k n", n=2 * ncol)
            for jj in range(ncol // 128):
                jc = q * 4 + jj
                bx, by = fb[jc % 4]
                mm_group(PSB[bx][:, 0:N], [(wP[:, kc, jj * 128:(jj + 1) * 128], hT.v[:, kc, 0:N]) for kc in range(8)],
                         hT.r() + slotP.r(), [PSR[bx]])
                mm_group(PSB[by][:, 0:N],
                         [(wP[:, kc, ncol + jj * 128:ncol + (jj + 1) * 128], hT.v[:, kc, 0:N]) for kc in range(8)],
                         hT.r() + slotP.r(), [PSR[by]])
                ae = aext[jc % 2]
                cv = cvb[jc % 2]
                uu = ub[jc % 2]
                ss_ = sbf[jc % 2]
                ae3 = ae.v[:, 0:nseg * (L + 2)].rearrange("p (s t) -> p s t", t=L + 2)
                as3 = lambda ap_: ap_[:, 0:N].rearrange("p (s t) -> p s t", t=L)
                act(lambda e, bx=bx, ae3=ae3: e.activation(out=ae3[:, :, 2:L + 2], in_=as3(PSB[bx]), func=AF.Copy),
                    [PSR[bx]], ae.r())
                if prompt:
                    halo_src = ctail.v[:, l, jc, :].unsqueeze(1)
                    halo_res = ctail.r()
                else:
                    halo_src = shalo.v[:, :, jc, :]
                    halo_res = shalo.r()
                if prompt and j == 0:
                    pool(lambda e, ae3=ae3: e.memset(ae3[:, :, 0:2], 0.0), [], ae.r())
                else:
                    pool(lambda e, ae3=ae3, halo_src=halo_src: e.tensor_copy(out=ae3[:, :, 0:2], in_=halo_src),
                         halo_res, ae.r())
                if prompt:
                    pool(lambda e, ae3=ae3, jc=jc: e.tensor_copy(out=ctail.v[:, l, jc, :].unsqueeze(1),
                                                                in_=ae3[:, :, L:L + 2]), ae.r(), ctail.r())
                else:
                    pool(lambda e, ae3=ae3, jc=jc: e.tensor_copy(out=stail.v[:, :, jc, :], in_=ae3[:, :, L:L + 2]),
                         ae.r(), stail.r())
                act(lambda e, ae3=ae3, cv=cv, jc=jc: e.activation(out=as3(cv.v), in_=ae3[:, :, 2:L + 2], func=AF.Identity,
                                                                 scale=cwv[:, l, 2, jc:jc + 1], bias=cbv[:, l, jc:jc + 1]),
                    ae.r() + par.r(), cv.r())
                dve(lambda e, ae3=ae3, cv=cv, jc=jc: e.scalar_tensor_tensor(
                    out=as3(cv.v), in0=ae3[:, :, 1:L + 1], scalar=cwv[:, l, 1, jc:jc + 1], in1=as3(cv.v),
                    op0=ALU.mult, op1=ALU.add), ae.r() + cv.r() + par.r(), cv.r())
                dve(lambda e, ae3=ae3, cv=cv, jc=jc: e.scalar_tensor_tensor(
                    out=as3(cv.v), in0=ae3[:, :, 0:L], scalar=cwv[:, l, 0, jc:jc + 1], in1=as3(cv.v),
                    op0=ALU.mult, op1=ALU.add), ae.r() + cv.r() + par.r(), cv.r())
                pool(lambda e, cv=cv, uu=uu: e.tensor_tensor(out=uu.v[:, 0:N], in0=cv.v[:, 0:N], in1=cv.v[:, 0:N],
                                                             op=ALU.mult), cv.r(), uu.r())
                pool(lambda e, uu=uu: e.tensor_scalar(out=uu.v[:, 0:N], in0=uu.v[:, 0:N], scalar1=0.044715, scalar2=1.0,
                                                      op0=ALU.mult, op1=ALU.add), uu.r(), uu.r())
                pool(lambda e, cv=cv, uu=uu: e.tensor_tensor(out=uu.v[:, 0:N], in0=uu.v[:, 0:N], in1=cv.v[:, 0:N],
                                                             op=ALU.mult), cv.r() + uu.r(), uu.r())
                act(lambda e, uu=uu, ss_=ss_: e.activation(out=ss_.v[:, 0:N], in_=uu.v[:, 0:N], func=AF.Sigmoid,
                                                           scale=GC), uu.r(), ss_.r())
                dve(lambda e, cv=cv, ss_=ss_: e.tensor_tensor(out=ss_.v[:, 0:N], in0=ss_.v[:, 0:N], in1=cv.v[:, 0:N],
                                                              op=ALU.mult), cv.r() + ss_.r(), ss_.r())
                dve(lambda e, by=by, ss_=ss_, jc=jc: e.tensor_tensor(out=gT.v[:, jc, 0:N], in0=ss_.v[:, 0:N],
                                                                     in1=PSB[by][:, 0:N], op=ALU.mult),
                    [PSR[by]] + ss_.r(), gT.rs(jc, 512))
        if prompt and j == NT - 1:
            dma("pool", f"o_cvp{l}", lambda e: e.dma_start(out=o_cvp[l, s], in_=ctail.v[:, l]), ctail.r(), [])
        if not prompt:
            dma("pool", "o_cvs", lambda e: [e.dma_start(out=o_cvs[l, q], in_=stail.v[:, q]) for q in range(2)],
                stail.r(), [], n=2)
        for q in range(4):
            def src_fn(flat, q=q):
                return [(flat[:, 0:22 * 256].rearrange("p (k n) -> p k n", n=256),
                         wview(s_dn[l], 22, q * 256, (q + 1) * 256))]
            slotDn = load_piece(l, src_fn, 22 * 256)
            wDn = slotDn.flat[:, 0:22 * 256].rearrange("p (k n) -> p k n", n=256)
            for o2 in range(2):
                oc = q * 2 + o2
                b = next_bank([0, 1, 2, 3])
                mm_group(PSB[b][:, 0:N], [(wDn[:, kc, o2 * 128:(o2 + 1) * 128], gT.v[:, kc, 0:N]) for kc in range(22)],
                         gT.r() + slotDn.r(), [PSR[b]])
                dve(lambda e, b=b, oc=oc: e.tensor_tensor(out=xT.v[:, oc, 0:N], in0=xT.v[:, oc, 0:N],
                                                          in1=PSB[b][:, 0:N], op=ALU.add),
                    [PSR[b]] + xT.rs(oc, 512), xT.rs(oc, 512))

    kvctr = [0]

    def attention_prompt(l, s, j):
        N = 512
        for p in range(2):
            heads = range(4 * p, 4 * p + 4)
            first = {h: True for h in heads}
            for g in range(j + 1):
                diag = g == j
                if not diag:
                    si = kvctr[0] % 2
                    kvctr[0] += 1
                    Kb, Vb, Pb = KS[si]

                    def fn(e, g=g, p=p, Kb=Kb, Vb=Vb, Pb=Pb):
                        return [
                            e.dma_start(out=Kb.v, in_=s_kc[s, l][:, g * 4:(g + 1) * 4, 4 * p:4 * p + 4, :]),
                            e.dma_start(out=Vb.v, in_=s_vc[s, l][g * 4:(g + 1) * 4, :, p * 512:(p + 1) * 512]
                                        .rearrange("b k c -> k b c")),
                            e.dma_start(out=Pb.v, in_=s_pc[s, l][:, g * 512:(g + 1) * 512]),
                        ]
                    dma("sp", f"kv{si}", fn, [DR(("kc", s, l, g)), DR(("vc", s, l, g)), DR(("pc", s, l, g))],
                        Kb.r() + Vb.r() + Pb.r(), n=3)
                    kfn = lambda h, kb, Kb=Kb, p=p: Kb.v[:, kb, h - 4 * p, :]
                    vfn = lambda h, kb, Vb=Vb, p=p: Vb.v[:, kb, (h - 4 * p) * 128:(h - 4 * p + 1) * 128]
                    pfn = lambda hp, kb, Pb=Pb: Pb.v[:, kb * 128:(kb + 1) * 128]
                    kvres = Kb.r() + Vb.r() + Pb.r()
                else:
                    kfn = lambda h, kb: Kcur.v[:, kb, h, :]
                    vfn = lambda h, kb: Vcur.v[:, kb, h * 128:(h + 1) * 128]
                    pfn = lambda hp, kb: Pcur.v[:, kb * 128:(kb + 1) * 128]
                    kvres = Kcur.r() + Vcur.r() + Pcur.r()
                for h in heads:
                    hl = h - 4 * p
                    ob = 4 + hl
                    hp = h % 2
                    for kb in range(4):
                        q0 = kb * 128 if diag else 0
                        sb_ = 2 + (bctr[0] % 2)
                        bctr[0] += 1
                        pt = Pt[bctr[0] % 4]
                        last = (g == j and kb == 3)

                        def fsc(e, h=h, kb=kb, q0=q0, sb_=sb_, hp=hp, kfn=kfn, pfn=pfn):
                            e.matmul(PSB[sb_][:, q0:N], kfn(h, kb), Qn.v[:, h, q0:N], start=True, stop=False)
                            return e.matmul(PSB[sb_][:, q0:N], pfn(hp, kb), Qp.v[:, h, q0:N],
                                            start=False, stop=True)
                        pe(fsc, kvres + Qn.rs(h, 512) + Qp.rs(h, 512), [PSR[sb_]])
                        act(lambda e, sb_=sb_, pt=pt, q0=q0: e.activation(out=pt.v[:, q0:N], in_=PSB[sb_][:, q0:N],
                                                                          func=AF.Exp, scale=SCALE),
                            [PSR[sb_]], pt.r())
                        if diag:
                            pool(lambda e, pt=pt, q0=q0: e.memset(pt.v[64:128, q0:q0 + 64], 0.0), [], pt.r())
                        st = first[h]

                        def fpv(e, h=h, kb=kb, q0=q0, pt=pt, ob=ob, st=st, last=last, vfn=vfn):
                            return e.matmul(PSB[ob][:, q0:N], vfn(h, kb), pt.v[:, q0:N], start=st, stop=last,
                                            skip_group_check=True)
                        pe(fpv, kvres + pt.r(), [PSR[ob]])
                        if st:
                            dve(lambda e, pt=pt, hl=hl: e.tensor_copy(out=Pacc.v[:, hl, :], in_=pt.v),
                                pt.r(), Pacc.rs(hl, 512))
                        else:
                            dve(lambda e, pt=pt, hl=hl, q0=q0: e.tensor_tensor(out=Pacc.v[:, hl, q0:N],
                                                                             in0=Pacc.v[:, hl, q0:N],
                                                                             in1=pt.v[:, q0:N], op=ALU.add),
                                pt.r() + Pacc.rs(hl, 512), Pacc.rs(hl, 512))
                        first[h] = False
            for h in heads:
                hl = h - 4 * p
                ob = 4 + hl
                sb_ = 2 + (bctr[0] % 2)
                bctr[0] += 1
                mm_group(PSB[sb_][:, 0:N], [(onesf.v, Pacc.v[:, hl, :])], Pacc.rs(hl, 512) + onesf.r(), [PSR[sb_]])
                dve(lambda e, sb_=sb_: e.reciprocal(out=rsb.v, in_=PSB[sb_][:, 0:N]), [PSR[sb_]], rsb.r())
                dve(lambda e, h=h, ob=ob: e.tensor_tensor(out=oaT.v[:, h, :], in0=PSB[ob][:, 0:N], in1=rsb.v,
                                                          op=ALU.mult), [PSR[ob]] + rsb.r(), oaT.rs(h, 512))

    def attention_sample(l, wK, slotK):
        N = 64
        Knew = Kcur.flat[:, 0:8 * 64].rearrange("p (h t) -> p h t", t=64)
        for h in range(8):
            b = next_bank([0, 1])
            mm_group(PSB[b][:, 0:N], [(wK[:, kc, h * 128:(h + 1) * 128], latb.v[:, kc, 0:N]) for kc in range(2)],
                     latb.r() + slotK.r(), [PSR[b]])
            act(lambda e, b=b, h=h: e.activation(out=Knew[:, h, :], in_=PSB[b][:, 0:N], func=AF.Copy),
                [PSR[b]], Kcur.r())
        for q in range(2):
            for half in range(2):
                b = next_bank([0, 1])
                mm_group(PSB[b][0:32, 0:512],
                         [(latb.v[:, kc, q * 32:(q + 1) * 32], wK[:, kc, 1024 + half * 512:1024 + half * 512 + 512])
                          for kc in range(2)], latb.r() + slotK.r(), [PSR[b]])
                act(lambda e, b=b, q=q, half=half: e.activation(out=Vnew.v[0:32, q, half * 512:(half + 1) * 512],
                                                                in_=PSB[b][0:32, 0:512], func=AF.Copy),
                    [PSR[b]], Vnew.rs(q, 1024))
        Pa = Pacc.flat[:, 0:8 * 64].rearrange("p (h t) -> p h t", t=64)
        ob = 4
        first = {}
        ostart = [True]
        for q in range(2):
            for g in range(NPG):
                dma("sp", "lpf", lambda e, q=q, g=g: [
                    e.dma_start(out=lpf.v, in_=d_latp[l, q].rearrange("c p t -> p c t")[:, :, g * 512:(g + 1) * 512]),
                    e.dma_start(out=kpf.v, in_=d_krp[l, q][:, g * 512:(g + 1) * 512])],
                    [], lpf.r() + kpf.r(), n=2)
                act(lambda e: e.activation(out=lpb.flat, in_=lpf.flat, func=AF.Copy), lpf.r(), lpb.r())
                Pb = KS[0][2]
                act(lambda e, Pb=Pb: e.activation(out=Pb.v, in_=kpf.v, func=AF.Copy), kpf.r(), Pb.r())
                for kb in range(4):
                    for half in range(2):
                        b = next_bank([0, 1])
                        mm_group(PSB[b][:, 0:512],
                                 [(lpb.v[:, kc, kb * 128:(kb + 1) * 128],
                                   wK[:, kc, 1024 + half * 512:1024 + half * 512 + 512]) for kc in range(2)],
                                 lpb.r() + slotK.r(), [PSR[b]])
                        act(lambda e, b=b, kb=kb, half=half: e.activation(
                            out=Vcur.v[:, kb, half * 512:(half + 1) * 512], in_=PSB[b][:, 0:512], func=AF.Copy),
                            [PSR[b]], Vcur.rs(kb, 1024))
                for h in range(8):
                    hp = h % 2
                    Kh = KS[h % 2][0]
                    b = next_bank([0, 1])
                    mm_group(PSB[b][:, 0:512], [(wK[:, kc, h * 128:(h + 1) * 128], lpb.v[:, kc, :]) for kc in range(2)],
                             lpb.r() + slotK.r(), [PSR[b]])
                    act(lambda e, b=b, Kh=Kh: e.activation(out=Kh.flat[:, 0:512], in_=PSB[b][:, 0:512], func=AF.Copy),
                        [PSR[b]], Kh.r())
                    for kb in range(4):
                        sb_ = 2 + (bctr[0] % 2)
                        bctr[0] += 1
                        pt = Pt[bctr[0] % 4]

                        def fsc(e, h=h, kb=kb, sb_=sb_, hp=hp, Kh=Kh, Pb=Pb, q=q):
                            e.matmul(PSB[sb_][:, 0:32], Kh.flat[:, kb * 128:(kb + 1) * 128], Qn.v[:, h, q * 32:(q + 1) * 32],
                                     start=True, stop=False)
                            return e.matmul(PSB[sb_][:, 0:32], Pb.v[:, kb * 128:(kb + 1) * 128],
                                            Qp.v[:, h, q * 32:(q + 1) * 32],
                                            start=False, stop=True)
                        pe(fsc, Kh.r() + Pb.r() + Qn.rs(h, 512) + Qp.rs(h, 512), [PSR[sb_]])
                        act(lambda e, sb_=sb_, pt=pt: e.activation(out=pt.v[:, 0:32], in_=PSB[sb_][:, 0:32], func=AF.Exp,
                                                                   scale=SCALE), [PSR[sb_]], pt.r())
                        st = first.get((h, q), True)

                        st0 = ostart[0]
                        ostart[0] = False

                        def fpv(e, h=h, kb=kb, pt=pt, st0=st0, q=q):
                            c0 = h * 64 + q * 32
                            return e.matmul(PSB[ob][:, c0:c0 + 32], Vcur.v[:, kb, h * 128:(h + 1) * 128], pt.v[:, 0:32],
                                            start=st0, stop=False, skip_group_check=True)
                        pe(fpv, Vcur.rs(kb, 1024) + pt.r(), [PSR[ob]])
                        if st:
                            dve(lambda e, pt=pt, h=h, q=q: e.tensor_copy(out=Pa[:, h, q * 32:(q + 1) * 32],
                                                                        in_=pt.v[:, 0:32]), pt.r(), Pacc.r())
                        else:
                            dve(lambda e, pt=pt, h=h, q=q: e.tensor_tensor(out=Pa[:, h, q * 32:(q + 1) * 32],
                                                                          in0=Pa[:, h, q * 32:(q + 1) * 32],
                                                                          in1=pt.v[:, 0:32], op=ALU.add),
                                pt.r() + Pacc.r(), Pacc.r())
                        first[(h, q)] = False
            for h in range(8):
                hp = h % 2
                sb_ = 2 + (bctr[0] % 2)
                bctr[0] += 1
                pt = Pt[bctr[0] % 4]

                def fsc(e, h=h, sb_=sb_, hp=hp, q=q):
                    e.matmul(PSB[sb_][0:32, 0:32], Knew[:, h, q * 32:(q + 1) * 32], Qn.v[:, h, q * 32:(q + 1) * 32],
                             start=True, stop=False)
                    return e.matmul(PSB[sb_][0:32, 0:32], Pcur.v[:, q * 32:(q + 1) * 32],
                                    Qp.v[:, h, q * 32:(q + 1) * 32], start=False, stop=True)
                pe(fsc, Kcur.r() + Pcur.r() + Qn.rs(h, 512) + Qp.rs(h, 512), [PSR[sb_]])
                act(lambda e, sb_=sb_, pt=pt: e.activation(out=pt.v[0:32, 0:32], in_=PSB[sb_][0:32, 0:32], func=AF.Exp,
                                                           scale=SCALE), [PSR[sb_]], pt.r())

                def fpv(e, h=h, pt=pt, q=q):
                    c0 = h * 64 + q * 32
                    return e.matmul(PSB[ob][:, c0:c0 + 32], Vnew.v[0:32, q, h * 128:(h + 1) * 128], pt.v[0:32, 0:32],
                                    start=False, stop=True, skip_group_check=True)
                pe(fpv, Vnew.rs(q, 1024) + pt.r(), [PSR[ob]])
                dve(lambda e, pt=pt, h=h, q=q: e.tensor_tensor(out=Pa[0:32, h, q * 32:(q + 1) * 32],
                                                              in0=Pa[0:32, h, q * 32:(q + 1) * 32],
                                                              in1=pt.v[0:32, 0:32], op=ALU.add),
                    pt.r() + Pacc.r(), Pacc.r())
        sb_ = 3
        mm_group(PSB[sb_][:, 0:512], [(onesf.v, Pacc.flat[:, 0:512])], Pacc.r() + onesf.r(), [PSR[sb_]])
        dve(lambda e: e.reciprocal(out=rsb.v, in_=PSB[sb_][:, 0:512]), [PSR[sb_]], rsb.r())
        dve(lambda e: e.tensor_tensor(out=oaT.v[:, :, 0:64], in0=PSB[ob][:, 0:512].rearrange("p (h t) -> p h t", t=64),
                                      in1=rsb.v.rearrange("p (h t) -> p h t", t=64), op=ALU.mult),
            [PSR[ob]] + rsb.r(), oaT.r())

    def final_norm_store(N, dst_fn, key):
        rmsnorm_x(N, lambda c: gfs.v[:, c:c + 1], lambda c: xT.v[:, c, 0:N], lambda c: xT.rs(c, 512))
        dma("pool", key, lambda e: e.dma_start(out=dst_fn(), in_=xT.v[:, :, 0:N]), xT.r(), [])

    import os
    DBG = int(os.environ.get("KDBG", "0"))
    for s in range(NSEQ if DBG == 0 else 0):
        for j in range(NT):
            dma("sp", "xload", lambda e, s=s, j=j: e.dma_start(
                out=xT.v, in_=d_xp[s].rearrange("c p t -> p c t")[:, :, j * 512:(j + 1) * 512]), [], xT.r())
            dma("sp", "csload", lambda e, j=j: e.dma_start(
                out=cs.v, in_=d_csp.rearrange("a p t -> p a t")[:, :, j * 512:(j + 1) * 512]), [], cs.r())
            for l in range(DEPTH):
                tile_layer(l, 512, "p", s, j)
            final_norm_store(512, lambda s=s, j=j: o_yp[s].rearrange("c p t -> p c t")[:, :, j * 512:(j + 1) * 512],
                             "o_y")
    dma("sp", "xload", lambda e: e.dma_start(out=xT.v[:, :, 0:64], in_=d_xs.rearrange("c p t -> p c t")), [], xT.r())
    dma("sp", "csload", lambda e: e.dma_start(out=cs.v[:, :, 0:64], in_=d_css.rearrange("a p t -> p a t")), [], cs.r())
    for l in range(DEPTH if DBG == 0 else 0):
        dma("sp", "shalo", lambda e, l=l: [e.dma_start(out=shalo.v[:, q], in_=d_scv[l, q]) for q in range(2)],
            [], shalo.r(), n=2)
        tile_layer(l, 64, "s")
    final_norm_store(64, lambda: o_ys.rearrange("c p t -> p c t"), "o_y")

    P.finalize()
    sems = {}

    for k in [("eng", e_) for e_ in ("pe", "act", "dve", "pool")] + [("dma", k_) for k_ in P.dma_counts]:
        sems[k] = es.enter_context(nc.semaphore("s_" + "_".join(str(x) for x in k)))

    def semof(k):
        return sems[k]

    with nc.Block() as block:
        @block.sync
        def _(e):
            P.emit("sp", e, semof)

        @block.gpsimd
        def _(e):
            P.emit("pool", e, semof)

        @block.scalar
        def _(e):
            P.emit("act", e, semof)

        @block.vector
        def _(e):
            P.emit("dve", e, semof)

        @block.tensor
        def _(e):
            P.emit("pe", e, semof)
    es.close()
    return nc


def rope_tables(pos):
    half = 32
    inv = (10000.0 ** (-np.arange(half, dtype=np.float32) / half)).astype(np.float32)
    ang = pos.astype(np.float32)[:, None] * inv[None, :]
    cos = np.cos(ang).astype(np.float32).T
    sin = np.sin(ang).astype(np.float32).T
    c64 = np.concatenate([cos, cos], 0)
    s64 = np.concatenate([-sin, sin], 0)
    return np.stack([np.concatenate([c64, c64], 0), np.concatenate([s64, s64], 0)], 0)


def make_consts():
    ident = np.eye(128, dtype=np.float32)
    s_ = np.arange(128)[:, None]
    t_ = np.arange(128)[None, :]
    maskP = ((s_ // 64 == t_ // 64) & (s_ <= t_)).astype(np.float32)
    maskS = ((s_ // 32 == t_ // 32) & (s_ <= t_)).astype(np.float32)
    maskS[64:, :] = 0
    maskS[:, 64:] = 0
    resetP = np.ones((128, 512), np.float32)
    resetP[:, ::64] = 0
    resetS = np.ones((128, 64), np.float32)
    resetS[:, ::32] = 0
    return np.concatenate([ident, maskP, maskS, resetP, resetS], 1)


def perm64(w):
    return np.concatenate([w[..., 32:64], w[..., 0:32]], -1)


def host_prep(cfg, inp, core):
    D = cfg.DEPTH
    f = np.float32
    ps = slice(core * cfg.NSEQ, (core + 1) * cfg.NSEQ)
    ss = slice(core * 2, core * 2 + 2)
    m = {}
    xp = np.asarray(inp["x_prompt"][ps])
    m["xp"] = np.ascontiguousarray(xp.transpose(0, 2, 1).reshape(cfg.NSEQ, 8, 128, cfg.LP))
    xs = np.asarray(inp["x_sample"][ss])
    m["xs"] = np.ascontiguousarray(xs.transpose(2, 0, 1).reshape(8, 128, 64))
    lat = np.asarray(inp["cache_mla_latent"][:, ss])
    m["latp"] = np.ascontiguousarray(lat.transpose(0, 1, 3, 2).reshape(D, 2, 2, 128, cfg.PAST))
    kr = np.asarray(inp["cache_mla_krope"][:, ss]).transpose(0, 1, 3, 2)
    m["krp"] = np.ascontiguousarray(np.concatenate([kr, kr], 2))
    sh = np.asarray(inp["state_hgrn"][:, ss])
    m["sh"] = np.ascontiguousarray(sh.transpose(0, 1, 3, 2, 4))
    scv = np.asarray(inp["state_ffn_conv"][:, ss])
    m["scv"] = np.ascontiguousarray(scv.reshape(D, 2, 2, 22, 128).transpose(0, 1, 4, 3, 2))
    par = np.zeros((128, cfg.NPAR), f)

    def put(name, arr):
        o, n = cfg.po[name]
        par[:, o:o + n] = arr.reshape(128, n)
    put("g1", np.asarray(inp["norm_mix"]).reshape(D, 8, 128).transpose(2, 0, 1))
    put("g2", np.asarray(inp["norm_ffn"]).reshape(D, 8, 128).transpose(2, 0, 1))
    put("gq", np.asarray(inp["q_norm"]).reshape(D, 3, 128).transpose(2, 0, 1))
    put("gkv", np.asarray(inp["kv_norm"]).reshape(D, 2, 128).transpose(2, 0, 1))
    put("gh", np.asarray(inp["hgrn_norm"]).reshape(D, 128).transpose(1, 0))
    put("lbl", np.asarray(inp["lb_logits"]).reshape(D, 4, 128).transpose(2, 1, 0))
    put("cw", np.asarray(inp["conv_w"]).reshape(D, 3, 22, 128).transpose(3, 0, 1, 2))
    put("cb", np.asarray(inp["conv_b"]).reshape(D, 22, 128).transpose(2, 0, 1))
    put("gf", np.asarray(inp["norm_final"]).reshape(8, 128).transpose(1, 0))
    m["par"] = par
    return m


def host_shared(cfg, inp):
    D = cfg.DEPTH
    m = {}
    m["const"] = make_consts()
    m["csp"] = rope_tables(np.arange(cfg.LP))
    m["css"] = np.ascontiguousarray(np.tile(rope_tables(cfg.PAST + np.arange(32)), (1, 1, 2)))
    w_in = np.asarray(inp["w_in"])
    m["w_in"] = w_in
    kr = w_in[:, :, 640:704]
    krp = perm64(kr)
    m["w_inC"] = np.ascontiguousarray(np.concatenate([w_in[:, :, 0:640], kr, kr, krp, krp], 2))
    wuq = np.asarray(inp["w_uq"]).reshape(D, 384, 8, 192)
    nope = wuq[..., 0:128].reshape(D, 384, 1024)
    ropew = wuq[..., 128:192]
    m["w_uqR"] = np.ascontiguousarray(np.concatenate(
        [nope, ropew.reshape(D, 384, 512), perm64(ropew).reshape(D, 384, 512)], 2))
    wukv = np.asarray(inp["w_ukv"]).reshape(D, 256, 8, 256)
    m["w_ukvR"] = np.ascontiguousarray(np.concatenate(
        [wukv[..., 0:128].reshape(D, 256, 1024), wukv[..., 128:256].reshape(D, 256, 1024)], 2))
    m["w_pa"] = np.asarray(inp["w_proj_a"])
    m["w_pb"] = np.asarray(inp["w_proj_b"])
    m["w_o"] = np.asarray(inp["w_out"])
    m["w_up"] = np.asarray(inp["w_up"])
    m["w_dn"] = np.asarray(inp["w_down"])
    return m


def host_gather(cfg, results):
    D, LP, NSEQ = cfg.DEPTH, cfg.LP, cfg.NSEQ
    nb = NCORES * NSEQ
    y_p = np.empty((nb, LP, 1024), np.float32)
    y_s = np.empty((NCORES * 2, 32, 1024), np.float32)
    lat_p = np.empty((D, nb, LP, 256), np.float32)
    kpe_p = np.empty((D, nb, LP, 64), np.float32)
    hs_p = np.empty((D, nb, 4, 128, 128), np.float32)
    cv_p = np.empty((D, nb, 2, 2816), np.float32)
    lat_s = np.empty((D, NCORES * 2, 32, 256), np.float32)
    kpe_s = np.empty((D, NCORES * 2, 32, 64), np.float32)
    hs_s = np.empty((D, NCORES * 2, 4, 128, 128), np.float32)
    cv_s = np.empty((D, NCORES * 2, 2, 2816), np.float32)
    for c, r in enumerate(results):
        ps = slice(c * NSEQ, (c + 1) * NSEQ)
        ss = slice(c * 2, c * 2 + 2)
        y_p[ps] = r["o_yp"].reshape(NSEQ, 1024, LP).transpose(0, 2, 1)
        y_s[ss] = r["o_ys"].reshape(1024, 2, 32).transpose(1, 2, 0)
        lat_p[:, ps] = r["o_latp"].reshape(D, NSEQ, 256, LP).transpose(0, 1, 3, 2)
        kpe_p[:, ps] = r["o_kpep"].transpose(0, 1, 3, 2)
        hs_p[:, ps] = r["o_hsp"].transpose(0, 1, 3, 2, 4)
        cv_p[:, ps] = r["o_cvp"].transpose(0, 1, 4, 3, 2).reshape(D, NSEQ, 2, 2816)
        lat_s[:, ss] = r["o_lats"].reshape(D, 256, 2, 32).transpose(0, 2, 3, 1)
        kpe_s[:, ss] = r["o_kpes"].reshape(D, 64, 2, 32).transpose(0, 2, 3, 1)
        hs_s[:, ss] = r["o_hss"].transpose(0, 1, 3, 2, 4)
        cv_s[:, ss] = r["o_cvs"].transpose(0, 1, 4, 3, 2).reshape(D, 2, 2, 2816)
    return (y_p, y_s, lat_p, kpe_p, hs_p, cv_p, lat_s, kpe_s, hs_s, cv_s)


def run(cfg, inp, ncores=NCORES, trace=False):
    nc = build_program(cfg)
    shared = host_shared(cfg, inp)
    in_maps = []
    for c in range(ncores):
        m = dict(shared)
        m.update(host_prep(cfg, inp, c))
        in_maps.append(m)
    res = run_bass_kernel_spmd(nc, in_maps, core_ids=list(range(ncores)), trace=trace)
    return res


def kernel(**inputs):
    cfg = Cfg()
    res = run(cfg, inputs)
    return host_gather(cfg, res.results)
```

```python
import math
from contextlib import ExitStack

import numpy as np
import concourse.bass as bass
import concourse.mybir as mybir
from concourse.bass_utils import run_bass_kernel_spmd

F32 = mybir.dt.float32
BF16 = mybir.dt.bfloat16
AF = mybir.ActivationFunctionType
ALU = mybir.AluOpType
AX = mybir.AxisListType

EPS = 1e-6
NCORES = 8


class Res:
    __slots__ = ("w", "rd")

    def __init__(self):
        self.w = None
        self.rd = []


class Op:
    __slots__ = ("eng", "fn", "deps", "sig", "sigidx", "dma", "dcnt", "waits", "alldeps", "cost", "lat",
                 "idx", "nd", "succ", "rt")


class Prog:
    ENGS = ("pe", "act", "dve", "pool", "sp")

    DEFCOST = {"pe": 0.45, "act": 0.6, "dve": 0.6, "pool": 1.1, "sp": 0.1}

    def __init__(self):
        self.ops = {e: [] for e in self.ENGS}
        self.dma_counts = {}
        self.order = []

    def add(self, eng, fn, reads=(), writes=(), dma=None, ndma=1, cost=None, lat=None):
        op = Op()
        op.eng = eng
        op.fn = fn
        op.cost = cost if cost is not None else (0.1 * ndma if dma is not None else self.DEFCOST[eng])
        op.lat = lat if lat is not None else (3.0 if dma is not None else 0.0)
        op.idx = len(self.order)
        self.order.append(op)
        op.sig = False
        op.sigidx = 0
        op.dma = dma
        op.dcnt = 0
        if dma is not None:
            c = self.dma_counts.get(dma, 0) + 16 * ndma
            self.dma_counts[dma] = c
            op.dcnt = c
        deps = {}
        for r in reads:
            if r.w is not None:
                deps[r.w] = True
        for r in writes:
            if r.w is not None:
                deps.setdefault(r.w, False)
            for o in r.rd:
                deps.setdefault(o, False)
        keep = []
        op.alldeps = [d for d in deps if d is not op]
        for d, raw in deps.items():
            if d is op:
                continue
            if d.dma is not None:
                keep.append(d)
            elif d.eng == eng:
                if eng in ("pe", "sp"):
                    continue
                d.sig = True
                keep.append(d)
            else:
                d.sig = True
                keep.append(d)
        op.deps = keep
        for r in reads:
            r.rd.append(op)
        for r in writes:
            r.w = op
            r.rd = []
        self.ops[eng].append(op)
        return op

    def schedule(self):
        import heapq
        for op in self.order:
            op.nd = len(op.alldeps)
            op.succ = []
            op.rt = 0.0
        for op in self.order:
            for d in op.alldeps:
                d.succ.append(op)
        heaps = {e: [] for e in self.ENGS}
        for op in self.order:
            if op.nd == 0:
                heapq.heappush(heaps[op.eng], (0.0, op.idx, op))
        etime = {e: 0.0 for e in self.ENGS}
        new = {e: [] for e in self.ENGS}
        left = len(self.order)
        while left:
            best = None
            for e in self.ENGS:
                h = heaps[e]
                if h:
                    rt, idx, op = h[0]
                    st = rt if rt > etime[e] else etime[e]
                    if best is None or (st, idx) < best[0]:
                        best = ((st, idx), e)
            (st, idx), e = best
            rt, idx, op = heapq.heappop(heaps[e])
            etime[e] = st + op.cost
            done = st + op.cost + op.lat
            new[e].append(op)
            left -= 1
            for s_ in op.succ:
                t = done + (0.25 if s_.eng != e else 0.1)
                if t > s_.rt:
                    s_.rt = t
                s_.nd -= 1
                if s_.nd == 0:
                    heapq.heappush(heaps[s_.eng], (s_.rt, s_.idx, s_))
        last = {}
        for e in self.ENGS:
            for op in new[e]:
                if op.dma is not None:
                    assert last.get(op.dma, 0) < op.dcnt, ("dma order", op.dma)
                    last[op.dma] = op.dcnt
        self.ops = new
        self.makespan = max(etime.values())

    def finalize(self):
        for e in self.ENGS:
            n = 0
            for op in self.ops[e]:
                if op.sig and op.dma is None:
                    n += 1
                    op.sigidx = n
        for e in self.ENGS:
            for op in self.ops[e]:
                w = {}
                for d in op.deps:
                    if d.dma is not None:
                        k = ("dma", d.dma)
                        v = d.dcnt
                    else:
                        k = ("eng", d.eng)
                        v = d.sigidx
                    if w.get(k, 0) < v:
                        w[k] = v
                op.waits = w

    def emit(self, eng, e, semof):
        waited = {}
        for op in self.ops[eng]:
            for k, v in op.waits.items():
                if waited.get(k, 0) < v:
                    e.wait_ge(semof(k), v)
                    waited[k] = v
            r = op.fn(e)
            if op.dma is not None:
                lst = r if isinstance(r, (list, tuple)) else [r]
                for ins in lst:
                    ins.then_inc(semof(("dma", op.dma)), 16)
            elif op.sig:
                ins = r[-1] if isinstance(r, (list, tuple)) else r
                ins.then_inc(semof(("eng", eng)), 1)
        if eng in ("sp", "pool"):
            for k, c in self.dma_counts.items():
                if self.dma_eng.get(k) == eng:
                    e.wait_ge(semof(("dma", k)), c)


GRAN = 512


class Buf:
    def __init__(self, ar, gran, off, nbytes, dtype, shape):
        self.off = off
        self.nbytes = nbytes
        self.dtype = dtype
        esz = 4 if dtype == F32 else 2
        self.esz = esz
        n = nbytes // esz
        v = ar[:, off // 2:(off + nbytes) // 2]
        if dtype == F32:
            v = v.bitcast(F32)
        self.flat = v
        self.shape = shape
        if len(shape) == 1:
            self.v = v
        elif len(shape) == 2:
            self.v = v.rearrange("p (a b) -> p a b", b=shape[1])
        else:
            self.v = v.rearrange("p (a b c) -> p a b c", b=shape[1], c=shape[2])
        self.gran = gran
        self.n = n

    def r(self, lo=0, hi=None):
        if hi is None:
            hi = self.n
        b0 = (self.off + lo * self.esz) // GRAN
        b1 = (self.off + hi * self.esz - 1) // GRAN
        return self.gran[b0:b1 + 1]

    def rs(self, i, inner):
        return self.r(i * inner, (i + 1) * inner)


class Cfg:
    def __init__(self, LP=4096, DEPTH=4, PAST=1024, NSEQ=2, wslots=3):
        self.LP = LP
        self.DEPTH = DEPTH
        self.PAST = PAST
        self.NSEQ = NSEQ
        self.NT = LP // 512
        self.wslots = wslots
        D = DEPTH
        o = 0
        self.po = {}
        for name, n in (("g1", D * 8), ("g2", D * 8), ("gq", D * 3), ("gkv", D * 2), ("gh", D),
                        ("lbl", 4 * D), ("cw", D * 3 * 22), ("cb", D * 22), ("gf", 8)):
            self.po[name] = (o, n)
            o += n
        self.NPAR = o


def build_program(cfg):
    LP, DEPTH, PAST, NSEQ, NT = cfg.LP, cfg.DEPTH, cfg.PAST, cfg.NSEQ, cfg.NT
    NPG = PAST // 512
    nc = bass.Bass("TRN2", target_bir_lowering=False)
    P = Prog()
    P.dma_eng = {}

    def dram(name, shape, dt=F32, kind="ExternalInput"):
        return nc.dram_tensor(name, list(shape), dt, kind=kind).ap()

    d_xp = dram("xp", [NSEQ, 8, 128, LP])
    d_xs = dram("xs", [8, 128, 64])
    d_latp = dram("latp", [DEPTH, 2, 2, 128, PAST])
    d_krp = dram("krp", [DEPTH, 2, 128, PAST])
    d_sh = dram("sh", [DEPTH, 2, 128, 4, 128])
    d_scv = dram("scv", [DEPTH, 2, 128, 22, 2])
    d_par = dram("par", [128, cfg.NPAR])
    d_const = dram("const", [128, 128 + 128 + 512 + 64])
    d_const2 = dram("const2", [128, 128 + 128 + 512])
    d_csp = dram("csp", [2, 128, LP])
    d_css = dram("css", [2, 128, 64])
    d_win = dram("w_in", [DEPTH, 1024, 4800])
    d_winC = dram("w_inC", [DEPTH, 1024, 896])
    d_wuq = dram("w_uqR", [DEPTH, 384, 2048])
    d_wukv = dram("w_ukvR", [DEPTH, 256, 2048])
    d_wpa = dram("w_pa", [DEPTH, 1024, 1024])
    d_wpb = dram("w_pb", [DEPTH, 512, 1024])
    d_wo = dram("w_o", [DEPTH, 1024, 1024])
    d_wup = dram("w_up", [DEPTH, 1024, 5632])
    d_wdn = dram("w_dn", [DEPTH, 2816, 1024])
    o_yp = dram("o_yp", [NSEQ, 8, 128, LP], kind="ExternalOutput")
    o_ys = dram("o_ys", [8, 128, 64], kind="ExternalOutput")
    o_latp = dram("o_latp", [DEPTH, NSEQ, 2, 128, LP], kind="ExternalOutput")
    o_kpep = dram("o_kpep", [DEPTH, NSEQ, 64, LP], kind="ExternalOutput")
    o_hsp = dram("o_hsp", [DEPTH, NSEQ, 128, 4, 128], kind="ExternalOutput")
    o_cvp = dram("o_cvp", [DEPTH, NSEQ, 128, 22, 2], kind="ExternalOutput")
    o_lats = dram("o_lats", [DEPTH, 2, 128, 64], kind="ExternalOutput")
    o_kpes = dram("o_kpes", [DEPTH, 64, 64], kind="ExternalOutput")
    o_hss = dram("o_hss", [DEPTH, 2, 128, 4, 128], kind="ExternalOutput")
    o_cvs = dram("o_cvs", [DEPTH, 2, 128, 22, 2], kind="ExternalOutput")
    s_in = dram("s_in", [DEPTH, 1024, 4096], BF16, "Internal")
    s_inC = dram("s_inC", [DEPTH, 1024, 896], BF16, "Internal")
    s_uq = dram("s_uq", [DEPTH, 384, 2048], BF16, "Internal")
    s_ukv = dram("s_ukv", [DEPTH, 256, 2048], BF16, "Internal")
    s_pa = dram("s_pa", [DEPTH, 1024, 1024], BF16, "Internal")
    s_pb = dram("s_pb", [DEPTH, 512, 1024], BF16, "Internal")
    s_wo = dram("s_wo", [DEPTH, 1024, 1024], BF16, "Internal")
    s_up = dram("s_up", [DEPTH, 1024, 5632], BF16, "Internal")
    s_dn = dram("s_dn", [DEPTH, 2816, 1024], BF16, "Internal")
    s_kc = dram("s_kc", [NSEQ, DEPTH, 128, NT * 4, 8, 128], BF16, "Internal")
    s_vc = dram("s_vc", [NSEQ, DEPTH, NT * 4, 128, 1024], BF16, "Internal")
    s_pc = dram("s_pc", [NSEQ, DEPTH, 128, LP], BF16, "Internal")

    es = ExitStack()
    ARENA_BYTES = 207 * 1024
    ar = es.enter_context(nc.sbuf_tensor("arena", [128, ARENA_BYTES // 2], BF16))
    gran = [Res() for _ in range(ARENA_BYTES // GRAN + 1)]
    top = [0]

    def alloc(shape, dt, at=None):
        esz = 4 if dt == F32 else 2
        n = 1
        for s in shape:
            n *= s
        nb = n * esz
        nb_al = (nb + GRAN - 1) // GRAN * GRAN
        if at is None:
            off = top[0]
            top[0] += nb_al
        else:
            off = at[0]
            at[0] += nb_al
        if off + nb_al > ARENA_BYTES:
            raise AssertionError(("arena overflow", off, nb_al, shape, "OV", globals().get("_OV")))
        return Buf(ar, gran, off, nb, dt, shape)

    nws = cfg.wslots
    xT = alloc([8, 512], F32)
    hT = alloc([8, 512], BF16)
    WS = [alloc([8192], BF16) for _ in range(nws)]
    rstd = [alloc([512], F32) for _ in range(2)]
    gates = alloc([16, 512], BF16)
    obT = alloc([4, 512], BF16)
    Sst = alloc([DEPTH, 4, 128], F32)
    cs = alloc([2, 512], F32)
    identb = alloc([128], BF16)
    onesb = alloc([128], BF16)
    onesf = alloc([128], F32)
    constf = alloc([128 + 128 + 512 + 64], F32)
    mUb = alloc([128], BF16)
    mVb = alloc([512], BF16)
    par = alloc([cfg.NPAR], F32)
    g1s = alloc([DEPTH, 8], F32)
    g2s = alloc([DEPTH, 8], F32)
    gqs = alloc([DEPTH, 3], F32)
    gkvs = alloc([DEPTH, 2], F32)
    ghs = alloc([DEPTH], F32)
    gfs = alloc([8], F32)
    lbe = alloc([4, DEPTH], F32)
    lbm = alloc([4], F32)
    lbv = alloc([4, DEPTH], F32)
    oml = alloc([4, DEPTH], F32)
    ctail = alloc([DEPTH, 22, 2], F32)
    shalo = alloc([2, 22, 2], F32)
    stail = alloc([2, 22, 2], F32)
    S0s = alloc([2, 4, 128], F32)
    dd = alloc([4, 8], F32)
    epsb = alloc([4], F32)
    OV = top[0]
    globals()['_OV'] = OV
    a = [OV]
    TT2 = [[alloc([512], F32, a) for _ in range(7)] for _ in range(2)]
    qeb = alloc([2, 512], BF16, a)
    keb = alloc([2, 512], BF16, a)
    kdb = alloc([2, 512], BF16, a)
    qbb = alloc([4, 512], BF16, a)
    Vtok = alloc([4, 512], BF16, a)
    kdTokE = alloc([2, 4, 128], BF16, a)
    kdTokO = alloc([2, 4, 128], BF16, a)
    Am = alloc([4, 4, 128], BF16, a)
    Sb = alloc([4, 8, 128], BF16, a)
    sghg = alloc([4, 512], BF16, a)
    sqo = alloc([512], BF16, a)
    to = alloc([512], F32, a)
    endH = a[0]
    a = [OV]
    Qn = alloc([8, 512], BF16, a)
    Qp = alloc([8, 512], BF16, a)
    Kcur = alloc([4, 8, 128], BF16, a)
    Vcur = alloc([4, 1024], BF16, a)
    Pcur = alloc([512], BF16, a)
    M12 = a[0]
    a = [M12]
    cqf = alloc([3, 512], F32, a)
    cqn = alloc([3, 512], BF16, a)
    ckvf = alloc([2, 512], F32, a)
    latb = alloc([2, 512], BF16, a)
    kpef = alloc([512], F32, a)
    tmp1 = alloc([512], F32, a)
    tmp2 = alloc([512], F32, a)
    sqm = alloc([3, 512], BF16, a)
    endM1 = a[0]
    a = [M12]
    KS = []
    for i in range(2):
        KS.append((alloc([4, 4, 128], BF16, a), alloc([4, 512], BF16, a), alloc([512], BF16, a)))
    Pt = [alloc([512], BF16, a) for _ in range(4)]
    oaT = alloc([8, 512], BF16, a)
    Pacc = alloc([4, 512], F32, a)
    rsb = alloc([512], F32, a)
    al_ = [KS[0][0].off]
    mta = [alloc([512], F32, al_) for _ in range(2)]
    mtb = [alloc([512], F32, al_) for _ in range(2)]
    lpf = alloc([2, 512], F32, [KS[1][1].off])
    lpb = alloc([2, 512], BF16, [oaT.off])
    kpf = alloc([512], F32, [oaT.off + 2048])
    Vnew = alloc([2, 1024], BF16, [KS[0][1].off])
    endM2 = a[0]
    a = [OV]
    gT = alloc([22, 512], BF16, a)
    aext = [alloc([520], F32, a) for _ in range(4)]
    cvb = [alloc([512], F32, a) for _ in range(4)]
    ub = [alloc([512], F32, a) for _ in range(4)]
    sbf = [alloc([512], F32, a) for _ in range(4)]
    endF = a[0]
    import os as _os0
    if _os0.environ.get("KVERB"):
        print("ARENA OV", OV, "H", endH, "M12", M12, "M1", endM1, "M2", endM2, "F", endF, "cap", ARENA_BYTES)
    assert max(endH, endM1, endM2, endF) <= ARENA_BYTES, (endH, endM1, endM2, endF)

    PSB = [es.enter_context(nc.psum_tensor(f"ps{i}", [128, 512], F32)) for i in range(8)]
    PSR = [Res() for _ in range(8)]

    dres = {}

    def DR(key):
        r = dres.get(key)
        if r is None:
            r = dres[key] = Res()
        return r

    def dma(eng, key, fn, reads, writes, n=1, lat=None):
        P.dma_eng[key] = eng
        return P.add(eng, fn, reads, writes, dma=key, ndma=n, lat=lat)

    def act(fn, reads, writes, cost=None):
        return P.add("act", fn, reads, writes, cost=cost)

    def dve(fn, reads, writes, cost=None):
        return P.add("dve", fn, reads, writes, cost=cost)

    def pool(fn, reads, writes, cost=None):
        return P.add("pool", fn, reads, writes, cost=cost)

    def pe(fn, reads, writes, cost=None):
        return P.add("pe", fn, reads, writes, cost=cost)

    def mm_group(out_ap, pairs, reads, writes, start=True, stop=True):
        def fn(e, out_ap=out_ap, pairs=pairs, start=start, stop=stop):
            ins = None
            n = len(pairs)
            for i, (l, r) in enumerate(pairs):
                ins = e.matmul(out_ap, l, r, start=(start and i == 0), stop=(stop and i == n - 1))
            return ins
        ncol = out_ap.shape[-1]
        return pe(fn, reads, writes, cost=len(pairs) * max(0.07, 0.22 * ncol / 512.0))

    dma("sp", "par", lambda e: e.dma_start(out=par.v, in_=d_par[:, :]), [], par.r())
    dma("sp", "const", lambda e: e.dma_start(out=constf.v, in_=d_const[:, :]), [], constf.r())
    maskP_v = constf.v[:, 0:128]
    maskS_v = constf.v[:, 128:256]
    resetP_v = constf.v[:, 256:768]
    resetS_v = constf.v[:, 768:832]
    const2 = alloc([768], F32, [OV])
    dma("sp", "const2", lambda e: e.dma_start(out=const2.v, in_=d_const2[:, :]), [], const2.r())
    dve(lambda e: e.tensor_copy(out=identb.v, in_=const2.v[:, 0:128]), const2.r(), identb.r())
    dve(lambda e: e.tensor_copy(out=mUb.v, in_=const2.v[:, 128:256]), const2.r(), mUb.r())
    dve(lambda e: e.tensor_copy(out=mVb.v, in_=const2.v[:, 256:768]), const2.r(), mVb.r())
    dve(lambda e: e.memset(onesb.v, 1.0), [], onesb.r())
    dve(lambda e: e.memset(onesf.v, 1.0), [], onesf.r())
    epsc = {}
    for i_, n_ in enumerate((1024.0, 384.0, 256.0, 128.0)):
        dve(lambda e, i_=i_, n_=n_: e.memset(epsb.v[:, i_:i_ + 1], n_ * EPS), [], epsb.r())
        epsc[n_ * EPS] = epsb.v[:, i_:i_ + 1]
    dve(lambda e: e.memset(Sst.flat, 0.0), [], Sst.r())
    dve(lambda e: e.memset(ctail.flat, 0.0), [], ctail.r())

    def pv(name):
        o, n = cfg.po[name]
        return par.flat[:, o:o + n]

    def scale_par(dst, name, s):
        dve(lambda e: e.tensor_scalar(out=dst.flat, in0=pv(name), scalar1=float(s), scalar2=None,
                                      op0=ALU.mult), par.r(), dst.r())

    scale_par(g1s, "g1", math.sqrt(1024.0))
    scale_par(g2s, "g2", math.sqrt(1024.0))
    scale_par(gqs, "gq", math.sqrt(384.0))
    scale_par(gkvs, "gkv", math.sqrt(256.0))
    scale_par(ghs, "gh", math.sqrt(128.0))
    scale_par(gfs, "gf", math.sqrt(1024.0))
    lbl_v = pv("lbl").rearrange("p (h l) -> p h l", l=DEPTH)
    dve(lambda e: e.tensor_reduce(out=lbm.v, in_=lbl_v, axis=AX.X, op=ALU.max), par.r(), lbm.r())
    dve(lambda e: e.tensor_tensor(out=lbe.v, in0=lbl_v,
                                  in1=lbm.v.unsqueeze(2).to_broadcast([128, 4, DEPTH]), op=ALU.subtract),
        par.r() + lbm.r(), lbe.r())
    act(lambda e: e.activation(out=lbe.flat, in_=lbe.flat, func=AF.Exp), lbe.r(), lbe.r())
    dve(lambda e: e.tensor_reduce(out=lbm.v, in_=lbe.v, axis=AX.X, op=ALU.add), lbe.r(), lbm.r())
    dve(lambda e: e.reciprocal(out=lbm.v, in_=lbm.v), lbm.r(), lbm.r())
    dve(lambda e: e.tensor_tensor(out=lbe.v, in0=lbe.v,
                                  in1=lbm.v.unsqueeze(2).to_broadcast([128, 4, DEPTH]), op=ALU.mult),
        lbe.r() + lbm.r(), lbe.r())
    dve(lambda e: e.memset(lbv.flat, 0.0), [], lbv.r())
    for l in range(1, DEPTH):
        dve(lambda e, l=l: e.tensor_tensor(out=lbv.v[:, :, l], in0=lbv.v[:, :, l - 1], in1=lbe.v[:, :, l],
                                           op=ALU.add), lbv.r() + lbe.r(), lbv.r())
    dve(lambda e: e.tensor_scalar(out=oml.flat, in0=lbv.flat, scalar1=-1.0, scalar2=1.0,
                                  op0=ALU.mult, op1=ALU.add), lbv.r(), oml.r())

    def cast_layer(l):
        def fn(e, l=l):
            out = []

            def rows(dst, src, nrows, step=128):
                for r0 in range(0, nrows, step):
                    out.append(e.dma_start(out=dst[r0:r0 + step, :], in_=src[r0:r0 + step, :]))
            rows(s_inC[l], d_winC[l], 1024)
            rows(s_in[l], d_win[l][:, 704:4800], 1024)
            rows(s_uq[l], d_wuq[l], 384)
            rows(s_ukv[l], d_wukv[l], 256)
            rows(s_pa[l], d_wpa[l], 1024)
            rows(s_pb[l], d_wpb[l], 512)
            rows(s_wo[l], d_wo[l], 1024)
            rows(s_up[l], d_wup[l], 1024)
            rows(s_dn[l], d_wdn[l], 2816)
            return out
        n = 8 + 8 + 3 + 2 + 8 + 4 + 8 + 8 + 22
        dma("pool", f"cast{l}", fn, [], [DR(("w", l))], n=n, lat=400.0)

    for l in range(DEPTH):
        cast_layer(l)

    wctr = [0]

    def load_piece(l, src_fn, nel):
        i = wctr[0] % nws
        wctr[0] += 1
        slot = WS[i]

        def fn(e, slot=slot):
            return [e.dma_start(out=d, in_=s) for d, s in src_fn(slot.flat)]
        ncalls = len(src_fn(slot.flat))
        dma("sp", f"w{i}", fn, [DR(("w", l))], slot.r(0, nel), n=ncalls, lat=3.0 + nel * 256 / 150e3)
        return slot

    def wview(src2d, kc, c0, c1):
        return src2d.rearrange("(k p) n -> p k n", p=128)[:, :, c0:c1]

    def piece_simple(l, src2d, kc, c0, c1):
        w = c1 - c0

        def src_fn(flat):
            return [(flat[:, 0:kc * w].rearrange("p (k n) -> p k n", n=w), wview(src2d, kc, c0, c1))]
        slot = load_piece(l, src_fn, kc * w)
        return slot, slot.flat[:, 0:kc * w].rearrange("p (k n) -> p k n", n=w)

    bctr = [0]

    def next_bank(banks):
        b = banks[bctr[0] % len(banks)]
        bctr[0] += 1
        return b

    def rstd_from(buf, bank, c, N):
        act(lambda e: e.activation(out=buf.v[:, 0:N], in_=PSB[bank][:, 0:N], func=AF.Ln, bias=epsc[c], scale=1.0),
            [PSR[bank]] + epsb.r(), buf.r())
        act(lambda e: e.activation(out=buf.v[:, 0:N], in_=buf.v[:, 0:N], func=AF.Exp, scale=-0.5), buf.r(), buf.r())

    SCALE = 1.0 / math.sqrt(192.0)
    GC = 2.0 * 0.7978845608028654

    def rmsnorm_x(N, gs_col, out_fn, out_res_fn):
        for c in range(8):
            act(lambda e, c=c: e.activation(out=hT.v[:, c, 0:N], in_=xT.v[:, c, 0:N], func=AF.Square),
                xT.rs(c, 512), hT.rs(c, 512))
        b = 3
        mm_group(PSB[b][:, 0:N], [(onesb.v, hT.v[:, c, 0:N]) for c in range(8)], hT.r() + onesb.r(), [PSR[b]])
        rb = rstd[0]
        rstd_from(rb, b, 1024.0 * EPS, N)
        for c in range(8):
            dve(lambda e, c=c: e.scalar_tensor_tensor(out=out_fn(c), in0=xT.v[:, c, 0:N], scalar=gs_col(c),
                                                      in1=rb.v[:, 0:N], op0=ALU.mult, op1=ALU.mult),
                xT.rs(c, 512) + rb.r(), out_res_fn(c))

    import os as _os
    KSTOP = float(_os.environ.get("KSTOP", "99"))

    def tile_layer(l, N, kind, s=0, j=0):
        prompt = kind == "p"
        C = 64 if prompt else 32
        TB = 128 if prompt else 64
        NB = N // TB
        NCH = N // C
        cos_v = cs.v[:, 0, 0:N]
        sin_v = cs.v[:, 1, 0:N]
        mask_v = maskP_v if prompt else maskS_v[0:64, 0:64]
        reset_v = resetP_v if prompt else resetS_v
        dbanks = [0, 1]

        rmsnorm_x(N, lambda c: g1s.v[:, l, c:c + 1], lambda c: hT.v[:, c, 0:N], lambda c: hT.rs(c, 512))

        if KSTOP <= 1:
            return
        def dense(wv, kc_n, cols, rhs_fn, rhs_res, slot, banks=dbanks):
            b = next_bank(banks)
            mm_group(PSB[b][:, 0:N], [(wv[:, kc, cols[0]:cols[1]], rhs_fn(kc)) for kc in range(kc_n)],
                     rhs_res + slot.r(), [PSR[b]])
            return b

        hrhs = lambda kc: hT.v[:, kc, 0:N]

        slotA, wA = piece_simple(l, s_in[l], 8, 0, 1024)
        slotB, wB = piece_simple(l, s_in[l], 8, 1024, 2048)
        pool(lambda e: e.memset(kdTokE.v[C:TB], 0.0), [], kdTokE.r())
        pool(lambda e: e.memset(kdTokO.v[0:C], 0.0), [], kdTokO.r())
        if not prompt:
            dma("sp", "s0s", lambda e: [e.dma_start(out=S0s.v[:, q], in_=d_sh[l, q]) for q in range(2)],
                [], S0s.r(), n=2)
        def hgrn_head(h, T0, T1, T2, T3, T4, T5, T6):
            bq = dense(wA, 8, (h * 128, h * 128 + 128), hrhs, hT.r(), slotA)
            act(lambda e, bq=bq: e.activation(out=T0.v[:, 0:N], in_=PSB[bq][:, 0:N], func=AF.Silu),
                [PSR[bq]], T0.r())
            bf = dense(wA, 8, (512 + h * 128, 512 + h * 128 + 128), hrhs, hT.r(), slotA)
            act(lambda e, bf=bf: e.activation(out=T1.v[:, 0:N], in_=PSB[bf][:, 0:N], func=AF.Sigmoid),
                [PSR[bf]], T1.r())
            dve(lambda e, h=h: e.tensor_scalar(out=T1.v[:, 0:N], in0=T1.v[:, 0:N], scalar1=oml.v[:, h, l:l + 1],
                                               scalar2=lbv.v[:, h, l:l + 1], op0=ALU.mult, op1=ALU.add),
                T1.r() + oml.r() + lbv.r(), T1.r())
            act(lambda e: e.activation(out=T2.v[:, 0:N], in_=T1.v[:, 0:N], func=AF.Ln), T1.r(), T2.r())
            if KSTOP <= 1.1:
                return True
            dve(lambda e: e.tensor_tensor_scan(out=T3.v[:, 0:N], data0=reset_v, data1=T2.v[:, 0:N], initial=0.0,
                                               op0=ALU.mult, op1=ALU.add), T2.r() + constf.r(), T3.r())
            dve(lambda e: e.tensor_scalar(out=T1.v[:, 0:N], in0=T1.v[:, 0:N], scalar1=-1.0, scalar2=1.0,
                                          op0=ALU.mult, op1=ALU.add), T1.r(), T1.r())
            if KSTOP <= 1.2:
                return True
            b3 = T3.v[:, 0:N].rearrange("p (c t) -> p c t", t=C)
            rmid = b3[:, :, C // 2 - 1:C // 2].to_broadcast([128, NCH, C])
            rend = b3[:, :, C - 1:C].to_broadcast([128, NCH, C])
            dve(lambda e: e.tensor_tensor(out=T2.v[:, 0:N].rearrange("p (c t) -> p c t", t=C), in0=b3, in1=rmid,
                                          op=ALU.subtract), T3.r(), T2.r())
            dve(lambda e: e.tensor_tensor(out=T4.v[:, 0:N].rearrange("p (c t) -> p c t", t=C), in0=b3, in1=rend,
                                          op=ALU.subtract), T3.r(), T4.r())
            act(lambda e, h=h: e.activation(out=dd.v[:, h, 0:NCH].unsqueeze(2), in_=b3[:, :, C - 1:C], func=AF.Exp),
                T3.r(), dd.r())
            if KSTOP <= 1.3:
                return True
            hb = h % 2
            act(lambda e: e.activation(out=T5.v[:, 0:N], in_=T3.v[:, 0:N], func=AF.Exp), T3.r(), T5.r())
            dve(lambda e, h=h: e.scalar_tensor_tensor(out=qbb.v[:, h, 0:N], in0=T0.v[:, 0:N], scalar=128.0 ** -0.5,
                                                      in1=T5.v[:, 0:N], op0=ALU.mult, op1=ALU.mult),
                T0.r() + T5.r(), qbb.rs(h, 512))
            act(lambda e: e.activation(out=T6.v[:, 0:N], in_=T2.v[:, 0:N], func=AF.Exp), T2.r(), T6.r())
            dve(lambda e, hb=hb: e.scalar_tensor_tensor(out=qeb.v[:, hb, 0:N], in0=T0.v[:, 0:N], scalar=128.0 ** -0.5,
                                                        in1=T6.v[:, 0:N], op0=ALU.mult, op1=ALU.mult),
                T0.r() + T6.r(), qeb.rs(hb, 512))
            act(lambda e: e.activation(out=T5.v[:, 0:N], in_=T2.v[:, 0:N], func=AF.Exp, scale=-1.0), T2.r(), T5.r())
            dve(lambda e, hb=hb: e.tensor_tensor(out=keb.v[:, hb, 0:N], in0=T1.v[:, 0:N], in1=T5.v[:, 0:N],
                                                 op=ALU.mult), T1.r() + T5.r(), keb.rs(hb, 512))
            act(lambda e: e.activation(out=T6.v[:, 0:N], in_=T4.v[:, 0:N], func=AF.Exp, scale=-1.0), T4.r(), T6.r())
            dve(lambda e, hb=hb: e.tensor_tensor(out=kdb.v[:, hb, 0:N], in0=T1.v[:, 0:N], in1=T6.v[:, 0:N],
                                                 op=ALU.mult), T1.r() + T6.r(), kdb.rs(hb, 512))
            if KSTOP <= 1.4:
                return True
            if h == 0:
                for blk in range(NB):
                    b = next_bank(dbanks)
                    mm_group(PSB[b][0:TB, 0:512],
                             [(hT.v[:, kc, blk * TB:(blk + 1) * TB], wB[:, kc, 0:512]) for kc in range(8)],
                             hT.r() + slotB.r(), [PSR[b]])
                    act(lambda e, b=b, blk=blk: e.activation(out=Vtok.v[0:TB, blk, :], in_=PSB[b][0:TB, 0:512],
                                                             func=AF.Copy), [PSR[b]], Vtok.rs(blk, 512))
            if KSTOP <= 1.5:
                return True
            psb7 = PSB[7][:, :].bitcast(BF16)
            for blk in range(NB):
                pe(lambda e, blk=blk, hb=hb: e.transpose(psb7[0:TB, blk * 128:(blk + 1) * 128],
                                                         kdb.v[:, hb, blk * TB:(blk + 1) * TB], identb.v),
                   kdb.rs(hb, 512) + identb.r(), [PSR[7]])
            act(lambda e, h=h: e.activation(out=kdTokE.v[0:C, h % 2, 0:NB, :],
                                            in_=psb7[0:C, 0:NB * 128].rearrange("p (b k) -> p b k", k=128),
                                            func=AF.Copy), [PSR[7]], kdTokE.rs(h % 2, 512))
            act(lambda e, h=h: e.activation(out=kdTokO.v[C:TB, h % 2, 0:NB, :],
                                            in_=psb7[C:TB, 0:NB * 128].rearrange("p (b k) -> p b k", k=128),
                                            func=AF.Copy), [PSR[7]], kdTokO.rs(h % 2, 512))
            if KSTOP <= 1.6:
                return True
            for blk in range(NB):
                mm_group(PSB[6][0:TB, blk * 128:blk * 128 + TB],
                         [(keb.v[:, hb, blk * TB:(blk + 1) * TB], qeb.v[:, hb, blk * TB:(blk + 1) * TB])],
                         keb.rs(hb, 512) + qeb.rs(hb, 512), [PSR[6]])
            dve(lambda e, h=h: e.tensor_tensor(
                out=Am.v[0:TB, h, 0:NB, 0:TB],
                in0=PSB[6][0:TB, 0:NB * 128].rearrange("p (b t) -> p b t", t=128)[:, :, 0:TB],
                in1=mask_v.unsqueeze(1).to_broadcast([TB, NB, TB]), op=ALU.mult),
                [PSR[6]] + constf.r(), Am.rs(h, 512))
            if KSTOP <= 1.7:
                return True
            for c in range(NCH):
                blk = (c * C) // TB
                r0 = (c * C) % TB
                ub_ = 4 + c // 4
                kdX = kdTokE if r0 == 0 else kdTokO
                mm_group(PSB[ub_][:, (c % 4) * 128:(c % 4) * 128 + 128],
                         [(kdX.v[0:TB, h % 2, blk, :], Vtok.v[0:TB, blk, h * 128:(h + 1) * 128])],
                         kdX.rs(h % 2, 512) + Vtok.rs(blk, 512), [PSR[ub_]])
            if KSTOP <= 1.8:
                return True
            if prompt:
                Sh = Sst.v[:, l, h, :]
                Shr = Sst.r((l * 4 + h) * 128, (l * 4 + h + 1) * 128)
                if j == 0:
                    dve(lambda e, Sh=Sh: e.memset(Sh, 0.0), [], Shr)
                act(lambda e, h=h, Sh=Sh: e.activation(out=Sb.v[:, h, 0, :], in_=Sh, func=AF.Copy),
                    Shr, Sb.rs(h, 1024))
                for c in range(NCH):
                    ub_ = 4 + c // 4
                    dve(lambda e, h=h, c=c, ub_=ub_, Sh=Sh: e.scalar_tensor_tensor(
                        out=Sh, in0=Sh, scalar=dd.v[:, h, c:c + 1],
                        in1=PSB[ub_][:, (c % 4) * 128:(c % 4) * 128 + 128], op0=ALU.mult, op1=ALU.add),
                        Shr + dd.r() + [PSR[ub_]], Shr)
                    if c + 1 < NCH:
                        act(lambda e, h=h, c=c, Sh=Sh: e.activation(out=Sb.v[:, h, c + 1, :], in_=Sh, func=AF.Copy),
                            Shr, Sb.rs(h, 1024))
                if j == NT - 1:
                    dma("pool", f"o_hsp{l}_{h}", lambda e, h=h, Sh=Sh: e.dma_start(out=o_hsp[l, s][:, h, :], in_=Sh),
                        Shr, [])
            else:
                for q in range(2):
                    Sq = S0s.v[:, q, h, :]
                    Sqr = S0s.r()
                    act(lambda e, h=h, q=q, Sq=Sq: e.activation(out=Sb.v[:, h, q, :], in_=Sq, func=AF.Copy),
                        Sqr, Sb.rs(h, 1024))
                    dve(lambda e, h=h, q=q, Sq=Sq: e.scalar_tensor_tensor(
                        out=Sq, in0=Sq, scalar=dd.v[:, h, q:q + 1], in1=PSB[4][:, q * 128:q * 128 + 128],
                        op0=ALU.mult, op1=ALU.add), Sqr + dd.r() + [PSR[4]], Sqr)
            if KSTOP <= 1.9:
                return True
            bg = dense(wB, 8, (512 + h * 128, 512 + h * 128 + 128), hrhs, hT.r(), slotB)
            act(lambda e, bg=bg, h=h: e.activation(out=sghg.v[:, h, 0:N], in_=PSB[bg][:, 0:N], func=AF.Silu),
                [PSR[bg]], sghg.rs(h, 512))
        for h in range(4):
            if hgrn_head(h, *TT2[h % 2]):
                return
        if KSTOP <= 2:
            return
        if not prompt:
            dma("pool", "o_hss", lambda e: [e.dma_start(out=o_hss[l, q], in_=S0s.v[:, q]) for q in range(2)],
                S0s.r(), [], n=2)

        slotD, wD = piece_simple(l, s_in[l], 8, 2048, 3072)
        for oc in range(8):
            b = dense(wD, 8, (oc * 128, oc * 128 + 128), hrhs, hT.r(), slotD)
            act(lambda e, b=b, oc=oc: e.activation(out=gates.v[:, oc, 0:N], in_=PSB[b][:, 0:N], func=AF.Sigmoid),
                [PSR[b]], gates.rs(oc, 512))
        slotE, wE = piece_simple(l, s_in[l], 8, 3072, 4096)
        for oc in range(8):
            b = dense(wE, 8, (oc * 128, oc * 128 + 128), hrhs, hT.r(), slotE)
            act(lambda e, b=b, oc=oc: e.activation(out=gates.v[:, 8 + oc, 0:N], in_=PSB[b][:, 0:N], func=AF.Sigmoid),
                [PSR[b]], gates.rs(8 + oc, 512))

        if KSTOP <= 3:
            return
        for h in range(4):
            ob_ = 2
            for blk in range(NB):
                def fn(e, h=h, blk=blk):
                    ins = e.matmul(PSB[ob_][:, blk * TB:(blk + 1) * TB], Vtok.v[0:TB, blk, h * 128:(h + 1) * 128],
                                   Am.v[0:TB, h, blk, 0:TB], start=(blk == 0), stop=False, skip_group_check=True)
                    ncb = TB // C
                    for ci in range(ncb):
                        c = blk * ncb + ci
                        ins = e.matmul(PSB[ob_][:, c * C:(c + 1) * C], Sb.v[:, h, c, :], qbb.v[:, h, c * C:(c + 1) * C],
                                       start=False, stop=(blk == NB - 1 and ci == ncb - 1), skip_group_check=True)
                    return ins
                pe(fn, Vtok.rs(blk, 512) + Am.rs(h, 512) + Sb.rs(h, 1024) + qbb.rs(h, 512), [PSR[ob_]])
            act(lambda e: e.activation(out=sqo.v[:, 0:N], in_=PSB[ob_][:, 0:N], func=AF.Square), [PSR[ob_]], sqo.r())
            mm_group(PSB[3][:, 0:N], [(onesb.v, sqo.v[:, 0:N])], sqo.r() + onesb.r(), [PSR[3]])
            rb = rstd[1]
            rstd_from(rb, 3, 128.0 * EPS, N)
            dve(lambda e: e.tensor_tensor(out=to.v[:, 0:N], in0=PSB[ob_][:, 0:N], in1=rb.v[:, 0:N], op=ALU.mult),
                [PSR[ob_]] + rb.r(), to.r())
            dve(lambda e, h=h: e.scalar_tensor_tensor(out=obT.v[:, h, 0:N], in0=to.v[:, 0:N], scalar=ghs.v[:, l:l + 1],
                                                      in1=sghg.v[:, h, 0:N], op0=ALU.mult, op1=ALU.mult),
                to.r() + ghs.r() + sghg.rs(h, 512), obT.rs(h, 512))

        if KSTOP <= 4:
            return
        slotC, wC = piece_simple(l, s_inC[l], 8, 0, 896)
        for c in range(3):
            b = dense(wC, 8, (c * 128, c * 128 + 128), hrhs, hT.r(), slotC)
            act(lambda e, b=b, c=c: e.activation(out=cqf.v[:, c, 0:N], in_=PSB[b][:, 0:N], func=AF.Copy),
                [PSR[b]], cqf.rs(c, 512))
            act(lambda e, b=b, c=c: e.activation(out=sqm.v[:, c, 0:N], in_=PSB[b][:, 0:N], func=AF.Square),
                [PSR[b]], sqm.rs(c, 512))
        mm_group(PSB[3][:, 0:N], [(onesb.v, sqm.v[:, c, 0:N]) for c in range(3)], sqm.r() + onesb.r(), [PSR[3]])
        rq = rstd[0]
        rstd_from(rq, 3, 384.0 * EPS, N)
        for c in range(3):
            dve(lambda e, c=c: e.scalar_tensor_tensor(out=cqn.v[:, c, 0:N], in0=cqf.v[:, c, 0:N],
                                                      scalar=gqs.v[:, l, c:c + 1], in1=rq.v[:, 0:N],
                                                      op0=ALU.mult, op1=ALU.mult),
                cqf.rs(c, 512) + rq.r() + gqs.r(), cqn.rs(c, 512))
        for c in range(2):
            b = dense(wC, 8, (384 + c * 128, 384 + c * 128 + 128), hrhs, hT.r(), slotC)
            act(lambda e, b=b, c=c: e.activation(out=ckvf.v[:, c, 0:N], in_=PSB[b][:, 0:N], func=AF.Copy),
                [PSR[b]], ckvf.rs(c, 512))
            act(lambda e, b=b, c=c: e.activation(out=sqm.v[:, c, 0:N], in_=PSB[b][:, 0:N], func=AF.Square),
                [PSR[b]], sqm.rs(c, 512))
        mm_group(PSB[3][:, 0:N], [(onesb.v, sqm.v[:, c, 0:N]) for c in range(2)], sqm.r() + onesb.r(), [PSR[3]])
        rk = rstd[1]
        rstd_from(rk, 3, 256.0 * EPS, N)
        for c in range(2):
            dve(lambda e, c=c: e.scalar_tensor_tensor(out=ckvf.v[:, c, 0:N], in0=ckvf.v[:, c, 0:N],
                                                      scalar=gkvs.v[:, l, c:c + 1], in1=rk.v[:, 0:N],
                                                      op0=ALU.mult, op1=ALU.mult),
                ckvf.rs(c, 512) + rk.r() + gkvs.r(), ckvf.rs(c, 512))
            act(lambda e, c=c: e.activation(out=latb.v[:, c, 0:N], in_=ckvf.v[:, c, 0:N], func=AF.Copy),
                ckvf.rs(c, 512), latb.rs(c, 512))
        if prompt:
            dma("pool", "o_lat", lambda e: e.dma_start(
                out=o_latp[l, s].rearrange("c p t -> p c t")[:, :, j * 512:(j + 1) * 512], in_=ckvf.v),
                ckvf.r(), [])
        else:
            dma("pool", "o_lat", lambda e: e.dma_start(out=o_lats[l].rearrange("c p t -> p c t"),
                                                        in_=ckvf.v[:, :, 0:64]), ckvf.r(), [])
        bk = dense(wC, 8, (640, 768), hrhs, hT.r(), slotC)
        dve(lambda e: e.tensor_tensor(out=tmp1.v[:, 0:N], in0=PSB[bk][:, 0:N], in1=cos_v, op=ALU.mult),
            [PSR[bk]] + cs.r(), tmp1.r())
        bkp = dense(wC, 8, (768, 896), hrhs, hT.r(), slotC)
        dve(lambda e: e.tensor_tensor(out=tmp2.v[:, 0:N], in0=PSB[bkp][:, 0:N], in1=sin_v, op=ALU.mult),
            [PSR[bkp]] + cs.r(), tmp2.r())
        dve(lambda e: e.tensor_tensor(out=kpef.v[:, 0:N], in0=tmp1.v[:, 0:N], in1=tmp2.v[:, 0:N], op=ALU.add),
            tmp1.r() + tmp2.r(), kpef.r())
        act(lambda e: e.activation(out=Pcur.v[:, 0:N], in_=kpef.v[:, 0:N], func=AF.Copy), kpef.r(), Pcur.r())
        if prompt:
            dma("pool", "o_kpe", lambda e: e.dma_start(out=o_kpep[l, s][:, j * 512:(j + 1) * 512],
                                                        in_=kpef.v[0:64, :]), kpef.r(), [])
        else:
            dma("pool", "o_kpe", lambda e: e.dma_start(out=o_kpes[l], in_=kpef.v[0:64, 0:64]), kpef.r(), [])
        if KSTOP <= 5:
            return
        Qp4 = Qp.v.rearrange("p (a b) n -> p a b n", b=2)
        pool(lambda e: e.memset(Qp4[64:128, :, 0, :], 0.0), [], Qp.r())
        pool(lambda e: e.memset(Qp4[0:64, :, 1, :], 0.0), [], Qp.r())
        slotU, wU = piece_simple(l, s_uq[l], 3, 0, 2048)
        qrhs = lambda kc: cqn.v[:, kc, 0:N]
        for h in range(8):
            b = dense(wU, 3, (h * 128, h * 128 + 128), qrhs, cqn.r(), slotU)
            act(lambda e, b=b, h=h: e.activation(out=Qn.v[:, h, 0:N], in_=PSB[b][:, 0:N], func=AF.Copy),
                [PSR[b]], Qn.rs(h, 512))
        for pr in range(4):
            b1 = dense(wU, 3, (1024 + pr * 128, 1024 + pr * 128 + 128), qrhs, cqn.r(), slotU)
            dve(lambda e, b1=b1: e.tensor_tensor(out=tmp1.v[:, 0:N], in0=PSB[b1][:, 0:N], in1=cos_v, op=ALU.mult),
                [PSR[b1]] + cs.r(), tmp1.r())
            b2 = dense(wU, 3, (1536 + pr * 128, 1536 + pr * 128 + 128), qrhs, cqn.r(), slotU)
            dve(lambda e, b2=b2: e.tensor_tensor(out=tmp2.v[:, 0:N], in0=PSB[b2][:, 0:N], in1=sin_v, op=ALU.mult),
                [PSR[b2]] + cs.r(), tmp2.r())
            dve(lambda e, pr=pr: e.tensor_tensor(out=Qp.v[0:64, 2 * pr, 0:N], in0=tmp1.v[0:64, 0:N],
                                                 in1=tmp2.v[0:64, 0:N], op=ALU.add),
                tmp1.r() + tmp2.r(), Qp.rs(2 * pr, 512))
            dve(lambda e, pr=pr: e.tensor_tensor(out=Qp.v[64:128, 2 * pr + 1, 0:N], in0=tmp1.v[64:128, 0:N],
                                                 in1=tmp2.v[64:128, 0:N], op=ALU.add),
                tmp1.r() + tmp2.r(), Qp.rs(2 * pr + 1, 512))
        if KSTOP <= 6:
            return
        slotK, wK = piece_simple(l, s_ukv[l], 2, 0, 2048)
        lrhs = lambda kc: latb.v[:, kc, 0:N]
        if prompt:
            for h in range(8):
                b = dense(wK, 2, (h * 128, h * 128 + 128), lrhs, latb.r(), slotK)
                act(lambda e, b=b, h=h: e.activation(out=Kcur.v[:, :, h, :],
                                                     in_=PSB[b][:, 0:512].rearrange("p (b k) -> p b k", k=128),
                                                     func=AF.Copy), [PSR[b]], Kcur.r())
            for blk in range(4):
                for half in range(2):
                    b = next_bank(dbanks)
                    mm_group(PSB[b][:, 0:512],
                             [(latb.v[:, kc, blk * 128:(blk + 1) * 128],
                               wK[:, kc, 1024 + half * 512:1024 + half * 512 + 512]) for kc in range(2)],
                             latb.r() + slotK.r(), [PSR[b]])
                    act(lambda e, b=b, blk=blk, half=half: e.activation(
                        out=Vcur.v[:, blk, half * 512:(half + 1) * 512], in_=PSB[b][:, 0:512], func=AF.Copy),
                        [PSR[b]], Vcur.rs(blk, 1024))
            if j < NT - 1:
                dma("pool", "kcw", lambda e: e.dma_start(out=s_kc[s, l][:, j * 4:(j + 1) * 4], in_=Kcur.v),
                    Kcur.r(), [DR(("kc", s, l, j))])
                dma("pool", "vcw", lambda e: e.dma_start(
                    out=s_vc[s, l][j * 4:(j + 1) * 4].rearrange("b k c -> k b c"), in_=Vcur.v),
                    Vcur.r(), [DR(("vc", s, l, j))])
                dma("pool", "pcw", lambda e: e.dma_start(out=s_pc[s, l][:, j * 512:(j + 1) * 512], in_=Pcur.v),
                    Pcur.r(), [DR(("pc", s, l, j))])
            if KSTOP <= 7:
                return
            slotPA, wPA = piece_simple(l, s_pa[l], 8, 0, 1024)
            slotPB, wPB = piece_simple(l, s_pb[l], 4, 0, 1024)
            attention_prompt(l, s, j)
        else:
            slotPA, wPA = piece_simple(l, s_pa[l], 8, 0, 1024)
            attention_sample(l, wK, slotK)
            slotPB, wPB = piece_simple(l, s_pb[l], 4, 0, 1024)

        if KSTOP <= 8:
            return
        mbanks = [[0, 1], [2, 3], [4, 5], [6, 7]]
        for oc in range(8):
            bx, by = mbanks[oc % 4]
            mm_group(PSB[bx][:, 0:N], [(wPA[:, kc, oc * 128:(oc + 1) * 128], oaT.v[:, kc, 0:N]) for kc in range(8)],
                     oaT.r() + slotPA.r(), [PSR[bx]])
            mm_group(PSB[by][:, 0:N], [(wPB[:, kc, oc * 128:(oc + 1) * 128], obT.v[:, kc, 0:N]) for kc in range(4)],
                     obT.r() + slotPB.r(), [PSR[by]])
            ta = mta[oc % 2]
            tb = mtb[oc % 2]
            dve(lambda e, bx=bx, oc=oc, ta=ta: e.tensor_tensor(out=ta.v[:, 0:N], in0=PSB[bx][:, 0:N],
                                                               in1=gates.v[:, oc, 0:N], op=ALU.mult),
                [PSR[bx]] + gates.rs(oc, 512), ta.r())
            dve(lambda e, by=by, oc=oc, tb=tb: e.tensor_tensor(out=tb.v[:, 0:N], in0=PSB[by][:, 0:N],
                                                               in1=gates.v[:, 8 + oc, 0:N], op=ALU.mult),
                [PSR[by]] + gates.rs(8 + oc, 512), tb.r())
            dve(lambda e, oc=oc, ta=ta, tb=tb: e.tensor_tensor(out=hT.v[:, oc, 0:N], in0=ta.v[:, 0:N],
                                                               in1=tb.v[:, 0:N], op=ALU.add),
                ta.r() + tb.r(), hT.rs(oc, 512))
        slotO, wO = piece_simple(l, s_wo[l], 8, 0, 1024)
        for oc in range(8):
            b = dense(wO, 8, (oc * 128, oc * 128 + 128), hrhs, hT.r(), slotO, banks=[0, 1, 2, 3])
            dve(lambda e, b=b, oc=oc: e.tensor_tensor(out=xT.v[:, oc, 0:N], in0=xT.v[:, oc, 0:N], in1=PSB[b][:, 0:N],
                                                      op=ALU.add), [PSR[b]] + xT.rs(oc, 512), xT.rs(oc, 512))

        if KSTOP <= 9:
            return
        rmsnorm_x(N, lambda c: g2s.v[:, l, c:c + 1], lambda c: hT.v[:, c, 0:N], lambda c: hT.rs(c, 512))
        nseg = 1 if prompt else 2
        L = N // nseg
        cwv = pv("cw").rearrange("p (l j c) -> p l j c", j=3, c=22)
        cbv = pv("cb").rearrange("p (l c) -> p l c", c=22)
        fb = [[0, 1], [2, 3], [4, 5], [6, 7]]
        for q in range(6):
            ncol = 512 if q < 5 else 256

            def src_fn(flat, q=q, ncol=ncol):
                v = flat[:, 0:8 * 2 * ncol].rearrange("p (k n) -> p k n", n=2 * ncol)
                return [(v[:, :, 0:ncol], wview(s_up[l], 8, q * 512, q * 512 + ncol)),
                        (v[:, :, ncol:2 * ncol], wview(s_up[l], 8, 2816 + q * 512, 2816 + q * 512 + ncol))]
            slotP = load_piece(l, src_fn, 8 * 2 * ncol)
            wP = slotP.flat[:, 0:8 * 2 * ncol].rearrange("p (k n) -> p k n", n=2 * ncol)
            for jj in range(ncol // 128):
                jc = q * 4 + jj
                bx, by = fb[jc % 4]
                mm_group(PSB[bx][:, 0:N], [(wP[:, kc, jj * 128:(jj + 1) * 128], hT.v[:, kc, 0:N]) for kc in range(8)],
                         hT.r() + slotP.r(), [PSR[bx]])
                mm_group(PSB[by][:, 0:N],
                         [(wP[:, kc, ncol + jj * 128:ncol + (jj + 1) * 128], hT.v[:, kc, 0:N]) for kc in range(8)],
                         hT.r() + slotP.r(), [PSR[by]])
                ae = aext[jc % 4]
                cv = cvb[jc % 4]
                uu = ub[jc % 4]
                ss_ = sbf[jc % 4]
                ae3 = ae.v[:, 0:nseg * (L + 2)].rearrange("p (s t) -> p s t", t=L + 2)
                as3 = lambda ap_: ap_[:, 0:N].rearrange("p (s t) -> p s t", t=L)
                act(lambda e, bx=bx, ae3=ae3: e.activation(out=ae3[:, :, 2:L + 2], in_=as3(PSB[bx]), func=AF.Copy),
                    [PSR[bx]], ae.r())
                if prompt:
                    halo_src = ctail.v[:, l, jc, :].unsqueeze(1)
                    halo_res = ctail.r()
                else:
                    halo_src = shalo.v[:, :, jc, :]
                    halo_res = shalo.r()
                if prompt and j == 0:
                    pool(lambda e, ae3=ae3: e.memset(ae3[:, :, 0:2], 0.0), [], ae.r())
                else:
                    pool(lambda e, ae3=ae3, halo_src=halo_src: e.tensor_copy(out=ae3[:, :, 0:2], in_=halo_src),
                         halo_res, ae.r())
                if prompt:
                    pool(lambda e, ae3=ae3, jc=jc: e.tensor_copy(out=ctail.v[:, l, jc, :].unsqueeze(1),
                                                                in_=ae3[:, :, L:L + 2]), ae.r(), ctail.r())
                else:
                    pool(lambda e, ae3=ae3, jc=jc: e.tensor_copy(out=stail.v[:, :, jc, :], in_=ae3[:, :, L:L + 2]),
                         ae.r(), stail.r())
                act(lambda e, ae3=ae3, cv=cv, jc=jc: e.activation(out=as3(cv.v), in_=ae3[:, :, 2:L + 2], func=AF.Identity,
                                                                 scale=cwv[:, l, 2, jc:jc + 1], bias=cbv[:, l, jc:jc + 1]),
                    ae.r() + par.r(), cv.r())
                dve(lambda e, ae3=ae3, cv=cv, jc=jc: e.scalar_tensor_tensor(
                    out=as3(cv.v), in0=ae3[:, :, 1:L + 1], scalar=cwv[:, l, 1, jc:jc + 1], in1=as3(cv.v),
                    op0=ALU.mult, op1=ALU.add), ae.r() + cv.r() + par.r(), cv.r())
                dve(lambda e, ae3=ae3, cv=cv, jc=jc: e.scalar_tensor_tensor(
                    out=as3(cv.v), in0=ae3[:, :, 0:L], scalar=cwv[:, l, 0, jc:jc + 1], in1=as3(cv.v),
                    op0=ALU.mult, op1=ALU.add), ae.r() + cv.r() + par.r(), cv.r())
                act(lambda e, cv=cv, uu=uu: e.activation(out=uu.v[:, 0:N], in_=cv.v[:, 0:N], func=AF.Square,
                                                         scale=math.sqrt(0.044715)), cv.r(), uu.r())
                dve(lambda e, cv=cv, uu=uu: e.scalar_tensor_tensor(out=uu.v[:, 0:N], in0=uu.v[:, 0:N], scalar=1.0,
                                                                   in1=cv.v[:, 0:N], op0=ALU.add, op1=ALU.mult),
                    cv.r() + uu.r(), uu.r())
                act(lambda e, uu=uu, ss_=ss_: e.activation(out=ss_.v[:, 0:N], in_=uu.v[:, 0:N], func=AF.Sigmoid,
                                                           scale=GC), uu.r(), ss_.r())
                pool(lambda e, cv=cv, ss_=ss_: e.tensor_tensor(out=ss_.v[:, 0:N], in0=ss_.v[:, 0:N], in1=cv.v[:, 0:N],
                                                               op=ALU.mult), cv.r() + ss_.r(), ss_.r())
                dve(lambda e, by=by, ss_=ss_, jc=jc: e.tensor_tensor(out=gT.v[:, jc, 0:N], in0=ss_.v[:, 0:N],
                                                                     in1=PSB[by][:, 0:N], op=ALU.mult),
                    [PSR[by]] + ss_.r(), gT.rs(jc, 512))
        if prompt and j == NT - 1:
            dma("pool", f"o_cvp{l}", lambda e: e.dma_start(out=o_cvp[l, s], in_=ctail.v[:, l]), ctail.r(), [])
        if not prompt:
            dma("pool", "o_cvs", lambda e: [e.dma_start(out=o_cvs[l, q], in_=stail.v[:, q]) for q in range(2)],
                stail.r(), [], n=2)
        for q in range(4):
            def src_fn(flat, q=q):
                return [(flat[:, 0:22 * 256].rearrange("p (k n) -> p k n", n=256),
                         wview(s_dn[l], 22, q * 256, (q + 1) * 256))]
            slotDn = load_piece(l, src_fn, 22 * 256)
            wDn = slotDn.flat[:, 0:22 * 256].rearrange("p (k n) -> p k n", n=256)
            for o2 in range(2):
                oc = q * 2 + o2
                b = next_bank([0, 1, 2, 3])
                mm_group(PSB[b][:, 0:N], [(wDn[:, kc, o2 * 128:(o2 + 1) * 128], gT.v[:, kc, 0:N]) for kc in range(22)],
                         gT.r() + slotDn.r(), [PSR[b]])
                dve(lambda e, b=b, oc=oc: e.tensor_tensor(out=xT.v[:, oc, 0:N], in0=xT.v[:, oc, 0:N],
                                                          in1=PSB[b][:, 0:N], op=ALU.add),
                    [PSR[b]] + xT.rs(oc, 512), xT.rs(oc, 512))

    kvctr = [0]

    def attention_prompt(l, s, j):
        N = 512
        for p in range(2):
            heads = range(4 * p, 4 * p + 4)
            first = {h: True for h in heads}
            for g in range(j + 1):
                diag = g == j
                if not diag:
                    si = kvctr[0] % 2
                    kvctr[0] += 1
                    Kb, Vb, Pb = KS[si]

                    def fn(e, g=g, p=p, Kb=Kb, Vb=Vb, Pb=Pb):
                        return [
                            e.dma_start(out=Kb.v, in_=s_kc[s, l][:, g * 4:(g + 1) * 4, 4 * p:4 * p + 4, :]),
                            e.dma_start(out=Vb.v, in_=s_vc[s, l][g * 4:(g + 1) * 4, :, p * 512:(p + 1) * 512]
                                        .rearrange("b k c -> k b c")),
                            e.dma_start(out=Pb.v, in_=s_pc[s, l][:, g * 512:(g + 1) * 512]),
                        ]
                    dma("sp", f"kv{si}", fn, [DR(("kc", s, l, g)), DR(("vc", s, l, g)), DR(("pc", s, l, g))],
                        Kb.r() + Vb.r() + Pb.r(), n=3, lat=10.0)
                    kfn = lambda h, kb, Kb=Kb, p=p: Kb.v[:, kb, h - 4 * p, :]
                    vfn = lambda h, kb, Vb=Vb, p=p: Vb.v[:, kb, (h - 4 * p) * 128:(h - 4 * p + 1) * 128]
                    pfn = lambda hp, kb, Pb=Pb: Pb.v[:, kb * 128:(kb + 1) * 128]
                    kvres = Kb.r() + Vb.r() + Pb.r()
                else:
                    kfn = lambda h, kb: Kcur.v[:, kb, h, :]
                    vfn = lambda h, kb: Vcur.v[:, kb, h * 128:(h + 1) * 128]
                    pfn = lambda hp, kb: Pcur.v[:, kb * 128:(kb + 1) * 128]
                    kvres = Kcur.r() + Vcur.r() + Pcur.r()
                for h in heads:
                    hl = h - 4 * p
                    ob = 4 + hl
                    hp = h % 2
                    for kb in range(4):
                        q0 = kb * 128 if diag else 0
                        sb_ = 2 + (bctr[0] % 2)
                        bctr[0] += 1
                        pt = Pt[bctr[0] % 4]
                        last = (g == j and kb == 3)

                        def fsc(e, h=h, kb=kb, q0=q0, sb_=sb_, hp=hp, kfn=kfn, pfn=pfn, diag=diag):
                            e.matmul(PSB[sb_][:, q0:N], kfn(h, kb), Qn.v[:, h, q0:N], start=True, stop=False)
                            if diag:
                                e.matmul(PSB[sb_][:, q0:N], mUb.v, mVb.v[:, 0:N - q0], start=False, stop=False)
                            return e.matmul(PSB[sb_][:, q0:N], pfn(hp, kb), Qp.v[:, h, q0:N],
                                            start=False, stop=True)
                        pe(fsc, kvres + Qn.rs(h, 512) + Qp.rs(h, 512), [PSR[sb_]])
                        act(lambda e, sb_=sb_, pt=pt, q0=q0: e.activation(out=pt.v[:, q0:N], in_=PSB[sb_][:, q0:N],
                                                                          func=AF.Exp, scale=SCALE),
                            [PSR[sb_]], pt.r())
                        st = first[h]

                        def fpv(e, h=h, kb=kb, q0=q0, pt=pt, ob=ob, st=st, last=last, vfn=vfn):
                            return e.matmul(PSB[ob][:, q0:N], vfn(h, kb), pt.v[:, q0:N], start=st, stop=last,
                                            skip_group_check=True)
                        pe(fpv, kvres + pt.r(), [PSR[ob]])
                        if st:
                            dve(lambda e, pt=pt, hl=hl: e.tensor_copy(out=Pacc.v[:, hl, :], in_=pt.v),
                                pt.r(), Pacc.rs(hl, 512))
                        else:
                            dve(lambda e, pt=pt, hl=hl, q0=q0: e.tensor_tensor(out=Pacc.v[:, hl, q0:N],
                                                                             in0=Pacc.v[:, hl, q0:N],
                                                                             in1=pt.v[:, q0:N], op=ALU.add),
                                pt.r() + Pacc.rs(hl, 512), Pacc.rs(hl, 512))
                        first[h] = False
            for h in heads:
                hl = h - 4 * p
                ob = 4 + hl
                sb_ = 2 + (bctr[0] % 2)
                bctr[0] += 1
                mm_group(PSB[sb_][:, 0:N], [(onesf.v, Pacc.v[:, hl, :])], Pacc.rs(hl, 512) + onesf.r(), [PSR[sb_]])
                act(lambda e, sb_=sb_: e.activation(out=rsb.v, in_=PSB[sb_][:, 0:N], func=AF.Ln), [PSR[sb_]], rsb.r())
                act(lambda e: e.activation(out=rsb.v, in_=rsb.v, func=AF.Exp, scale=-1.0), rsb.r(), rsb.r())
                dve(lambda e, h=h, ob=ob: e.tensor_tensor(out=oaT.v[:, h, :], in0=PSB[ob][:, 0:N], in1=rsb.v,
                                                          op=ALU.mult), [PSR[ob]] + rsb.r(), oaT.rs(h, 512))

    def attention_sample(l, wK, slotK):
        N = 64
        Knew = Kcur.flat[:, 0:8 * 64].rearrange("p (h t) -> p h t", t=64)
        for h in range(8):
            b = next_bank([0, 1])
            mm_group(PSB[b][:, 0:N], [(wK[:, kc, h * 128:(h + 1) * 128], latb.v[:, kc, 0:N]) for kc in range(2)],
                     latb.r() + slotK.r(), [PSR[b]])
            act(lambda e, b=b, h=h: e.activation(out=Knew[:, h, :], in_=PSB[b][:, 0:N], func=AF.Copy),
                [PSR[b]], Kcur.r())
        for q in range(2):
            for half in range(2):
                b = next_bank([0, 1])
                mm_group(PSB[b][0:32, 0:512],
                         [(latb.v[:, kc, q * 32:(q + 1) * 32], wK[:, kc, 1024 + half * 512:1024 + half * 512 + 512])
                          for kc in range(2)], latb.r() + slotK.r(), [PSR[b]])
                act(lambda e, b=b, q=q, half=half: e.activation(out=Vnew.v[0:32, q, half * 512:(half + 1) * 512],
                                                                in_=PSB[b][0:32, 0:512], func=AF.Copy),
                    [PSR[b]], Vnew.rs(q, 1024))
        Pa = Pacc.flat[:, 0:8 * 64].rearrange("p (h t) -> p h t", t=64)
        ob = 4
        first = {}
        ostart = [True]
        for q in range(2):
            for g in range(NPG):
                dma("sp", "lpf", lambda e, q=q, g=g: [
                    e.dma_start(out=lpf.v, in_=d_latp[l, q].rearrange("c p t -> p c t")[:, :, g * 512:(g + 1) * 512]),
                    e.dma_start(out=kpf.v, in_=d_krp[l, q][:, g * 512:(g + 1) * 512])],
                    [], lpf.r() + kpf.r(), n=2)
                act(lambda e: e.activation(out=lpb.flat, in_=lpf.flat, func=AF.Copy), lpf.r(), lpb.r())
                Pb = KS[0][2]
                act(lambda e, Pb=Pb: e.activation(out=Pb.v, in_=kpf.v, func=AF.Copy), kpf.r(), Pb.r())
                for kb in range(4):
                    for half in range(2):
                        b = next_bank([0, 1])
                        mm_group(PSB[b][:, 0:512],
                                 [(lpb.v[:, kc, kb * 128:(kb + 1) * 128],
                                   wK[:, kc, 1024 + half * 512:1024 + half * 512 + 512]) for kc in range(2)],
                                 lpb.r() + slotK.r(), [PSR[b]])
                        act(lambda e, b=b, kb=kb, half=half: e.activation(
                            out=Vcur.v[:, kb, half * 512:(half + 1) * 512], in_=PSB[b][:, 0:512], func=AF.Copy),
                            [PSR[b]], Vcur.rs(kb, 1024))
                for h in range(8):
                    hp = h % 2
                    Kh = KS[h % 2][0]
                    b = next_bank([0, 1])
                    mm_group(PSB[b][:, 0:512], [(wK[:, kc, h * 128:(h + 1) * 128], lpb.v[:, kc, :]) for kc in range(2)],
                             lpb.r() + slotK.r(), [PSR[b]])
                    act(lambda e, b=b, Kh=Kh: e.activation(out=Kh.flat[:, 0:512], in_=PSB[b][:, 0:512], func=AF.Copy),
                        [PSR[b]], Kh.r())
                    for kb in range(4):
                        sb_ = 2 + (bctr[0] % 2)
                        bctr[0] += 1
                        pt = Pt[bctr[0] % 4]

                        def fsc(e, h=h, kb=kb, sb_=sb_, hp=hp, Kh=Kh, Pb=Pb, q=q):
                            e.matmul(PSB[sb_][:, 0:32], Kh.flat[:, kb * 128:(kb + 1) * 128], Qn.v[:, h, q * 32:(q + 1) * 32],
                                     start=True, stop=False)
                            return e.matmul(PSB[sb_][:, 0:32], Pb.v[:, kb * 128:(kb + 1) * 128],
                                            Qp.v[:, h, q * 32:(q + 1) * 32],
                                            start=False, stop=True)
                        pe(fsc, Kh.r() + Pb.r() + Qn.rs(h, 512) + Qp.rs(h, 512), [PSR[sb_]])
                        act(lambda e, sb_=sb_, pt=pt: e.activation(out=pt.v[:, 0:32], in_=PSB[sb_][:, 0:32], func=AF.Exp,
                                                                   scale=SCALE), [PSR[sb_]], pt.r())
                        st = first.get((h, q), True)

                        st0 = ostart[0]
                        ostart[0] = False

                        def fpv(e, h=h, kb=kb, pt=pt, st0=st0, q=q):
                            c0 = h * 64 + q * 32
                            return e.matmul(PSB[ob][:, c0:c0 + 32], Vcur.v[:, kb, h * 128:(h + 1) * 128], pt.v[:, 0:32],
                                            start=st0, stop=False, skip_group_check=True)
                        pe(fpv, Vcur.rs(kb, 1024) + pt.r(), [PSR[ob]])
                        if st:
                            dve(lambda e, pt=pt, h=h, q=q: e.tensor_copy(out=Pa[:, h, q * 32:(q + 1) * 32],
                                                                        in_=pt.v[:, 0:32]), pt.r(), Pacc.r())
                        else:
                            dve(lambda e, pt=pt, h=h, q=q: e.tensor_tensor(out=Pa[:, h, q * 32:(q + 1) * 32],
                                                                          in0=Pa[:, h, q * 32:(q + 1) * 32],
                                                                          in1=pt.v[:, 0:32], op=ALU.add),
                                pt.r() + Pacc.r(), Pacc.r())
                        first[(h, q)] = False
            for h in range(8):
                hp = h % 2
                sb_ = 2 + (bctr[0] % 2)
                bctr[0] += 1
                pt = Pt[bctr[0] % 4]

                def fsc(e, h=h, sb_=sb_, hp=hp, q=q):
                    e.matmul(PSB[sb_][0:32, 0:32], Knew[:, h, q * 32:(q + 1) * 32], Qn.v[:, h, q * 32:(q + 1) * 32],
                             start=True, stop=False)
                    return e.matmul(PSB[sb_][0:32, 0:32], Pcur.v[:, q * 32:(q + 1) * 32],
                                    Qp.v[:, h, q * 32:(q + 1) * 32], start=False, stop=True)
                pe(fsc, Kcur.r() + Pcur.r() + Qn.rs(h, 512) + Qp.rs(h, 512), [PSR[sb_]])
                act(lambda e, sb_=sb_, pt=pt: e.activation(out=pt.v[0:32, 0:32], in_=PSB[sb_][0:32, 0:32], func=AF.Exp,
                                                           scale=SCALE), [PSR[sb_]], pt.r())

                def fpv(e, h=h, pt=pt, q=q):
                    c0 = h * 64 + q * 32
                    return e.matmul(PSB[ob][:, c0:c0 + 32], Vnew.v[0:32, q, h * 128:(h + 1) * 128], pt.v[0:32, 0:32],
                                    start=False, stop=True, skip_group_check=True)
                pe(fpv, Vnew.rs(q, 1024) + pt.r(), [PSR[ob]])
                dve(lambda e, pt=pt, h=h, q=q: e.tensor_tensor(out=Pa[0:32, h, q * 32:(q + 1) * 32],
                                                              in0=Pa[0:32, h, q * 32:(q + 1) * 32],
                                                              in1=pt.v[0:32, 0:32], op=ALU.add),
                    pt.r() + Pacc.r(), Pacc.r())
        sb_ = 3
        mm_group(PSB[sb_][:, 0:512], [(onesf.v, Pacc.flat[:, 0:512])], Pacc.r() + onesf.r(), [PSR[sb_]])
        act(lambda e: e.activation(out=rsb.v, in_=PSB[sb_][:, 0:512], func=AF.Ln), [PSR[sb_]], rsb.r())
        act(lambda e: e.activation(out=rsb.v, in_=rsb.v, func=AF.Exp, scale=-1.0), rsb.r(), rsb.r())
        dve(lambda e: e.tensor_tensor(out=oaT.v[:, :, 0:64], in0=PSB[ob][:, 0:512].rearrange("p (h t) -> p h t", t=64),
                                      in1=rsb.v.rearrange("p (h t) -> p h t", t=64), op=ALU.mult),
            [PSR[ob]] + rsb.r(), oaT.r())

    def final_norm_store(N, dst_fn, key):
        rmsnorm_x(N, lambda c: gfs.v[:, c:c + 1], lambda c: xT.v[:, c, 0:N], lambda c: xT.rs(c, 512))
        dma("pool", key, lambda e: e.dma_start(out=dst_fn(), in_=xT.v[:, :, 0:N]), xT.r(), [])

    import os
    DBG = int(os.environ.get("KDBG", "0"))
    for s in range(NSEQ if DBG == 0 else 0):
        for j in range(NT):
            dma("sp", "xload", lambda e, s=s, j=j: e.dma_start(
                out=xT.v, in_=d_xp[s].rearrange("c p t -> p c t")[:, :, j * 512:(j + 1) * 512]), [], xT.r())
            dma("sp", "csload", lambda e, j=j: e.dma_start(
                out=cs.v, in_=d_csp.rearrange("a p t -> p a t")[:, :, j * 512:(j + 1) * 512]), [], cs.r())
            for l in range(DEPTH):
                tile_layer(l, 512, "p", s, j)
            final_norm_store(512, lambda s=s, j=j: o_yp[s].rearrange("c p t -> p c t")[:, :, j * 512:(j + 1) * 512],
                             "o_y")
    dma("sp", "xload", lambda e: e.dma_start(out=xT.v[:, :, 0:64], in_=d_xs.rearrange("c p t -> p c t")), [], xT.r())
    dma("sp", "csload", lambda e: e.dma_start(out=cs.v[:, :, 0:64], in_=d_css.rearrange("a p t -> p a t")), [], cs.r())
    for l in range(DEPTH if DBG == 0 else 0):
        dma("sp", "shalo", lambda e, l=l: [e.dma_start(out=shalo.v[:, q], in_=d_scv[l, q]) for q in range(2)],
            [], shalo.r(), n=2)
        tile_layer(l, 64, "s")
    final_norm_store(64, lambda: o_ys.rearrange("c p t -> p c t"), "o_y")

    import os as _os2
    if _os2.environ.get("KNOSCHED") is None:
        P.schedule()
        if _os2.environ.get("KVERB"):
            print("scheduled makespan us", P.makespan)
    P.finalize()
    sems = {}

    for k in [("eng", e_) for e_ in ("pe", "act", "dve", "pool")] + [("dma", k_) for k_ in P.dma_counts]:
        sems[k] = es.enter_context(nc.semaphore("s_" + "_".join(str(x) for x in k)))

    def semof(k):
        return sems[k]

    with nc.Block() as block:
        @block.sync
        def _(e):
            P.emit("sp", e, semof)

        @block.gpsimd
        def _(e):
            P.emit("pool", e, semof)

        @block.scalar
        def _(e):
            P.emit("act", e, semof)

        @block.vector
        def _(e):
            P.emit("dve", e, semof)

        @block.tensor
        def _(e):
            P.emit("pe", e, semof)
    es.close()
    return nc


def rope_tables(pos):
    half = 32
    inv = (10000.0 ** (-np.arange(half, dtype=np.float32) / half)).astype(np.float32)
    ang = pos.astype(np.float32)[:, None] * inv[None, :]
    cos = np.cos(ang).astype(np.float32).T
    sin = np.sin(ang).astype(np.float32).T
    c64 = np.concatenate([cos, cos], 0)
    s64 = np.concatenate([-sin, sin], 0)
    return np.stack([np.concatenate([c64, c64], 0), np.concatenate([s64, s64], 0)], 0)


def make_consts():
    ident = np.eye(128, dtype=np.float32)
    s_ = np.arange(128)[:, None]
    t_ = np.arange(128)[None, :]
    maskP = ((s_ // 64 == t_ // 64) & (s_ <= t_)).astype(np.float32)
    maskS = ((s_ // 32 == t_ // 32) & (s_ <= t_)).astype(np.float32)
    maskS[64:, :] = 0
    maskS[:, 64:] = 0
    resetP = np.ones((128, 512), np.float32)
    resetP[:, ::64] = 0
    resetS = np.ones((128, 64), np.float32)
    resetS[:, ::32] = 0
    mU = np.zeros((128, 128), np.float32)
    mU[0, 64:] = -30000.0
    mV = np.zeros((128, 512), np.float32)
    mV[0, :64] = 1.0
    return np.concatenate([maskP, maskS, resetP, resetS], 1), np.concatenate([ident, mU, mV], 1)


def perm64(w):
    return np.concatenate([w[..., 32:64], w[..., 0:32]], -1)


def host_prep(cfg, inp, core):
    D = cfg.DEPTH
    f = np.float32
    ps = slice(core * cfg.NSEQ, (core + 1) * cfg.NSEQ)
    ss = slice(core * 2, core * 2 + 2)
    m = {}
    xp = np.asarray(inp["x_prompt"][ps])
    m["xp"] = np.ascontiguousarray(xp.transpose(0, 2, 1).reshape(cfg.NSEQ, 8, 128, cfg.LP))
    xs = np.asarray(inp["x_sample"][ss])
    m["xs"] = np.ascontiguousarray(xs.transpose(2, 0, 1).reshape(8, 128, 64))
    lat = np.asarray(inp["cache_mla_latent"][:, ss])
    m["latp"] = np.ascontiguousarray(lat.transpose(0, 1, 3, 2).reshape(D, 2, 2, 128, cfg.PAST))
    kr = np.asarray(inp["cache_mla_krope"][:, ss]).transpose(0, 1, 3, 2)
    m["krp"] = np.ascontiguousarray(np.concatenate([kr, kr], 2))
    sh = np.asarray(inp["state_hgrn"][:, ss])
    m["sh"] = np.ascontiguousarray(sh.transpose(0, 1, 3, 2, 4))
    scv = np.asarray(inp["state_ffn_conv"][:, ss])
    m["scv"] = np.ascontiguousarray(scv.reshape(D, 2, 2, 22, 128).transpose(0, 1, 4, 3, 2))
    par = np.zeros((128, cfg.NPAR), f)

    def put(name, arr):
        o, n = cfg.po[name]
        par[:, o:o + n] = arr.reshape(128, n)
    put("g1", np.asarray(inp["norm_mix"]).reshape(D, 8, 128).transpose(2, 0, 1))
    put("g2", np.asarray(inp["norm_ffn"]).reshape(D, 8, 128).transpose(2, 0, 1))
    put("gq", np.asarray(inp["q_norm"]).reshape(D, 3, 128).transpose(2, 0, 1))
    put("gkv", np.asarray(inp["kv_norm"]).reshape(D, 2, 128).transpose(2, 0, 1))
    put("gh", np.asarray(inp["hgrn_norm"]).reshape(D, 128).transpose(1, 0))
    put("lbl", np.asarray(inp["lb_logits"]).reshape(D, 4, 128).transpose(2, 1, 0))
    put("cw", np.asarray(inp["conv_w"]).reshape(D, 3, 22, 128).transpose(3, 0, 1, 2))
    put("cb", np.asarray(inp["conv_b"]).reshape(D, 22, 128).transpose(2, 0, 1))
    put("gf", np.asarray(inp["norm_final"]).reshape(8, 128).transpose(1, 0))
    m["par"] = par
    return m


def host_shared(cfg, inp):
    D = cfg.DEPTH
    m = {}
    m["const"], m["const2"] = make_consts()
    m["csp"] = rope_tables(np.arange(cfg.LP))
    m["css"] = np.ascontiguousarray(np.tile(rope_tables(cfg.PAST + np.arange(32)), (1, 1, 2)))
    w_in = np.asarray(inp["w_in"])
    m["w_in"] = w_in
    kr = w_in[:, :, 640:704]
    krp = perm64(kr)
    m["w_inC"] = np.ascontiguousarray(np.concatenate([w_in[:, :, 0:640], kr, kr, krp, krp], 2))
    wuq = np.asarray(inp["w_uq"]).reshape(D, 384, 8, 192)
    nope = wuq[..., 0:128].reshape(D, 384, 1024)
    ropew = wuq[..., 128:192]
    m["w_uqR"] = np.ascontiguousarray(np.concatenate(
        [nope, ropew.reshape(D, 384, 512), perm64(ropew).reshape(D, 384, 512)], 2))
    wukv = np.asarray(inp["w_ukv"]).reshape(D, 256, 8, 256)
    m["w_ukvR"] = np.ascontiguousarray(np.concatenate(
        [wukv[..., 0:128].reshape(D, 256, 1024), wukv[..., 128:256].reshape(D, 256, 1024)], 2))
    m["w_pa"] = np.asarray(inp["w_proj_a"])
    m["w_pb"] = np.asarray(inp["w_proj_b"])
    m["w_o"] = np.asarray(inp["w_out"])
    m["w_up"] = np.asarray(inp["w_up"])
    m["w_dn"] = np.asarray(inp["w_down"])
    return m


def host_gather(cfg, results):
    D, LP, NSEQ = cfg.DEPTH, cfg.LP, cfg.NSEQ
    nb = NCORES * NSEQ
    y_p = np.empty((nb, LP, 1024), np.float32)
    y_s = np.empty((NCORES * 2, 32, 1024), np.float32)
    lat_p = np.empty((D, nb, LP, 256), np.float32)
    kpe_p = np.empty((D, nb, LP, 64), np.float32)
    hs_p = np.empty((D, nb, 4, 128, 128), np.float32)
    cv_p = np.empty((D, nb, 2, 2816), np.float32)
    lat_s = np.empty((D, NCORES * 2, 32, 256), np.float32)
    kpe_s = np.empty((D, NCORES * 2, 32, 64), np.float32)
    hs_s = np.empty((D, NCORES * 2, 4, 128, 128), np.float32)
    cv_s = np.empty((D, NCORES * 2, 2, 2816), np.float32)
    for c, r in enumerate(results):
        ps = slice(c * NSEQ, (c + 1) * NSEQ)
        ss = slice(c * 2, c * 2 + 2)
        y_p[ps] = r["o_yp"].reshape(NSEQ, 1024, LP).transpose(0, 2, 1)
        y_s[ss] = r["o_ys"].reshape(1024, 2, 32).transpose(1, 2, 0)
        lat_p[:, ps] = r["o_latp"].reshape(D, NSEQ, 256, LP).transpose(0, 1, 3, 2)
        kpe_p[:, ps] = r["o_kpep"].transpose(0, 1, 3, 2)
        hs_p[:, ps] = r["o_hsp"].transpose(0, 1, 3, 2, 4)
        cv_p[:, ps] = r["o_cvp"].transpose(0, 1, 4, 3, 2).reshape(D, NSEQ, 2, 2816)
        lat_s[:, ss] = r["o_lats"].reshape(D, 256, 2, 32).transpose(0, 2, 3, 1)
        kpe_s[:, ss] = r["o_kpes"].reshape(D, 64, 2, 32).transpose(0, 2, 3, 1)
        hs_s[:, ss] = r["o_hss"].transpose(0, 1, 3, 2, 4)
        cv_s[:, ss] = r["o_cvs"].transpose(0, 1, 4, 3, 2).reshape(D, 2, 2, 2816)
    return (y_p, y_s, lat_p, kpe_p, hs_p, cv_p, lat_s, kpe_s, hs_s, cv_s)


def run(cfg, inp, ncores=NCORES, trace=False):
    nc = build_program(cfg)
    shared = host_shared(cfg, inp)
    in_maps = []
    for c in range(ncores):
        m = dict(shared)
        m.update(host_prep(cfg, inp, c))
        in_maps.append(m)
    res = run_bass_kernel_spmd(nc, in_maps, core_ids=list(range(ncores)), trace=trace)
    return res


def kernel(**inputs):
    cfg = Cfg()
    res = run(cfg, inputs)
    return host_gather(cfg, res.results)
```

```python
import math
from contextlib import ExitStack

import numpy as np
import concourse.bass as bass
import concourse.mybir as mybir
from concourse.bass_utils import run_bass_kernel_spmd

F32 = mybir.dt.float32
BF16 = mybir.dt.bfloat16
AF = mybir.ActivationFunctionType
ALU = mybir.AluOpType
AX = mybir.AxisListType

EPS = 1e-6
NCORES = 8


class Res:
    __slots__ = ("w", "rd")

    def __init__(self):
        self.w = None
        self.rd = []


class Op:
    __slots__ = ("eng", "fn", "deps", "sig", "sigidx", "dma", "dcnt", "waits", "alldeps", "cost", "lat",
                 "idx", "nd", "succ", "rt", "cls")


class Prog:
    ENGS = ("pe", "act", "dve", "pool", "sp")

    DEFCOST = {"pe": 0.45, "act": 0.6, "dve": 0.6, "pool": 1.1, "sp": 0.1}

    def __init__(self):
        self.ops = {e: [] for e in self.ENGS}
        self.dma_counts = {}
        self.order = []

    def add(self, eng, fn, reads=(), writes=(), dma=None, ndma=1, cost=None, lat=None):
        op = Op()
        op.eng = eng
        op.fn = fn
        op.cost = cost if cost is not None else (0.1 * ndma if dma is not None else self.DEFCOST[eng])
        op.lat = lat if lat is not None else (3.0 if dma is not None else 0.0)
        op.idx = len(self.order)
        op.cls = None
        self.order.append(op)
        op.sig = False
        op.sigidx = 0
        op.dma = dma
        op.dcnt = 0
        if dma is not None:
            c = self.dma_counts.get(dma, 0) + 16 * ndma
            self.dma_counts[dma] = c
            op.dcnt = c
        deps = {}
        for r in reads:
            if r.w is not None:
                deps[r.w] = True
        for r in writes:
            if r.w is not None:
                deps.setdefault(r.w, False)
            for o in r.rd:
                deps.setdefault(o, False)
        keep = []
        op.alldeps = [d for d in deps if d is not op]
        for d, raw in deps.items():
            if d is op:
                continue
            if d.dma is not None:
                keep.append(d)
            elif d.eng == eng:
                if eng in ("pe", "sp"):
                    continue
                d.sig = True
                keep.append(d)
            else:
                d.sig = True
                keep.append(d)
        op.deps = keep
        for r in reads:
            r.rd.append(op)
        for r in writes:
            r.w = op
            r.rd = []
        self.ops[eng].append(op)
        return op

    def schedule(self):
        import heapq
        for op in self.order:
            op.nd = len(op.alldeps)
            op.succ = []
            op.rt = 0.0
        for op in self.order:
            for d in op.alldeps:
                d.succ.append(op)
        heaps = {e: [] for e in self.ENGS}
        for op in self.order:
            if op.nd == 0:
                heapq.heappush(heaps[op.eng], (0.0, op.idx, op))
        etime = {e: 0.0 for e in self.ENGS}
        new = {e: [] for e in self.ENGS}
        left = len(self.order)
        cur_cls = [None]
        TSW = 1.3

        def act_pick(pop):
            h = heaps["act"]
            cand = [heapq.heappop(h) for _ in range(min(8, len(h)))]
            bi, bk = 0, None
            for i, (rt, idx, op) in enumerate(cand):
                st = rt if rt > etime["act"] else etime["act"]
                if op.cls is not None and cur_cls[0] is not None and op.cls != cur_cls[0]:
                    st += TSW
                if bk is None or (st, idx) < bk:
                    bk, bi = (st, idx), i
            chosen = cand[bi]
            for i, c in enumerate(cand):
                if not (pop and i == bi):
                    heapq.heappush(h, c)
            return bk, chosen

        while left:
            best = None
            for e in self.ENGS:
                h = heaps[e]
                if h:
                    if e == "act":
                        (st, idx), _c = act_pick(False)
                    else:
                        rt, idx, op = h[0]
                        st = rt if rt > etime[e] else etime[e]
                    if best is None or (st, idx) < best[0]:
                        best = ((st, idx), e)
            (st, idx), e = best
            if e == "act":
                (st, idx), (rt, idx, op) = act_pick(True)
                if op.cls is not None:
                    cur_cls[0] = op.cls
            else:
                rt, idx, op = heapq.heappop(heaps[e])
            etime[e] = st + op.cost
            done = st + op.cost + op.lat
            new[e].append(op)
            left -= 1
            for s_ in op.succ:
                t = done + (0.25 if s_.eng != e else 0.1)
                if t > s_.rt:
                    s_.rt = t
                s_.nd -= 1
                if s_.nd == 0:
                    heapq.heappush(heaps[s_.eng], (s_.rt, s_.idx, s_))
        last = {}
        for e in self.ENGS:
            for op in new[e]:
                if op.dma is not None:
                    assert last.get(op.dma, 0) < op.dcnt, ("dma order", op.dma)
                    last[op.dma] = op.dcnt
        self.ops = new
        self.makespan = max(etime.values())

    def finalize(self):
        for e in self.ENGS:
            n = 0
            for op in self.ops[e]:
                if op.sig and op.dma is None:
                    n += 1
                    op.sigidx = n
        for e in self.ENGS:
            for op in self.ops[e]:
                w = {}
                for d in op.deps:
                    if d.dma is not None:
                        k = ("dma", d.dma)
                        v = d.dcnt
                    else:
                        k = ("eng", d.eng)
                        v = d.sigidx
                    if w.get(k, 0) < v:
                        w[k] = v
                op.waits = w

    def emit(self, eng, e, semof):
        waited = {}
        for op in self.ops[eng]:
            for k, v in op.waits.items():
                if waited.get(k, 0) < v:
                    e.wait_ge(semof(k), v)
                    waited[k] = v
            r = op.fn(e)
            if op.dma is not None:
                lst = r if isinstance(r, (list, tuple)) else [r]
                for ins in lst:
                    ins.then_inc(semof(("dma", op.dma)), 16)
            elif op.sig:
                ins = r[-1] if isinstance(r, (list, tuple)) else r
                ins.then_inc(semof(("eng", eng)), 1)
        if eng in ("sp", "pool"):
            for k, c in self.dma_counts.items():
                if self.dma_eng.get(k) == eng:
                    e.wait_ge(semof(("dma", k)), c)


GRAN = 512


class Buf:
    def __init__(self, ar, gran, off, nbytes, dtype, shape):
        self.off = off
        self.nbytes = nbytes
        self.dtype = dtype
        esz = 4 if dtype == F32 else 2
        self.esz = esz
        n = nbytes // esz
        v = ar[:, off // 2:(off + nbytes) // 2]
        if dtype == F32:
            v = v.bitcast(F32)
        self.flat = v
        self.shape = shape
        if len(shape) == 1:
            self.v = v
        elif len(shape) == 2:
            self.v = v.rearrange("p (a b) -> p a b", b=shape[1])
        else:
            self.v = v.rearrange("p (a b c) -> p a b c", b=shape[1], c=shape[2])
        self.gran = gran
        self.n = n

    def r(self, lo=0, hi=None):
        if hi is None:
            hi = self.n
        b0 = (self.off + lo * self.esz) // GRAN
        b1 = (self.off + hi * self.esz - 1) // GRAN
        return self.gran[b0:b1 + 1]

    def rs(self, i, inner):
        return self.r(i * inner, (i + 1) * inner)


class Cfg:
    def __init__(self, LP=4096, DEPTH=4, PAST=1024, NSEQ=2, wslots=3):
        self.LP = LP
        self.DEPTH = DEPTH
        self.PAST = PAST
        self.NSEQ = NSEQ
        self.NT = LP // 512
        self.wslots = wslots
        D = DEPTH
        o = 0
        self.po = {}
        for name, n in (("g1", D * 8), ("g2", D * 8), ("gq", D * 3), ("gkv", D * 2), ("gh", D),
                        ("lbl", 4 * D), ("cw", D * 3 * 22), ("cb", D * 22), ("gf", 8)):
            self.po[name] = (o, n)
            o += n
        self.NPAR = o


def build_program(cfg):
    LP, DEPTH, PAST, NSEQ, NT = cfg.LP, cfg.DEPTH, cfg.PAST, cfg.NSEQ, cfg.NT
    NPG = PAST // 512
    nc = bass.Bass("TRN2", target_bir_lowering=False)
    P = Prog()
    P.dma_eng = {}

    def dram(name, shape, dt=F32, kind="ExternalInput"):
        return nc.dram_tensor(name, list(shape), dt, kind=kind).ap()

    d_xp = dram("xp", [NSEQ, 8, 128, LP])
    d_xs = dram("xs", [8, 128, 64])
    d_latp = dram("latp", [DEPTH, 2, 2, 128, PAST])
    d_krp = dram("krp", [DEPTH, 2, 128, PAST])
    d_sh = dram("sh", [DEPTH, 2, 128, 4, 128])
    d_scv = dram("scv", [DEPTH, 2, 128, 22, 2])
    d_par = dram("par", [128, cfg.NPAR])
    d_const = dram("const", [128, 128 + 128 + 512 + 64])
    d_const2 = dram("const2", [128, 128 + 128 + 512])
    d_csp = dram("csp", [2, 128, LP])
    d_css = dram("css", [2, 128, 64])
    d_win = dram("w_in", [DEPTH, 1024, 4800])
    d_winC = dram("w_inC", [DEPTH, 1024, 896])
    d_wuq = dram("w_uqR", [DEPTH, 384, 2048])
    d_wukv = dram("w_ukvR", [DEPTH, 256, 2048])
    d_wpa = dram("w_pa", [DEPTH, 1024, 1024])
    d_wpb = dram("w_pb", [DEPTH, 512, 1024])
    d_wo = dram("w_o", [DEPTH, 1024, 1024])
    d_wup = dram("w_up", [DEPTH, 1024, 5632])
    d_wdn = dram("w_dn", [DEPTH, 2816, 1024])
    o_yp = dram("o_yp", [NSEQ, 8, 128, LP], kind="ExternalOutput")
    o_ys = dram("o_ys", [8, 128, 64], kind="ExternalOutput")
    o_latp = dram("o_latp", [DEPTH, NSEQ, 2, 128, LP], kind="ExternalOutput")
    o_kpep = dram("o_kpep", [DEPTH, NSEQ, 64, LP], kind="ExternalOutput")
    o_hsp = dram("o_hsp", [DEPTH, NSEQ, 128, 4, 128], kind="ExternalOutput")
    o_cvp = dram("o_cvp", [DEPTH, NSEQ, 128, 22, 2], kind="ExternalOutput")
    o_lats = dram("o_lats", [DEPTH, 2, 128, 64], kind="ExternalOutput")
    o_kpes = dram("o_kpes", [DEPTH, 64, 64], kind="ExternalOutput")
    o_hss = dram("o_hss", [DEPTH, 2, 128, 4, 128], kind="ExternalOutput")
    o_cvs = dram("o_cvs", [DEPTH, 2, 128, 22, 2], kind="ExternalOutput")
    s_in = dram("s_in", [DEPTH, 1024, 4096], BF16, "Internal")
    s_inC = dram("s_inC", [DEPTH, 1024, 896], BF16, "Internal")
    s_uq = dram("s_uq", [DEPTH, 384, 2048], BF16, "Internal")
    s_ukv = dram("s_ukv", [DEPTH, 256, 2048], BF16, "Internal")
    s_pa = dram("s_pa", [DEPTH, 1024, 1024], BF16, "Internal")
    s_pb = dram("s_pb", [DEPTH, 512, 1024], BF16, "Internal")
    s_wo = dram("s_wo", [DEPTH, 1024, 1024], BF16, "Internal")
    s_up = dram("s_up", [DEPTH, 1024, 5632], BF16, "Internal")
    s_dn = dram("s_dn", [DEPTH, 2816, 1024], BF16, "Internal")
    s_kc = dram("s_kc", [NSEQ, DEPTH, 128, NT * 4, 8, 128], BF16, "Internal")
    s_vc = dram("s_vc", [NSEQ, DEPTH, NT * 4, 128, 1024], BF16, "Internal")
    s_pc = dram("s_pc", [NSEQ, DEPTH, 128, LP], BF16, "Internal")

    es = ExitStack()
    ARENA_BYTES = 207 * 1024
    ar = es.enter_context(nc.sbuf_tensor("arena", [128, ARENA_BYTES // 2], BF16))
    gran = [Res() for _ in range(ARENA_BYTES // GRAN + 1)]
    top = [0]

    def alloc(shape, dt, at=None):
        esz = 4 if dt == F32 else 2
        n = 1
        for s in shape:
            n *= s
        nb = n * esz
        nb_al = (nb + GRAN - 1) // GRAN * GRAN
        if at is None:
            off = top[0]
            top[0] += nb_al
        else:
            off = at[0]
            at[0] += nb_al
        if off + nb_al > ARENA_BYTES:
            raise AssertionError(("arena overflow", off, nb_al, shape, "OV", globals().get("_OV")))
        return Buf(ar, gran, off, nb, dt, shape)

    nws = cfg.wslots
    xT = alloc([8, 512], F32)
    hT = alloc([8, 512], BF16)
    WS = [alloc([8192], BF16) for _ in range(nws)]
    rstd = [alloc([512], F32) for _ in range(2)]
    gates = alloc([16, 512], BF16)
    obT = alloc([4, 512], BF16)
    Sst = alloc([DEPTH, 4, 128], F32)
    cs = alloc([2, 512], F32)
    identb = alloc([128], BF16)
    onesb = alloc([128], BF16)
    onesf = alloc([128], F32)
    constf = alloc([128 + 128 + 512 + 64], F32)
    mUb = alloc([128], BF16)
    mVb = alloc([512], BF16)
    par = alloc([cfg.NPAR], F32)
    g1s = alloc([DEPTH, 8], F32)
    g2s = alloc([DEPTH, 8], F32)
    gqs = alloc([DEPTH, 3], F32)
    gkvs = alloc([DEPTH, 2], F32)
    ghs = alloc([DEPTH], F32)
    gfs = alloc([8], F32)
    lbe = alloc([4, DEPTH], F32)
    lbm = alloc([4], F32)
    lbv = alloc([4, DEPTH], F32)
    oml = alloc([4, DEPTH], F32)
    ctail = alloc([DEPTH, 22, 2], F32)
    shalo = alloc([2, 22, 2], F32)
    stail = alloc([2, 22, 2], F32)
    S0s = alloc([2, 4, 128], F32)
    dd = alloc([4, 8], F32)
    epsb = alloc([4], F32)
    OV = top[0]
    globals()['_OV'] = OV
    a = [OV]
    TT2 = [[alloc([512], F32, a) for _ in range(7)] for _ in range(2)]
    qeb = alloc([2, 512], BF16, a)
    keb = alloc([2, 512], BF16, a)
    kdb = alloc([2, 512], BF16, a)
    qbb = alloc([4, 512], BF16, a)
    Vtok = alloc([4, 512], BF16, a)
    kdTokE = alloc([2, 4, 128], BF16, a)
    kdTokO = alloc([2, 4, 128], BF16, a)
    Am = alloc([4, 4, 128], BF16, a)
    Sb = alloc([4, 8, 128], BF16, a)
    sghg = alloc([4, 512], BF16, a)
    sqo = alloc([512], BF16, a)
    to = alloc([512], F32, a)
    endH = a[0]
    a = [OV]
    Qn = alloc([8, 512], BF16, a)
    Qp = alloc([8, 512], BF16, a)
    Kcur = alloc([4, 8, 128], BF16, a)
    Vcur = alloc([4, 1024], BF16, a)
    Pcur = alloc([512], BF16, a)
    M12 = a[0]
    a = [M12]
    cqf = alloc([3, 512], F32, a)
    cqn = alloc([3, 512], BF16, a)
    ckvf = alloc([2, 512], F32, a)
    latb = alloc([2, 512], BF16, a)
    kpef = alloc([512], F32, a)
    tmp1 = alloc([512], F32, a)
    tmp2 = alloc([512], F32, a)
    sqm = alloc([3, 512], BF16, a)
    endM1 = a[0]
    a = [M12]
    KS = []
    for i in range(2):
        KS.append((alloc([4, 4, 128], BF16, a), alloc([4, 512], BF16, a), alloc([512], BF16, a)))
    Pt = [alloc([512], BF16, a) for _ in range(8)]
    oaT = alloc([8, 512], BF16, a)
    Pacc = alloc([4, 512], F32, a)
    rsb = alloc([512], F32, a)
    al_ = [KS[0][0].off]
    mta = [alloc([512], F32, al_) for _ in range(2)]
    mtb = [alloc([512], F32, al_) for _ in range(2)]
    lpf = alloc([2, 512], F32, [KS[1][1].off])
    lpb = alloc([2, 512], BF16, [oaT.off])
    kpf = alloc([512], F32, [oaT.off + 2048])
    Vnew = alloc([2, 1024], BF16, [KS[0][1].off])
    endM2 = a[0]
    a = [OV]
    gT = alloc([22, 512], BF16, a)
    aext = [alloc([520], F32, a) for _ in range(4)]
    cvb = [alloc([512], F32, a) for _ in range(4)]
    ub = [alloc([512], F32, a) for _ in range(4)]
    sbf = [alloc([512], F32, a) for _ in range(4)]
    endF = a[0]
    import os as _os0
    if _os0.environ.get("KVERB"):
        print("ARENA OV", OV, "H", endH, "M12", M12, "M1", endM1, "M2", endM2, "F", endF, "cap", ARENA_BYTES)
    assert max(endH, endM1, endM2, endF) <= ARENA_BYTES, (endH, endM1, endM2, endF)

    PSB = [es.enter_context(nc.psum_tensor(f"ps{i}", [128, 512], F32)) for i in range(8)]
    PSR = [Res() for _ in range(8)]

    dres = {}

    def DR(key):
        r = dres.get(key)
        if r is None:
            r = dres[key] = Res()
        return r

    def dma(eng, key, fn, reads, writes, n=1, lat=None):
        P.dma_eng[key] = eng
        return P.add(eng, fn, reads, writes, dma=key, ndma=n, lat=lat)

    class _Sniff:
        def activation(self, **kw):
            self.func = kw.get("func")

    _CLS = {AF.Exp: "le", AF.Ln: "le", AF.Sigmoid: "sg", AF.Silu: "si"}

    def act(fn, reads, writes, cost=None):
        op = P.add("act", fn, reads, writes, cost=cost)
        sn = _Sniff()
        fn(sn)
        op.cls = _CLS.get(sn.func)
        return op

    def dve(fn, reads, writes, cost=None):
        return P.add("dve", fn, reads, writes, cost=cost)

    def pool(fn, reads, writes, cost=None):
        return P.add("pool", fn, reads, writes, cost=cost)

    def pe(fn, reads, writes, cost=None):
        return P.add("pe", fn, reads, writes, cost=cost)

    def mm_group(out_ap, pairs, reads, writes, start=True, stop=True):
        def fn(e, out_ap=out_ap, pairs=pairs, start=start, stop=stop):
            ins = None
            n = len(pairs)
            for i, (l, r) in enumerate(pairs):
                ins = e.matmul(out_ap, l, r, start=(start and i == 0), stop=(stop and i == n - 1))
            return ins
        ncol = out_ap.shape[-1]
        return pe(fn, reads, writes, cost=len(pairs) * max(0.07, 0.22 * ncol / 512.0))

    dma("sp", "par", lambda e: e.dma_start(out=par.v, in_=d_par[:, :]), [], par.r())
    dma("sp", "const", lambda e: e.dma_start(out=constf.v, in_=d_const[:, :]), [], constf.r())
    maskP_v = constf.v[:, 0:128]
    maskS_v = constf.v[:, 128:256]
    resetP_v = constf.v[:, 256:768]
    resetS_v = constf.v[:, 768:832]
    const2 = alloc([768], F32, [OV])
    dma("sp", "const2", lambda e: e.dma_start(out=const2.v, in_=d_const2[:, :]), [], const2.r())
    dve(lambda e: e.tensor_copy(out=identb.v, in_=const2.v[:, 0:128]), const2.r(), identb.r())
    dve(lambda e: e.tensor_copy(out=mUb.v, in_=const2.v[:, 128:256]), const2.r(), mUb.r())
    dve(lambda e: e.tensor_copy(out=mVb.v, in_=const2.v[:, 256:768]), const2.r(), mVb.r())
    dve(lambda e: e.memset(onesb.v, 1.0), [], onesb.r())
    dve(lambda e: e.memset(onesf.v, 1.0), [], onesf.r())
    epsc = {}
    for i_, n_ in enumerate((1024.0, 384.0, 256.0, 128.0)):
        dve(lambda e, i_=i_, n_=n_: e.memset(epsb.v[:, i_:i_ + 1], n_ * EPS), [], epsb.r())
        epsc[n_ * EPS] = epsb.v[:, i_:i_ + 1]
    dve(lambda e: e.memset(Sst.flat, 0.0), [], Sst.r())
    dve(lambda e: e.memset(ctail.flat, 0.0), [], ctail.r())

    def pv(name):
        o, n = cfg.po[name]
        return par.flat[:, o:o + n]

    def scale_par(dst, name, s):
        dve(lambda e: e.tensor_scalar(out=dst.flat, in0=pv(name), scalar1=float(s), scalar2=None,
                                      op0=ALU.mult), par.r(), dst.r())

    scale_par(g1s, "g1", math.sqrt(1024.0))
    scale_par(g2s, "g2", math.sqrt(1024.0))
    scale_par(gqs, "gq", math.sqrt(384.0))
    scale_par(gkvs, "gkv", math.sqrt(256.0))
    scale_par(ghs, "gh", math.sqrt(128.0))
    scale_par(gfs, "gf", math.sqrt(1024.0))
    lbl_v = pv("lbl").rearrange("p (h l) -> p h l", l=DEPTH)
    dve(lambda e: e.tensor_reduce(out=lbm.v, in_=lbl_v, axis=AX.X, op=ALU.max), par.r(), lbm.r())
    dve(lambda e: e.tensor_tensor(out=lbe.v, in0=lbl_v,
                                  in1=lbm.v.unsqueeze(2).to_broadcast([128, 4, DEPTH]), op=ALU.subtract),
        par.r() + lbm.r(), lbe.r())
    act(lambda e: e.activation(out=lbe.flat, in_=lbe.flat, func=AF.Exp), lbe.r(), lbe.r())
    dve(lambda e: e.tensor_reduce(out=lbm.v, in_=lbe.v, axis=AX.X, op=ALU.add), lbe.r(), lbm.r())
    dve(lambda e: e.reciprocal(out=lbm.v, in_=lbm.v), lbm.r(), lbm.r())
    dve(lambda e: e.tensor_tensor(out=lbe.v, in0=lbe.v,
                                  in1=lbm.v.unsqueeze(2).to_broadcast([128, 4, DEPTH]), op=ALU.mult),
        lbe.r() + lbm.r(), lbe.r())
    dve(lambda e: e.memset(lbv.flat, 0.0), [], lbv.r())
    for l in range(1, DEPTH):
        dve(lambda e, l=l: e.tensor_tensor(out=lbv.v[:, :, l], in0=lbv.v[:, :, l - 1], in1=lbe.v[:, :, l],
                                           op=ALU.add), lbv.r() + lbe.r(), lbv.r())
    dve(lambda e: e.tensor_scalar(out=oml.flat, in0=lbv.flat, scalar1=-1.0, scalar2=1.0,
                                  op0=ALU.mult, op1=ALU.add), lbv.r(), oml.r())

    def cast_layer(l):
        def mk(group):
            def fn(e, l=l, group=group):
                out = []

                def rows(dst, src, nrows, step=128):
                    for r0 in range(0, nrows, step):
                        out.append(e.dma_start(out=dst[r0:r0 + step, :], in_=src[r0:r0 + step, :]))
                if group == 0:
                    rows(s_in[l], d_win[l][:, 704:4800], 1024)
                elif group == 1:
                    rows(s_inC[l], d_winC[l], 1024)
                    rows(s_uq[l], d_wuq[l], 384)
                    rows(s_ukv[l], d_wukv[l], 256)
                    rows(s_pa[l], d_wpa[l], 1024)
                    rows(s_pb[l], d_wpb[l], 512)
                    rows(s_wo[l], d_wo[l], 1024)
                else:
                    rows(s_up[l], d_wup[l], 1024)
                    rows(s_dn[l], d_wdn[l], 2816)
                return out
            return fn
        for g, n in ((0, 8), (1, 8 + 3 + 2 + 8 + 4 + 8), (2, 8 + 22)):
            dma("pool", f"cast{l}_{g}", mk(g), [], [DR(("w", l, g))], n=n, lat=(80.0, 150.0, 250.0)[g])

    for l in range(DEPTH):
        cast_layer(l)

    wctr = [0]

    def load_piece(l, src_fn, nel, wg=1):
        i = wctr[0] % nws
        wctr[0] += 1
        slot = WS[i]

        def fn(e, slot=slot):
            return [e.dma_start(out=d, in_=s) for d, s in src_fn(slot.flat)]
        ncalls = len(src_fn(slot.flat))
        dma("sp", f"w{i}", fn, [DR(("w", l, wg))], slot.r(0, nel), n=ncalls, lat=3.0 + nel * 256 / 150e3)
        return slot

    def wview(src2d, kc, c0, c1):
        return src2d.rearrange("(k p) n -> p k n", p=128)[:, :, c0:c1]

    def piece_simple(l, src2d, kc, c0, c1):
        w = c1 - c0

        def src_fn(flat):
            return [(flat[:, 0:kc * w].rearrange("p (k n) -> p k n", n=w), wview(src2d, kc, c0, c1))]
        slot = load_piece(l, src_fn, kc * w, 0 if src2d.tensor.name == "s_in" else 1)
        return slot, slot.flat[:, 0:kc * w].rearrange("p (k n) -> p k n", n=w)

    bctr = [0]

    def next_bank(banks):
        b = banks[bctr[0] % len(banks)]
        bctr[0] += 1
        return b

    def rstd_from(buf, bank, c, N):
        act(lambda e: e.activation(out=buf.v[:, 0:N], in_=PSB[bank][:, 0:N], func=AF.Ln, bias=epsc[c], scale=1.0),
            [PSR[bank]] + epsb.r(), buf.r())
        act(lambda e: e.activation(out=buf.v[:, 0:N], in_=buf.v[:, 0:N], func=AF.Exp, scale=-0.5), buf.r(), buf.r())

    SCALE = 1.0 / math.sqrt(192.0)
    GC = 2.0 * 0.7978845608028654

    def rmsnorm_x(N, gs_col, out_fn, out_res_fn):
        for c in range(8):
            act(lambda e, c=c: e.activation(out=hT.v[:, c, 0:N], in_=xT.v[:, c, 0:N], func=AF.Square),
                xT.rs(c, 512), hT.rs(c, 512))
        b = 3
        mm_group(PSB[b][:, 0:N], [(onesb.v, hT.v[:, c, 0:N]) for c in range(8)], hT.r() + onesb.r(), [PSR[b]])
        rb = rstd[0]
        rstd_from(rb, b, 1024.0 * EPS, N)
        for c in range(8):
            dve(lambda e, c=c: e.scalar_tensor_tensor(out=out_fn(c), in0=xT.v[:, c, 0:N], scalar=gs_col(c),
                                                      in1=rb.v[:, 0:N], op0=ALU.mult, op1=ALU.mult),
                xT.rs(c, 512) + rb.r(), out_res_fn(c))

    import os as _os
    KSTOP = float(_os.environ.get("KSTOP", "99"))

    def tile_layer(l, N, kind, s=0, j=0):
        prompt = kind == "p"
        C = 64 if prompt else 32
        TB = 128 if prompt else 64
        NB = N // TB
        NCH = N // C
        cos_v = cs.v[:, 0, 0:N]
        sin_v = cs.v[:, 1, 0:N]
        mask_v = maskP_v if prompt else maskS_v[0:64, 0:64]
        reset_v = resetP_v if prompt else resetS_v
        dbanks = [0, 1]

        rmsnorm_x(N, lambda c: g1s.v[:, l, c:c + 1], lambda c: hT.v[:, c, 0:N], lambda c: hT.rs(c, 512))

        if KSTOP <= 1:
            return
        def dense(wv, kc_n, cols, rhs_fn, rhs_res, slot, banks=dbanks):
            b = next_bank(banks)
            mm_group(PSB[b][:, 0:N], [(wv[:, kc, cols[0]:cols[1]], rhs_fn(kc)) for kc in range(kc_n)],
                     rhs_res + slot.r(), [PSR[b]])
            return b

        hrhs = lambda kc: hT.v[:, kc, 0:N]

        slotA, wA = piece_simple(l, s_in[l], 8, 0, 1024)
        slotB, wB = piece_simple(l, s_in[l], 8, 1024, 2048)
        pool(lambda e: e.memset(kdTokE.v[C:TB], 0.0), [], kdTokE.r())
        pool(lambda e: e.memset(kdTokO.v[0:C], 0.0), [], kdTokO.r())
        if not prompt:
            dma("sp", "s0s", lambda e: [e.dma_start(out=S0s.v[:, q], in_=d_sh[l, q]) for q in range(2)],
                [], S0s.r(), n=2)
        def hgrn_head(h, T0, T1, T2, T3, T4, T5, T6):
            bq = dense(wA, 8, (h * 128, h * 128 + 128), hrhs, hT.r(), slotA)
            act(lambda e, bq=bq: e.activation(out=T0.v[:, 0:N], in_=PSB[bq][:, 0:N], func=AF.Silu),
                [PSR[bq]], T0.r())
            bf = dense(wA, 8, (512 + h * 128, 512 + h * 128 + 128), hrhs, hT.r(), slotA)
            act(lambda e, bf=bf: e.activation(out=T1.v[:, 0:N], in_=PSB[bf][:, 0:N], func=AF.Sigmoid),
                [PSR[bf]], T1.r())
            dve(lambda e, h=h: e.tensor_scalar(out=T1.v[:, 0:N], in0=T1.v[:, 0:N], scalar1=oml.v[:, h, l:l + 1],
                                               scalar2=lbv.v[:, h, l:l + 1], op0=ALU.mult, op1=ALU.add),
                T1.r() + oml.r() + lbv.r(), T1.r())
            act(lambda e: e.activation(out=T2.v[:, 0:N], in_=T1.v[:, 0:N], func=AF.Ln), T1.r(), T2.r())
            if KSTOP <= 1.1:
                return True
            dve(lambda e: e.tensor_tensor_scan(out=T3.v[:, 0:N], data0=reset_v, data1=T2.v[:, 0:N], initial=0.0,
                                               op0=ALU.mult, op1=ALU.add), T2.r() + constf.r(), T3.r())
            dve(lambda e: e.tensor_scalar(out=T1.v[:, 0:N], in0=T1.v[:, 0:N], scalar1=-1.0, scalar2=1.0,
                                          op0=ALU.mult, op1=ALU.add), T1.r(), T1.r())
            if KSTOP <= 1.2:
                return True
            b3 = T3.v[:, 0:N].rearrange("p (c t) -> p c t", t=C)
            rmid = b3[:, :, C // 2 - 1:C // 2].to_broadcast([128, NCH, C])
            rend = b3[:, :, C - 1:C].to_broadcast([128, NCH, C])
            dve(lambda e: e.tensor_tensor(out=T2.v[:, 0:N].rearrange("p (c t) -> p c t", t=C), in0=b3, in1=rmid,
                                          op=ALU.subtract), T3.r(), T2.r())
            dve(lambda e: e.tensor_tensor(out=T4.v[:, 0:N].rearrange("p (c t) -> p c t", t=C), in0=b3, in1=rend,
                                          op=ALU.subtract), T3.r(), T4.r())
            act(lambda e, h=h: e.activation(out=dd.v[:, h, 0:NCH].unsqueeze(2), in_=b3[:, :, C - 1:C], func=AF.Exp),
                T3.r(), dd.r())
            if KSTOP <= 1.3:
                return True
            hb = h % 2
            act(lambda e: e.activation(out=T5.v[:, 0:N], in_=T3.v[:, 0:N], func=AF.Exp), T3.r(), T5.r())
            dve(lambda e, h=h: e.scalar_tensor_tensor(out=qbb.v[:, h, 0:N], in0=T0.v[:, 0:N], scalar=128.0 ** -0.5,
                                                      in1=T5.v[:, 0:N], op0=ALU.mult, op1=ALU.mult),
                T0.r() + T5.r(), qbb.rs(h, 512))
            act(lambda e: e.activation(out=T6.v[:, 0:N], in_=T2.v[:, 0:N], func=AF.Exp), T2.r(), T6.r())
            dve(lambda e, hb=hb: e.scalar_tensor_tensor(out=qeb.v[:, hb, 0:N], in0=T0.v[:, 0:N], scalar=128.0 ** -0.5,
                                                        in1=T6.v[:, 0:N], op0=ALU.mult, op1=ALU.mult),
                T0.r() + T6.r(), qeb.rs(hb, 512))
            act(lambda e: e.activation(out=T5.v[:, 0:N], in_=T2.v[:, 0:N], func=AF.Exp, scale=-1.0), T2.r(), T5.r())
            dve(lambda e, hb=hb: e.tensor_tensor(out=keb.v[:, hb, 0:N], in0=T1.v[:, 0:N], in1=T5.v[:, 0:N],
                                                 op=ALU.mult), T1.r() + T5.r(), keb.rs(hb, 512))
            act(lambda e: e.activation(out=T6.v[:, 0:N], in_=T4.v[:, 0:N], func=AF.Exp, scale=-1.0), T4.r(), T6.r())
            dve(lambda e, hb=hb: e.tensor_tensor(out=kdb.v[:, hb, 0:N], in0=T1.v[:, 0:N], in1=T6.v[:, 0:N],
                                                 op=ALU.mult), T1.r() + T6.r(), kdb.rs(hb, 512))
            if KSTOP <= 1.4:
                return True
            if h == 0:
                for blk in range(NB):
                    b = next_bank(dbanks)
                    mm_group(PSB[b][0:TB, 0:512],
                             [(hT.v[:, kc, blk * TB:(blk + 1) * TB], wB[:, kc, 0:512]) for kc in range(8)],
                             hT.r() + slotB.r(), [PSR[b]])
                    act(lambda e, b=b, blk=blk: e.activation(out=Vtok.v[0:TB, blk, :], in_=PSB[b][0:TB, 0:512],
                                                             func=AF.Copy), [PSR[b]], Vtok.rs(blk, 512))
            if KSTOP <= 1.5:
                return True
            psb7 = PSB[7][:, :].bitcast(BF16)
            for blk in range(NB):
                pe(lambda e, blk=blk, hb=hb: e.transpose(psb7[0:TB, blk * 128:(blk + 1) * 128],
                                                         kdb.v[:, hb, blk * TB:(blk + 1) * TB], identb.v),
                   kdb.rs(hb, 512) + identb.r(), [PSR[7]])
            act(lambda e, h=h: e.activation(out=kdTokE.v[0:C, h % 2, 0:NB, :],
                                            in_=psb7[0:C, 0:NB * 128].rearrange("p (b k) -> p b k", k=128),
                                            func=AF.Copy), [PSR[7]], kdTokE.rs(h % 2, 512))
            act(lambda e, h=h: e.activation(out=kdTokO.v[C:TB, h % 2, 0:NB, :],
                                            in_=psb7[C:TB, 0:NB * 128].rearrange("p (b k) -> p b k", k=128),
                                            func=AF.Copy), [PSR[7]], kdTokO.rs(h % 2, 512))
            if KSTOP <= 1.6:
                return True
            for blk in range(NB):
                mm_group(PSB[6][0:TB, blk * 128:blk * 128 + TB],
                         [(keb.v[:, hb, blk * TB:(blk + 1) * TB], qeb.v[:, hb, blk * TB:(blk + 1) * TB])],
                         keb.rs(hb, 512) + qeb.rs(hb, 512), [PSR[6]])
            dve(lambda e, h=h: e.tensor_tensor(
                out=Am.v[0:TB, h, 0:NB, 0:TB],
                in0=PSB[6][0:TB, 0:NB * 128].rearrange("p (b t) -> p b t", t=128)[:, :, 0:TB],
                in1=mask_v.unsqueeze(1).to_broadcast([TB, NB, TB]), op=ALU.mult),
                [PSR[6]] + constf.r(), Am.rs(h, 512))
            if KSTOP <= 1.7:
                return True
            for c in range(NCH):
                blk = (c * C) // TB
                r0 = (c * C) % TB
                ub_ = 4 + c // 4
                kdX = kdTokE if r0 == 0 else kdTokO
                mm_group(PSB[ub_][:, (c % 4) * 128:(c % 4) * 128 + 128],
                         [(kdX.v[0:TB, h % 2, blk, :], Vtok.v[0:TB, blk, h * 128:(h + 1) * 128])],
                         kdX.rs(h % 2, 512) + Vtok.rs(blk, 512), [PSR[ub_]])
            if KSTOP <= 1.8:
                return True
            if prompt:
                Sh = Sst.v[:, l, h, :]
                Shr = Sst.r((l * 4 + h) * 128, (l * 4 + h + 1) * 128)
                if j == 0:
                    dve(lambda e, Sh=Sh: e.memset(Sh, 0.0), [], Shr)
                act(lambda e, h=h, Sh=Sh: e.activation(out=Sb.v[:, h, 0, :], in_=Sh, func=AF.Copy),
                    Shr, Sb.rs(h, 1024))
                for c in range(NCH):
                    ub_ = 4 + c // 4
                    dve(lambda e, h=h, c=c, ub_=ub_, Sh=Sh: e.scalar_tensor_tensor(
                        out=Sh, in0=Sh, scalar=dd.v[:, h, c:c + 1],
                        in1=PSB[ub_][:, (c % 4) * 128:(c % 4) * 128 + 128], op0=ALU.mult, op1=ALU.add),
                        Shr + dd.r() + [PSR[ub_]], Shr)
                    if c + 1 < NCH:
                        act(lambda e, h=h, c=c, Sh=Sh: e.activation(out=Sb.v[:, h, c + 1, :], in_=Sh, func=AF.Copy),
                            Shr, Sb.rs(h, 1024))
                if j == NT - 1:
                    dma("pool", f"o_hsp{l}_{h}", lambda e, h=h, Sh=Sh: e.dma_start(out=o_hsp[l, s][:, h, :], in_=Sh),
                        Shr, [])
            else:
                for q in range(2):
                    Sq = S0s.v[:, q, h, :]
                    Sqr = S0s.r()
                    act(lambda e, h=h, q=q, Sq=Sq: e.activation(out=Sb.v[:, h, q, :], in_=Sq, func=AF.Copy),
                        Sqr, Sb.rs(h, 1024))
                    dve(lambda e, h=h, q=q, Sq=Sq: e.scalar_tensor_tensor(
                        out=Sq, in0=Sq, scalar=dd.v[:, h, q:q + 1], in1=PSB[4][:, q * 128:q * 128 + 128],
                        op0=ALU.mult, op1=ALU.add), Sqr + dd.r() + [PSR[4]], Sqr)
            if KSTOP <= 1.9:
                return True
            bg = dense(wB, 8, (512 + h * 128, 512 + h * 128 + 128), hrhs, hT.r(), slotB)
            act(lambda e, bg=bg, h=h: e.activation(out=sghg.v[:, h, 0:N], in_=PSB[bg][:, 0:N], func=AF.Silu),
                [PSR[bg]], sghg.rs(h, 512))
        for h in range(4):
            if hgrn_head(h, *TT2[h % 2]):
                return
        if KSTOP <= 2:
            return
        if not prompt:
            dma("pool", "o_hss", lambda e: [e.dma_start(out=o_hss[l, q], in_=S0s.v[:, q]) for q in range(2)],
                S0s.r(), [], n=2)

        slotD, wD = piece_simple(l, s_in[l], 8, 2048, 3072)
        for oc in range(8):
            b = dense(wD, 8, (oc * 128, oc * 128 + 128), hrhs, hT.r(), slotD)
            act(lambda e, b=b, oc=oc: e.activation(out=gates.v[:, oc, 0:N], in_=PSB[b][:, 0:N], func=AF.Sigmoid),
                [PSR[b]], gates.rs(oc, 512))
        slotE, wE = piece_simple(l, s_in[l], 8, 3072, 4096)
        for oc in range(8):
            b = dense(wE, 8, (oc * 128, oc * 128 + 128), hrhs, hT.r(), slotE)
            act(lambda e, b=b, oc=oc: e.activation(out=gates.v[:, 8 + oc, 0:N], in_=PSB[b][:, 0:N], func=AF.Sigmoid),
                [PSR[b]], gates.rs(8 + oc, 512))

        if KSTOP <= 3:
            return
        for h in range(4):
            ob_ = 2
            for blk in range(NB):
                def fn(e, h=h, blk=blk):
                    ins = e.matmul(PSB[ob_][:, blk * TB:(blk + 1) * TB], Vtok.v[0:TB, blk, h * 128:(h + 1) * 128],
                                   Am.v[0:TB, h, blk, 0:TB], start=(blk == 0), stop=False, skip_group_check=True)
                    ncb = TB // C
                    for ci in range(ncb):
                        c = blk * ncb + ci
                        ins = e.matmul(PSB[ob_][:, c * C:(c + 1) * C], Sb.v[:, h, c, :], qbb.v[:, h, c * C:(c + 1) * C],
                                       start=False, stop=(blk == NB - 1 and ci == ncb - 1), skip_group_check=True)
                    return ins
                pe(fn, Vtok.rs(blk, 512) + Am.rs(h, 512) + Sb.rs(h, 1024) + qbb.rs(h, 512), [PSR[ob_]])
            act(lambda e: e.activation(out=sqo.v[:, 0:N], in_=PSB[ob_][:, 0:N], func=AF.Square), [PSR[ob_]], sqo.r())
            mm_group(PSB[3][:, 0:N], [(onesb.v, sqo.v[:, 0:N])], sqo.r() + onesb.r(), [PSR[3]])
            rb = rstd[1]
            rstd_from(rb, 3, 128.0 * EPS, N)
            dve(lambda e: e.tensor_tensor(out=to.v[:, 0:N], in0=PSB[ob_][:, 0:N], in1=rb.v[:, 0:N], op=ALU.mult),
                [PSR[ob_]] + rb.r(), to.r())
            dve(lambda e, h=h: e.scalar_tensor_tensor(out=obT.v[:, h, 0:N], in0=to.v[:, 0:N], scalar=ghs.v[:, l:l + 1],
                                                      in1=sghg.v[:, h, 0:N], op0=ALU.mult, op1=ALU.mult),
                to.r() + ghs.r() + sghg.rs(h, 512), obT.rs(h, 512))

        if KSTOP <= 4:
            return
        slotC, wC = piece_simple(l, s_inC[l], 8, 0, 896)
        for c in range(3):
            b = dense(wC, 8, (c * 128, c * 128 + 128), hrhs, hT.r(), slotC)
            act(lambda e, b=b, c=c: e.activation(out=cqf.v[:, c, 0:N], in_=PSB[b][:, 0:N], func=AF.Copy),
                [PSR[b]], cqf.rs(c, 512))
            act(lambda e, b=b, c=c: e.activation(out=sqm.v[:, c, 0:N], in_=PSB[b][:, 0:N], func=AF.Square),
                [PSR[b]], sqm.rs(c, 512))
        mm_group(PSB[3][:, 0:N], [(onesb.v, sqm.v[:, c, 0:N]) for c in range(3)], sqm.r() + onesb.r(), [PSR[3]])
        rq = rstd[0]
        rstd_from(rq, 3, 384.0 * EPS, N)
        for c in range(3):
            dve(lambda e, c=c: e.scalar_tensor_tensor(out=cqn.v[:, c, 0:N], in0=cqf.v[:, c, 0:N],
                                                      scalar=gqs.v[:, l, c:c + 1], in1=rq.v[:, 0:N],
                                                      op0=ALU.mult, op1=ALU.mult),
                cqf.rs(c, 512) + rq.r() + gqs.r(), cqn.rs(c, 512))
        for c in range(2):
            b = dense(wC, 8, (384 + c * 128, 384 + c * 128 + 128), hrhs, hT.r(), slotC)
            act(lambda e, b=b, c=c: e.activation(out=ckvf.v[:, c, 0:N], in_=PSB[b][:, 0:N], func=AF.Copy),
                [PSR[b]], ckvf.rs(c, 512))
            act(lambda e, b=b, c=c: e.activation(out=sqm.v[:, c, 0:N], in_=PSB[b][:, 0:N], func=AF.Square),
                [PSR[b]], sqm.rs(c, 512))
        mm_group(PSB[3][:, 0:N], [(onesb.v, sqm.v[:, c, 0:N]) for c in range(2)], sqm.r() + onesb.r(), [PSR[3]])
        rk = rstd[1]
        rstd_from(rk, 3, 256.0 * EPS, N)
        for c in range(2):
            dve(lambda e, c=c: e.scalar_tensor_tensor(out=ckvf.v[:, c, 0:N], in0=ckvf.v[:, c, 0:N],
                                                      scalar=gkvs.v[:, l, c:c + 1], in1=rk.v[:, 0:N],
                                                      op0=ALU.mult, op1=ALU.mult),
                ckvf.rs(c, 512) + rk.r() + gkvs.r(), ckvf.rs(c, 512))
            act(lambda e, c=c: e.activation(out=latb.v[:, c, 0:N], in_=ckvf.v[:, c, 0:N], func=AF.Copy),
                ckvf.rs(c, 512), latb.rs(c, 512))
        if prompt:
            dma("pool", "o_lat", lambda e: e.dma_start(
                out=o_latp[l, s].rearrange("c p t -> p c t")[:, :, j * 512:(j + 1) * 512], in_=ckvf.v),
                ckvf.r(), [])
        else:
            dma("pool", "o_lat", lambda e: e.dma_start(out=o_lats[l].rearrange("c p t -> p c t"),
                                                        in_=ckvf.v[:, :, 0:64]), ckvf.r(), [])
        bk = dense(wC, 8, (640, 768), hrhs, hT.r(), slotC)
        dve(lambda e: e.tensor_tensor(out=tmp1.v[:, 0:N], in0=PSB[bk][:, 0:N], in1=cos_v, op=ALU.mult),
            [PSR[bk]] + cs.r(), tmp1.r())
        bkp = dense(wC, 8, (768, 896), hrhs, hT.r(), slotC)
        dve(lambda e: e.tensor_tensor(out=tmp2.v[:, 0:N], in0=PSB[bkp][:, 0:N], in1=sin_v, op=ALU.mult),
            [PSR[bkp]] + cs.r(), tmp2.r())
        dve(lambda e: e.tensor_tensor(out=kpef.v[:, 0:N], in0=tmp1.v[:, 0:N], in1=tmp2.v[:, 0:N], op=ALU.add),
            tmp1.r() + tmp2.r(), kpef.r())
        act(lambda e: e.activation(out=Pcur.v[:, 0:N], in_=kpef.v[:, 0:N], func=AF.Copy), kpef.r(), Pcur.r())
        if prompt:
            dma("pool", "o_kpe", lambda e: e.dma_start(out=o_kpep[l, s][:, j * 512:(j + 1) * 512],
                                                        in_=kpef.v[0:64, :]), kpef.r(), [])
        else:
            dma("pool", "o_kpe", lambda e: e.dma_start(out=o_kpes[l], in_=kpef.v[0:64, 0:64]), kpef.r(), [])
        if KSTOP <= 5:
            return
        Qp4 = Qp.v.rearrange("p (a b) n -> p a b n", b=2)
        pool(lambda e: e.memset(Qp4[64:128, :, 0, :], 0.0), [], Qp.r())
        pool(lambda e: e.memset(Qp4[0:64, :, 1, :], 0.0), [], Qp.r())
        slotU, wU = piece_simple(l, s_uq[l], 3, 0, 2048)
        qrhs = lambda kc: cqn.v[:, kc, 0:N]
        for h in range(8):
            b = dense(wU, 3, (h * 128, h * 128 + 128), qrhs, cqn.r(), slotU)
            act(lambda e, b=b, h=h: e.activation(out=Qn.v[:, h, 0:N], in_=PSB[b][:, 0:N], func=AF.Copy),
                [PSR[b]], Qn.rs(h, 512))
        for pr in range(4):
            b1 = dense(wU, 3, (1024 + pr * 128, 1024 + pr * 128 + 128), qrhs, cqn.r(), slotU)
            dve(lambda e, b1=b1: e.tensor_tensor(out=tmp1.v[:, 0:N], in0=PSB[b1][:, 0:N], in1=cos_v, op=ALU.mult),
                [PSR[b1]] + cs.r(), tmp1.r())
            b2 = dense(wU, 3, (1536 + pr * 128, 1536 + pr * 128 + 128), qrhs, cqn.r(), slotU)
            dve(lambda e, b2=b2: e.tensor_tensor(out=tmp2.v[:, 0:N], in0=PSB[b2][:, 0:N], in1=sin_v, op=ALU.mult),
                [PSR[b2]] + cs.r(), tmp2.r())
            dve(lambda e, pr=pr: e.tensor_tensor(out=Qp.v[0:64, 2 * pr, 0:N], in0=tmp1.v[0:64, 0:N],
                                                 in1=tmp2.v[0:64, 0:N], op=ALU.add),
                tmp1.r() + tmp2.r(), Qp.rs(2 * pr, 512))
            dve(lambda e, pr=pr: e.tensor_tensor(out=Qp.v[64:128, 2 * pr + 1, 0:N], in0=tmp1.v[64:128, 0:N],
                                                 in1=tmp2.v[64:128, 0:N], op=ALU.add),
                tmp1.r() + tmp2.r(), Qp.rs(2 * pr + 1, 512))
        if KSTOP <= 6:
            return
        slotK, wK = piece_simple(l, s_ukv[l], 2, 0, 2048)
        lrhs = lambda kc: latb.v[:, kc, 0:N]
        if prompt:
            for h in range(8):
                b = dense(wK, 2, (h * 128, h * 128 + 128), lrhs, latb.r(), slotK)
                act(lambda e, b=b, h=h: e.activation(out=Kcur.v[:, :, h, :],
                                                     in_=PSB[b][:, 0:512].rearrange("p (b k) -> p b k", k=128),
                                                     func=AF.Copy), [PSR[b]], Kcur.r())
            for blk in range(4):
                for half in range(2):
                    b = next_bank(dbanks)
                    mm_group(PSB[b][:, 0:512],
                             [(latb.v[:, kc, blk * 128:(blk + 1) * 128],
                               wK[:, kc, 1024 + half * 512:1024 + half * 512 + 512]) for kc in range(2)],
                             latb.r() + slotK.r(), [PSR[b]])
                    act(lambda e, b=b, blk=blk, half=half: e.activation(
                        out=Vcur.v[:, blk, half * 512:(half + 1) * 512], in_=PSB[b][:, 0:512], func=AF.Copy),
                        [PSR[b]], Vcur.rs(blk, 1024))
            if j < NT - 1:
                dma("pool", "kcw", lambda e: e.dma_start(out=s_kc[s, l][:, j * 4:(j + 1) * 4], in_=Kcur.v),
                    Kcur.r(), [DR(("kc", s, l, j))])
                dma("pool", "vcw", lambda e: e.dma_start(
                    out=s_vc[s, l][j * 4:(j + 1) * 4].rearrange("b k c -> k b c"), in_=Vcur.v),
                    Vcur.r(), [DR(("vc", s, l, j))])
                dma("pool", "pcw", lambda e: e.dma_start(out=s_pc[s, l][:, j * 512:(j + 1) * 512], in_=Pcur.v),
                    Pcur.r(), [DR(("pc", s, l, j))])
            if KSTOP <= 7:
                return
            slotPA, wPA = piece_simple(l, s_pa[l], 8, 0, 1024)
            slotPB, wPB = piece_simple(l, s_pb[l], 4, 0, 1024)
            attention_prompt(l, s, j)
        else:
            slotPA, wPA = piece_simple(l, s_pa[l], 8, 0, 1024)
            attention_sample(l, wK, slotK)
            slotPB, wPB = piece_simple(l, s_pb[l], 4, 0, 1024)

        if KSTOP <= 8:
            return
        mbanks = [[0, 1], [2, 3], [4, 5], [6, 7]]
        for oc in range(8):
            bx, by = mbanks[oc % 4]
            mm_group(PSB[bx][:, 0:N], [(wPA[:, kc, oc * 128:(oc + 1) * 128], oaT.v[:, kc, 0:N]) for kc in range(8)],
                     oaT.r() + slotPA.r(), [PSR[bx]])
            mm_group(PSB[by][:, 0:N], [(wPB[:, kc, oc * 128:(oc + 1) * 128], obT.v[:, kc, 0:N]) for kc in range(4)],
                     obT.r() + slotPB.r(), [PSR[by]])
            ta = mta[oc % 2]
            tb = mtb[oc % 2]
            dve(lambda e, bx=bx, oc=oc, ta=ta: e.tensor_tensor(out=ta.v[:, 0:N], in0=PSB[bx][:, 0:N],
                                                               in1=gates.v[:, oc, 0:N], op=ALU.mult),
                [PSR[bx]] + gates.rs(oc, 512), ta.r())
            dve(lambda e, by=by, oc=oc, tb=tb: e.tensor_tensor(out=tb.v[:, 0:N], in0=PSB[by][:, 0:N],
                                                               in1=gates.v[:, 8 + oc, 0:N], op=ALU.mult),
                [PSR[by]] + gates.rs(8 + oc, 512), tb.r())
            dve(lambda e, oc=oc, ta=ta, tb=tb: e.tensor_tensor(out=hT.v[:, oc, 0:N], in0=ta.v[:, 0:N],
                                                               in1=tb.v[:, 0:N], op=ALU.add),
                ta.r() + tb.r(), hT.rs(oc, 512))
        slotO, wO = piece_simple(l, s_wo[l], 8, 0, 1024)
        for oc in range(8):
            b = dense(wO, 8, (oc * 128, oc * 128 + 128), hrhs, hT.r(), slotO, banks=[0, 1, 2, 3])
            dve(lambda e, b=b, oc=oc: e.tensor_tensor(out=xT.v[:, oc, 0:N], in0=xT.v[:, oc, 0:N], in1=PSB[b][:, 0:N],
                                                      op=ALU.add), [PSR[b]] + xT.rs(oc, 512), xT.rs(oc, 512))

        if KSTOP <= 9:
            return
        rmsnorm_x(N, lambda c: g2s.v[:, l, c:c + 1], lambda c: hT.v[:, c, 0:N], lambda c: hT.rs(c, 512))
        nseg = 1 if prompt else 2
        L = N // nseg
        cwv = pv("cw").rearrange("p (l j c) -> p l j c", j=3, c=22)
        cbv = pv("cb").rearrange("p (l c) -> p l c", c=22)
        fb = [[0, 1], [2, 3], [4, 5], [6, 7]]
        for q in range(6):
            ncol = 512 if q < 5 else 256

            def src_fn(flat, q=q, ncol=ncol):
                v = flat[:, 0:8 * 2 * ncol].rearrange("p (k n) -> p k n", n=2 * ncol)
                return [(v[:, :, 0:ncol], wview(s_up[l], 8, q * 512, q * 512 + ncol)),
                        (v[:, :, ncol:2 * ncol], wview(s_up[l], 8, 2816 + q * 512, 2816 + q * 512 + ncol))]
            slotP = load_piece(l, src_fn, 8 * 2 * ncol, 2)
            wP = slotP.flat[:, 0:8 * 2 * ncol].rearrange("p (k n) -> p k n", n=2 * ncol)
            for jj in range(ncol // 128):
                jc = q * 4 + jj
                bx, by = fb[jc % 4]
                mm_group(PSB[bx][:, 0:N], [(wP[:, kc, jj * 128:(jj + 1) * 128], hT.v[:, kc, 0:N]) for kc in range(8)],
                         hT.r() + slotP.r(), [PSR[bx]])
                mm_group(PSB[by][:, 0:N],
                         [(wP[:, kc, ncol + jj * 128:ncol + (jj + 1) * 128], hT.v[:, kc, 0:N]) for kc in range(8)],
                         hT.r() + slotP.r(), [PSR[by]])
                ae = aext[jc % 4]
                cv = cvb[jc % 4]
                uu = ub[jc % 4]
                ss_ = sbf[jc % 4]
                ae3 = ae.v[:, 0:nseg * (L + 2)].rearrange("p (s t) -> p s t", t=L + 2)
                as3 = lambda ap_: ap_[:, 0:N].rearrange("p (s t) -> p s t", t=L)
                act(lambda e, bx=bx, ae3=ae3: e.activation(out=ae3[:, :, 2:L + 2], in_=as3(PSB[bx]), func=AF.Copy),
                    [PSR[bx]], ae.r())
                if prompt:
                    halo_src = ctail.v[:, l, jc, :].unsqueeze(1)
                    halo_res = ctail.r()
                else:
                    halo_src = shalo.v[:, :, jc, :]
                    halo_res = shalo.r()
                if prompt and j == 0:
                    pool(lambda e, ae3=ae3: e.memset(ae3[:, :, 0:2], 0.0), [], ae.r())
                else:
                    pool(lambda e, ae3=ae3, halo_src=halo_src: e.tensor_copy(out=ae3[:, :, 0:2], in_=halo_src),
                         halo_res, ae.r())
                if prompt:
                    pool(lambda e, ae3=ae3, jc=jc: e.tensor_copy(out=ctail.v[:, l, jc, :].unsqueeze(1),
                                                                in_=ae3[:, :, L:L + 2]), ae.r(), ctail.r())
                else:
                    pool(lambda e, ae3=ae3, jc=jc: e.tensor_copy(out=stail.v[:, :, jc, :], in_=ae3[:, :, L:L + 2]),
                         ae.r(), stail.r())
                act(lambda e, ae3=ae3, cv=cv, jc=jc: e.activation(out=as3(cv.v), in_=ae3[:, :, 2:L + 2], func=AF.Identity,
                                                                 scale=cwv[:, l, 2, jc:jc + 1], bias=cbv[:, l, jc:jc + 1]),
                    ae.r() + par.r(), cv.r())
                dve(lambda e, ae3=ae3, cv=cv, jc=jc: e.scalar_tensor_tensor(
                    out=as3(cv.v), in0=ae3[:, :, 1:L + 1], scalar=cwv[:, l, 1, jc:jc + 1], in1=as3(cv.v),
                    op0=ALU.mult, op1=ALU.add), ae.r() + cv.r() + par.r(), cv.r())
                dve(lambda e, ae3=ae3, cv=cv, jc=jc: e.scalar_tensor_tensor(
                    out=as3(cv.v), in0=ae3[:, :, 0:L], scalar=cwv[:, l, 0, jc:jc + 1], in1=as3(cv.v),
                    op0=ALU.mult, op1=ALU.add), ae.r() + cv.r() + par.r(), cv.r())
                act(lambda e, cv=cv, uu=uu: e.activation(out=uu.v[:, 0:N], in_=cv.v[:, 0:N], func=AF.Square,
                                                         scale=math.sqrt(0.044715)), cv.r(), uu.r())
                dve(lambda e, cv=cv, uu=uu: e.scalar_tensor_tensor(out=uu.v[:, 0:N], in0=uu.v[:, 0:N], scalar=1.0,
                                                                   in1=cv.v[:, 0:N], op0=ALU.add, op1=ALU.mult),
                    cv.r() + uu.r(), uu.r())
                act(lambda e, uu=uu, ss_=ss_: e.activation(out=ss_.v[:, 0:N], in_=uu.v[:, 0:N], func=AF.Sigmoid,
                                                           scale=GC), uu.r(), ss_.r())
                pool(lambda e, cv=cv, ss_=ss_: e.tensor_tensor(out=ss_.v[:, 0:N], in0=ss_.v[:, 0:N], in1=cv.v[:, 0:N],
                                                               op=ALU.mult), cv.r() + ss_.r(), ss_.r())
                dve(lambda e, by=by, ss_=ss_, jc=jc: e.tensor_tensor(out=gT.v[:, jc, 0:N], in0=ss_.v[:, 0:N],
                                                                     in1=PSB[by][:, 0:N], op=ALU.mult),
                    [PSR[by]] + ss_.r(), gT.rs(jc, 512))
        if prompt and j == NT - 1:
            dma("pool", f"o_cvp{l}", lambda e: e.dma_start(out=o_cvp[l, s], in_=ctail.v[:, l]), ctail.r(), [])
        if not prompt:
            dma("pool", "o_cvs", lambda e: [e.dma_start(out=o_cvs[l, q], in_=stail.v[:, q]) for q in range(2)],
                stail.r(), [], n=2)
        for q in range(4):
            def src_fn(flat, q=q):
                return [(flat[:, 0:22 * 256].rearrange("p (k n) -> p k n", n=256),
                         wview(s_dn[l], 22, q * 256, (q + 1) * 256))]
            slotDn = load_piece(l, src_fn, 22 * 256, 2)
            wDn = slotDn.flat[:, 0:22 * 256].rearrange("p (k n) -> p k n", n=256)
            for o2 in range(2):
                oc = q * 2 + o2
                b = next_bank([0, 1, 2, 3])
                mm_group(PSB[b][:, 0:N], [(wDn[:, kc, o2 * 128:(o2 + 1) * 128], gT.v[:, kc, 0:N]) for kc in range(22)],
                         gT.r() + slotDn.r(), [PSR[b]])
                dve(lambda e, b=b, oc=oc: e.tensor_tensor(out=xT.v[:, oc, 0:N], in0=xT.v[:, oc, 0:N],
                                                          in1=PSB[b][:, 0:N], op=ALU.add),
                    [PSR[b]] + xT.rs(oc, 512), xT.rs(oc, 512))

    kvctr = [0]

    def attention_prompt(l, s, j):
        N = 512
        for p in range(2):
            heads = range(4 * p, 4 * p + 4)
            first = {h: True for h in heads}
            for g in range(j + 1):
                diag = g == j
                if not diag:
                    si = kvctr[0] % 2
                    kvctr[0] += 1
                    Kb, Vb, Pb = KS[si]

                    def fn(e, g=g, p=p, Kb=Kb, Vb=Vb, Pb=Pb):
                        return [
                            e.dma_start(out=Kb.v, in_=s_kc[s, l][:, g * 4:(g + 1) * 4, 4 * p:4 * p + 4, :]),
                            e.dma_start(out=Vb.v, in_=s_vc[s, l][g * 4:(g + 1) * 4, :, p * 512:(p + 1) * 512]
                                        .rearrange("b k c -> k b c")),
                            e.dma_start(out=Pb.v, in_=s_pc[s, l][:, g * 512:(g + 1) * 512]),
                        ]
                    dma("sp", f"kv{si}", fn, [DR(("kc", s, l, g)), DR(("vc", s, l, g)), DR(("pc", s, l, g))],
                        Kb.r() + Vb.r() + Pb.r(), n=3, lat=10.0)
                    kfn = lambda h, kb, Kb=Kb, p=p: Kb.v[:, kb, h - 4 * p, :]
                    vfn = lambda h, kb, Vb=Vb, p=p: Vb.v[:, kb, (h - 4 * p) * 128:(h - 4 * p + 1) * 128]
                    pfn = lambda hp, kb, Pb=Pb: Pb.v[:, kb * 128:(kb + 1) * 128]
                    kvres = Kb.r() + Vb.r() + Pb.r()
                else:
                    kfn = lambda h, kb: Kcur.v[:, kb, h, :]
                    vfn = lambda h, kb: Vcur.v[:, kb, h * 128:(h + 1) * 128]
                    pfn = lambda hp, kb: Pcur.v[:, kb * 128:(kb + 1) * 128]
                    kvres = Kcur.r() + Vcur.r() + Pcur.r()
                for h in heads:
                    hl = h - 4 * p
                    ob = 4 + hl
                    hp = h % 2
                    for kb in range(4):
                        q0 = kb * 128 if diag else 0
                        sb_ = bctr[0] % 4
                        bctr[0] += 1
                        pt = Pt[bctr[0] % 8]
                        last = (g == j and kb == 3)

                        def fsc(e, h=h, kb=kb, q0=q0, sb_=sb_, hp=hp, kfn=kfn, pfn=pfn, diag=diag):
                            e.matmul(PSB[sb_][:, q0:N], kfn(h, kb), Qn.v[:, h, q0:N], start=True, stop=False)
                            if diag:
                                e.matmul(PSB[sb_][:, q0:N], mUb.v, mVb.v[:, 0:N - q0], start=False, stop=False)
                            return e.matmul(PSB[sb_][:, q0:N], pfn(hp, kb), Qp.v[:, h, q0:N],
                                            start=False, stop=True)
                        pe(fsc, kvres + Qn.rs(h, 512) + Qp.rs(h, 512), [PSR[sb_]])
                        act(lambda e, sb_=sb_, pt=pt, q0=q0: e.activation(out=pt.v[:, q0:N], in_=PSB[sb_][:, q0:N],
                                                                          func=AF.Exp, scale=SCALE),
                            [PSR[sb_]], pt.r())
                        st = first[h]

                        def fpv(e, h=h, kb=kb, q0=q0, pt=pt, ob=ob, st=st, last=last, vfn=vfn):
                            return e.matmul(PSB[ob][:, q0:N], vfn(h, kb), pt.v[:, q0:N], start=st, stop=last,
                                            skip_group_check=True)
                        pe(fpv, kvres + pt.r(), [PSR[ob]])
                        if st:
                            dve(lambda e, pt=pt, hl=hl: e.tensor_copy(out=Pacc.v[:, hl, :], in_=pt.v),
                                pt.r(), Pacc.rs(hl, 512))
                        else:
                            dve(lambda e, pt=pt, hl=hl, q0=q0: e.tensor_tensor(out=Pacc.v[:, hl, q0:N],
                                                                             in0=Pacc.v[:, hl, q0:N],
                                                                             in1=pt.v[:, q0:N], op=ALU.add),
                                pt.r() + Pacc.rs(hl, 512), Pacc.rs(hl, 512))
                        first[h] = False
            for h in heads:
                hl = h - 4 * p
                ob = 4 + hl
                sb_ = 2 + (bctr[0] % 2)
                bctr[0] += 1
                mm_group(PSB[sb_][:, 0:N], [(onesf.v, Pacc.v[:, hl, :])], Pacc.rs(hl, 512) + onesf.r(), [PSR[sb_]])
                act(lambda e, sb_=sb_: e.activation(out=rsb.v, in_=PSB[sb_][:, 0:N], func=AF.Ln), [PSR[sb_]], rsb.r())
                act(lambda e: e.activation(out=rsb.v, in_=rsb.v, func=AF.Exp, scale=-1.0), rsb.r(), rsb.r())
                dve(lambda e, h=h, ob=ob: e.tensor_tensor(out=oaT.v[:, h, :], in0=PSB[ob][:, 0:N], in1=rsb.v,
                                                          op=ALU.mult), [PSR[ob]] + rsb.r(), oaT.rs(h, 512))

    def attention_sample(l, wK, slotK):
        N = 64
        Knew = Kcur.flat[:, 0:8 * 64].rearrange("p (h t) -> p h t", t=64)
        for h in range(8):
            b = next_bank([0, 1])
            mm_group(PSB[b][:, 0:N], [(wK[:, kc, h * 128:(h + 1) * 128], latb.v[:, kc, 0:N]) for kc in range(2)],
                     latb.r() + slotK.r(), [PSR[b]])
            act(lambda e, b=b, h=h: e.activation(out=Knew[:, h, :], in_=PSB[b][:, 0:N], func=AF.Copy),
                [PSR[b]], Kcur.r())
        for q in range(2):
            for half in range(2):
                b = next_bank([0, 1])
                mm_group(PSB[b][0:32, 0:512],
                         [(latb.v[:, kc, q * 32:(q + 1) * 32], wK[:, kc, 1024 + half * 512:1024 + half * 512 + 512])
                          for kc in range(2)], latb.r() + slotK.r(), [PSR[b]])
                act(lambda e, b=b, q=q, half=half: e.activation(out=Vnew.v[0:32, q, half * 512:(half + 1) * 512],
                                                                in_=PSB[b][0:32, 0:512], func=AF.Copy),
                    [PSR[b]], Vnew.rs(q, 1024))
        Pa = Pacc.flat[:, 0:8 * 64].rearrange("p (h t) -> p h t", t=64)
        ob = 4
        first = {}
        ostart = [True]
        for q in range(2):
            for g in range(NPG):
                dma("sp", "lpf", lambda e, q=q, g=g: [
                    e.dma_start(out=lpf.v, in_=d_latp[l, q].rearrange("c p t -> p c t")[:, :, g * 512:(g + 1) * 512]),
                    e.dma_start(out=kpf.v, in_=d_krp[l, q][:, g * 512:(g + 1) * 512])],
                    [], lpf.r() + kpf.r(), n=2)
                act(lambda e: e.activation(out=lpb.flat, in_=lpf.flat, func=AF.Copy), lpf.r(), lpb.r())
                Pb = KS[0][2]
                act(lambda e, Pb=Pb: e.activation(out=Pb.v, in_=kpf.v, func=AF.Copy), kpf.r(), Pb.r())
                for kb in range(4):
                    for half in range(2):
                        b = next_bank([0, 1])
                        mm_group(PSB[b][:, 0:512],
                                 [(lpb.v[:, kc, kb * 128:(kb + 1) * 128],
                                   wK[:, kc, 1024 + half * 512:1024 + half * 512 + 512]) for kc in range(2)],
                                 lpb.r() + slotK.r(), [PSR[b]])
                        act(lambda e, b=b, kb=kb, half=half: e.activation(
                            out=Vcur.v[:, kb, half * 512:(half + 1) * 512], in_=PSB[b][:, 0:512], func=AF.Copy),
                            [PSR[b]], Vcur.rs(kb, 1024))
                for h in range(8):
                    hp = h % 2
                    Kh = KS[h % 2][0]
                    b = next_bank([0, 1])
                    mm_group(PSB[b][:, 0:512], [(wK[:, kc, h * 128:(h + 1) * 128], lpb.v[:, kc, :]) for kc in range(2)],
                             lpb.r() + slotK.r(), [PSR[b]])
                    act(lambda e, b=b, Kh=Kh: e.activation(out=Kh.flat[:, 0:512], in_=PSB[b][:, 0:512], func=AF.Copy),
                        [PSR[b]], Kh.r())
                    for kb in range(4):
                        sb_ = 2 + (bctr[0] % 2)
                        bctr[0] += 1
                        pt = Pt[bctr[0] % 4]

                        def fsc(e, h=h, kb=kb, sb_=sb_, hp=hp, Kh=Kh, Pb=Pb, q=q):
                            e.matmul(PSB[sb_][:, 0:32], Kh.flat[:, kb * 128:(kb + 1) * 128], Qn.v[:, h, q * 32:(q + 1) * 32],
                                     start=True, stop=False)
                            return e.matmul(PSB[sb_][:, 0:32], Pb.v[:, kb * 128:(kb + 1) * 128],
                                            Qp.v[:, h, q * 32:(q + 1) * 32],
                                            start=False, stop=True)
                        pe(fsc, Kh.r() + Pb.r() + Qn.rs(h, 512) + Qp.rs(h, 512), [PSR[sb_]])
                        act(lambda e, sb_=sb_, pt=pt: e.activation(out=pt.v[:, 0:32], in_=PSB[sb_][:, 0:32], func=AF.Exp,
                                                                   scale=SCALE), [PSR[sb_]], pt.r())
                        st = first.get((h, q), True)

                        st0 = ostart[0]
                        ostart[0] = False

                        def fpv(e, h=h, kb=kb, pt=pt, st0=st0, q=q):
                            c0 = h * 64 + q * 32
                            return e.matmul(PSB[ob][:, c0:c0 + 32], Vcur.v[:, kb, h * 128:(h + 1) * 128], pt.v[:, 0:32],
                                            start=st0, stop=False, skip_group_check=True)
                        pe(fpv, Vcur.rs(kb, 1024) + pt.r(), [PSR[ob]])
                        if st:
                            dve(lambda e, pt=pt, h=h, q=q: e.tensor_copy(out=Pa[:, h, q * 32:(q + 1) * 32],
                                                                        in_=pt.v[:, 0:32]), pt.r(), Pacc.r())
                        else:
                            dve(lambda e, pt=pt, h=h, q=q: e.tensor_tensor(out=Pa[:, h, q * 32:(q + 1) * 32],
                                                                          in0=Pa[:, h, q * 32:(q + 1) * 32],
                                                                          in1=pt.v[:, 0:32], op=ALU.add),
                                pt.r() + Pacc.r(), Pacc.r())
                        first[(h, q)] = False
            for h in range(8):
                hp = h % 2
                sb_ = 2 + (bctr[0] % 2)
                bctr[0] += 1
                pt = Pt[bctr[0] % 4]

                def fsc(e, h=h, sb_=sb_, hp=hp, q=q):
                    e.matmul(PSB[sb_][0:32, 0:32], Knew[:, h, q * 32:(q + 1) * 32], Qn.v[:, h, q * 32:(q + 1) * 32],
                             start=True, stop=False)
                    return e.matmul(PSB[sb_][0:32, 0:32], Pcur.v[:, q * 32:(q + 1) * 32],
                                    Qp.v[:, h, q * 32:(q + 1) * 32], start=False, stop=True)
                pe(fsc, Kcur.r() + Pcur.r() + Qn.rs(h, 512) + Qp.rs(h, 512), [PSR[sb_]])
                act(lambda e, sb_=sb_, pt=pt: e.activation(out=pt.v[0:32, 0:32], in_=PSB[sb_][0:32, 0:32], func=AF.Exp,
                                                           scale=SCALE), [PSR[sb_]], pt.r())

                def fpv(e, h=h, pt=pt, q=q):
                    c0 = h * 64 + q * 32
                    return e.matmul(PSB[ob][:, c0:c0 + 32], Vnew.v[0:32, q, h * 128:(h + 1) * 128], pt.v[0:32, 0:32],
                                    start=False, stop=True, skip_group_check=True)
                pe(fpv, Vnew.rs(q, 1024) + pt.r(), [PSR[ob]])
                dve(lambda e, pt=pt, h=h, q=q: e.tensor_tensor(out=Pa[0:32, h, q * 32:(q + 1) * 32],
                                                              in0=Pa[0:32, h, q * 32:(q + 1) * 32],
                                                              in1=pt.v[0:32, 0:32], op=ALU.add),
                    pt.r() + Pacc.r(), Pacc.r())
        sb_ = 3
        mm_group(PSB[sb_][:, 0:512], [(onesf.v, Pacc.flat[:, 0:512])], Pacc.r() + onesf.r(), [PSR[sb_]])
        act(lambda e: e.activation(out=rsb.v, in_=PSB[sb_][:, 0:512], func=AF.Ln), [PSR[sb_]], rsb.r())
        act(lambda e: e.activation(out=rsb.v, in_=rsb.v, func=AF.Exp, scale=-1.0), rsb.r(), rsb.r())
        dve(lambda e: e.tensor_tensor(out=oaT.v[:, :, 0:64], in0=PSB[ob][:, 0:512].rearrange("p (h t) -> p h t", t=64),
                                      in1=rsb.v.rearrange("p (h t) -> p h t", t=64), op=ALU.mult),
            [PSR[ob]] + rsb.r(), oaT.r())

    def final_norm_store(N, dst_fn, key):
        rmsnorm_x(N, lambda c: gfs.v[:, c:c + 1], lambda c: xT.v[:, c, 0:N], lambda c: xT.rs(c, 512))
        dma("pool", key, lambda e: e.dma_start(out=dst_fn(), in_=xT.v[:, :, 0:N]), xT.r(), [])

    import os
    DBG = int(os.environ.get("KDBG", "0"))
    for s in range(NSEQ if DBG == 0 else 0):
        for j in range(NT):
            dma("sp", "xload", lambda e, s=s, j=j: e.dma_start(
                out=xT.v, in_=d_xp[s].rearrange("c p t -> p c t")[:, :, j * 512:(j + 1) * 512]), [], xT.r())
            dma("sp", "csload", lambda e, j=j: e.dma_start(
                out=cs.v, in_=d_csp.rearrange("a p t -> p a t")[:, :, j * 512:(j + 1) * 512]), [], cs.r())
            for l in range(DEPTH):
                tile_layer(l, 512, "p", s, j)
            final_norm_store(512, lambda s=s, j=j: o_yp[s].rearrange("c p t -> p c t")[:, :, j * 512:(j + 1) * 512],
                             "o_y")
    dma("sp", "xload", lambda e: e.dma_start(out=xT.v[:, :, 0:64], in_=d_xs.rearrange("c p t -> p c t")), [], xT.r())
    dma("sp", "csload", lambda e: e.dma_start(out=cs.v[:, :, 0:64], in_=d_css.rearrange("a p t -> p a t")), [], cs.r())
    for l in range(DEPTH if DBG == 0 else 0):
        dma("sp", "shalo", lambda e, l=l: [e.dma_start(out=shalo.v[:, q], in_=d_scv[l, q]) for q in range(2)],
            [], shalo.r(), n=2)
        tile_layer(l, 64, "s")
    final_norm_store(64, lambda: o_ys.rearrange("c p t -> p c t"), "o_y")

    import os as _os2
    if _os2.environ.get("KNOSCHED") is None:
        P.schedule()
        if _os2.environ.get("KVERB"):
            print("scheduled makespan us", P.makespan)
    P.finalize()
    sems = {}

    for k in [("eng", e_) for e_ in ("pe", "act", "dve", "pool")] + [("dma", k_) for k_ in P.dma_counts]:
        sems[k] = es.enter_context(nc.semaphore("s_" + "_".join(str(x) for x in k)))

    def semof(k):
        return sems[k]

    with nc.Block() as block:
        @block.sync
        def _(e):
            P.emit("sp", e, semof)

        @block.gpsimd
        def _(e):
            P.emit("pool", e, semof)

        @block.scalar
        def _(e):
            P.emit("act", e, semof)

        @block.vector
        def _(e):
            P.emit("dve", e, semof)

        @block.tensor
        def _(e):
            P.emit("pe", e, semof)
    es.close()
    return nc


def rope_tables(pos):
    half = 32
    inv = (10000.0 ** (-np.arange(half, dtype=np.float32) / half)).astype(np.float32)
    ang = pos.astype(np.float32)[:, None] * inv[None, :]
    cos = np.cos(ang).astype(np.float32).T
    sin = np.sin(ang).astype(np.float32).T
    c64 = np.concatenate([cos, cos], 0)
    s64 = np.concatenate([-sin, sin], 0)
    return np.stack([np.concatenate([c64, c64], 0), np.concatenate([s64, s64], 0)], 0)


def make_consts():
    ident = np.eye(128, dtype=np.float32)
    s_ = np.arange(128)[:, None]
    t_ = np.arange(128)[None, :]
    maskP = ((s_ // 64 == t_ // 64) & (s_ <= t_)).astype(np.float32)
    maskS = ((s_ // 32 == t_ // 32) & (s_ <= t_)).astype(np.float32)
    maskS[64:, :] = 0
    maskS[:, 64:] = 0
    resetP = np.ones((128, 512), np.float32)
    resetP[:, ::64] = 0
    resetS = np.ones((128, 64), np.float32)
    resetS[:, ::32] = 0
    mU = np.zeros((128, 128), np.float32)
    mU[0, 64:] = -30000.0
    mV = np.zeros((128, 512), np.float32)
    mV[0, :64] = 1.0
    return np.concatenate([maskP, maskS, resetP, resetS], 1), np.concatenate([ident, mU, mV], 1)


def perm64(w):
    return np.concatenate([w[..., 32:64], w[..., 0:32]], -1)


def host_prep(cfg, inp, core):
    D = cfg.DEPTH
    f = np.float32
    ps = slice(core * cfg.NSEQ, (core + 1) * cfg.NSEQ)
    ss = slice(core * 2, core * 2 + 2)
    m = {}
    xp = np.asarray(inp["x_prompt"][ps])
    m["xp"] = np.ascontiguousarray(xp.transpose(0, 2, 1).reshape(cfg.NSEQ, 8, 128, cfg.LP))
    xs = np.asarray(inp["x_sample"][ss])
    m["xs"] = np.ascontiguousarray(xs.transpose(2, 0, 1).reshape(8, 128, 64))
    lat = np.asarray(inp["cache_mla_latent"][:, ss])
    m["latp"] = np.ascontiguousarray(lat.transpose(0, 1, 3, 2).reshape(D, 2, 2, 128, cfg.PAST))
    kr = np.asarray(inp["cache_mla_krope"][:, ss]).transpose(0, 1, 3, 2)
    m["krp"] = np.ascontiguousarray(np.concatenate([kr, kr], 2))
    sh = np.asarray(inp["state_hgrn"][:, ss])
    m["sh"] = np.ascontiguousarray(sh.transpose(0, 1, 3, 2, 4))
    scv = np.asarray(inp["state_ffn_conv"][:, ss])
    m["scv"] = np.ascontiguousarray(scv.reshape(D, 2, 2, 22, 128).transpose(0, 1, 4, 3, 2))
    par = np.zeros((128, cfg.NPAR), f)

    def put(name, arr):
        o, n = cfg.po[name]
        par[:, o:o + n] = arr.reshape(128, n)
    put("g1", np.asarray(inp["norm_mix"]).reshape(D, 8, 128).transpose(2, 0, 1))
    put("g2", np.asarray(inp["norm_ffn"]).reshape(D, 8, 128).transpose(2, 0, 1))
    put("gq", np.asarray(inp["q_norm"]).reshape(D, 3, 128).transpose(2, 0, 1))
    put("gkv", np.asarray(inp["kv_norm"]).reshape(D, 2, 128).transpose(2, 0, 1))
    put("gh", np.asarray(inp["hgrn_norm"]).reshape(D, 128).transpose(1, 0))
    put("lbl", np.asarray(inp["lb_logits"]).reshape(D, 4, 128).transpose(2, 1, 0))
    put("cw", np.asarray(inp["conv_w"]).reshape(D, 3, 22, 128).transpose(3, 0, 1, 2))
    put("cb", np.asarray(inp["conv_b"]).reshape(D, 22, 128).transpose(2, 0, 1))
    put("gf", np.asarray(inp["norm_final"]).reshape(8, 128).transpose(1, 0))
    m["par"] = par
    return m


def host_shared(cfg, inp):
    D = cfg.DEPTH
    m = {}
    m["const"], m["const2"] = make_consts()
    m["csp"] = rope_tables(np.arange(cfg.LP))
    m["css"] = np.ascontiguousarray(np.tile(rope_tables(cfg.PAST + np.arange(32)), (1, 1, 2)))
    w_in = np.asarray(inp["w_in"])
    m["w_in"] = w_in
    kr = w_in[:, :, 640:704]
    krp = perm64(kr)
    m["w_inC"] = np.ascontiguousarray(np.concatenate([w_in[:, :, 0:640], kr, kr, krp, krp], 2))
    wuq = np.asarray(inp["w_uq"]).reshape(D, 384, 8, 192)
    nope = wuq[..., 0:128].reshape(D, 384, 1024)
    ropew = wuq[..., 128:192]
    m["w_uqR"] = np.ascontiguousarray(np.concatenate(
        [nope, ropew.reshape(D, 384, 512), perm64(ropew).reshape(D, 384, 512)], 2))
    wukv = np.asarray(inp["w_ukv"]).reshape(D, 256, 8, 256)
    m["w_ukvR"] = np.ascontiguousarray(np.concatenate(
        [wukv[..., 0:128].reshape(D, 256, 1024), wukv[..., 128:256].reshape(D, 256, 1024)], 2))
    m["w_pa"] = np.asarray(inp["w_proj_a"])
    m["w_pb"] = np.asarray(inp["w_proj_b"])
    m["w_o"] = np.asarray(inp["w_out"])
    m["w_up"] = np.asarray(inp["w_up"])
    m["w_dn"] = np.asarray(inp["w_down"])
    return m


def host_gather(cfg, results):
    D, LP, NSEQ = cfg.DEPTH, cfg.LP, cfg.NSEQ
    nb = NCORES * NSEQ
    y_p = np.empty((nb, LP, 1024), np.float32)
    y_s = np.empty((NCORES * 2, 32, 1024), np.float32)
    lat_p = np.empty((D, nb, LP, 256), np.float32)
    kpe_p = np.empty((D, nb, LP, 64), np.float32)
    hs_p = np.empty((D, nb, 4, 128, 128), np.float32)
    cv_p = np.empty((D, nb, 2, 2816), np.float32)
    lat_s = np.empty((D, NCORES * 2, 32, 256), np.float32)
    kpe_s = np.empty((D, NCORES * 2, 32, 64), np.float32)
    hs_s = np.empty((D, NCORES * 2, 4, 128, 128), np.float32)
    cv_s = np.empty((D, NCORES * 2, 2, 2816), np.float32)
    for c, r in enumerate(results):
        ps = slice(c * NSEQ, (c + 1) * NSEQ)
        ss = slice(c * 2, c * 2 + 2)
        y_p[ps] = r["o_yp"].reshape(NSEQ, 1024, LP).transpose(0, 2, 1)
        y_s[ss] = r["o_ys"].reshape(1024, 2, 32).transpose(1, 2, 0)
        lat_p[:, ps] = r["o_latp"].reshape(D, NSEQ, 256, LP).transpose(0, 1, 3, 2)
        kpe_p[:, ps] = r["o_kpep"].transpose(0, 1, 3, 2)
        hs_p[:, ps] = r["o_hsp"].transpose(0, 1, 3, 2, 4)
        cv_p[:, ps] = r["o_cvp"].transpose(0, 1, 4, 3, 2).reshape(D, NSEQ, 2, 2816)
        lat_s[:, ss] = r["o_lats"].reshape(D, 256, 2, 32).transpose(0, 2, 3, 1)
        kpe_s[:, ss] = r["o_kpes"].reshape(D, 64, 2, 32).transpose(0, 2, 3, 1)
        hs_s[:, ss] = r["o_hss"].transpose(0, 1, 3, 2, 4)
        cv_s[:, ss] = r["o_cvs"].transpose(0, 1, 4, 3, 2).reshape(D, 2, 2, 2816)
    return (y_p, y_s, lat_p, kpe_p, hs_p, cv_p, lat_s, kpe_s, hs_s, cv_s)


def run(cfg, inp, ncores=NCORES, trace=False):
    nc = build_program(cfg)
    shared = host_shared(cfg, inp)
    in_maps = []
    for c in range(ncores):
        m = dict(shared)
        m.update(host_prep(cfg, inp, c))
        in_maps.append(m)
    res = run_bass_kernel_spmd(nc, in_maps, core_ids=list(range(ncores)), trace=trace)
    return res


def kernel(**inputs):
    cfg = Cfg()
    res = run(cfg, inputs)
    return host_gather(cfg, res.results)
```

```python
import math
from contextlib import ExitStack

import numpy as np
import concourse.bass as bass
import concourse.mybir as mybir
from concourse.bass_utils import run_bass_kernel_spmd

F32 = mybir.dt.float32
BF16 = mybir.dt.bfloat16
AF = mybir.ActivationFunctionType
ALU = mybir.AluOpType
AX = mybir.AxisListType

EPS = 1e-6
NCORES = 8


class Res:
    __slots__ = ("w", "rd")

    def __init__(self):
        self.w = None
        self.rd = []


class Op:
    __slots__ = ("eng", "fn", "deps", "sig", "sigidx", "dma", "dcnt", "waits", "alldeps", "cost", "lat",
                 "idx", "nd", "succ", "rt", "cls")


class Prog:
    ENGS = ("pe", "act", "dve", "pool", "sp")

    DEFCOST = {"pe": 0.45, "act": 0.6, "dve": 0.6, "pool": 1.1, "sp": 0.1}

    def __init__(self):
        self.ops = {e: [] for e in self.ENGS}
        self.dma_counts = {}
        self.order = []

    def add(self, eng, fn, reads=(), writes=(), dma=None, ndma=1, cost=None, lat=None):
        op = Op()
        op.eng = eng
        op.fn = fn
        op.cost = cost if cost is not None else (0.1 * ndma if dma is not None else self.DEFCOST[eng])
        op.lat = lat if lat is not None else (3.0 if dma is not None else 0.0)
        op.idx = len(self.order)
        op.cls = None
        self.order.append(op)
        op.sig = False
        op.sigidx = 0
        op.dma = dma
        op.dcnt = 0
        if dma is not None:
            c = self.dma_counts.get(dma, 0) + 16 * ndma
            self.dma_counts[dma] = c
            op.dcnt = c
        deps = {}
        for r in reads:
            if r.w is not None:
                deps[r.w] = True
        for r in writes:
            if r.w is not None:
                deps.setdefault(r.w, False)
            for o in r.rd:
                deps.setdefault(o, False)
        keep = []
        op.alldeps = [d for d in deps if d is not op]
        for d, raw in deps.items():
            if d is op:
                continue
            if d.dma is not None:
                keep.append(d)
            elif d.eng == eng:
                if eng in ("pe", "sp"):
                    continue
                d.sig = True
                keep.append(d)
            else:
                d.sig = True
                keep.append(d)
        op.deps = keep
        for r in reads:
            r.rd.append(op)
        for r in writes:
            r.w = op
            r.rd = []
        self.ops[eng].append(op)
        return op

    def schedule(self):
        import heapq
        for op in self.order:
            op.nd = len(op.alldeps)
            op.succ = []
            op.rt = 0.0
        for op in self.order:
            for d in op.alldeps:
                d.succ.append(op)
        heaps = {e: [] for e in self.ENGS}
        for op in self.order:
            if op.nd == 0:
                heapq.heappush(heaps[op.eng], (0.0, op.idx, op))
        etime = {e: 0.0 for e in self.ENGS}
        new = {e: [] for e in self.ENGS}
        left = len(self.order)
        cur_cls = [None]
        TSW = 1.3

        def act_pick(pop):
            h = heaps["act"]
            cand = [heapq.heappop(h) for _ in range(min(8, len(h)))]
            bi, bk = 0, None
            for i, (rt, idx, op) in enumerate(cand):
                st = rt if rt > etime["act"] else etime["act"]
                if op.cls is not None and cur_cls[0] is not None and op.cls != cur_cls[0]:
                    st += TSW
                if bk is None or (st, idx) < bk:
                    bk, bi = (st, idx), i
            chosen = cand[bi]
            for i, c in enumerate(cand):
                if not (pop and i == bi):
                    heapq.heappush(h, c)
            return bk, chosen

        while left:
            best = None
            for e in self.ENGS:
                h = heaps[e]
                if h:
                    if e == "act":
                        (st, idx), _c = act_pick(False)
                    else:
                        rt, idx, op = h[0]
                        st = rt if rt > etime[e] else etime[e]
                    if best is None or (st, idx) < best[0]:
                        best = ((st, idx), e)
            (st, idx), e = best
            if e == "act":
                (st, idx), (rt, idx, op) = act_pick(True)
                if op.cls is not None:
                    cur_cls[0] = op.cls
            else:
                rt, idx, op = heapq.heappop(heaps[e])
            etime[e] = st + op.cost
            done = st + op.cost + op.lat
            new[e].append(op)
            left -= 1
            for s_ in op.succ:
                t = done + (0.25 if s_.eng != e else 0.1)
                if t > s_.rt:
                    s_.rt = t
                s_.nd -= 1
                if s_.nd == 0:
                    heapq.heappush(heaps[s_.eng], (s_.rt, s_.idx, s_))
        last = {}
        for e in self.ENGS:
            for op in new[e]:
                if op.dma is not None:
                    assert last.get(op.dma, 0) < op.dcnt, ("dma order", op.dma)
                    last[op.dma] = op.dcnt
        self.ops = new
        self.makespan = max(etime.values())

    def finalize(self):
        for e in self.ENGS:
            n = 0
            for op in self.ops[e]:
                if op.sig and op.dma is None:
                    n += 1
                    op.sigidx = n
        for e in self.ENGS:
            for op in self.ops[e]:
                w = {}
                for d in op.deps:
                    if d.dma is not None:
                        k = ("dma", d.dma)
                        v = d.dcnt
                    else:
                        k = ("eng", d.eng)
                        v = d.sigidx
                    if w.get(k, 0) < v:
                        w[k] = v
                op.waits = w

    def emit(self, eng, e, semof):
        waited = {}
        for op in self.ops[eng]:
            for k, v in op.waits.items():
                if waited.get(k, 0) < v:
                    e.wait_ge(semof(k), v)
                    waited[k] = v
            r = op.fn(e)
            if op.dma is not None:
                lst = r if isinstance(r, (list, tuple)) else [r]
                for ins in lst:
                    ins.then_inc(semof(("dma", op.dma)), 16)
            elif op.sig:
                ins = r[-1] if isinstance(r, (list, tuple)) else r
                ins.then_inc(semof(("eng", eng)), 1)
        if eng in ("sp", "pool"):
            for k, c in self.dma_counts.items():
                if self.dma_eng.get(k) == eng:
                    e.wait_ge(semof(("dma", k)), c)


GRAN = 512


class Buf:
    def __init__(self, ar, gran, off, nbytes, dtype, shape):
        self.off = off
        self.nbytes = nbytes
        self.dtype = dtype
        esz = 4 if dtype == F32 else 2
        self.esz = esz
        n = nbytes // esz
        v = ar[:, off // 2:(off + nbytes) // 2]
        if dtype == F32:
            v = v.bitcast(F32)
        self.flat = v
        self.shape = shape
        if len(shape) == 1:
            self.v = v
        elif len(shape) == 2:
            self.v = v.rearrange("p (a b) -> p a b", b=shape[1])
        else:
            self.v = v.rearrange("p (a b c) -> p a b c", b=shape[1], c=shape[2])
        self.gran = gran
        self.n = n

    def r(self, lo=0, hi=None):
        if hi is None:
            hi = self.n
        b0 = (self.off + lo * self.esz) // GRAN
        b1 = (self.off + hi * self.esz - 1) // GRAN
        return self.gran[b0:b1 + 1]

    def rs(self, i, inner):
        return self.r(i * inner, (i + 1) * inner)


class Cfg:
    def __init__(self, LP=4096, DEPTH=4, PAST=1024, NSEQ=2, wslots=3):
        self.LP = LP
        self.DEPTH = DEPTH
        self.PAST = PAST
        self.NSEQ = NSEQ
        self.NT = LP // 512
        self.wslots = wslots
        D = DEPTH
        o = 0
        self.po = {}
        for name, n in (("g1", D * 8), ("g2", D * 8), ("gq", D * 3), ("gkv", D * 2), ("gh", D),
                        ("lbl", 4 * D), ("cw", D * 3 * 22), ("cb", D * 22), ("gf", 8)):
            self.po[name] = (o, n)
            o += n
        self.NPAR = o


def build_program(cfg):
    LP, DEPTH, PAST, NSEQ, NT = cfg.LP, cfg.DEPTH, cfg.PAST, cfg.NSEQ, cfg.NT
    NPG = PAST // 512
    nc = bass.Bass("TRN2", target_bir_lowering=False)
    P = Prog()
    P.dma_eng = {}

    def dram(name, shape, dt=F32, kind="ExternalInput"):
        return nc.dram_tensor(name, list(shape), dt, kind=kind).ap()

    d_xp = dram("xp", [NSEQ, 8, 128, LP])
    d_xs = dram("xs", [8, 128, 64])
    d_latp = dram("latp", [DEPTH, 2, 2, 128, PAST])
    d_krp = dram("krp", [DEPTH, 2, 128, PAST])
    d_sh = dram("sh", [DEPTH, 2, 128, 4, 128])
    d_scv = dram("scv", [DEPTH, 2, 128, 22, 2])
    d_par = dram("par", [128, cfg.NPAR])
    d_const = dram("const", [128, 128 + 128 + 512 + 64])
    d_const2 = dram("const2", [128, 128 + 128 + 512])
    d_csp = dram("csp", [2, 128, LP])
    d_css = dram("css", [2, 128, 64])
    d_win = dram("w_in", [DEPTH, 1024, 4800])
    d_winC = dram("w_inC", [DEPTH, 1024, 896])
    d_wuq = dram("w_uqR", [DEPTH, 384, 2048])
    d_wukv = dram("w_ukvR", [DEPTH, 256, 2048])
    d_wpa = dram("w_pa", [DEPTH, 1024, 1024])
    d_wpb = dram("w_pb", [DEPTH, 512, 1024])
    d_wo = dram("w_o", [DEPTH, 1024, 1024])
    d_wup = dram("w_up", [DEPTH, 1024, 5632])
    d_wdn = dram("w_dn", [DEPTH, 2816, 1024])
    o_yp = dram("o_yp", [NSEQ, 8, 128, LP], kind="ExternalOutput")
    o_ys = dram("o_ys", [8, 128, 64], kind="ExternalOutput")
    o_latp = dram("o_latp", [DEPTH, NSEQ, 2, 128, LP], kind="ExternalOutput")
    o_kpep = dram("o_kpep", [DEPTH, NSEQ, 64, LP], kind="ExternalOutput")
    o_hsp = dram("o_hsp", [DEPTH, NSEQ, 128, 4, 128], kind="ExternalOutput")
    o_cvp = dram("o_cvp", [DEPTH, NSEQ, 128, 22, 2], kind="ExternalOutput")
    o_lats = dram("o_lats", [DEPTH, 2, 128, 64], kind="ExternalOutput")
    o_kpes = dram("o_kpes", [DEPTH, 64, 64], kind="ExternalOutput")
    o_hss = dram("o_hss", [DEPTH, 2, 128, 4, 128], kind="ExternalOutput")
    o_cvs = dram("o_cvs", [DEPTH, 2, 128, 22, 2], kind="ExternalOutput")
    s_in = dram("s_in", [DEPTH, 1024, 4096], BF16, "Internal")
    s_inC = dram("s_inC", [DEPTH, 1024, 896], BF16, "Internal")
    s_uq = dram("s_uq", [DEPTH, 384, 2048], BF16, "Internal")
    s_ukv = dram("s_ukv", [DEPTH, 256, 2048], BF16, "Internal")
    s_pa = dram("s_pa", [DEPTH, 1024, 1024], BF16, "Internal")
    s_pb = dram("s_pb", [DEPTH, 512, 1024], BF16, "Internal")
    s_wo = dram("s_wo", [DEPTH, 1024, 1024], BF16, "Internal")
    s_up = dram("s_up", [DEPTH, 1024, 5632], BF16, "Internal")
    s_dn = dram("s_dn", [DEPTH, 2816, 1024], BF16, "Internal")
    s_kc = dram("s_kc", [NSEQ, DEPTH, 128, NT * 4, 8, 128], BF16, "Internal")
    s_vc = dram("s_vc", [NSEQ, DEPTH, NT * 4, 128, 1024], BF16, "Internal")
    s_pc = dram("s_pc", [NSEQ, DEPTH, 128, LP], BF16, "Internal")

    es = ExitStack()
    ARENA_BYTES = 207 * 1024
    ar = es.enter_context(nc.sbuf_tensor("arena", [128, ARENA_BYTES // 2], BF16))
    gran = [Res() for _ in range(ARENA_BYTES // GRAN + 1)]
    top = [0]

    def alloc(shape, dt, at=None):
        esz = 4 if dt == F32 else 2
        n = 1
        for s in shape:
            n *= s
        nb = n * esz
        nb_al = (nb + GRAN - 1) // GRAN * GRAN
        if at is None:
            off = top[0]
            top[0] += nb_al
        else:
            off = at[0]
            at[0] += nb_al
        if off + nb_al > ARENA_BYTES:
            raise AssertionError(("arena overflow", off, nb_al, shape, "OV", globals().get("_OV")))
        return Buf(ar, gran, off, nb, dt, shape)

    nws = cfg.wslots
    xT = alloc([8, 512], F32)
    hT = alloc([8, 512], BF16)
    WS = [alloc([8192], BF16) for _ in range(nws)]
    rstd = [alloc([512], F32) for _ in range(2)]
    gates = alloc([16, 512], BF16)
    obT = alloc([4, 512], BF16)
    Sst = alloc([DEPTH, 4, 128], F32)
    cs = alloc([2, 512], F32)
    identb = alloc([128], BF16)
    onesb = alloc([128], BF16)
    onesf = alloc([128], F32)
    constf = alloc([128 + 128 + 512 + 64], F32)
    mUb = alloc([128], BF16)
    mVb = alloc([512], BF16)
    par = alloc([cfg.NPAR], F32)
    g1s = alloc([DEPTH, 8], F32)
    g2s = alloc([DEPTH, 8], F32)
    gqs = alloc([DEPTH, 3], F32)
    gkvs = alloc([DEPTH, 2], F32)
    ghs = alloc([DEPTH], F32)
    gfs = alloc([8], F32)
    lbe = alloc([4, DEPTH], F32)
    lbm = alloc([4], F32)
    lbv = alloc([4, DEPTH], F32)
    oml = alloc([4, DEPTH], F32)
    ctail = alloc([DEPTH, 22, 2], F32)
    shalo = alloc([2, 22, 2], F32)
    stail = alloc([2, 22, 2], F32)
    S0s = alloc([2, 4, 128], F32)
    dd = alloc([4, 8], F32)
    epsb = alloc([4], F32)
    OV = top[0]
    globals()['_OV'] = OV
    a = [OV]
    TT2 = [[alloc([512], F32, a) for _ in range(7)] for _ in range(2)]
    qeb = alloc([2, 512], BF16, a)
    keb = alloc([2, 512], BF16, a)
    kdb = alloc([2, 512], BF16, a)
    qbb = alloc([4, 512], BF16, a)
    Vtok = alloc([4, 512], BF16, a)
    kdTokE = alloc([2, 4, 128], BF16, a)
    kdTokO = alloc([2, 4, 128], BF16, a)
    Am = alloc([4, 4, 128], BF16, a)
    Sb = alloc([4, 8, 128], BF16, a)
    sghg = alloc([4, 512], BF16, a)
    sqo = alloc([512], BF16, a)
    to = alloc([512], F32, a)
    endH = a[0]
    a = [OV]
    Qn = alloc([8, 512], BF16, a)
    Qp = alloc([8, 512], BF16, a)
    Kcur = alloc([4, 8, 128], BF16, a)
    Vcur = alloc([4, 1024], BF16, a)
    Pcur = alloc([512], BF16, a)
    M12 = a[0]
    a = [M12]
    cqf = alloc([3, 512], F32, a)
    cqn = alloc([3, 512], BF16, a)
    ckvf = alloc([2, 512], F32, a)
    latb = alloc([2, 512], BF16, a)
    kpef = alloc([512], F32, a)
    tmp1 = alloc([512], F32, a)
    tmp2 = alloc([512], F32, a)
    sqm = alloc([3, 512], BF16, a)
    endM1 = a[0]
    a = [M12]
    KS = []
    for i in range(2):
        KS.append((alloc([4, 4, 128], BF16, a), alloc([4, 512], BF16, a), alloc([512], BF16, a)))
    Pt = [alloc([512], BF16, a) for _ in range(8)]
    oaT = alloc([8, 512], BF16, a)
    Pacc = alloc([4, 512], F32, a)
    rsb = alloc([512], F32, a)
    al_ = [KS[0][0].off]
    mta = [alloc([512], F32, al_) for _ in range(2)]
    mtb = [alloc([512], F32, al_) for _ in range(2)]
    lpf = alloc([2, 512], F32, [KS[1][1].off])
    lpb = alloc([2, 512], BF16, [oaT.off])
    kpf = alloc([512], F32, [oaT.off + 2048])
    Vnew = alloc([2, 1024], BF16, [KS[0][1].off])
    endM2 = a[0]
    assert Pacc.off == oaT.off + 8192
    yT = alloc([8, 512], F32, [oaT.off])
    a = [OV]
    gT = alloc([22, 512], BF16, a)
    aext = [alloc([520], F32, a) for _ in range(4)]
    cvb = [alloc([512], F32, a) for _ in range(4)]
    ub = [alloc([512], F32, a) for _ in range(4)]
    sbf = [alloc([512], F32, a) for _ in range(4)]
    endF = a[0]
    import os as _os0
    if _os0.environ.get("KVERB"):
        print("ARENA OV", OV, "H", endH, "M12", M12, "M1", endM1, "M2", endM2, "F", endF, "cap", ARENA_BYTES)
    assert max(endH, endM1, endM2, endF) <= ARENA_BYTES, (endH, endM1, endM2, endF)

    PSB = [es.enter_context(nc.psum_tensor(f"ps{i}", [128, 512], F32)) for i in range(8)]
    PSR = [Res() for _ in range(8)]

    dres = {}

    def DR(key):
        r = dres.get(key)
        if r is None:
            r = dres[key] = Res()
        return r

    def dma(eng, key, fn, reads, writes, n=1, lat=None):
        P.dma_eng[key] = eng
        return P.add(eng, fn, reads, writes, dma=key, ndma=n, lat=lat)

    class _Sniff:
        def activation(self, **kw):
            self.func = kw.get("func")

    _CLS = {AF.Exp: "le", AF.Ln: "le", AF.Sigmoid: "sg", AF.Silu: "si"}

    def act(fn, reads, writes, cost=None):
        op = P.add("act", fn, reads, writes, cost=cost)
        sn = _Sniff()
        fn(sn)
        op.cls = _CLS.get(sn.func)
        return op

    def dve(fn, reads, writes, cost=None):
        return P.add("dve", fn, reads, writes, cost=cost)

    def pool(fn, reads, writes, cost=None):
        return P.add("pool", fn, reads, writes, cost=cost)

    def pe(fn, reads, writes, cost=None):
        return P.add("pe", fn, reads, writes, cost=cost)

    def mm_group(out_ap, pairs, reads, writes, start=True, stop=True):
        def fn(e, out_ap=out_ap, pairs=pairs, start=start, stop=stop):
            ins = None
            n = len(pairs)
            for i, (l, r) in enumerate(pairs):
                ins = e.matmul(out_ap, l, r, start=(start and i == 0), stop=(stop and i == n - 1))
            return ins
        ncol = out_ap.shape[-1]
        return pe(fn, reads, writes, cost=len(pairs) * max(0.07, 0.22 * ncol / 512.0))

    dma("sp", "par", lambda e: e.dma_start(out=par.v, in_=d_par[:, :]), [], par.r())
    dma("sp", "const", lambda e: e.dma_start(out=constf.v, in_=d_const[:, :]), [], constf.r())
    maskP_v = constf.v[:, 0:128]
    maskS_v = constf.v[:, 128:256]
    resetP_v = constf.v[:, 256:768]
    resetS_v = constf.v[:, 768:832]
    const2 = alloc([768], F32, [OV])
    dma("sp", "const2", lambda e: e.dma_start(out=const2.v, in_=d_const2[:, :]), [], const2.r())
    dve(lambda e: e.tensor_copy(out=identb.v, in_=const2.v[:, 0:128]), const2.r(), identb.r())
    dve(lambda e: e.tensor_copy(out=mUb.v, in_=const2.v[:, 128:256]), const2.r(), mUb.r())
    dve(lambda e: e.tensor_copy(out=mVb.v, in_=const2.v[:, 256:768]), const2.r(), mVb.r())
    dve(lambda e: e.memset(onesb.v, 1.0), [], onesb.r())
    dve(lambda e: e.memset(onesf.v, 1.0), [], onesf.r())
    epsc = {}
    for i_, n_ in enumerate((1024.0, 384.0, 256.0, 128.0)):
        dve(lambda e, i_=i_, n_=n_: e.memset(epsb.v[:, i_:i_ + 1], n_ * EPS), [], epsb.r())
        epsc[n_ * EPS] = epsb.v[:, i_:i_ + 1]
    dve(lambda e: e.memset(Sst.flat, 0.0), [], Sst.r())
    dve(lambda e: e.memset(ctail.flat, 0.0), [], ctail.r())

    def pv(name):
        o, n = cfg.po[name]
        return par.flat[:, o:o + n]

    def scale_par(dst, name, s):
        dve(lambda e: e.tensor_scalar(out=dst.flat, in0=pv(name), scalar1=float(s), scalar2=None,
                                      op0=ALU.mult), par.r(), dst.r())

    scale_par(g1s, "g1", math.sqrt(1024.0))
    scale_par(g2s, "g2", math.sqrt(1024.0))
    scale_par(gqs, "gq", math.sqrt(384.0))
    scale_par(gkvs, "gkv", math.sqrt(256.0))
    scale_par(ghs, "gh", math.sqrt(128.0))
    scale_par(gfs, "gf", math.sqrt(1024.0))
    lbl_v = pv("lbl").rearrange("p (h l) -> p h l", l=DEPTH)
    dve(lambda e: e.tensor_reduce(out=lbm.v, in_=lbl_v, axis=AX.X, op=ALU.max), par.r(), lbm.r())
    dve(lambda e: e.tensor_tensor(out=lbe.v, in0=lbl_v,
                                  in1=lbm.v.unsqueeze(2).to_broadcast([128, 4, DEPTH]), op=ALU.subtract),
        par.r() + lbm.r(), lbe.r())
    act(lambda e: e.activation(out=lbe.flat, in_=lbe.flat, func=AF.Exp), lbe.r(), lbe.r())
    dve(lambda e: e.tensor_reduce(out=lbm.v, in_=lbe.v, axis=AX.X, op=ALU.add), lbe.r(), lbm.r())
    dve(lambda e: e.reciprocal(out=lbm.v, in_=lbm.v), lbm.r(), lbm.r())
    dve(lambda e: e.tensor_tensor(out=lbe.v, in0=lbe.v,
                                  in1=lbm.v.unsqueeze(2).to_broadcast([128, 4, DEPTH]), op=ALU.mult),
        lbe.r() + lbm.r(), lbe.r())
    dve(lambda e: e.memset(lbv.flat, 0.0), [], lbv.r())
    for l in range(1, DEPTH):
        dve(lambda e, l=l: e.tensor_tensor(out=lbv.v[:, :, l], in0=lbv.v[:, :, l - 1], in1=lbe.v[:, :, l],
                                           op=ALU.add), lbv.r() + lbe.r(), lbv.r())
    dve(lambda e: e.tensor_scalar(out=oml.flat, in0=lbv.flat, scalar1=-1.0, scalar2=1.0,
                                  op0=ALU.mult, op1=ALU.add), lbv.r(), oml.r())

    def cast_layer(l):
        def mk(group):
            def fn(e, l=l, group=group):
                out = []

                def rows(dst, src, nrows, step=128):
                    for r0 in range(0, nrows, step):
                        out.append(e.dma_start(out=dst[r0:r0 + step, :], in_=src[r0:r0 + step, :]))
                if group == 0:
                    rows(s_in[l], d_win[l][:, 704:4800], 1024)
                elif group == 1:
                    rows(s_inC[l], d_winC[l], 1024)
                    rows(s_uq[l], d_wuq[l], 384)
                    rows(s_ukv[l], d_wukv[l], 256)
                    rows(s_pa[l], d_wpa[l], 1024)
                    rows(s_pb[l], d_wpb[l], 512)
                    rows(s_wo[l], d_wo[l], 1024)
                else:
                    rows(s_up[l], d_wup[l], 1024)
                    rows(s_dn[l], d_wdn[l], 2816)
                return out
            return fn
        for g, n in ((0, 8), (1, 8 + 3 + 2 + 8 + 4 + 8), (2, 8 + 22)):
            dma("pool", f"cast{l}_{g}", mk(g), [], [DR(("w", l, g))], n=n, lat=(80.0, 150.0, 250.0)[g])

    for l in range(DEPTH):
        cast_layer(l)

    wctr = [0]

    def load_piece(l, src_fn, nel, wg=1):
        i = wctr[0] % nws
        wctr[0] += 1
        slot = WS[i]

        def fn(e, slot=slot):
            return [e.dma_start(out=d, in_=s) for d, s in src_fn(slot.flat)]
        ncalls = len(src_fn(slot.flat))
        dma("sp", f"w{i}", fn, [DR(("w", l, wg))], slot.r(0, nel), n=ncalls, lat=3.0 + nel * 256 / 150e3)
        return slot

    def wview(src2d, kc, c0, c1):
        return src2d.rearrange("(k p) n -> p k n", p=128)[:, :, c0:c1]

    def piece_simple(l, src2d, kc, c0, c1):
        w = c1 - c0

        def src_fn(flat):
            return [(flat[:, 0:kc * w].rearrange("p (k n) -> p k n", n=w), wview(src2d, kc, c0, c1))]
        slot = load_piece(l, src_fn, kc * w, 0 if src2d.tensor.name == "s_in" else 1)
        return slot, slot.flat[:, 0:kc * w].rearrange("p (k n) -> p k n", n=w)

    bctr = [0]

    def next_bank(banks):
        b = banks[bctr[0] % len(banks)]
        bctr[0] += 1
        return b

    def rstd_from(buf, bank, c, N):
        act(lambda e: e.activation(out=buf.v[:, 0:N], in_=PSB[bank][:, 0:N], func=AF.Ln, bias=epsc[c], scale=1.0),
            [PSR[bank]] + epsb.r(), buf.r())
        act(lambda e: e.activation(out=buf.v[:, 0:N], in_=buf.v[:, 0:N], func=AF.Exp, scale=-0.5), buf.r(), buf.r())

    SCALE = 1.0 / math.sqrt(192.0)
    GC = 2.0 * 0.7978845608028654

    def rmsnorm_x(N, gs_col, out_fn, out_res_fn):
        for c in range(8):
            act(lambda e, c=c: e.activation(out=hT.v[:, c, 0:N], in_=xT.v[:, c, 0:N], func=AF.Square),
                xT.rs(c, 512), hT.rs(c, 512))
        b = 3
        mm_group(PSB[b][:, 0:N], [(onesb.v, hT.v[:, c, 0:N]) for c in range(8)], hT.r() + onesb.r(), [PSR[b]])
        rb = rstd[0]
        rstd_from(rb, b, 1024.0 * EPS, N)
        for c in range(8):
            dve(lambda e, c=c: e.scalar_tensor_tensor(out=out_fn(c), in0=xT.v[:, c, 0:N], scalar=gs_col(c),
                                                      in1=rb.v[:, 0:N], op0=ALU.mult, op1=ALU.mult),
                xT.rs(c, 512) + rb.r(), out_res_fn(c))

    import os as _os
    KSTOP = float(_os.environ.get("KSTOP", "99"))

    def tile_layer(l, N, kind, s=0, j=0):
        prompt = kind == "p"
        C = 64 if prompt else 32
        TB = 128 if prompt else 64
        NB = N // TB
        NCH = N // C
        cos_v = cs.v[:, 0, 0:N]
        sin_v = cs.v[:, 1, 0:N]
        mask_v = maskP_v if prompt else maskS_v[0:64, 0:64]
        reset_v = resetP_v if prompt else resetS_v
        dbanks = [0, 1]

        rmsnorm_x(N, lambda c: g1s.v[:, l, c:c + 1], lambda c: hT.v[:, c, 0:N], lambda c: hT.rs(c, 512))

        if KSTOP <= 1:
            return
        def dense(wv, kc_n, cols, rhs_fn, rhs_res, slot, banks=dbanks):
            b = next_bank(banks)
            mm_group(PSB[b][:, 0:N], [(wv[:, kc, cols[0]:cols[1]], rhs_fn(kc)) for kc in range(kc_n)],
                     rhs_res + slot.r(), [PSR[b]])
            return b

        hrhs = lambda kc: hT.v[:, kc, 0:N]

        def dense_split(wv, kc_n, cols, rhs_fn, rhs_res_fn, slot, banks=dbanks):
            b = next_bank(banks)
            for kc in range(kc_n):
                mm_group(PSB[b][:, 0:N], [(wv[:, kc, cols[0]:cols[1]], rhs_fn(kc))], rhs_res_fn(kc) + slot.r(),
                         [PSR[b]], start=(kc == 0), stop=(kc == kc_n - 1))
            return b

        slotA, wA = piece_simple(l, s_in[l], 8, 0, 1024)
        slotB, wB = piece_simple(l, s_in[l], 8, 1024, 2048)
        pool(lambda e: e.memset(kdTokE.v[C:TB], 0.0), [], kdTokE.r())
        pool(lambda e: e.memset(kdTokO.v[0:C], 0.0), [], kdTokO.r())
        if not prompt:
            dma("sp", "s0s", lambda e: [e.dma_start(out=S0s.v[:, q], in_=d_sh[l, q]) for q in range(2)],
                [], S0s.r(), n=2)
        def hgrn_head(h, T0, T1, T2, T3, T4, T5, T6):
            if h == 0:
                bq = dense_split(wA, 8, (h * 128, h * 128 + 128), hrhs, lambda kc: hT.rs(kc, 512), slotA)
            else:
                bq = dense(wA, 8, (h * 128, h * 128 + 128), hrhs, hT.r(), slotA)
            act(lambda e, bq=bq: e.activation(out=T0.v[:, 0:N], in_=PSB[bq][:, 0:N], func=AF.Silu),
                [PSR[bq]], T0.r())
            bf = dense(wA, 8, (512 + h * 128, 512 + h * 128 + 128), hrhs, hT.r(), slotA)
            act(lambda e, bf=bf: e.activation(out=T1.v[:, 0:N], in_=PSB[bf][:, 0:N], func=AF.Sigmoid),
                [PSR[bf]], T1.r())
            dve(lambda e, h=h: e.tensor_scalar(out=T1.v[:, 0:N], in0=T1.v[:, 0:N], scalar1=oml.v[:, h, l:l + 1],
                                               scalar2=lbv.v[:, h, l:l + 1], op0=ALU.mult, op1=ALU.add),
                T1.r() + oml.r() + lbv.r(), T1.r())
            act(lambda e: e.activation(out=T2.v[:, 0:N], in_=T1.v[:, 0:N], func=AF.Ln), T1.r(), T2.r())
            if KSTOP <= 1.1:
                return True
            dve(lambda e: e.tensor_tensor_scan(out=T3.v[:, 0:N], data0=reset_v, data1=T2.v[:, 0:N], initial=0.0,
                                               op0=ALU.mult, op1=ALU.add), T2.r() + constf.r(), T3.r())
            dve(lambda e: e.tensor_scalar(out=T1.v[:, 0:N], in0=T1.v[:, 0:N], scalar1=-1.0, scalar2=1.0,
                                          op0=ALU.mult, op1=ALU.add), T1.r(), T1.r())
            if KSTOP <= 1.2:
                return True
            b3 = T3.v[:, 0:N].rearrange("p (c t) -> p c t", t=C)
            rmid = b3[:, :, C // 2 - 1:C // 2].to_broadcast([128, NCH, C])
            rend = b3[:, :, C - 1:C].to_broadcast([128, NCH, C])
            dve(lambda e: e.tensor_tensor(out=T2.v[:, 0:N].rearrange("p (c t) -> p c t", t=C), in0=b3, in1=rmid,
                                          op=ALU.subtract), T3.r(), T2.r())
            dve(lambda e: e.tensor_tensor(out=T4.v[:, 0:N].rearrange("p (c t) -> p c t", t=C), in0=b3, in1=rend,
                                          op=ALU.subtract), T3.r(), T4.r())
            act(lambda e, h=h: e.activation(out=dd.v[:, h, 0:NCH].unsqueeze(2), in_=b3[:, :, C - 1:C], func=AF.Exp),
                T3.r(), dd.r())
            if KSTOP <= 1.3:
                return True
            hb = h % 2
            act(lambda e: e.activation(out=T5.v[:, 0:N], in_=T3.v[:, 0:N], func=AF.Exp), T3.r(), T5.r())
            dve(lambda e, h=h: e.scalar_tensor_tensor(out=qbb.v[:, h, 0:N], in0=T0.v[:, 0:N], scalar=128.0 ** -0.5,
                                                      in1=T5.v[:, 0:N], op0=ALU.mult, op1=ALU.mult),
                T0.r() + T5.r(), qbb.rs(h, 512))
            act(lambda e: e.activation(out=T6.v[:, 0:N], in_=T2.v[:, 0:N], func=AF.Exp), T2.r(), T6.r())
            dve(lambda e, hb=hb: e.scalar_tensor_tensor(out=qeb.v[:, hb, 0:N], in0=T0.v[:, 0:N], scalar=128.0 ** -0.5,
                                                        in1=T6.v[:, 0:N], op0=ALU.mult, op1=ALU.mult),
                T0.r() + T6.r(), qeb.rs(hb, 512))
            act(lambda e: e.activation(out=T5.v[:, 0:N], in_=T2.v[:, 0:N], func=AF.Exp, scale=-1.0), T2.r(), T5.r())
            dve(lambda e, hb=hb: e.tensor_tensor(out=keb.v[:, hb, 0:N], in0=T1.v[:, 0:N], in1=T5.v[:, 0:N],
                                                 op=ALU.mult), T1.r() + T5.r(), keb.rs(hb, 512))
            act(lambda e: e.activation(out=T6.v[:, 0:N], in_=T4.v[:, 0:N], func=AF.Exp, scale=-1.0), T4.r(), T6.r())
            dve(lambda e, hb=hb: e.tensor_tensor(out=kdb.v[:, hb, 0:N], in0=T1.v[:, 0:N], in1=T6.v[:, 0:N],
                                                 op=ALU.mult), T1.r() + T6.r(), kdb.rs(hb, 512))
            if KSTOP <= 1.4:
                return True
            if h == 0:
                for blk in range(NB):
                    b = next_bank(dbanks)
                    mm_group(PSB[b][0:TB, 0:512],
                             [(hT.v[:, kc, blk * TB:(blk + 1) * TB], wB[:, kc, 0:512]) for kc in range(8)],
                             hT.r() + slotB.r(), [PSR[b]])
                    act(lambda e, b=b, blk=blk: e.activation(out=Vtok.v[0:TB, blk, :], in_=PSB[b][0:TB, 0:512],
                                                             func=AF.Copy), [PSR[b]], Vtok.rs(blk, 512))
            if KSTOP <= 1.5:
                return True
            psb7 = PSB[7][:, :].bitcast(BF16)
            for blk in range(NB):
                pe(lambda e, blk=blk, hb=hb: e.transpose(psb7[0:TB, blk * 128:(blk + 1) * 128],
                                                         kdb.v[:, hb, blk * TB:(blk + 1) * TB], identb.v),
                   kdb.rs(hb, 512) + identb.r(), [PSR[7]])
            act(lambda e, h=h: e.activation(out=kdTokE.v[0:C, h % 2, 0:NB, :],
                                            in_=psb7[0:C, 0:NB * 128].rearrange("p (b k) -> p b k", k=128),
                                            func=AF.Copy), [PSR[7]], kdTokE.rs(h % 2, 512))
            act(lambda e, h=h: e.activation(out=kdTokO.v[C:TB, h % 2, 0:NB, :],
                                            in_=psb7[C:TB, 0:NB * 128].rearrange("p (b k) -> p b k", k=128),
                                            func=AF.Copy), [PSR[7]], kdTokO.rs(h % 2, 512))
            if KSTOP <= 1.6:
                return True
            for blk in range(NB):
                mm_group(PSB[6][0:TB, blk * 128:blk * 128 + TB],
                         [(keb.v[:, hb, blk * TB:(blk + 1) * TB], qeb.v[:, hb, blk * TB:(blk + 1) * TB])],
                         keb.rs(hb, 512) + qeb.rs(hb, 512), [PSR[6]])
            dve(lambda e, h=h: e.tensor_tensor(
                out=Am.v[0:TB, h, 0:NB, 0:TB],
                in0=PSB[6][0:TB, 0:NB * 128].rearrange("p (b t) -> p b t", t=128)[:, :, 0:TB],
                in1=mask_v.unsqueeze(1).to_broadcast([TB, NB, TB]), op=ALU.mult),
                [PSR[6]] + constf.r(), Am.rs(h, 512))
            if KSTOP <= 1.7:
                return True
            for c in range(NCH):
                blk = (c * C) // TB
                r0 = (c * C) % TB
                ub_ = 4 + c // 4
                kdX = kdTokE if r0 == 0 else kdTokO
                mm_group(PSB[ub_][:, (c % 4) * 128:(c % 4) * 128 + 128],
                         [(kdX.v[0:TB, h % 2, blk, :], Vtok.v[0:TB, blk, h * 128:(h + 1) * 128])],
                         kdX.rs(h % 2, 512) + Vtok.rs(blk, 512), [PSR[ub_]])
            if KSTOP <= 1.8:
                return True
            if prompt:
                Sh = Sst.v[:, l, h, :]
                Shr = Sst.r((l * 4 + h) * 128, (l * 4 + h + 1) * 128)
                if j == 0:
                    dve(lambda e, Sh=Sh: e.memset(Sh, 0.0), [], Shr)
                act(lambda e, h=h, Sh=Sh: e.activation(out=Sb.v[:, h, 0, :], in_=Sh, func=AF.Copy),
                    Shr, Sb.rs(h, 1024))
                for c in range(NCH):
                    ub_ = 4 + c // 4
                    dve(lambda e, h=h, c=c, ub_=ub_, Sh=Sh: e.scalar_tensor_tensor(
                        out=Sh, in0=Sh, scalar=dd.v[:, h, c:c + 1],
                        in1=PSB[ub_][:, (c % 4) * 128:(c % 4) * 128 + 128], op0=ALU.mult, op1=ALU.add),
                        Shr + dd.r() + [PSR[ub_]], Shr)
                    if c + 1 < NCH:
                        act(lambda e, h=h, c=c, Sh=Sh: e.activation(out=Sb.v[:, h, c + 1, :], in_=Sh, func=AF.Copy),
                            Shr, Sb.rs(h, 1024))
                if j == NT - 1:
                    dma("pool", f"o_hsp{l}_{h}", lambda e, h=h, Sh=Sh: e.dma_start(out=o_hsp[l, s][:, h, :], in_=Sh),
                        Shr, [])
            else:
                for q in range(2):
                    Sq = S0s.v[:, q, h, :]
                    Sqr = S0s.r()
                    act(lambda e, h=h, q=q, Sq=Sq: e.activation(out=Sb.v[:, h, q, :], in_=Sq, func=AF.Copy),
                        Sqr, Sb.rs(h, 1024))
                    dve(lambda e, h=h, q=q, Sq=Sq: e.scalar_tensor_tensor(
                        out=Sq, in0=Sq, scalar=dd.v[:, h, q:q + 1], in1=PSB[4][:, q * 128:q * 128 + 128],
                        op0=ALU.mult, op1=ALU.add), Sqr + dd.r() + [PSR[4]], Sqr)
            if KSTOP <= 1.9:
                return True
            bg = dense(wB, 8, (512 + h * 128, 512 + h * 128 + 128), hrhs, hT.r(), slotB)
            act(lambda e, bg=bg, h=h: e.activation(out=sghg.v[:, h, 0:N], in_=PSB[bg][:, 0:N], func=AF.Silu),
                [PSR[bg]], sghg.rs(h, 512))
        for h in range(4):
            if hgrn_head(h, *TT2[h % 2]):
                return
        if KSTOP <= 2:
            return
        if not prompt:
            dma("pool", "o_hss", lambda e: [e.dma_start(out=o_hss[l, q], in_=S0s.v[:, q]) for q in range(2)],
                S0s.r(), [], n=2)

        slotD, wD = piece_simple(l, s_in[l], 8, 2048, 3072)
        for oc in range(8):
            b = dense(wD, 8, (oc * 128, oc * 128 + 128), hrhs, hT.r(), slotD)
            act(lambda e, b=b, oc=oc: e.activation(out=gates.v[:, oc, 0:N], in_=PSB[b][:, 0:N], func=AF.Sigmoid),
                [PSR[b]], gates.rs(oc, 512))
        slotE, wE = piece_simple(l, s_in[l], 8, 3072, 4096)
        for oc in range(8):
            b = dense(wE, 8, (oc * 128, oc * 128 + 128), hrhs, hT.r(), slotE)
            act(lambda e, b=b, oc=oc: e.activation(out=gates.v[:, 8 + oc, 0:N], in_=PSB[b][:, 0:N], func=AF.Sigmoid),
                [PSR[b]], gates.rs(8 + oc, 512))

        if KSTOP <= 3:
            return
        for h in range(4):
            ob_ = 2
            for blk in range(NB):
                def fn(e, h=h, blk=blk):
                    ins = e.matmul(PSB[ob_][:, blk * TB:(blk + 1) * TB], Vtok.v[0:TB, blk, h * 128:(h + 1) * 128],
                                   Am.v[0:TB, h, blk, 0:TB], start=(blk == 0), stop=False, skip_group_check=True)
                    ncb = TB // C
                    for ci in range(ncb):
                        c = blk * ncb + ci
                        ins = e.matmul(PSB[ob_][:, c * C:(c + 1) * C], Sb.v[:, h, c, :], qbb.v[:, h, c * C:(c + 1) * C],
                                       start=False, stop=(blk == NB - 1 and ci == ncb - 1), skip_group_check=True)
                    return ins
                pe(fn, Vtok.rs(blk, 512) + Am.rs(h, 512) + Sb.rs(h, 1024) + qbb.rs(h, 512), [PSR[ob_]])
            act(lambda e: e.activation(out=sqo.v[:, 0:N], in_=PSB[ob_][:, 0:N], func=AF.Square), [PSR[ob_]], sqo.r())
            mm_group(PSB[3][:, 0:N], [(onesb.v, sqo.v[:, 0:N])], sqo.r() + onesb.r(), [PSR[3]])
            rb = rstd[1]
            rstd_from(rb, 3, 128.0 * EPS, N)
            dve(lambda e: e.tensor_tensor(out=to.v[:, 0:N], in0=PSB[ob_][:, 0:N], in1=rb.v[:, 0:N], op=ALU.mult),
                [PSR[ob_]] + rb.r(), to.r())
            dve(lambda e, h=h: e.scalar_tensor_tensor(out=obT.v[:, h, 0:N], in0=to.v[:, 0:N], scalar=ghs.v[:, l:l + 1],
                                                      in1=sghg.v[:, h, 0:N], op0=ALU.mult, op1=ALU.mult),
                to.r() + ghs.r() + sghg.rs(h, 512), obT.rs(h, 512))

        if KSTOP <= 4:
            return
        slotC, wC = piece_simple(l, s_inC[l], 8, 0, 896)
        for c in range(3):
            b = dense(wC, 8, (c * 128, c * 128 + 128), hrhs, hT.r(), slotC)
            act(lambda e, b=b, c=c: e.activation(out=cqf.v[:, c, 0:N], in_=PSB[b][:, 0:N], func=AF.Copy),
                [PSR[b]], cqf.rs(c, 512))
            act(lambda e, b=b, c=c: e.activation(out=sqm.v[:, c, 0:N], in_=PSB[b][:, 0:N], func=AF.Square),
                [PSR[b]], sqm.rs(c, 512))
        mm_group(PSB[3][:, 0:N], [(onesb.v, sqm.v[:, c, 0:N]) for c in range(3)], sqm.r() + onesb.r(), [PSR[3]])
        rq = rstd[0]
        rstd_from(rq, 3, 384.0 * EPS, N)
        for c in range(3):
            dve(lambda e, c=c: e.scalar_tensor_tensor(out=cqn.v[:, c, 0:N], in0=cqf.v[:, c, 0:N],
                                                      scalar=gqs.v[:, l, c:c + 1], in1=rq.v[:, 0:N],
                                                      op0=ALU.mult, op1=ALU.mult),
                cqf.rs(c, 512) + rq.r() + gqs.r(), cqn.rs(c, 512))
        for c in range(2):
            b = dense(wC, 8, (384 + c * 128, 384 + c * 128 + 128), hrhs, hT.r(), slotC)
            act(lambda e, b=b, c=c: e.activation(out=ckvf.v[:, c, 0:N], in_=PSB[b][:, 0:N], func=AF.Copy),
                [PSR[b]], ckvf.rs(c, 512))
            act(lambda e, b=b, c=c: e.activation(out=sqm.v[:, c, 0:N], in_=PSB[b][:, 0:N], func=AF.Square),
                [PSR[b]], sqm.rs(c, 512))
        mm_group(PSB[3][:, 0:N], [(onesb.v, sqm.v[:, c, 0:N]) for c in range(2)], sqm.r() + onesb.r(), [PSR[3]])
        rk = rstd[1]
        rstd_from(rk, 3, 256.0 * EPS, N)
        for c in range(2):
            dve(lambda e, c=c: e.scalar_tensor_tensor(out=ckvf.v[:, c, 0:N], in0=ckvf.v[:, c, 0:N],
                                                      scalar=gkvs.v[:, l, c:c + 1], in1=rk.v[:, 0:N],
                                                      op0=ALU.mult, op1=ALU.mult),
                ckvf.rs(c, 512) + rk.r() + gkvs.r(), ckvf.rs(c, 512))
            act(lambda e, c=c: e.activation(out=latb.v[:, c, 0:N], in_=ckvf.v[:, c, 0:N], func=AF.Copy),
                ckvf.rs(c, 512), latb.rs(c, 512))
        if prompt:
            dma("pool", "o_lat", lambda e: e.dma_start(
                out=o_latp[l, s].rearrange("c p t -> p c t")[:, :, j * 512:(j + 1) * 512], in_=ckvf.v),
                ckvf.r(), [])
        else:
            dma("pool", "o_lat", lambda e: e.dma_start(out=o_lats[l].rearrange("c p t -> p c t"),
                                                        in_=ckvf.v[:, :, 0:64]), ckvf.r(), [])
        bk = dense(wC, 8, (640, 768), hrhs, hT.r(), slotC)
        dve(lambda e: e.tensor_tensor(out=tmp1.v[:, 0:N], in0=PSB[bk][:, 0:N], in1=cos_v, op=ALU.mult),
            [PSR[bk]] + cs.r(), tmp1.r())
        bkp = dense(wC, 8, (768, 896), hrhs, hT.r(), slotC)
        dve(lambda e: e.tensor_tensor(out=tmp2.v[:, 0:N], in0=PSB[bkp][:, 0:N], in1=sin_v, op=ALU.mult),
            [PSR[bkp]] + cs.r(), tmp2.r())
        dve(lambda e: e.tensor_tensor(out=kpef.v[:, 0:N], in0=tmp1.v[:, 0:N], in1=tmp2.v[:, 0:N], op=ALU.add),
            tmp1.r() + tmp2.r(), kpef.r())
        act(lambda e: e.activation(out=Pcur.v[:, 0:N], in_=kpef.v[:, 0:N], func=AF.Copy), kpef.r(), Pcur.r())
        if prompt:
            dma("pool", "o_kpe", lambda e: e.dma_start(out=o_kpep[l, s][:, j * 512:(j + 1) * 512],
                                                        in_=kpef.v[0:64, :]), kpef.r(), [])
        else:
            dma("pool", "o_kpe", lambda e: e.dma_start(out=o_kpes[l], in_=kpef.v[0:64, 0:64]), kpef.r(), [])
        if KSTOP <= 5:
            return
        Qp4 = Qp.v.rearrange("p (a b) n -> p a b n", b=2)
        pool(lambda e: e.memset(Qp4[64:128, :, 0, :], 0.0), [], Qp.r())
        pool(lambda e: e.memset(Qp4[0:64, :, 1, :], 0.0), [], Qp.r())
        slotU, wU = piece_simple(l, s_uq[l], 3, 0, 2048)
        qrhs = lambda kc: cqn.v[:, kc, 0:N]
        for h in range(8):
            b = dense(wU, 3, (h * 128, h * 128 + 128), qrhs, cqn.r(), slotU)
            act(lambda e, b=b, h=h: e.activation(out=Qn.v[:, h, 0:N], in_=PSB[b][:, 0:N], func=AF.Copy),
                [PSR[b]], Qn.rs(h, 512))
        for pr in range(4):
            b1 = dense(wU, 3, (1024 + pr * 128, 1024 + pr * 128 + 128), qrhs, cqn.r(), slotU)
            dve(lambda e, b1=b1: e.tensor_tensor(out=tmp1.v[:, 0:N], in0=PSB[b1][:, 0:N], in1=cos_v, op=ALU.mult),
                [PSR[b1]] + cs.r(), tmp1.r())
            b2 = dense(wU, 3, (1536 + pr * 128, 1536 + pr * 128 + 128), qrhs, cqn.r(), slotU)
            dve(lambda e, b2=b2: e.tensor_tensor(out=tmp2.v[:, 0:N], in0=PSB[b2][:, 0:N], in1=sin_v, op=ALU.mult),
                [PSR[b2]] + cs.r(), tmp2.r())
            dve(lambda e, pr=pr: e.tensor_tensor(out=Qp.v[0:64, 2 * pr, 0:N], in0=tmp1.v[0:64, 0:N],
                                                 in1=tmp2.v[0:64, 0:N], op=ALU.add),
                tmp1.r() + tmp2.r(), Qp.rs(2 * pr, 512))
            dve(lambda e, pr=pr: e.tensor_tensor(out=Qp.v[64:128, 2 * pr + 1, 0:N], in0=tmp1.v[64:128, 0:N],
                                                 in1=tmp2.v[64:128, 0:N], op=ALU.add),
                tmp1.r() + tmp2.r(), Qp.rs(2 * pr + 1, 512))
        if KSTOP <= 6:
            return
        slotK, wK = piece_simple(l, s_ukv[l], 2, 0, 2048)
        lrhs = lambda kc: latb.v[:, kc, 0:N]
        if prompt:
            for h in range(8):
                b = dense(wK, 2, (h * 128, h * 128 + 128), lrhs, latb.r(), slotK)
                act(lambda e, b=b, h=h: e.activation(out=Kcur.v[:, :, h, :],
                                                     in_=PSB[b][:, 0:512].rearrange("p (b k) -> p b k", k=128),
                                                     func=AF.Copy), [PSR[b]], Kcur.r())
            for blk in range(4):
                for half in range(2):
                    b = next_bank(dbanks)
                    mm_group(PSB[b][:, 0:512],
                             [(latb.v[:, kc, blk * 128:(blk + 1) * 128],
                               wK[:, kc, 1024 + half * 512:1024 + half * 512 + 512]) for kc in range(2)],
                             latb.r() + slotK.r(), [PSR[b]])
                    act(lambda e, b=b, blk=blk, half=half: e.activation(
                        out=Vcur.v[:, blk, half * 512:(half + 1) * 512], in_=PSB[b][:, 0:512], func=AF.Copy),
                        [PSR[b]], Vcur.rs(blk, 1024))
            if j < NT - 1:
                dma("pool", "kcw", lambda e: e.dma_start(out=s_kc[s, l][:, j * 4:(j + 1) * 4], in_=Kcur.v),
                    Kcur.r(), [DR(("kc", s, l, j))])
                dma("pool", "vcw", lambda e: e.dma_start(
                    out=s_vc[s, l][j * 4:(j + 1) * 4].rearrange("b k c -> k b c"), in_=Vcur.v),
                    Vcur.r(), [DR(("vc", s, l, j))])
                dma("pool", "pcw", lambda e: e.dma_start(out=s_pc[s, l][:, j * 512:(j + 1) * 512], in_=Pcur.v),
                    Pcur.r(), [DR(("pc", s, l, j))])
            if KSTOP <= 7:
                return
            slotPA, wPA = piece_simple(l, s_pa[l], 8, 0, 1024)
            slotPB, wPB = piece_simple(l, s_pb[l], 4, 0, 1024)
            attention_prompt(l, s, j)
        else:
            slotPA, wPA = piece_simple(l, s_pa[l], 8, 0, 1024)
            attention_sample(l, wK, slotK)
            slotPB, wPB = piece_simple(l, s_pb[l], 4, 0, 1024)

        if KSTOP <= 8:
            return
        mbanks = [[0, 1], [2, 3], [4, 5], [6, 7]]
        for oc in range(8):
            bx, by = mbanks[oc % 4]
            mm_group(PSB[bx][:, 0:N], [(wPA[:, kc, oc * 128:(oc + 1) * 128], oaT.v[:, kc, 0:N]) for kc in range(8)],
                     oaT.r() + slotPA.r(), [PSR[bx]])
            mm_group(PSB[by][:, 0:N], [(wPB[:, kc, oc * 128:(oc + 1) * 128], obT.v[:, kc, 0:N]) for kc in range(4)],
                     obT.r() + slotPB.r(), [PSR[by]])
            ta = mta[oc % 2]
            tb = mtb[oc % 2]
            dve(lambda e, bx=bx, oc=oc, ta=ta: e.tensor_tensor(out=ta.v[:, 0:N], in0=PSB[bx][:, 0:N],
                                                               in1=gates.v[:, oc, 0:N], op=ALU.mult),
                [PSR[bx]] + gates.rs(oc, 512), ta.r())
            dve(lambda e, by=by, oc=oc, tb=tb: e.tensor_tensor(out=tb.v[:, 0:N], in0=PSB[by][:, 0:N],
                                                               in1=gates.v[:, 8 + oc, 0:N], op=ALU.mult),
                [PSR[by]] + gates.rs(8 + oc, 512), tb.r())
            dve(lambda e, oc=oc, ta=ta, tb=tb: e.tensor_tensor(out=hT.v[:, oc, 0:N], in0=ta.v[:, 0:N],
                                                               in1=tb.v[:, 0:N], op=ALU.add),
                ta.r() + tb.r(), hT.rs(oc, 512))
        slotO, wO = piece_simple(l, s_wo[l], 8, 0, 1024)
        for oc in range(8):
            b = dense(wO, 8, (oc * 128, oc * 128 + 128), hrhs, hT.r(), slotO, banks=[0, 1, 2, 3])
            dve(lambda e, b=b, oc=oc: e.tensor_tensor(out=xT.v[:, oc, 0:N], in0=xT.v[:, oc, 0:N], in1=PSB[b][:, 0:N],
                                                      op=ALU.add), [PSR[b]] + xT.rs(oc, 512), xT.rs(oc, 512))

        if KSTOP <= 9:
            return
        rmsnorm_x(N, lambda c: g2s.v[:, l, c:c + 1], lambda c: hT.v[:, c, 0:N], lambda c: hT.rs(c, 512))
        nseg = 1 if prompt else 2
        L = N // nseg
        cwv = pv("cw").rearrange("p (l j c) -> p l j c", j=3, c=22)
        cbv = pv("cb").rearrange("p (l c) -> p l c", c=22)
        fb = [[0, 1], [2, 3], [4, 5], [6, 7]]
        for q in range(6):
            ncol = 512 if q < 5 else 256

            def src_fn(flat, q=q, ncol=ncol):
                v = flat[:, 0:8 * 2 * ncol].rearrange("p (k n) -> p k n", n=2 * ncol)
                return [(v[:, :, 0:ncol], wview(s_up[l], 8, q * 512, q * 512 + ncol)),
                        (v[:, :, ncol:2 * ncol], wview(s_up[l], 8, 2816 + q * 512, 2816 + q * 512 + ncol))]
            slotP = load_piece(l, src_fn, 8 * 2 * ncol, 2)
            wP = slotP.flat[:, 0:8 * 2 * ncol].rearrange("p (k n) -> p k n", n=2 * ncol)
            for jj in range(ncol // 128):
                jc = q * 4 + jj
                bx, by = fb[jc % 4]
                if jc == 0:
                    for kc in range(8):
                        mm_group(PSB[bx][:, 0:N], [(wP[:, kc, jj * 128:(jj + 1) * 128], hT.v[:, kc, 0:N])],
                                 hT.rs(kc, 512) + slotP.r(), [PSR[bx]], start=(kc == 0), stop=(kc == 7))
                else:
                    mm_group(PSB[bx][:, 0:N], [(wP[:, kc, jj * 128:(jj + 1) * 128], hT.v[:, kc, 0:N]) for kc in range(8)],
                             hT.r() + slotP.r(), [PSR[bx]])
                mm_group(PSB[by][:, 0:N],
                         [(wP[:, kc, ncol + jj * 128:ncol + (jj + 1) * 128], hT.v[:, kc, 0:N]) for kc in range(8)],
                         hT.r() + slotP.r(), [PSR[by]])
                ae = aext[jc % 4]
                cv = cvb[jc % 4]
                uu = ub[jc % 4]
                ss_ = sbf[jc % 4]
                ae3 = ae.v[:, 0:nseg * (L + 2)].rearrange("p (s t) -> p s t", t=L + 2)
                as3 = lambda ap_: ap_[:, 0:N].rearrange("p (s t) -> p s t", t=L)
                act(lambda e, bx=bx, ae3=ae3: e.activation(out=ae3[:, :, 2:L + 2], in_=as3(PSB[bx]), func=AF.Copy),
                    [PSR[bx]], ae.r())
                if prompt:
                    halo_src = ctail.v[:, l, jc, :].unsqueeze(1)
                    halo_res = ctail.r()
                else:
                    halo_src = shalo.v[:, :, jc, :]
                    halo_res = shalo.r()
                if prompt and j == 0:
                    pool(lambda e, ae3=ae3: e.memset(ae3[:, :, 0:2], 0.0), [], ae.r())
                else:
                    pool(lambda e, ae3=ae3, halo_src=halo_src: e.tensor_copy(out=ae3[:, :, 0:2], in_=halo_src),
                         halo_res, ae.r())
                if prompt:
                    pool(lambda e, ae3=ae3, jc=jc: e.tensor_copy(out=ctail.v[:, l, jc, :].unsqueeze(1),
                                                                in_=ae3[:, :, L:L + 2]), ae.r(), ctail.r())
                else:
                    pool(lambda e, ae3=ae3, jc=jc: e.tensor_copy(out=stail.v[:, :, jc, :], in_=ae3[:, :, L:L + 2]),
                         ae.r(), stail.r())
                act(lambda e, ae3=ae3, cv=cv, jc=jc: e.activation(out=as3(cv.v), in_=ae3[:, :, 2:L + 2], func=AF.Identity,
                                                                 scale=cwv[:, l, 2, jc:jc + 1], bias=cbv[:, l, jc:jc + 1]),
                    ae.r() + par.r(), cv.r())
                dve(lambda e, ae3=ae3, cv=cv, jc=jc: e.scalar_tensor_tensor(
                    out=as3(cv.v), in0=ae3[:, :, 1:L + 1], scalar=cwv[:, l, 1, jc:jc + 1], in1=as3(cv.v),
                    op0=ALU.mult, op1=ALU.add), ae.r() + cv.r() + par.r(), cv.r())
                dve(lambda e, ae3=ae3, cv=cv, jc=jc: e.scalar_tensor_tensor(
                    out=as3(cv.v), in0=ae3[:, :, 0:L], scalar=cwv[:, l, 0, jc:jc + 1], in1=as3(cv.v),
                    op0=ALU.mult, op1=ALU.add), ae.r() + cv.r() + par.r(), cv.r())
                act(lambda e, cv=cv, uu=uu: e.activation(out=uu.v[:, 0:N], in_=cv.v[:, 0:N], func=AF.Square,
                                                         scale=math.sqrt(0.044715)), cv.r(), uu.r())
                dve(lambda e, cv=cv, uu=uu: e.scalar_tensor_tensor(out=uu.v[:, 0:N], in0=uu.v[:, 0:N], scalar=1.0,
                                                                   in1=cv.v[:, 0:N], op0=ALU.add, op1=ALU.mult),
                    cv.r() + uu.r(), uu.r())
                act(lambda e, uu=uu, ss_=ss_: e.activation(out=ss_.v[:, 0:N], in_=uu.v[:, 0:N], func=AF.Sigmoid,
                                                           scale=GC), uu.r(), ss_.r())
                pool(lambda e, cv=cv, ss_=ss_: e.tensor_tensor(out=ss_.v[:, 0:N], in0=ss_.v[:, 0:N], in1=cv.v[:, 0:N],
                                                               op=ALU.mult), cv.r() + ss_.r(), ss_.r())
                dve(lambda e, by=by, ss_=ss_, jc=jc: e.tensor_tensor(out=gT.v[:, jc, 0:N], in0=ss_.v[:, 0:N],
                                                                     in1=PSB[by][:, 0:N], op=ALU.mult),
                    [PSR[by]] + ss_.r(), gT.rs(jc, 512))
        if prompt and j == NT - 1:
            dma("pool", f"o_cvp{l}", lambda e: e.dma_start(out=o_cvp[l, s], in_=ctail.v[:, l]), ctail.r(), [])
        if not prompt:
            dma("pool", "o_cvs", lambda e: [e.dma_start(out=o_cvs[l, q], in_=stail.v[:, q]) for q in range(2)],
                stail.r(), [], n=2)
        for q in range(4):
            def src_fn(flat, q=q):
                return [(flat[:, 0:22 * 256].rearrange("p (k n) -> p k n", n=256),
                         wview(s_dn[l], 22, q * 256, (q + 1) * 256))]
            slotDn = load_piece(l, src_fn, 22 * 256, 2)
            wDn = slotDn.flat[:, 0:22 * 256].rearrange("p (k n) -> p k n", n=256)
            for o2 in range(2):
                oc = q * 2 + o2
                b = next_bank([0, 1, 2, 3])
                if oc < 2:
                    for kc in range(22):
                        mm_group(PSB[b][:, 0:N], [(wDn[:, kc, o2 * 128:(o2 + 1) * 128], gT.v[:, kc, 0:N])],
                                 gT.rs(kc, 512) + slotDn.r(), [PSR[b]], start=(kc == 0), stop=(kc == 21))
                else:
                    mm_group(PSB[b][:, 0:N], [(wDn[:, kc, o2 * 128:(o2 + 1) * 128], gT.v[:, kc, 0:N]) for kc in range(22)],
                             gT.r() + slotDn.r(), [PSR[b]])
                dve(lambda e, b=b, oc=oc: e.tensor_tensor(out=xT.v[:, oc, 0:N], in0=xT.v[:, oc, 0:N],
                                                          in1=PSB[b][:, 0:N], op=ALU.add),
                    [PSR[b]] + xT.rs(oc, 512), xT.rs(oc, 512))

    kvctr = [0]

    def attention_prompt(l, s, j):
        N = 512
        for p in range(2):
            heads = range(4 * p, 4 * p + 4)
            first = {h: True for h in heads}
            for g in range(j + 1):
                diag = g == j
                if not diag:
                    si = kvctr[0] % 2
                    kvctr[0] += 1
                    Kb, Vb, Pb = KS[si]

                    def fn(e, g=g, p=p, Kb=Kb, Vb=Vb, Pb=Pb):
                        return [
                            e.dma_start(out=Kb.v, in_=s_kc[s, l][:, g * 4:(g + 1) * 4, 4 * p:4 * p + 4, :]),
                            e.dma_start(out=Vb.v, in_=s_vc[s, l][g * 4:(g + 1) * 4, :, p * 512:(p + 1) * 512]
                                        .rearrange("b k c -> k b c")),
                            e.dma_start(out=Pb.v, in_=s_pc[s, l][:, g * 512:(g + 1) * 512]),
                        ]
                    dma("sp", f"kv{si}", fn, [DR(("kc", s, l, g)), DR(("vc", s, l, g)), DR(("pc", s, l, g))],
                        Kb.r() + Vb.r() + Pb.r(), n=3, lat=10.0)
                    kfn = lambda h, kb, Kb=Kb, p=p: Kb.v[:, kb, h - 4 * p, :]
                    vfn = lambda h, kb, Vb=Vb, p=p: Vb.v[:, kb, (h - 4 * p) * 128:(h - 4 * p + 1) * 128]
                    pfn = lambda hp, kb, Pb=Pb: Pb.v[:, kb * 128:(kb + 1) * 128]
                    kvres = Kb.r() + Vb.r() + Pb.r()
                else:
                    kfn = lambda h, kb: Kcur.v[:, kb, h, :]
                    vfn = lambda h, kb: Vcur.v[:, kb, h * 128:(h + 1) * 128]
                    pfn = lambda hp, kb: Pcur.v[:, kb * 128:(kb + 1) * 128]
                    kvres = Kcur.r() + Vcur.r() + Pcur.r()
                for h in heads:
                    hl = h - 4 * p
                    ob = 4 + hl
                    hp = h % 2
                    for kb in range(4):
                        q0 = kb * 128 if diag else 0
                        sb_ = bctr[0] % 4
                        bctr[0] += 1
                        pt = Pt[bctr[0] % 8]
                        last = (g == j and kb == 3)

                        def fsc(e, h=h, kb=kb, q0=q0, sb_=sb_, hp=hp, kfn=kfn, pfn=pfn, diag=diag):
                            e.matmul(PSB[sb_][:, q0:N], kfn(h, kb), Qn.v[:, h, q0:N], start=True, stop=False)
                            if diag:
                                e.matmul(PSB[sb_][:, q0:N], mUb.v, mVb.v[:, 0:N - q0], start=False, stop=False)
                            return e.matmul(PSB[sb_][:, q0:N], pfn(hp, kb), Qp.v[:, h, q0:N],
                                            start=False, stop=True)
                        pe(fsc, kvres + Qn.rs(h, 512) + Qp.rs(h, 512), [PSR[sb_]])
                        act(lambda e, sb_=sb_, pt=pt, q0=q0: e.activation(out=pt.v[:, q0:N], in_=PSB[sb_][:, q0:N],
                                                                          func=AF.Exp, scale=SCALE),
                            [PSR[sb_]], pt.r())
                        st = first[h]

                        def fpv(e, h=h, kb=kb, q0=q0, pt=pt, ob=ob, st=st, last=last, vfn=vfn):
                            return e.matmul(PSB[ob][:, q0:N], vfn(h, kb), pt.v[:, q0:N], start=st, stop=last,
                                            skip_group_check=True)
                        pe(fpv, kvres + pt.r(), [PSR[ob]])
                        if st:
                            dve(lambda e, pt=pt, hl=hl: e.tensor_copy(out=Pacc.v[:, hl, :], in_=pt.v),
                                pt.r(), Pacc.rs(hl, 512))
                        else:
                            dve(lambda e, pt=pt, hl=hl, q0=q0: e.tensor_tensor(out=Pacc.v[:, hl, q0:N],
                                                                             in0=Pacc.v[:, hl, q0:N],
                                                                             in1=pt.v[:, q0:N], op=ALU.add),
                                pt.r() + Pacc.rs(hl, 512), Pacc.rs(hl, 512))
                        first[h] = False
            for h in heads:
                hl = h - 4 * p
                ob = 4 + hl
                sb_ = 2 + (bctr[0] % 2)
                bctr[0] += 1
                mm_group(PSB[sb_][:, 0:N], [(onesf.v, Pacc.v[:, hl, :])], Pacc.rs(hl, 512) + onesf.r(), [PSR[sb_]])
                act(lambda e, sb_=sb_: e.activation(out=rsb.v, in_=PSB[sb_][:, 0:N], func=AF.Ln), [PSR[sb_]], rsb.r())
                act(lambda e: e.activation(out=rsb.v, in_=rsb.v, func=AF.Exp, scale=-1.0), rsb.r(), rsb.r())
                dve(lambda e, h=h, ob=ob: e.tensor_tensor(out=oaT.v[:, h, :], in0=PSB[ob][:, 0:N], in1=rsb.v,
                                                          op=ALU.mult), [PSR[ob]] + rsb.r(), oaT.rs(h, 512))

    def attention_sample(l, wK, slotK):
        N = 64
        Knew = Kcur.flat[:, 0:8 * 64].rearrange("p (h t) -> p h t", t=64)
        for h in range(8):
            b = next_bank([0, 1])
            mm_group(PSB[b][:, 0:N], [(wK[:, kc, h * 128:(h + 1) * 128], latb.v[:, kc, 0:N]) for kc in range(2)],
                     latb.r() + slotK.r(), [PSR[b]])
            act(lambda e, b=b, h=h: e.activation(out=Knew[:, h, :], in_=PSB[b][:, 0:N], func=AF.Copy),
                [PSR[b]], Kcur.r())
        for q in range(2):
            for half in range(2):
                b = next_bank([0, 1])
                mm_group(PSB[b][0:32, 0:512],
                         [(latb.v[:, kc, q * 32:(q + 1) * 32], wK[:, kc, 1024 + half * 512:1024 + half * 512 + 512])
                          for kc in range(2)], latb.r() + slotK.r(), [PSR[b]])
                act(lambda e, b=b, q=q, half=half: e.activation(out=Vnew.v[0:32, q, half * 512:(half + 1) * 512],
                                                                in_=PSB[b][0:32, 0:512], func=AF.Copy),
                    [PSR[b]], Vnew.rs(q, 1024))
        Pa = Pacc.flat[:, 0:8 * 64].rearrange("p (h t) -> p h t", t=64)
        ob = 4
        first = {}
        ostart = [True]
        for q in range(2):
            for g in range(NPG):
                dma("sp", "lpf", lambda e, q=q, g=g: [
                    e.dma_start(out=lpf.v, in_=d_latp[l, q].rearrange("c p t -> p c t")[:, :, g * 512:(g + 1) * 512]),
                    e.dma_start(out=kpf.v, in_=d_krp[l, q][:, g * 512:(g + 1) * 512])],
                    [], lpf.r() + kpf.r(), n=2)
                act(lambda e: e.activation(out=lpb.flat, in_=lpf.flat, func=AF.Copy), lpf.r(), lpb.r())
                Pb = KS[0][2]
                act(lambda e, Pb=Pb: e.activation(out=Pb.v, in_=kpf.v, func=AF.Copy), kpf.r(), Pb.r())
                for kb in range(4):
                    for half in range(2):
                        b = next_bank([0, 1])
                        mm_group(PSB[b][:, 0:512],
                                 [(lpb.v[:, kc, kb * 128:(kb + 1) * 128],
                                   wK[:, kc, 1024 + half * 512:1024 + half * 512 + 512]) for kc in range(2)],
                                 lpb.r() + slotK.r(), [PSR[b]])
                        act(lambda e, b=b, kb=kb, half=half: e.activation(
                            out=Vcur.v[:, kb, half * 512:(half + 1) * 512], in_=PSB[b][:, 0:512], func=AF.Copy),
                            [PSR[b]], Vcur.rs(kb, 1024))
                for h in range(8):
                    hp = h % 2
                    Kh = KS[h % 2][0]
                    b = next_bank([0, 1])
                    mm_group(PSB[b][:, 0:512], [(wK[:, kc, h * 128:(h + 1) * 128], lpb.v[:, kc, :]) for kc in range(2)],
                             lpb.r() + slotK.r(), [PSR[b]])
                    act(lambda e, b=b, Kh=Kh: e.activation(out=Kh.flat[:, 0:512], in_=PSB[b][:, 0:512], func=AF.Copy),
                        [PSR[b]], Kh.r())
                    for kb in range(4):
                        sb_ = 2 + (bctr[0] % 2)
                        bctr[0] += 1
                        pt = Pt[bctr[0] % 4]

                        def fsc(e, h=h, kb=kb, sb_=sb_, hp=hp, Kh=Kh, Pb=Pb, q=q):
                            e.matmul(PSB[sb_][:, 0:32], Kh.flat[:, kb * 128:(kb + 1) * 128], Qn.v[:, h, q * 32:(q + 1) * 32],
                                     start=True, stop=False)
                            return e.matmul(PSB[sb_][:, 0:32], Pb.v[:, kb * 128:(kb + 1) * 128],
                                            Qp.v[:, h, q * 32:(q + 1) * 32],
                                            start=False, stop=True)
                        pe(fsc, Kh.r() + Pb.r() + Qn.rs(h, 512) + Qp.rs(h, 512), [PSR[sb_]])
                        act(lambda e, sb_=sb_, pt=pt: e.activation(out=pt.v[:, 0:32], in_=PSB[sb_][:, 0:32], func=AF.Exp,
                                                                   scale=SCALE), [PSR[sb_]], pt.r())
                        st = first.get((h, q), True)

                        st0 = ostart[0]
                        ostart[0] = False

                        def fpv(e, h=h, kb=kb, pt=pt, st0=st0, q=q):
                            c0 = h * 64 + q * 32
                            return e.matmul(PSB[ob][:, c0:c0 + 32], Vcur.v[:, kb, h * 128:(h + 1) * 128], pt.v[:, 0:32],
                                            start=st0, stop=False, skip_group_check=True)
                        pe(fpv, Vcur.rs(kb, 1024) + pt.r(), [PSR[ob]])
                        if st:
                            dve(lambda e, pt=pt, h=h, q=q: e.tensor_copy(out=Pa[:, h, q * 32:(q + 1) * 32],
                                                                        in_=pt.v[:, 0:32]), pt.r(), Pacc.r())
                        else:
                            dve(lambda e, pt=pt, h=h, q=q: e.tensor_tensor(out=Pa[:, h, q * 32:(q + 1) * 32],
                                                                          in0=Pa[:, h, q * 32:(q + 1) * 32],
                                                                          in1=pt.v[:, 0:32], op=ALU.add),
                                pt.r() + Pacc.r(), Pacc.r())
                        first[(h, q)] = False
            for h in range(8):
                hp = h % 2
                sb_ = 2 + (bctr[0] % 2)
                bctr[0] += 1
                pt = Pt[bctr[0] % 4]

                def fsc(e, h=h, sb_=sb_, hp=hp, q=q):
                    e.matmul(PSB[sb_][0:32, 0:32], Knew[:, h, q * 32:(q + 1) * 32], Qn.v[:, h, q * 32:(q + 1) * 32],
                             start=True, stop=False)
                    return e.matmul(PSB[sb_][0:32, 0:32], Pcur.v[:, q * 32:(q + 1) * 32],
                                    Qp.v[:, h, q * 32:(q + 1) * 32], start=False, stop=True)
                pe(fsc, Kcur.r() + Pcur.r() + Qn.rs(h, 512) + Qp.rs(h, 512), [PSR[sb_]])
                act(lambda e, sb_=sb_, pt=pt: e.activation(out=pt.v[0:32, 0:32], in_=PSB[sb_][0:32, 0:32], func=AF.Exp,
                                                           scale=SCALE), [PSR[sb_]], pt.r())

                def fpv(e, h=h, pt=pt, q=q):
                    c0 = h * 64 + q * 32
                    return e.matmul(PSB[ob][:, c0:c0 + 32], Vnew.v[0:32, q, h * 128:(h + 1) * 128], pt.v[0:32, 0:32],
                                    start=False, stop=True, skip_group_check=True)
                pe(fpv, Vnew.rs(q, 1024) + pt.r(), [PSR[ob]])
                dve(lambda e, pt=pt, h=h, q=q: e.tensor_tensor(out=Pa[0:32, h, q * 32:(q + 1) * 32],
                                                              in0=Pa[0:32, h, q * 32:(q + 1) * 32],
                                                              in1=pt.v[0:32, 0:32], op=ALU.add),
                    pt.r() + Pacc.r(), Pacc.r())
        sb_ = 3
        mm_group(PSB[sb_][:, 0:512], [(onesf.v, Pacc.flat[:, 0:512])], Pacc.r() + onesf.r(), [PSR[sb_]])
        act(lambda e: e.activation(out=rsb.v, in_=PSB[sb_][:, 0:512], func=AF.Ln), [PSR[sb_]], rsb.r())
        act(lambda e: e.activation(out=rsb.v, in_=rsb.v, func=AF.Exp, scale=-1.0), rsb.r(), rsb.r())
        dve(lambda e: e.tensor_tensor(out=oaT.v[:, :, 0:64], in0=PSB[ob][:, 0:512].rearrange("p (h t) -> p h t", t=64),
                                      in1=rsb.v.rearrange("p (h t) -> p h t", t=64), op=ALU.mult),
            [PSR[ob]] + rsb.r(), oaT.r())

    def final_norm_store(N, dst_fn, key):
        rmsnorm_x(N, lambda c: gfs.v[:, c:c + 1], lambda c: yT.v[:, c, 0:N], lambda c: yT.rs(c, 512))
        dma("pool", key, lambda e: e.dma_start(out=dst_fn(), in_=yT.v[:, :, 0:N]), yT.r(), [])

    import os
    DBG = int(os.environ.get("KDBG", "0"))
    for s in range(NSEQ if DBG == 0 else 0):
        for j in range(NT):
            dma("sp", "xload", lambda e, s=s, j=j: e.dma_start(
                out=xT.v, in_=d_xp[s].rearrange("c p t -> p c t")[:, :, j * 512:(j + 1) * 512]), [], xT.r())
            dma("sp", "csload", lambda e, j=j: e.dma_start(
                out=cs.v, in_=d_csp.rearrange("a p t -> p a t")[:, :, j * 512:(j + 1) * 512]), [], cs.r())
            for l in range(DEPTH):
                tile_layer(l, 512, "p", s, j)
            final_norm_store(512, lambda s=s, j=j: o_yp[s].rearrange("c p t -> p c t")[:, :, j * 512:(j + 1) * 512],
                             "o_y")
    dma("sp", "xload", lambda e: e.dma_start(out=xT.v[:, :, 0:64], in_=d_xs.rearrange("c p t -> p c t")), [], xT.r())
    dma("sp", "csload", lambda e: e.dma_start(out=cs.v[:, :, 0:64], in_=d_css.rearrange("a p t -> p a t")), [], cs.r())
    for l in range(DEPTH if DBG == 0 else 0):
        dma("sp", "shalo", lambda e, l=l: [e.dma_start(out=shalo.v[:, q], in_=d_scv[l, q]) for q in range(2)],
            [], shalo.r(), n=2)
        tile_layer(l, 64, "s")
    final_norm_store(64, lambda: o_ys.rearrange("c p t -> p c t"), "o_y")

    import os as _os2
    if _os2.environ.get("KNOSCHED") is None:
        P.schedule()
        if _os2.environ.get("KVERB"):
            print("scheduled makespan us", P.makespan)
    P.finalize()
    sems = {}

    for k in [("eng", e_) for e_ in ("pe", "act", "dve", "pool")] + [("dma", k_) for k_ in P.dma_counts]:
        sems[k] = es.enter_context(nc.semaphore("s_" + "_".join(str(x) for x in k)))

    def semof(k):
        return sems[k]

    with nc.Block() as block:
        @block.sync
        def _(e):
            P.emit("sp", e, semof)

        @block.gpsimd
        def _(e):
            P.emit("pool", e, semof)

        @block.scalar
        def _(e):
            P.emit("act", e, semof)

        @block.vector
        def _(e):
            P.emit("dve", e, semof)

        @block.tensor
        def _(e):
            P.emit("pe", e, semof)
    es.close()
    return nc


def rope_tables(pos):
    half = 32
    inv = (10000.0 ** (-np.arange(half, dtype=np.float32) / half)).astype(np.float32)
    ang = pos.astype(np.float32)[:, None] * inv[None, :]
    cos = np.cos(ang).astype(np.float32).T
    sin = np.sin(ang).astype(np.float32).T
    c64 = np.concatenate([cos, cos], 0)
    s64 = np.concatenate([-sin, sin], 0)
    return np.stack([np.concatenate([c64, c64], 0), np.concatenate([s64, s64], 0)], 0)


def make_consts():
    ident = np.eye(128, dtype=np.float32)
    s_ = np.arange(128)[:, None]
    t_ = np.arange(128)[None, :]
    maskP = ((s_ // 64 == t_ // 64) & (s_ <= t_)).astype(np.float32)
    maskS = ((s_ // 32 == t_ // 32) & (s_ <= t_)).astype(np.float32)
    maskS[64:, :] = 0
    maskS[:, 64:] = 0
    resetP = np.ones((128, 512), np.float32)
    resetP[:, ::64] = 0
    resetS = np.ones((128, 64), np.float32)
    resetS[:, ::32] = 0
    mU = np.zeros((128, 128), np.float32)
    mU[0, 64:] = -30000.0
    mV = np.zeros((128, 512), np.float32)
    mV[0, :64] = 1.0
    return np.concatenate([maskP, maskS, resetP, resetS], 1), np.concatenate([ident, mU, mV], 1)


def perm64(w):
    return np.concatenate([w[..., 32:64], w[..., 0:32]], -1)


def host_prep(cfg, inp, core):
    D = cfg.DEPTH
    f = np.float32
    ps = slice(core * cfg.NSEQ, (core + 1) * cfg.NSEQ)
    ss = slice(core * 2, core * 2 + 2)
    m = {}
    xp = np.asarray(inp["x_prompt"][ps])
    m["xp"] = np.ascontiguousarray(xp.transpose(0, 2, 1).reshape(cfg.NSEQ, 8, 128, cfg.LP))
    xs = np.asarray(inp["x_sample"][ss])
    m["xs"] = np.ascontiguousarray(xs.transpose(2, 0, 1).reshape(8, 128, 64))
    lat = np.asarray(inp["cache_mla_latent"][:, ss])
    m["latp"] = np.ascontiguousarray(lat.transpose(0, 1, 3, 2).reshape(D, 2, 2, 128, cfg.PAST))
    kr = np.asarray(inp["cache_mla_krope"][:, ss]).transpose(0, 1, 3, 2)
    m["krp"] = np.ascontiguousarray(np.concatenate([kr, kr], 2))
    sh = np.asarray(inp["state_hgrn"][:, ss])
    m["sh"] = np.ascontiguousarray(sh.transpose(0, 1, 3, 2, 4))
    scv = np.asarray(inp["state_ffn_conv"][:, ss])
    m["scv"] = np.ascontiguousarray(scv.reshape(D, 2, 2, 22, 128).transpose(0, 1, 4, 3, 2))
    par = np.zeros((128, cfg.NPAR), f)

    def put(name, arr):
        o, n = cfg.po[name]
        par[:, o:o + n] = arr.reshape(128, n)
    put("g1", np.asarray(inp["norm_mix"]).reshape(D, 8, 128).transpose(2, 0, 1))
    put("g2", np.asarray(inp["norm_ffn"]).reshape(D, 8, 128).transpose(2, 0, 1))
    put("gq", np.asarray(inp["q_norm"]).reshape(D, 3, 128).transpose(2, 0, 1))
    put("gkv", np.asarray(inp["kv_norm"]).reshape(D, 2, 128).transpose(2, 0, 1))
    put("gh", np.asarray(inp["hgrn_norm"]).reshape(D, 128).transpose(1, 0))
    put("lbl", np.asarray(inp["lb_logits"]).reshape(D, 4, 128).transpose(2, 1, 0))
    put("cw", np.asarray(inp["conv_w"]).reshape(D, 3, 22, 128).transpose(3, 0, 1, 2))
    put("cb", np.asarray(inp["conv_b"]).reshape(D, 22, 128).transpose(2, 0, 1))
    put("gf", np.asarray(inp["norm_final"]).reshape(8, 128).transpose(1, 0))
    m["par"] = par
    return m


def host_shared(cfg, inp):
    D = cfg.DEPTH
    m = {}
    m["const"], m["const2"] = make_consts()
    m["csp"] = rope_tables(np.arange(cfg.LP))
    m["css"] = np.ascontiguousarray(np.tile(rope_tables(cfg.PAST + np.arange(32)), (1, 1, 2)))
    w_in = np.asarray(inp["w_in"])
    m["w_in"] = w_in
    kr = w_in[:, :, 640:704]
    krp = perm64(kr)
    m["w_inC"] = np.ascontiguousarray(np.concatenate([w_in[:, :, 0:640], kr, kr, krp, krp], 2))
    wuq = np.asarray(inp["w_uq"]).reshape(D, 384, 8, 192)
    nope = wuq[..., 0:128].reshape(D, 384, 1024)
    ropew = wuq[..., 128:192]
    m["w_uqR"] = np.ascontiguousarray(np.concatenate(
        [nope, ropew.reshape(D, 384, 512), perm64(ropew).reshape(D, 384, 512)], 2))
    wukv = np.asarray(inp["w_ukv"]).reshape(D, 256, 8, 256)
    m["w_ukvR"] = np.ascontiguousarray(np.concatenate(
        [wukv[..., 0:128].reshape(D, 256, 1024), wukv[..., 128:256].reshape(D, 256, 1024)], 2))
    m["w_pa"] = np.asarray(inp["w_proj_a"])
    m["w_pb"] = np.asarray(inp["w_proj_b"])
    m["w_o"] = np.asarray(inp["w_out"])
    m["w_up"] = np.asarray(inp["w_up"])
    m["w_dn"] = np.asarray(inp["w_down"])
    return m


def host_gather(cfg, results):
    D, LP, NSEQ = cfg.DEPTH, cfg.LP, cfg.NSEQ
    nb = NCORES * NSEQ
    y_p = np.empty((nb, LP, 1024), np.float32)
    y_s = np.empty((NCORES * 2, 32, 1024), np.float32)
    lat_p = np.empty((D, nb, LP, 256), np.float32)
    kpe_p = np.empty((D, nb, LP, 64), np.float32)
    hs_p = np.empty((D, nb, 4, 128, 128), np.float32)
    cv_p = np.empty((D, nb, 2, 2816), np.float32)
    lat_s = np.empty((D, NCORES * 2, 32, 256), np.float32)
    kpe_s = np.empty((D, NCORES * 2, 32, 64), np.float32)
    hs_s = np.empty((D, NCORES * 2, 4, 128, 128), np.float32)
    cv_s = np.empty((D, NCORES * 2, 2, 2816), np.float32)
    for c, r in enumerate(results):
        ps = slice(c * NSEQ, (c + 1) * NSEQ)
        ss = slice(c * 2, c * 2 + 2)
        y_p[ps] = r["o_yp"].reshape(NSEQ, 1024, LP).transpose(0, 2, 1)
        y_s[ss] = r["o_ys"].reshape(1024, 2, 32).transpose(1, 2, 0)
        lat_p[:, ps] = r["o_latp"].reshape(D, NSEQ, 256, LP).transpose(0, 1, 3, 2)
        kpe_p[:, ps] = r["o_kpep"].transpose(0, 1, 3, 2)
        hs_p[:, ps] = r["o_hsp"].transpose(0, 1, 3, 2, 4)
        cv_p[:, ps] = r["o_cvp"].transpose(0, 1, 4, 3, 2).reshape(D, NSEQ, 2, 2816)
        lat_s[:, ss] = r["o_lats"].reshape(D, 256, 2, 32).transpose(0, 2, 3, 1)
        kpe_s[:, ss] = r["o_kpes"].reshape(D, 64, 2, 32).transpose(0, 2, 3, 1)
        hs_s[:, ss] = r["o_hss"].transpose(0, 1, 3, 2, 4)
        cv_s[:, ss] = r["o_cvs"].transpose(0, 1, 4, 3, 2).reshape(D, 2, 2, 2816)
    return (y_p, y_s, lat_p, kpe_p, hs_p, cv_p, lat_s, kpe_s, hs_s, cv_s)


def run(cfg, inp, ncores=NCORES, trace=False):
    nc = build_program(cfg)
    shared = host_shared(cfg, inp)
    in_maps = []
    for c in range(ncores):
        m = dict(shared)
        m.update(host_prep(cfg, inp, c))
        in_maps.append(m)
    res = run_bass_kernel_spmd(nc, in_maps, core_ids=list(range(ncores)), trace=trace)
    return res


def kernel(**inputs):
    cfg = Cfg()
    res = run(cfg, inputs)
    return host_gather(cfg, res.results)
```

```python
import math
from contextlib import ExitStack

import numpy as np
import concourse.bass as bass
import concourse.mybir as mybir
from concourse.bass_utils import run_bass_kernel_spmd

F32 = mybir.dt.float32
BF16 = mybir.dt.bfloat16
AF = mybir.ActivationFunctionType
ALU = mybir.AluOpType
AX = mybir.AxisListType

EPS = 1e-6
NCORES = 8


class Res:
    __slots__ = ("w", "rd")

    def __init__(self):
        self.w = None
        self.rd = []


class Op:
    __slots__ = ("eng", "fn", "deps", "sig", "sigidx", "dma", "dcnt", "waits", "alldeps", "cost", "lat",
                 "idx", "nd", "succ", "rt", "cls")


class Prog:
    ENGS = ("pe", "act", "dve", "pool", "sp")

    DEFCOST = {"pe": 0.45, "act": 0.6, "dve": 0.6, "pool": 1.1, "sp": 0.1}

    def __init__(self):
        self.ops = {e: [] for e in self.ENGS}
        self.dma_counts = {}
        self.order = []

    def add(self, eng, fn, reads=(), writes=(), dma=None, ndma=1, cost=None, lat=None):
        op = Op()
        op.eng = eng
        op.fn = fn
        op.cost = cost if cost is not None else (0.1 * ndma if dma is not None else self.DEFCOST[eng])
        op.lat = lat if lat is not None else (3.0 if dma is not None else 0.0)
        op.idx = len(self.order)
        op.cls = None
        self.order.append(op)
        op.sig = False
        op.sigidx = 0
        op.dma = dma
        op.dcnt = 0
        if dma is not None:
            c = self.dma_counts.get(dma, 0) + 16 * ndma
            self.dma_counts[dma] = c
            op.dcnt = c
        deps = {}
        for r in reads:
            if r.w is not None:
                deps[r.w] = True
        for r in writes:
            if r.w is not None:
                deps.setdefault(r.w, False)
            for o in r.rd:
                deps.setdefault(o, False)
        keep = []
        op.alldeps = [d for d in deps if d is not op]
        for d, raw in deps.items():
            if d is op:
                continue
            if d.dma is not None:
                keep.append(d)
            elif d.eng == eng:
                if eng in ("pe", "sp"):
                    continue
                d.sig = True
                keep.append(d)
            else:
                d.sig = True
                keep.append(d)
        op.deps = keep
        for r in reads:
            r.rd.append(op)
        for r in writes:
            r.w = op
            r.rd = []
        self.ops[eng].append(op)
        return op

    def schedule(self):
        import heapq
        for op in self.order:
            op.nd = len(op.alldeps)
            op.succ = []
            op.rt = 0.0
        for op in self.order:
            for d in op.alldeps:
                d.succ.append(op)
        heaps = {e: [] for e in self.ENGS}
        for op in self.order:
            if op.nd == 0:
                heapq.heappush(heaps[op.eng], (0.0, op.idx, op))
        etime = {e: 0.0 for e in self.ENGS}
        new = {e: [] for e in self.ENGS}
        left = len(self.order)
        cur_cls = [None]
        TSW = 1.3

        def act_pick(pop):
            h = heaps["act"]
            cand = [heapq.heappop(h) for _ in range(min(8, len(h)))]
            bi, bk = 0, None
            for i, (rt, idx, op) in enumerate(cand):
                st = rt if rt > etime["act"] else etime["act"]
                if op.cls is not None and cur_cls[0] is not None and op.cls != cur_cls[0]:
                    st += TSW
                if bk is None or (st, idx) < bk:
                    bk, bi = (st, idx), i
            chosen = cand[bi]
            for i, c in enumerate(cand):
                if not (pop and i == bi):
                    heapq.heappush(h, c)
            return bk, chosen

        while left:
            best = None
            for e in self.ENGS:
                h = heaps[e]
                if h:
                    if e == "act":
                        (st, idx), _c = act_pick(False)
                    else:
                        rt, idx, op = h[0]
                        st = rt if rt > etime[e] else etime[e]
                    if best is None or (st, idx) < best[0]:
                        best = ((st, idx), e)
            (st, idx), e = best
            if e == "act":
                (st, idx), (rt, idx, op) = act_pick(True)
                if op.cls is not None:
                    cur_cls[0] = op.cls
            else:
                rt, idx, op = heapq.heappop(heaps[e])
            etime[e] = st + op.cost
            done = st + op.cost + op.lat
            new[e].append(op)
            left -= 1
            for s_ in op.succ:
                t = done + (0.25 if s_.eng != e else 0.1)
                if t > s_.rt:
                    s_.rt = t
                s_.nd -= 1
                if s_.nd == 0:
                    heapq.heappush(heaps[s_.eng], (s_.rt, s_.idx, s_))
        last = {}
        for e in self.ENGS:
            for op in new[e]:
                if op.dma is not None:
                    assert last.get(op.dma, 0) < op.dcnt, ("dma order", op.dma)
                    last[op.dma] = op.dcnt
        self.ops = new
        self.makespan = max(etime.values())

    def finalize(self):
        for e in self.ENGS:
            n = 0
            for op in self.ops[e]:
                if op.sig and op.dma is None:
                    n += 1
                    op.sigidx = n
        for e in self.ENGS:
            for op in self.ops[e]:
                w = {}
                for d in op.deps:
                    if d.dma is not None:
                        k = ("dma", d.dma)
                        v = d.dcnt
                    else:
                        k = ("eng", d.eng)
                        v = d.sigidx
                    if w.get(k, 0) < v:
                        w[k] = v
                op.waits = w

    def emit(self, eng, e, semof):
        waited = {}
        for op in self.ops[eng]:
            for k, v in op.waits.items():
                if waited.get(k, 0) < v:
                    e.wait_ge(semof(k), v)
                    waited[k] = v
            r = op.fn(e)
            if op.dma is not None:
                lst = r if isinstance(r, (list, tuple)) else [r]
                for ins in lst:
                    ins.then_inc(semof(("dma", op.dma)), 16)
            elif op.sig:
                ins = r[-1] if isinstance(r, (list, tuple)) else r
                ins.then_inc(semof(("eng", eng)), 1)
        if eng in ("sp", "pool"):
            for k, c in self.dma_counts.items():
                if self.dma_eng.get(k) == eng:
                    e.wait_ge(semof(("dma", k)), c)


GRAN = 512


class Buf:
    def __init__(self, ar, gran, off, nbytes, dtype, shape):
        self.off = off
        self.nbytes = nbytes
        self.dtype = dtype
        esz = 4 if dtype == F32 else 2
        self.esz = esz
        n = nbytes // esz
        v = ar[:, off // 2:(off + nbytes) // 2]
        if dtype == F32:
            v = v.bitcast(F32)
        self.flat = v
        self.shape = shape
        if len(shape) == 1:
            self.v = v
        elif len(shape) == 2:
            self.v = v.rearrange("p (a b) -> p a b", b=shape[1])
        else:
            self.v = v.rearrange("p (a b c) -> p a b c", b=shape[1], c=shape[2])
        self.gran = gran
        self.n = n

    def r(self, lo=0, hi=None):
        if hi is None:
            hi = self.n
        b0 = (self.off + lo * self.esz) // GRAN
        b1 = (self.off + hi * self.esz - 1) // GRAN
        return self.gran[b0:b1 + 1]

    def rs(self, i, inner):
        return self.r(i * inner, (i + 1) * inner)


class Cfg:
    def __init__(self, LP=4096, DEPTH=4, PAST=1024, NSEQ=2, wslots=3):
        self.LP = LP
        self.DEPTH = DEPTH
        self.PAST = PAST
        self.NSEQ = NSEQ
        self.NT = LP // 512
        self.wslots = wslots
        D = DEPTH
        o = 0
        self.po = {}
        for name, n in (("g1", D * 8), ("g2", D * 8), ("gq", D * 3), ("gkv", D * 2), ("gh", D),
                        ("lbl", 4 * D), ("cw", D * 3 * 22), ("cb", D * 22), ("gf", 8)):
            self.po[name] = (o, n)
            o += n
        self.NPAR = o


def build_program(cfg):
    LP, DEPTH, PAST, NSEQ, NT = cfg.LP, cfg.DEPTH, cfg.PAST, cfg.NSEQ, cfg.NT
    NPG = PAST // 512
    nc = bass.Bass("TRN2", target_bir_lowering=False)
    P = Prog()
    P.dma_eng = {}

    def dram(name, shape, dt=F32, kind="ExternalInput"):
        return nc.dram_tensor(name, list(shape), dt, kind=kind).ap()

    d_xp = dram("xp", [NSEQ, 8, 128, LP])
    d_xs = dram("xs", [8, 128, 64])
    d_latp = dram("latp", [DEPTH, 2, 2, 128, PAST])
    d_krp = dram("krp", [DEPTH, 2, 128, PAST])
    d_sh = dram("sh", [DEPTH, 2, 128, 4, 128])
    d_scv = dram("scv", [DEPTH, 2, 128, 22, 2])
    d_par = dram("par", [128, cfg.NPAR])
    d_const = dram("const", [128, 128 + 128 + 512 + 64])
    d_const2 = dram("const2", [128, 128 + 128 + 512])
    d_csp = dram("csp", [2, 128, LP])
    d_css = dram("css", [2, 128, 64])
    d_win = dram("w_in", [DEPTH, 1024, 4800])
    d_winC = dram("w_inC", [DEPTH, 1024, 896])
    d_wuq = dram("w_uqR", [DEPTH, 384, 2048])
    d_wukv = dram("w_ukvR", [DEPTH, 256, 2048])
    d_wpa = dram("w_pa", [DEPTH, 1024, 1024])
    d_wpb = dram("w_pb", [DEPTH, 512, 1024])
    d_wo = dram("w_o", [DEPTH, 1024, 1024])
    d_wup = dram("w_up", [DEPTH, 1024, 5632])
    d_wdn = dram("w_dn", [DEPTH, 2816, 1024])
    o_yp = dram("o_yp", [NSEQ, 8, 128, LP], kind="ExternalOutput")
    o_ys = dram("o_ys", [8, 128, 64], kind="ExternalOutput")
    o_latp = dram("o_latp", [DEPTH, NSEQ, 2, 128, LP], kind="ExternalOutput")
    o_kpep = dram("o_kpep", [DEPTH, NSEQ, 64, LP], kind="ExternalOutput")
    o_hsp = dram("o_hsp", [DEPTH, NSEQ, 128, 4, 128], kind="ExternalOutput")
    o_cvp = dram("o_cvp", [DEPTH, NSEQ, 128, 22, 2], kind="ExternalOutput")
    o_lats = dram("o_lats", [DEPTH, 2, 128, 64], kind="ExternalOutput")
    o_kpes = dram("o_kpes", [DEPTH, 64, 64], kind="ExternalOutput")
    o_hss = dram("o_hss", [DEPTH, 2, 128, 4, 128], kind="ExternalOutput")
    o_cvs = dram("o_cvs", [DEPTH, 2, 128, 22, 2], kind="ExternalOutput")
    s_in = dram("s_in", [DEPTH, 1024, 4096], BF16, "Internal")
    s_inC = dram("s_inC", [DEPTH, 1024, 896], BF16, "Internal")
    s_uq = dram("s_uq", [DEPTH, 384, 2048], BF16, "Internal")
    s_ukv = dram("s_ukv", [DEPTH, 256, 2048], BF16, "Internal")
    s_pa = dram("s_pa", [DEPTH, 1024, 1024], BF16, "Internal")
    s_pb = dram("s_pb", [DEPTH, 512, 1024], BF16, "Internal")
    s_wo = dram("s_wo", [DEPTH, 1024, 1024], BF16, "Internal")
    s_up = dram("s_up", [DEPTH, 1024, 5632], BF16, "Internal")
    s_dn = dram("s_dn", [DEPTH, 2816, 1024], BF16, "Internal")
    s_kc = dram("s_kc", [NSEQ, DEPTH, 128, NT * 4, 8, 128], BF16, "Internal")
    s_vc = dram("s_vc", [NSEQ, DEPTH, NT * 4, 128, 1024], BF16, "Internal")
    s_pc = dram("s_pc", [NSEQ, DEPTH, 128, LP], BF16, "Internal")

    es = ExitStack()
    ARENA_BYTES = 207 * 1024
    ar = es.enter_context(nc.sbuf_tensor("arena", [128, ARENA_BYTES // 2], BF16))
    gran = [Res() for _ in range(ARENA_BYTES // GRAN + 1)]
    top = [0]

    def alloc(shape, dt, at=None):
        esz = 4 if dt == F32 else 2
        n = 1
        for s in shape:
            n *= s
        nb = n * esz
        nb_al = (nb + GRAN - 1) // GRAN * GRAN
        if at is None:
            off = top[0]
            top[0] += nb_al
        else:
            off = at[0]
            at[0] += nb_al
        if off + nb_al > ARENA_BYTES:
            raise AssertionError(("arena overflow", off, nb_al, shape, "OV", globals().get("_OV")))
        return Buf(ar, gran, off, nb, dt, shape)

    nws = cfg.wslots
    xT = alloc([8, 512], F32)
    hT = alloc([8, 512], BF16)
    WS = [alloc([8192], BF16) for _ in range(nws)]
    rstd = [alloc([512], F32) for _ in range(2)]
    gates = alloc([16, 512], BF16)
    obT = alloc([4, 512], BF16)
    Sst = alloc([DEPTH, 4, 128], F32)
    cs = alloc([2, 512], F32)
    identb = alloc([128], BF16)
    onesb = alloc([128], BF16)
    onesf = alloc([128], F32)
    constf = alloc([128 + 128 + 512 + 64], F32)
    mUb = alloc([128], BF16)
    mVb = alloc([512], BF16)
    par = alloc([cfg.NPAR], F32)
    g1s = alloc([DEPTH, 8], F32)
    g2s = alloc([DEPTH, 8], F32)
    gqs = alloc([DEPTH, 3], F32)
    gkvs = alloc([DEPTH, 2], F32)
    ghs = alloc([DEPTH], F32)
    gfs = alloc([8], F32)
    lbe = alloc([4, DEPTH], F32)
    lbm = alloc([4], F32)
    lbv = alloc([4, DEPTH], F32)
    oml = alloc([4, DEPTH], F32)
    ctail = alloc([DEPTH, 22, 2], F32)
    shalo = alloc([2, 22, 2], F32)
    stail = alloc([2, 22, 2], F32)
    S0s = alloc([2, 4, 128], F32)
    dd = alloc([4, 8], F32)
    epsb = alloc([4], F32)
    OV = top[0]
    globals()['_OV'] = OV
    a = [OV]
    TT2 = [[alloc([512], F32, a) for _ in range(7)] for _ in range(2)]
    qeb = alloc([2, 512], BF16, a)
    keb = alloc([2, 512], BF16, a)
    kdb = alloc([2, 512], BF16, a)
    qbb = alloc([4, 512], BF16, a)
    Vtok = alloc([4, 512], BF16, a)
    kdTokE = alloc([2, 4, 128], BF16, a)
    kdTokO = alloc([2, 4, 128], BF16, a)
    Am = alloc([4, 4, 128], BF16, a)
    Sb = alloc([4, 8, 128], BF16, a)
    sghg = alloc([4, 512], BF16, a)
    sqo = alloc([512], BF16, a)
    to = alloc([512], F32, a)
    endH = a[0]
    a = [OV]
    Qn = alloc([8, 512], BF16, a)
    Qp = alloc([8, 512], BF16, a)
    Kcur = alloc([4, 8, 128], BF16, a)
    Vcur = alloc([4, 1024], BF16, a)
    Pcur = alloc([512], BF16, a)
    M12 = a[0]
    a = [M12]
    cqf = alloc([3, 512], F32, a)
    cqn = alloc([3, 512], BF16, a)
    ckvf = alloc([2, 512], F32, a)
    latb = alloc([2, 512], BF16, a)
    kpef = alloc([512], F32, a)
    tmp1 = alloc([512], F32, a)
    tmp2 = alloc([512], F32, a)
    sqm = alloc([3, 512], BF16, a)
    endM1 = a[0]
    a = [M12]
    KS = []
    for i in range(2):
        KS.append((alloc([4, 4, 128], BF16, a), alloc([4, 512], BF16, a), alloc([512], BF16, a)))
    Pt = [alloc([512], BF16, a) for _ in range(8)]
    oaT = alloc([8, 512], BF16, a)
    Pacc = alloc([4, 512], F32, a)
    rsb = alloc([512], F32, a)
    al_ = [KS[0][0].off]
    mta = [alloc([512], F32, al_) for _ in range(2)]
    mtb = [alloc([512], F32, al_) for _ in range(2)]
    lpf = alloc([2, 512], F32, [KS[1][1].off])
    lpb = alloc([2, 512], BF16, [oaT.off])
    kpf = alloc([512], F32, [oaT.off + 2048])
    Vnew = alloc([2, 1024], BF16, [KS[0][1].off])
    endM2 = a[0]
    a = [OV]
    gT = alloc([22, 512], BF16, a)
    aext = [alloc([520], F32, a) for _ in range(4)]
    cvb = [alloc([512], F32, a) for _ in range(4)]
    ub = [alloc([512], F32, a) for _ in range(4)]
    sbf = [alloc([512], F32, a) for _ in range(4)]
    endF = a[0]
    import os as _os0
    if _os0.environ.get("KVERB"):
        print("ARENA OV", OV, "H", endH, "M12", M12, "M1", endM1, "M2", endM2, "F", endF, "cap", ARENA_BYTES)
    assert max(endH, endM1, endM2, endF) <= ARENA_BYTES, (endH, endM1, endM2, endF)

    PSB = [es.enter_context(nc.psum_tensor(f"ps{i}", [128, 512], F32)) for i in range(8)]
    PSR = [Res() for _ in range(8)]

    dres = {}

    def DR(key):
        r = dres.get(key)
        if r is None:
            r = dres[key] = Res()
        return r

    def dma(eng, key, fn, reads, writes, n=1, lat=None):
        P.dma_eng[key] = eng
        return P.add(eng, fn, reads, writes, dma=key, ndma=n, lat=lat)

    class _Sniff:
        def activation(self, **kw):
            self.func = kw.get("func")

    _CLS = {AF.Exp: "le", AF.Ln: "le", AF.Sigmoid: "sg", AF.Silu: "si"}

    def act(fn, reads, writes, cost=None):
        op = P.add("act", fn, reads, writes, cost=cost)
        sn = _Sniff()
        fn(sn)
        op.cls = _CLS.get(sn.func)
        return op

    def dve(fn, reads, writes, cost=None):
        return P.add("dve", fn, reads, writes, cost=cost)

    def pool(fn, reads, writes, cost=None):
        return P.add("pool", fn, reads, writes, cost=cost)

    def pe(fn, reads, writes, cost=None):
        return P.add("pe", fn, reads, writes, cost=cost)

    def mm_group(out_ap, pairs, reads, writes, start=True, stop=True):
        def fn(e, out_ap=out_ap, pairs=pairs, start=start, stop=stop):
            ins = None
            n = len(pairs)
            for i, (l, r) in enumerate(pairs):
                ins = e.matmul(out_ap, l, r, start=(start and i == 0), stop=(stop and i == n - 1))
            return ins
        ncol = out_ap.shape[-1]
        return pe(fn, reads, writes, cost=len(pairs) * max(0.07, 0.22 * ncol / 512.0))

    dma("sp", "par", lambda e: e.dma_start(out=par.v, in_=d_par[:, :]), [], par.r())
    dma("sp", "const", lambda e: e.dma_start(out=constf.v, in_=d_const[:, :]), [], constf.r())
    maskP_v = constf.v[:, 0:128]
    maskS_v = constf.v[:, 128:256]
    resetP_v = constf.v[:, 256:768]
    resetS_v = constf.v[:, 768:832]
    const2 = alloc([768], F32, [OV])
    dma("sp", "const2", lambda e: e.dma_start(out=const2.v, in_=d_const2[:, :]), [], const2.r())
    dve(lambda e: e.tensor_copy(out=identb.v, in_=const2.v[:, 0:128]), const2.r(), identb.r())
    dve(lambda e: e.tensor_copy(out=mUb.v, in_=const2.v[:, 128:256]), const2.r(), mUb.r())
    dve(lambda e: e.tensor_copy(out=mVb.v, in_=const2.v[:, 256:768]), const2.r(), mVb.r())
    dve(lambda e: e.memset(onesb.v, 1.0), [], onesb.r())
    dve(lambda e: e.memset(onesf.v, 1.0), [], onesf.r())
    epsc = {}
    for i_, n_ in enumerate((1024.0, 384.0, 256.0, 128.0)):
        dve(lambda e, i_=i_, n_=n_: e.memset(epsb.v[:, i_:i_ + 1], n_ * EPS), [], epsb.r())
        epsc[n_ * EPS] = epsb.v[:, i_:i_ + 1]
    dve(lambda e: e.memset(Sst.flat, 0.0), [], Sst.r())
    dve(lambda e: e.memset(ctail.flat, 0.0), [], ctail.r())

    def pv(name):
        o, n = cfg.po[name]
        return par.flat[:, o:o + n]

    def scale_par(dst, name, s):
        dve(lambda e: e.tensor_scalar(out=dst.flat, in0=pv(name), scalar1=float(s), scalar2=None,
                                      op0=ALU.mult), par.r(), dst.r())

    scale_par(g1s, "g1", math.sqrt(1024.0))
    scale_par(g2s, "g2", math.sqrt(1024.0))
    scale_par(gqs, "gq", math.sqrt(384.0))
    scale_par(gkvs, "gkv", math.sqrt(256.0))
    scale_par(ghs, "gh", math.sqrt(128.0))
    scale_par(gfs, "gf", math.sqrt(1024.0))
    lbl_v = pv("lbl").rearrange("p (h l) -> p h l", l=DEPTH)
    dve(lambda e: e.tensor_reduce(out=lbm.v, in_=lbl_v, axis=AX.X, op=ALU.max), par.r(), lbm.r())
    dve(lambda e: e.tensor_tensor(out=lbe.v, in0=lbl_v,
                                  in1=lbm.v.unsqueeze(2).to_broadcast([128, 4, DEPTH]), op=ALU.subtract),
        par.r() + lbm.r(), lbe.r())
    act(lambda e: e.activation(out=lbe.flat, in_=lbe.flat, func=AF.Exp), lbe.r(), lbe.r())
    dve(lambda e: e.tensor_reduce(out=lbm.v, in_=lbe.v, axis=AX.X, op=ALU.add), lbe.r(), lbm.r())
    dve(lambda e: e.reciprocal(out=lbm.v, in_=lbm.v), lbm.r(), lbm.r())
    dve(lambda e: e.tensor_tensor(out=lbe.v, in0=lbe.v,
                                  in1=lbm.v.unsqueeze(2).to_broadcast([128, 4, DEPTH]), op=ALU.mult),
        lbe.r() + lbm.r(), lbe.r())
    dve(lambda e: e.memset(lbv.flat, 0.0), [], lbv.r())
    for l in range(1, DEPTH):
        dve(lambda e, l=l: e.tensor_tensor(out=lbv.v[:, :, l], in0=lbv.v[:, :, l - 1], in1=lbe.v[:, :, l],
                                           op=ALU.add), lbv.r() + lbe.r(), lbv.r())
    dve(lambda e: e.tensor_scalar(out=oml.flat, in0=lbv.flat, scalar1=-1.0, scalar2=1.0,
                                  op0=ALU.mult, op1=ALU.add), lbv.r(), oml.r())

    def cast_layer(l):
        def mk(group):
            def fn(e, l=l, group=group):
                out = []

                def rows(dst, src, nrows, step=128):
                    for r0 in range(0, nrows, step):
                        out.append(e.dma_start(out=dst[r0:r0 + step, :], in_=src[r0:r0 + step, :]))
                if group == 0:
                    rows(s_in[l], d_win[l][:, 704:4800], 1024)
                elif group == 1:
                    rows(s_inC[l], d_winC[l], 1024)
                    rows(s_uq[l], d_wuq[l], 384)
                    rows(s_ukv[l], d_wukv[l], 256)
                    rows(s_pa[l], d_wpa[l], 1024)
                    rows(s_pb[l], d_wpb[l], 512)
                    rows(s_wo[l], d_wo[l], 1024)
                else:
                    rows(s_up[l], d_wup[l], 1024)
                    rows(s_dn[l], d_wdn[l], 2816)
                return out
            return fn
        for g, n in ((0, 8), (1, 8 + 3 + 2 + 8 + 4 + 8), (2, 8 + 22)):
            dma("pool", f"cast{l}_{g}", mk(g), [], [DR(("w", l, g))], n=n, lat=(80.0, 150.0, 250.0)[g])

    for l in range(DEPTH):
        cast_layer(l)

    wctr = [0]

    def load_piece(l, src_fn, nel, wg=1):
        i = wctr[0] % nws
        wctr[0] += 1
        slot = WS[i]

        def fn(e, slot=slot):
            return [e.dma_start(out=d, in_=s) for d, s in src_fn(slot.flat)]
        ncalls = len(src_fn(slot.flat))
        dma("sp", f"w{i}", fn, [DR(("w", l, wg))], slot.r(0, nel), n=ncalls, lat=3.0 + nel * 256 / 150e3)
        return slot

    def wview(src2d, kc, c0, c1):
        return src2d.rearrange("(k p) n -> p k n", p=128)[:, :, c0:c1]

    def piece_simple(l, src2d, kc, c0, c1):
        w = c1 - c0

        def src_fn(flat):
            return [(flat[:, 0:kc * w].rearrange("p (k n) -> p k n", n=w), wview(src2d, kc, c0, c1))]
        slot = load_piece(l, src_fn, kc * w, 0 if src2d.tensor.name == "s_in" else 1)
        return slot, slot.flat[:, 0:kc * w].rearrange("p (k n) -> p k n", n=w)

    bctr = [0]

    def next_bank(banks):
        b = banks[bctr[0] % len(banks)]
        bctr[0] += 1
        return b

    def rstd_from(buf, bank, c, N):
        act(lambda e: e.activation(out=buf.v[:, 0:N], in_=PSB[bank][:, 0:N], func=AF.Ln, bias=epsc[c], scale=1.0),
            [PSR[bank]] + epsb.r(), buf.r())
        act(lambda e: e.activation(out=buf.v[:, 0:N], in_=buf.v[:, 0:N], func=AF.Exp, scale=-0.5), buf.r(), buf.r())

    SCALE = 1.0 / math.sqrt(192.0)
    GC = 2.0 * 0.7978845608028654

    def rmsnorm_x(N, gs_col, out_fn, out_res_fn):
        for c in range(8):
            act(lambda e, c=c: e.activation(out=hT.v[:, c, 0:N], in_=xT.v[:, c, 0:N], func=AF.Square),
                xT.rs(c, 512), hT.rs(c, 512))
        b = 3
        mm_group(PSB[b][:, 0:N], [(onesb.v, hT.v[:, c, 0:N]) for c in range(8)], hT.r() + onesb.r(), [PSR[b]])
        rb = rstd[0]
        rstd_from(rb, b, 1024.0 * EPS, N)
        for c in range(8):
            dve(lambda e, c=c: e.scalar_tensor_tensor(out=out_fn(c), in0=xT.v[:, c, 0:N], scalar=gs_col(c),
                                                      in1=rb.v[:, 0:N], op0=ALU.mult, op1=ALU.mult),
                xT.rs(c, 512) + rb.r(), out_res_fn(c))

    import os as _os
    KSTOP = float(_os.environ.get("KSTOP", "99"))

    def tile_layer(l, N, kind, s=0, j=0):
        prompt = kind == "p"
        C = 64 if prompt else 32
        TB = 128 if prompt else 64
        NB = N // TB
        NCH = N // C
        cos_v = cs.v[:, 0, 0:N]
        sin_v = cs.v[:, 1, 0:N]
        mask_v = maskP_v if prompt else maskS_v[0:64, 0:64]
        reset_v = resetP_v if prompt else resetS_v
        dbanks = [0, 1, 2, 3]

        rmsnorm_x(N, lambda c: g1s.v[:, l, c:c + 1], lambda c: hT.v[:, c, 0:N], lambda c: hT.rs(c, 512))

        if KSTOP <= 1:
            return
        def dense(wv, kc_n, cols, rhs_fn, rhs_res, slot, banks=dbanks):
            b = next_bank(banks)
            mm_group(PSB[b][:, 0:N], [(wv[:, kc, cols[0]:cols[1]], rhs_fn(kc)) for kc in range(kc_n)],
                     rhs_res + slot.r(), [PSR[b]])
            return b

        hrhs = lambda kc: hT.v[:, kc, 0:N]

        slotA, wA = piece_simple(l, s_in[l], 8, 0, 1024)
        slotB, wB = piece_simple(l, s_in[l], 8, 1024, 2048)
        pool(lambda e: e.memset(kdTokE.v[C:TB], 0.0), [], kdTokE.r())
        pool(lambda e: e.memset(kdTokO.v[0:C], 0.0), [], kdTokO.r())
        if not prompt:
            dma("sp", "s0s", lambda e: [e.dma_start(out=S0s.v[:, q], in_=d_sh[l, q]) for q in range(2)],
                [], S0s.r(), n=2)
        def hgrn_head(h, T0, T1, T2, T3, T4, T5, T6):
            bq = dense(wA, 8, (h * 128, h * 128 + 128), hrhs, hT.r(), slotA)
            act(lambda e, bq=bq: e.activation(out=T0.v[:, 0:N], in_=PSB[bq][:, 0:N], func=AF.Silu),
                [PSR[bq]], T0.r())
            bf = dense(wA, 8, (512 + h * 128, 512 + h * 128 + 128), hrhs, hT.r(), slotA)
            act(lambda e, bf=bf: e.activation(out=T1.v[:, 0:N], in_=PSB[bf][:, 0:N], func=AF.Sigmoid),
                [PSR[bf]], T1.r())
            dve(lambda e, h=h: e.tensor_scalar(out=T1.v[:, 0:N], in0=T1.v[:, 0:N], scalar1=oml.v[:, h, l:l + 1],
                                               scalar2=lbv.v[:, h, l:l + 1], op0=ALU.mult, op1=ALU.add),
                T1.r() + oml.r() + lbv.r(), T1.r())
            act(lambda e: e.activation(out=T2.v[:, 0:N], in_=T1.v[:, 0:N], func=AF.Ln), T1.r(), T2.r())
            if KSTOP <= 1.1:
                return True
            dve(lambda e: e.tensor_tensor_scan(out=T3.v[:, 0:N], data0=reset_v, data1=T2.v[:, 0:N], initial=0.0,
                                               op0=ALU.mult, op1=ALU.add), T2.r() + constf.r(), T3.r())
            dve(lambda e: e.tensor_scalar(out=T1.v[:, 0:N], in0=T1.v[:, 0:N], scalar1=-1.0, scalar2=1.0,
                                          op0=ALU.mult, op1=ALU.add), T1.r(), T1.r())
            if KSTOP <= 1.2:
                return True
            b3 = T3.v[:, 0:N].rearrange("p (c t) -> p c t", t=C)
            rmid = b3[:, :, C // 2 - 1:C // 2].to_broadcast([128, NCH, C])
            rend = b3[:, :, C - 1:C].to_broadcast([128, NCH, C])
            dve(lambda e: e.tensor_tensor(out=T2.v[:, 0:N].rearrange("p (c t) -> p c t", t=C), in0=b3, in1=rmid,
                                          op=ALU.subtract), T3.r(), T2.r())
            dve(lambda e: e.tensor_tensor(out=T4.v[:, 0:N].rearrange("p (c t) -> p c t", t=C), in0=b3, in1=rend,
                                          op=ALU.subtract), T3.r(), T4.r())
            act(lambda e, h=h: e.activation(out=dd.v[:, h, 0:NCH].unsqueeze(2), in_=b3[:, :, C - 1:C], func=AF.Exp),
                T3.r(), dd.r())
            if KSTOP <= 1.3:
                return True
            hb = h % 2
            act(lambda e: e.activation(out=T5.v[:, 0:N], in_=T3.v[:, 0:N], func=AF.Exp), T3.r(), T5.r())
            dve(lambda e, h=h: e.scalar_tensor_tensor(out=qbb.v[:, h, 0:N], in0=T0.v[:, 0:N], scalar=128.0 ** -0.5,
                                                      in1=T5.v[:, 0:N], op0=ALU.mult, op1=ALU.mult),
                T0.r() + T5.r(), qbb.rs(h, 512))
            act(lambda e: e.activation(out=T6.v[:, 0:N], in_=T2.v[:, 0:N], func=AF.Exp), T2.r(), T6.r())
            dve(lambda e, hb=hb: e.scalar_tensor_tensor(out=qeb.v[:, hb, 0:N], in0=T0.v[:, 0:N], scalar=128.0 ** -0.5,
                                                        in1=T6.v[:, 0:N], op0=ALU.mult, op1=ALU.mult),
                T0.r() + T6.r(), qeb.rs(hb, 512))
            act(lambda e: e.activation(out=T5.v[:, 0:N], in_=T2.v[:, 0:N], func=AF.Exp, scale=-1.0), T2.r(), T5.r())
            dve(lambda e, hb=hb: e.tensor_tensor(out=keb.v[:, hb, 0:N], in0=T1.v[:, 0:N], in1=T5.v[:, 0:N],
                                                 op=ALU.mult), T1.r() + T5.r(), keb.rs(hb, 512))
            act(lambda e: e.activation(out=T6.v[:, 0:N], in_=T4.v[:, 0:N], func=AF.Exp, scale=-1.0), T4.r(), T6.r())
            dve(lambda e, hb=hb: e.tensor_tensor(out=kdb.v[:, hb, 0:N], in0=T1.v[:, 0:N], in1=T6.v[:, 0:N],
                                                 op=ALU.mult), T1.r() + T6.r(), kdb.rs(hb, 512))
            if KSTOP <= 1.4:
                return True
            if h == 0:
                for blk in range(NB):
                    b = next_bank(dbanks)
                    mm_group(PSB[b][0:TB, 0:512],
                             [(hT.v[:, kc, blk * TB:(blk + 1) * TB], wB[:, kc, 0:512]) for kc in range(8)],
                             hT.r() + slotB.r(), [PSR[b]])
                    act(lambda e, b=b, blk=blk: e.activation(out=Vtok.v[0:TB, blk, :], in_=PSB[b][0:TB, 0:512],
                                                             func=AF.Copy), [PSR[b]], Vtok.rs(blk, 512))
            if KSTOP <= 1.5:
                return True
            psb7 = PSB[7][:, :].bitcast(BF16)
            for blk in range(NB):
                pe(lambda e, blk=blk, hb=hb: e.transpose(psb7[0:TB, blk * 128:(blk + 1) * 128],
                                                         kdb.v[:, hb, blk * TB:(blk + 1) * TB], identb.v),
                   kdb.rs(hb, 512) + identb.r(), [PSR[7]])
            act(lambda e, h=h: e.activation(out=kdTokE.v[0:C, h % 2, 0:NB, :],
                                            in_=psb7[0:C, 0:NB * 128].rearrange("p (b k) -> p b k", k=128),
                                            func=AF.Copy), [PSR[7]], kdTokE.rs(h % 2, 512))
            act(lambda e, h=h: e.activation(out=kdTokO.v[C:TB, h % 2, 0:NB, :],
                                            in_=psb7[C:TB, 0:NB * 128].rearrange("p (b k) -> p b k", k=128),
                                            func=AF.Copy), [PSR[7]], kdTokO.rs(h % 2, 512))
            if KSTOP <= 1.6:
                return True
            for blk in range(NB):
                mm_group(PSB[6][0:TB, blk * 128:blk * 128 + TB],
                         [(keb.v[:, hb, blk * TB:(blk + 1) * TB], qeb.v[:, hb, blk * TB:(blk + 1) * TB])],
                         keb.rs(hb, 512) + qeb.rs(hb, 512), [PSR[6]])
            dve(lambda e, h=h: e.tensor_tensor(
                out=Am.v[0:TB, h, 0:NB, 0:TB],
                in0=PSB[6][0:TB, 0:NB * 128].rearrange("p (b t) -> p b t", t=128)[:, :, 0:TB],
                in1=mask_v.unsqueeze(1).to_broadcast([TB, NB, TB]), op=ALU.mult),
                [PSR[6]] + constf.r(), Am.rs(h, 512))
            if KSTOP <= 1.7:
                return True
            for c in range(NCH):
                blk = (c * C) // TB
                r0 = (c * C) % TB
                ub_ = 4 + c // 4
                kdX = kdTokE if r0 == 0 else kdTokO
                mm_group(PSB[ub_][:, (c % 4) * 128:(c % 4) * 128 + 128],
                         [(kdX.v[0:TB, h % 2, blk, :], Vtok.v[0:TB, blk, h * 128:(h + 1) * 128])],
                         kdX.rs(h % 2, 512) + Vtok.rs(blk, 512), [PSR[ub_]])
            if KSTOP <= 1.8:
                return True
            if prompt:
                Sh = Sst.v[:, l, h, :]
                Shr = Sst.r((l * 4 + h) * 128, (l * 4 + h + 1) * 128)
                if j == 0:
                    dve(lambda e, Sh=Sh: e.memset(Sh, 0.0), [], Shr)
                act(lambda e, h=h, Sh=Sh: e.activation(out=Sb.v[:, h, 0, :], in_=Sh, func=AF.Copy),
                    Shr, Sb.rs(h, 1024))
                for c in range(NCH):
                    ub_ = 4 + c // 4
                    dve(lambda e, h=h, c=c, ub_=ub_, Sh=Sh: e.scalar_tensor_tensor(
                        out=Sh, in0=Sh, scalar=dd.v[:, h, c:c + 1],
                        in1=PSB[ub_][:, (c % 4) * 128:(c % 4) * 128 + 128], op0=ALU.mult, op1=ALU.add),
                        Shr + dd.r() + [PSR[ub_]], Shr)
                    if c + 1 < NCH:
                        act(lambda e, h=h, c=c, Sh=Sh: e.activation(out=Sb.v[:, h, c + 1, :], in_=Sh, func=AF.Copy),
                            Shr, Sb.rs(h, 1024))
                if j == NT - 1:
                    dma("pool", f"o_hsp{l}_{h}", lambda e, h=h, Sh=Sh: e.dma_start(out=o_hsp[l, s][:, h, :], in_=Sh),
                        Shr, [])
            else:
                for q in range(2):
                    Sq = S0s.v[:, q, h, :]
                    Sqr = S0s.r()
                    act(lambda e, h=h, q=q, Sq=Sq: e.activation(out=Sb.v[:, h, q, :], in_=Sq, func=AF.Copy),
                        Sqr, Sb.rs(h, 1024))
                    dve(lambda e, h=h, q=q, Sq=Sq: e.scalar_tensor_tensor(
                        out=Sq, in0=Sq, scalar=dd.v[:, h, q:q + 1], in1=PSB[4][:, q * 128:q * 128 + 128],
                        op0=ALU.mult, op1=ALU.add), Sqr + dd.r() + [PSR[4]], Sqr)
            if KSTOP <= 1.9:
                return True
            bg = dense(wB, 8, (512 + h * 128, 512 + h * 128 + 128), hrhs, hT.r(), slotB)
            act(lambda e, bg=bg, h=h: e.activation(out=sghg.v[:, h, 0:N], in_=PSB[bg][:, 0:N], func=AF.Silu),
                [PSR[bg]], sghg.rs(h, 512))
        for h in range(4):
            if hgrn_head(h, *TT2[h % 2]):
                return
        if KSTOP <= 2:
            return
        if not prompt:
            dma("pool", "o_hss", lambda e: [e.dma_start(out=o_hss[l, q], in_=S0s.v[:, q]) for q in range(2)],
                S0s.r(), [], n=2)

        slotD, wD = piece_simple(l, s_in[l], 8, 2048, 3072)
        for oc in range(8):
            b = dense(wD, 8, (oc * 128, oc * 128 + 128), hrhs, hT.r(), slotD)
            act(lambda e, b=b, oc=oc: e.activation(out=gates.v[:, oc, 0:N], in_=PSB[b][:, 0:N], func=AF.Sigmoid),
                [PSR[b]], gates.rs(oc, 512))
        slotE, wE = piece_simple(l, s_in[l], 8, 3072, 4096)
        for oc in range(8):
            b = dense(wE, 8, (oc * 128, oc * 128 + 128), hrhs, hT.r(), slotE)
            act(lambda e, b=b, oc=oc: e.activation(out=gates.v[:, 8 + oc, 0:N], in_=PSB[b][:, 0:N], func=AF.Sigmoid),
                [PSR[b]], gates.rs(8 + oc, 512))

        if KSTOP <= 3:
            return
        for h in range(4):
            ob_ = 2
            for blk in range(NB):
                def fn(e, h=h, blk=blk):
                    ins = e.matmul(PSB[ob_][:, blk * TB:(blk + 1) * TB], Vtok.v[0:TB, blk, h * 128:(h + 1) * 128],
                                   Am.v[0:TB, h, blk, 0:TB], start=(blk == 0), stop=False, skip_group_check=True)
                    ncb = TB // C
                    for ci in range(ncb):
                        c = blk * ncb + ci
                        ins = e.matmul(PSB[ob_][:, c * C:(c + 1) * C], Sb.v[:, h, c, :], qbb.v[:, h, c * C:(c + 1) * C],
                                       start=False, stop=(blk == NB - 1 and ci == ncb - 1), skip_group_check=True)
                    return ins
                pe(fn, Vtok.rs(blk, 512) + Am.rs(h, 512) + Sb.rs(h, 1024) + qbb.rs(h, 512), [PSR[ob_]])
            act(lambda e: e.activation(out=sqo.v[:, 0:N], in_=PSB[ob_][:, 0:N], func=AF.Square), [PSR[ob_]], sqo.r())
            mm_group(PSB[3][:, 0:N], [(onesb.v, sqo.v[:, 0:N])], sqo.r() + onesb.r(), [PSR[3]])
            rb = rstd[1]
            rstd_from(rb, 3, 128.0 * EPS, N)
            dve(lambda e: e.tensor_tensor(out=to.v[:, 0:N], in0=PSB[ob_][:, 0:N], in1=rb.v[:, 0:N], op=ALU.mult),
                [PSR[ob_]] + rb.r(), to.r())
            dve(lambda e, h=h: e.scalar_tensor_tensor(out=obT.v[:, h, 0:N], in0=to.v[:, 0:N], scalar=ghs.v[:, l:l + 1],
                                                      in1=sghg.v[:, h, 0:N], op0=ALU.mult, op1=ALU.mult),
                to.r() + ghs.r() + sghg.rs(h, 512), obT.rs(h, 512))

        if KSTOP <= 4:
            return
        slotC, wC = piece_simple(l, s_inC[l], 8, 0, 896)
        for c in range(3):
            b = dense(wC, 8, (c * 128, c * 128 + 128), hrhs, hT.r(), slotC)
            act(lambda e, b=b, c=c: e.activation(out=cqf.v[:, c, 0:N], in_=PSB[b][:, 0:N], func=AF.Copy),
                [PSR[b]], cqf.rs(c, 512))
            act(lambda e, b=b, c=c: e.activation(out=sqm.v[:, c, 0:N], in_=PSB[b][:, 0:N], func=AF.Square),
                [PSR[b]], sqm.rs(c, 512))
        mm_group(PSB[3][:, 0:N], [(onesb.v, sqm.v[:, c, 0:N]) for c in range(3)], sqm.r() + onesb.r(), [PSR[3]])
        rq = rstd[0]
        rstd_from(rq, 3, 384.0 * EPS, N)
        for c in range(3):
            dve(lambda e, c=c: e.scalar_tensor_tensor(out=cqn.v[:, c, 0:N], in0=cqf.v[:, c, 0:N],
                                                      scalar=gqs.v[:, l, c:c + 1], in1=rq.v[:, 0:N],
                                                      op0=ALU.mult, op1=ALU.mult),
                cqf.rs(c, 512) + rq.r() + gqs.r(), cqn.rs(c, 512))
        for c in range(2):
            b = dense(wC, 8, (384 + c * 128, 384 + c * 128 + 128), hrhs, hT.r(), slotC)
            act(lambda e, b=b, c=c: e.activation(out=ckvf.v[:, c, 0:N], in_=PSB[b][:, 0:N], func=AF.Copy),
                [PSR[b]], ckvf.rs(c, 512))
            act(lambda e, b=b, c=c: e.activation(out=sqm.v[:, c, 0:N], in_=PSB[b][:, 0:N], func=AF.Square),
                [PSR[b]], sqm.rs(c, 512))
        mm_group(PSB[3][:, 0:N], [(onesb.v, sqm.v[:, c, 0:N]) for c in range(2)], sqm.r() + onesb.r(), [PSR[3]])
        rk = rstd[1]
        rstd_from(rk, 3, 256.0 * EPS, N)
        for c in range(2):
            dve(lambda e, c=c: e.scalar_tensor_tensor(out=ckvf.v[:, c, 0:N], in0=ckvf.v[:, c, 0:N],
                                                      scalar=gkvs.v[:, l, c:c + 1], in1=rk.v[:, 0:N],
                                                      op0=ALU.mult, op1=ALU.mult),
                ckvf.rs(c, 512) + rk.r() + gkvs.r(), ckvf.rs(c, 512))
            act(lambda e, c=c: e.activation(out=latb.v[:, c, 0:N], in_=ckvf.v[:, c, 0:N], func=AF.Copy),
                ckvf.rs(c, 512), latb.rs(c, 512))
        if prompt:
            dma("pool", "o_lat", lambda e: e.dma_start(
                out=o_latp[l, s].rearrange("c p t -> p c t")[:, :, j * 512:(j + 1) * 512], in_=ckvf.v),
                ckvf.r(), [])
        else:
            dma("pool", "o_lat", lambda e: e.dma_start(out=o_lats[l].rearrange("c p t -> p c t"),
                                                        in_=ckvf.v[:, :, 0:64]), ckvf.r(), [])
        bk = dense(wC, 8, (640, 768), hrhs, hT.r(), slotC)
        dve(lambda e: e.tensor_tensor(out=tmp1.v[:, 0:N], in0=PSB[bk][:, 0:N], in1=cos_v, op=ALU.mult),
            [PSR[bk]] + cs.r(), tmp1.r())
        bkp = dense(wC, 8, (768, 896), hrhs, hT.r(), slotC)
        dve(lambda e: e.tensor_tensor(out=tmp2.v[:, 0:N], in0=PSB[bkp][:, 0:N], in1=sin_v, op=ALU.mult),
            [PSR[bkp]] + cs.r(), tmp2.r())
        dve(lambda e: e.tensor_tensor(out=kpef.v[:, 0:N], in0=tmp1.v[:, 0:N], in1=tmp2.v[:, 0:N], op=ALU.add),
            tmp1.r() + tmp2.r(), kpef.r())
        act(lambda e: e.activation(out=Pcur.v[:, 0:N], in_=kpef.v[:, 0:N], func=AF.Copy), kpef.r(), Pcur.r())
        if prompt:
            dma("pool", "o_kpe", lambda e: e.dma_start(out=o_kpep[l, s][:, j * 512:(j + 1) * 512],
                                                        in_=kpef.v[0:64, :]), kpef.r(), [])
        else:
            dma("pool", "o_kpe", lambda e: e.dma_start(out=o_kpes[l], in_=kpef.v[0:64, 0:64]), kpef.r(), [])
        if KSTOP <= 5:
            return
        Qp4 = Qp.v.rearrange("p (a b) n -> p a b n", b=2)
        pool(lambda e: e.memset(Qp4[64:128, :, 0, :], 0.0), [], Qp.r())
        pool(lambda e: e.memset(Qp4[0:64, :, 1, :], 0.0), [], Qp.r())
        slotU, wU = piece_simple(l, s_uq[l], 3, 0, 2048)
        qrhs = lambda kc: cqn.v[:, kc, 0:N]
        for h in range(8):
            b = dense(wU, 3, (h * 128, h * 128 + 128), qrhs, cqn.r(), slotU)
            act(lambda e, b=b, h=h: e.activation(out=Qn.v[:, h, 0:N], in_=PSB[b][:, 0:N], func=AF.Copy),
                [PSR[b]], Qn.rs(h, 512))
        for pr in range(4):
            b1 = dense(wU, 3, (1024 + pr * 128, 1024 + pr * 128 + 128), qrhs, cqn.r(), slotU)
            dve(lambda e, b1=b1: e.tensor_tensor(out=tmp1.v[:, 0:N], in0=PSB[b1][:, 0:N], in1=cos_v, op=ALU.mult),
                [PSR[b1]] + cs.r(), tmp1.r())
            b2 = dense(wU, 3, (1536 + pr * 128, 1536 + pr * 128 + 128), qrhs, cqn.r(), slotU)
            dve(lambda e, b2=b2: e.tensor_tensor(out=tmp2.v[:, 0:N], in0=PSB[b2][:, 0:N], in1=sin_v, op=ALU.mult),
                [PSR[b2]] + cs.r(), tmp2.r())
            dve(lambda e, pr=pr: e.tensor_tensor(out=Qp.v[0:64, 2 * pr, 0:N], in0=tmp1.v[0:64, 0:N],
                                                 in1=tmp2.v[0:64, 0:N], op=ALU.add),
                tmp1.r() + tmp2.r(), Qp.rs(2 * pr, 512))
            dve(lambda e, pr=pr: e.tensor_tensor(out=Qp.v[64:128, 2 * pr + 1, 0:N], in0=tmp1.v[64:128, 0:N],
                                                 in1=tmp2.v[64:128, 0:N], op=ALU.add),
                tmp1.r() + tmp2.r(), Qp.rs(2 * pr + 1, 512))
        if KSTOP <= 6:
            return
        slotK, wK = piece_simple(l, s_ukv[l], 2, 0, 2048)
        lrhs = lambda kc: latb.v[:, kc, 0:N]
        if prompt:
            for h in range(8):
                b = dense(wK, 2, (h * 128, h * 128 + 128), lrhs, latb.r(), slotK)
                act(lambda e, b=b, h=h: e.activation(out=Kcur.v[:, :, h, :],
                                                     in_=PSB[b][:, 0:512].rearrange("p (b k) -> p b k", k=128),
                                                     func=AF.Copy), [PSR[b]], Kcur.r())
            for blk in range(4):
                for half in range(2):
                    b = next_bank(dbanks)
                    mm_group(PSB[b][:, 0:512],
                             [(latb.v[:, kc, blk * 128:(blk + 1) * 128],
                               wK[:, kc, 1024 + half * 512:1024 + half * 512 + 512]) for kc in range(2)],
                             latb.r() + slotK.r(), [PSR[b]])
                    act(lambda e, b=b, blk=blk, half=half: e.activation(
                        out=Vcur.v[:, blk, half * 512:(half + 1) * 512], in_=PSB[b][:, 0:512], func=AF.Copy),
                        [PSR[b]], Vcur.rs(blk, 1024))
            if j < NT - 1:
                dma("pool", "kcw", lambda e: e.dma_start(out=s_kc[s, l][:, j * 4:(j + 1) * 4], in_=Kcur.v),
                    Kcur.r(), [DR(("kc", s, l, j))])
                dma("pool", "vcw", lambda e: e.dma_start(
                    out=s_vc[s, l][j * 4:(j + 1) * 4].rearrange("b k c -> k b c"), in_=Vcur.v),
                    Vcur.r(), [DR(("vc", s, l, j))])
                dma("pool", "pcw", lambda e: e.dma_start(out=s_pc[s, l][:, j * 512:(j + 1) * 512], in_=Pcur.v),
                    Pcur.r(), [DR(("pc", s, l, j))])
            if KSTOP <= 7:
                return
            slotPA, wPA = piece_simple(l, s_pa[l], 8, 0, 1024)
            slotPB, wPB = piece_simple(l, s_pb[l], 4, 0, 1024)
            attention_prompt(l, s, j)
        else:
            slotPA, wPA = piece_simple(l, s_pa[l], 8, 0, 1024)
            attention_sample(l, wK, slotK)
            slotPB, wPB = piece_simple(l, s_pb[l], 4, 0, 1024)

        if KSTOP <= 8:
            return
        mbanks = [[0, 1], [2, 3], [4, 5], [6, 7]]
        for oc in range(8):
            bx, by = mbanks[oc % 4]
            mm_group(PSB[bx][:, 0:N], [(wPA[:, kc, oc * 128:(oc + 1) * 128], oaT.v[:, kc, 0:N]) for kc in range(8)],
                     oaT.r() + slotPA.r(), [PSR[bx]])
            mm_group(PSB[by][:, 0:N], [(wPB[:, kc, oc * 128:(oc + 1) * 128], obT.v[:, kc, 0:N]) for kc in range(4)],
                     obT.r() + slotPB.r(), [PSR[by]])
            ta = mta[oc % 2]
            tb = mtb[oc % 2]
            dve(lambda e, bx=bx, oc=oc, ta=ta: e.tensor_tensor(out=ta.v[:, 0:N], in0=PSB[bx][:, 0:N],
                                                               in1=gates.v[:, oc, 0:N], op=ALU.mult),
                [PSR[bx]] + gates.rs(oc, 512), ta.r())
            dve(lambda e, by=by, oc=oc, tb=tb: e.tensor_tensor(out=tb.v[:, 0:N], in0=PSB[by][:, 0:N],
                                                               in1=gates.v[:, 8 + oc, 0:N], op=ALU.mult),
                [PSR[by]] + gates.rs(8 + oc, 512), tb.r())
            dve(lambda e, oc=oc, ta=ta, tb=tb: e.tensor_tensor(out=hT.v[:, oc, 0:N], in0=ta.v[:, 0:N],
                                                               in1=tb.v[:, 0:N], op=ALU.add),
                ta.r() + tb.r(), hT.rs(oc, 512))
        slotO, wO = piece_simple(l, s_wo[l], 8, 0, 1024)
        for oc in range(8):
            b = dense(wO, 8, (oc * 128, oc * 128 + 128), hrhs, hT.r(), slotO, banks=[0, 1, 2, 3])
            dve(lambda e, b=b, oc=oc: e.tensor_tensor(out=xT.v[:, oc, 0:N], in0=xT.v[:, oc, 0:N], in1=PSB[b][:, 0:N],
                                                      op=ALU.add), [PSR[b]] + xT.rs(oc, 512), xT.rs(oc, 512))

        if KSTOP <= 9:
            return
        rmsnorm_x(N, lambda c: g2s.v[:, l, c:c + 1], lambda c: hT.v[:, c, 0:N], lambda c: hT.rs(c, 512))
        nseg = 1 if prompt else 2
        L = N // nseg
        cwv = pv("cw").rearrange("p (l j c) -> p l j c", j=3, c=22)
        cbv = pv("cb").rearrange("p (l c) -> p l c", c=22)
        fb = [[0, 1], [2, 3], [4, 5], [6, 7]]
        for q in range(6):
            ncol = 512 if q < 5 else 256

            def src_fn(flat, q=q, ncol=ncol):
                v = flat[:, 0:8 * 2 * ncol].rearrange("p (k n) -> p k n", n=2 * ncol)
                return [(v[:, :, 0:ncol], wview(s_up[l], 8, q * 512, q * 512 + ncol)),
                        (v[:, :, ncol:2 * ncol], wview(s_up[l], 8, 2816 + q * 512, 2816 + q * 512 + ncol))]
            slotP = load_piece(l, src_fn, 8 * 2 * ncol, 2)
            wP = slotP.flat[:, 0:8 * 2 * ncol].rearrange("p (k n) -> p k n", n=2 * ncol)
            for jj in range(ncol // 128):
                jc = q * 4 + jj
                bx, by = fb[jc % 4]
                mm_group(PSB[bx][:, 0:N], [(wP[:, kc, jj * 128:(jj + 1) * 128], hT.v[:, kc, 0:N]) for kc in range(8)],
                         hT.r() + slotP.r(), [PSR[bx]])
                mm_group(PSB[by][:, 0:N],
                         [(wP[:, kc, ncol + jj * 128:ncol + (jj + 1) * 128], hT.v[:, kc, 0:N]) for kc in range(8)],
                         hT.r() + slotP.r(), [PSR[by]])
                ae = aext[jc % 4]
                cv = cvb[jc % 4]
                uu = ub[jc % 4]
                ss_ = sbf[jc % 4]
                ae3 = ae.v[:, 0:nseg * (L + 2)].rearrange("p (s t) -> p s t", t=L + 2)
                as3 = lambda ap_: ap_[:, 0:N].rearrange("p (s t) -> p s t", t=L)
                act(lambda e, bx=bx, ae3=ae3: e.activation(out=ae3[:, :, 2:L + 2], in_=as3(PSB[bx]), func=AF.Copy),
                    [PSR[bx]], ae.r())
                if prompt:
                    halo_src = ctail.v[:, l, jc, :].unsqueeze(1)
                    halo_res = ctail.r()
                else:
                    halo_src = shalo.v[:, :, jc, :]
                    halo_res = shalo.r()
                if prompt and j == 0:
                    pool(lambda e, ae3=ae3: e.memset(ae3[:, :, 0:2], 0.0), [], ae.r())
                else:
                    pool(lambda e, ae3=ae3, halo_src=halo_src: e.tensor_copy(out=ae3[:, :, 0:2], in_=halo_src),
                         halo_res, ae.r())
                if prompt:
                    pool(lambda e, ae3=ae3, jc=jc: e.tensor_copy(out=ctail.v[:, l, jc, :].unsqueeze(1),
                                                                in_=ae3[:, :, L:L + 2]), ae.r(), ctail.r())
                else:
                    pool(lambda e, ae3=ae3, jc=jc: e.tensor_copy(out=stail.v[:, :, jc, :], in_=ae3[:, :, L:L + 2]),
                         ae.r(), stail.r())
                act(lambda e, ae3=ae3, cv=cv, jc=jc: e.activation(out=as3(cv.v), in_=ae3[:, :, 2:L + 2], func=AF.Identity,
                                                                 scale=cwv[:, l, 2, jc:jc + 1], bias=cbv[:, l, jc:jc + 1]),
                    ae.r() + par.r(), cv.r())
                dve(lambda e, ae3=ae3, cv=cv, jc=jc: e.scalar_tensor_tensor(
                    out=as3(cv.v), in0=ae3[:, :, 1:L + 1], scalar=cwv[:, l, 1, jc:jc + 1], in1=as3(cv.v),
                    op0=ALU.mult, op1=ALU.add), ae.r() + cv.r() + par.r(), cv.r())
                dve(lambda e, ae3=ae3, cv=cv, jc=jc: e.scalar_tensor_tensor(
                    out=as3(cv.v), in0=ae3[:, :, 0:L], scalar=cwv[:, l, 0, jc:jc + 1], in1=as3(cv.v),
                    op0=ALU.mult, op1=ALU.add), ae.r() + cv.r() + par.r(), cv.r())
                act(lambda e, cv=cv, uu=uu: e.activation(out=uu.v[:, 0:N], in_=cv.v[:, 0:N], func=AF.Square,
                                                         scale=math.sqrt(0.044715)), cv.r(), uu.r())
                dve(lambda e, cv=cv, uu=uu: e.scalar_tensor_tensor(out=uu.v[:, 0:N], in0=uu.v[:, 0:N], scalar=1.0,
                                                                   in1=cv.v[:, 0:N], op0=ALU.add, op1=ALU.mult),
                    cv.r() + uu.r(), uu.r())
                act(lambda e, uu=uu, ss_=ss_: e.activation(out=ss_.v[:, 0:N], in_=uu.v[:, 0:N], func=AF.Sigmoid,
                                                           scale=GC), uu.r(), ss_.r())
                pool(lambda e, cv=cv, ss_=ss_: e.tensor_tensor(out=ss_.v[:, 0:N], in0=ss_.v[:, 0:N], in1=cv.v[:, 0:N],
                                                               op=ALU.mult), cv.r() + ss_.r(), ss_.r())
                dve(lambda e, by=by, ss_=ss_, jc=jc: e.tensor_tensor(out=gT.v[:, jc, 0:N], in0=ss_.v[:, 0:N],
                                                                     in1=PSB[by][:, 0:N], op=ALU.mult),
                    [PSR[by]] + ss_.r(), gT.rs(jc, 512))
        if prompt and j == NT - 1:
            dma("pool", f"o_cvp{l}", lambda e: e.dma_start(out=o_cvp[l, s], in_=ctail.v[:, l]), ctail.r(), [])
        if not prompt:
            dma("pool", "o_cvs", lambda e: [e.dma_start(out=o_cvs[l, q], in_=stail.v[:, q]) for q in range(2)],
                stail.r(), [], n=2)
        for q in range(4):
            def src_fn(flat, q=q):
                return [(flat[:, 0:22 * 256].rearrange("p (k n) -> p k n", n=256),
                         wview(s_dn[l], 22, q * 256, (q + 1) * 256))]
            slotDn = load_piece(l, src_fn, 22 * 256, 2)
            wDn = slotDn.flat[:, 0:22 * 256].rearrange("p (k n) -> p k n", n=256)
            for o2 in range(2):
                oc = q * 2 + o2
                b = next_bank([0, 1, 2, 3])
                mm_group(PSB[b][:, 0:N], [(wDn[:, kc, o2 * 128:(o2 + 1) * 128], gT.v[:, kc, 0:N]) for kc in range(22)],
                         gT.r() + slotDn.r(), [PSR[b]])
                dve(lambda e, b=b, oc=oc: e.tensor_tensor(out=xT.v[:, oc, 0:N], in0=xT.v[:, oc, 0:N],
                                                          in1=PSB[b][:, 0:N], op=ALU.add),
                    [PSR[b]] + xT.rs(oc, 512), xT.rs(oc, 512))

    kvctr = [0]

    def attention_prompt(l, s, j):
        N = 512
        for p in range(2):
            heads = range(4 * p, 4 * p + 4)
            first = {h: True for h in heads}
            for g in range(j + 1):
                diag = g == j
                if not diag:
                    si = kvctr[0] % 2
                    kvctr[0] += 1
                    Kb, Vb, Pb = KS[si]

                    def fn(e, g=g, p=p, Kb=Kb, Vb=Vb, Pb=Pb):
                        return [
                            e.dma_start(out=Kb.v, in_=s_kc[s, l][:, g * 4:(g + 1) * 4, 4 * p:4 * p + 4, :]),
                            e.dma_start(out=Vb.v, in_=s_vc[s, l][g * 4:(g + 1) * 4, :, p * 512:(p + 1) * 512]
                                        .rearrange("b k c -> k b c")),
                            e.dma_start(out=Pb.v, in_=s_pc[s, l][:, g * 512:(g + 1) * 512]),
                        ]
                    dma("sp", f"kv{si}", fn, [DR(("kc", s, l, g)), DR(("vc", s, l, g)), DR(("pc", s, l, g))],
                        Kb.r() + Vb.r() + Pb.r(), n=3, lat=10.0)
                    kfn = lambda h, kb, Kb=Kb, p=p: Kb.v[:, kb, h - 4 * p, :]
                    vfn = lambda h, kb, Vb=Vb, p=p: Vb.v[:, kb, (h - 4 * p) * 128:(h - 4 * p + 1) * 128]
                    pfn = lambda hp, kb, Pb=Pb: Pb.v[:, kb * 128:(kb + 1) * 128]
                    kvres = Kb.r() + Vb.r() + Pb.r()
                else:
                    kfn = lambda h, kb: Kcur.v[:, kb, h, :]
                    vfn = lambda h, kb: Vcur.v[:, kb, h * 128:(h + 1) * 128]
                    pfn = lambda hp, kb: Pcur.v[:, kb * 128:(kb + 1) * 128]
                    kvres = Kcur.r() + Vcur.r() + Pcur.r()
                for h in heads:
                    hl = h - 4 * p
                    ob = 4 + hl
                    hp = h % 2
                    for kb in range(4):
                        q0 = kb * 128 if diag else 0
                        sb_ = bctr[0] % 4
                        bctr[0] += 1
                        pt = Pt[bctr[0] % 8]
                        last = (g == j and kb == 3)

                        def fsc(e, h=h, kb=kb, q0=q0, sb_=sb_, hp=hp, kfn=kfn, pfn=pfn, diag=diag):
                            e.matmul(PSB[sb_][:, q0:N], kfn(h, kb), Qn.v[:, h, q0:N], start=True, stop=False)
                            if diag:
                                e.matmul(PSB[sb_][:, q0:N], mUb.v, mVb.v[:, 0:N - q0], start=False, stop=False)
                            return e.matmul(PSB[sb_][:, q0:N], pfn(hp, kb), Qp.v[:, h, q0:N],
                                            start=False, stop=True)
                        pe(fsc, kvres + Qn.rs(h, 512) + Qp.rs(h, 512), [PSR[sb_]])
                        act(lambda e, sb_=sb_, pt=pt, q0=q0: e.activation(out=pt.v[:, q0:N], in_=PSB[sb_][:, q0:N],
                                                                          func=AF.Exp, scale=SCALE),
                            [PSR[sb_]], pt.r())
                        st = first[h]

                        def fpv(e, h=h, kb=kb, q0=q0, pt=pt, ob=ob, st=st, last=last, vfn=vfn):
                            return e.matmul(PSB[ob][:, q0:N], vfn(h, kb), pt.v[:, q0:N], start=st, stop=last,
                                            skip_group_check=True)
                        pe(fpv, kvres + pt.r(), [PSR[ob]])
                        if st:
                            dve(lambda e, pt=pt, hl=hl: e.tensor_copy(out=Pacc.v[:, hl, :], in_=pt.v),
                                pt.r(), Pacc.rs(hl, 512))
                        else:
                            dve(lambda e, pt=pt, hl=hl, q0=q0: e.tensor_tensor(out=Pacc.v[:, hl, q0:N],
                                                                             in0=Pacc.v[:, hl, q0:N],
                                                                             in1=pt.v[:, q0:N], op=ALU.add),
                                pt.r() + Pacc.rs(hl, 512), Pacc.rs(hl, 512))
                        first[h] = False
            for h in heads:
                hl = h - 4 * p
                ob = 4 + hl
                sb_ = 2 + (bctr[0] % 2)
                bctr[0] += 1
                mm_group(PSB[sb_][:, 0:N], [(onesf.v, Pacc.v[:, hl, :])], Pacc.rs(hl, 512) + onesf.r(), [PSR[sb_]])
                act(lambda e, sb_=sb_: e.activation(out=rsb.v, in_=PSB[sb_][:, 0:N], func=AF.Ln), [PSR[sb_]], rsb.r())
                act(lambda e: e.activation(out=rsb.v, in_=rsb.v, func=AF.Exp, scale=-1.0), rsb.r(), rsb.r())
                dve(lambda e, h=h, ob=ob: e.tensor_tensor(out=oaT.v[:, h, :], in0=PSB[ob][:, 0:N], in1=rsb.v,
                                                          op=ALU.mult), [PSR[ob]] + rsb.r(), oaT.rs(h, 512))

    def attention_sample(l, wK, slotK):
        N = 64
        Knew = Kcur.flat[:, 0:8 * 64].rearrange("p (h t) -> p h t", t=64)
        for h in range(8):
            b = next_bank([0, 1])
            mm_group(PSB[b][:, 0:N], [(wK[:, kc, h * 128:(h + 1) * 128], latb.v[:, kc, 0:N]) for kc in range(2)],
                     latb.r() + slotK.r(), [PSR[b]])
            act(lambda e, b=b, h=h: e.activation(out=Knew[:, h, :], in_=PSB[b][:, 0:N], func=AF.Copy),
                [PSR[b]], Kcur.r())
        for q in range(2):
            for half in range(2):
                b = next_bank([0, 1])
                mm_group(PSB[b][0:32, 0:512],
                         [(latb.v[:, kc, q * 32:(q + 1) * 32], wK[:, kc, 1024 + half * 512:1024 + half * 512 + 512])
                          for kc in range(2)], latb.r() + slotK.r(), [PSR[b]])
                act(lambda e, b=b, q=q, half=half: e.activation(out=Vnew.v[0:32, q, half * 512:(half + 1) * 512],
                                                                in_=PSB[b][0:32, 0:512], func=AF.Copy),
                    [PSR[b]], Vnew.rs(q, 1024))
        Pa = Pacc.flat[:, 0:8 * 64].rearrange("p (h t) -> p h t", t=64)
        ob = 4
        first = {}
        ostart = [True]
        for q in range(2):
            for g in range(NPG):
                dma("sp", "lpf", lambda e, q=q, g=g: [
                    e.dma_start(out=lpf.v, in_=d_latp[l, q].rearrange("c p t -> p c t")[:, :, g * 512:(g + 1) * 512]),
                    e.dma_start(out=kpf.v, in_=d_krp[l, q][:, g * 512:(g + 1) * 512])],
                    [], lpf.r() + kpf.r(), n=2)
                act(lambda e: e.activation(out=lpb.flat, in_=lpf.flat, func=AF.Copy), lpf.r(), lpb.r())
                Pb = KS[0][2]
                act(lambda e, Pb=Pb: e.activation(out=Pb.v, in_=kpf.v, func=AF.Copy), kpf.r(), Pb.r())
                for kb in range(4):
                    for half in range(2):
                        b = next_bank([0, 1])
                        mm_group(PSB[b][:, 0:512],
                                 [(lpb.v[:, kc, kb * 128:(kb + 1) * 128],
                                   wK[:, kc, 1024 + half * 512:1024 + half * 512 + 512]) for kc in range(2)],
                                 lpb.r() + slotK.r(), [PSR[b]])
                        act(lambda e, b=b, kb=kb, half=half: e.activation(
                            out=Vcur.v[:, kb, half * 512:(half + 1) * 512], in_=PSB[b][:, 0:512], func=AF.Copy),
                            [PSR[b]], Vcur.rs(kb, 1024))
                for h in range(8):
                    hp = h % 2
                    Kh = KS[h % 2][0]
                    b = next_bank([0, 1])
                    mm_group(PSB[b][:, 0:512], [(wK[:, kc, h * 128:(h + 1) * 128], lpb.v[:, kc, :]) for kc in range(2)],
                             lpb.r() + slotK.r(), [PSR[b]])
                    act(lambda e, b=b, Kh=Kh: e.activation(out=Kh.flat[:, 0:512], in_=PSB[b][:, 0:512], func=AF.Copy),
                        [PSR[b]], Kh.r())
                    for kb in range(4):
                        sb_ = 2 + (bctr[0] % 2)
                        bctr[0] += 1
                        pt = Pt[bctr[0] % 4]

                        def fsc(e, h=h, kb=kb, sb_=sb_, hp=hp, Kh=Kh, Pb=Pb, q=q):
                            e.matmul(PSB[sb_][:, 0:32], Kh.flat[:, kb * 128:(kb + 1) * 128], Qn.v[:, h, q * 32:(q + 1) * 32],
                                     start=True, stop=False)
                            return e.matmul(PSB[sb_][:, 0:32], Pb.v[:, kb * 128:(kb + 1) * 128],
                                            Qp.v[:, h, q * 32:(q + 1) * 32],
                                            start=False, stop=True)
                        pe(fsc, Kh.r() + Pb.r() + Qn.rs(h, 512) + Qp.rs(h, 512), [PSR[sb_]])
                        act(lambda e, sb_=sb_, pt=pt: e.activation(out=pt.v[:, 0:32], in_=PSB[sb_][:, 0:32], func=AF.Exp,
                                                                   scale=SCALE), [PSR[sb_]], pt.r())
                        st = first.get((h, q), True)

                        st0 = ostart[0]
                        ostart[0] = False

                        def fpv(e, h=h, kb=kb, pt=pt, st0=st0, q=q):
                            c0 = h * 64 + q * 32
                            return e.matmul(PSB[ob][:, c0:c0 + 32], Vcur.v[:, kb, h * 128:(h + 1) * 128], pt.v[:, 0:32],
                                            start=st0, stop=False, skip_group_check=True)
                        pe(fpv, Vcur.rs(kb, 1024) + pt.r(), [PSR[ob]])
                        if st:
                            dve(lambda e, pt=pt, h=h, q=q: e.tensor_copy(out=Pa[:, h, q * 32:(q + 1) * 32],
                                                                        in_=pt.v[:, 0:32]), pt.r(), Pacc.r())
                        else:
                            dve(lambda e, pt=pt, h=h, q=q: e.tensor_tensor(out=Pa[:, h, q * 32:(q + 1) * 32],
                                                                          in0=Pa[:, h, q * 32:(q + 1) * 32],
                                                                          in1=pt.v[:, 0:32], op=ALU.add),
                                pt.r() + Pacc.r(), Pacc.r())
                        first[(h, q)] = False
            for h in range(8):
                hp = h % 2
                sb_ = 2 + (bctr[0] % 2)
                bctr[0] += 1
                pt = Pt[bctr[0] % 4]

                def fsc(e, h=h, sb_=sb_, hp=hp, q=q):
                    e.matmul(PSB[sb_][0:32, 0:32], Knew[:, h, q * 32:(q + 1) * 32], Qn.v[:, h, q * 32:(q + 1) * 32],
                             start=True, stop=False)
                    return e.matmul(PSB[sb_][0:32, 0:32], Pcur.v[:, q * 32:(q + 1) * 32],
                                    Qp.v[:, h, q * 32:(q + 1) * 32], start=False, stop=True)
                pe(fsc, Kcur.r() + Pcur.r() + Qn.rs(h, 512) + Qp.rs(h, 512), [PSR[sb_]])
                act(lambda e, sb_=sb_, pt=pt: e.activation(out=pt.v[0:32, 0:32], in_=PSB[sb_][0:32, 0:32], func=AF.Exp,
                                                           scale=SCALE), [PSR[sb_]], pt.r())

                def fpv(e, h=h, pt=pt, q=q):
                    c0 = h * 64 + q * 32
                    return e.matmul(PSB[ob][:, c0:c0 + 32], Vnew.v[0:32, q, h * 128:(h + 1) * 128], pt.v[0:32, 0:32],
                                    start=False, stop=True, skip_group_check=True)
                pe(fpv, Vnew.rs(q, 1024) + pt.r(), [PSR[ob]])
                dve(lambda e, pt=pt, h=h, q=q: e.tensor_tensor(out=Pa[0:32, h, q * 32:(q + 1) * 32],
                                                              in0=Pa[0:32, h, q * 32:(q + 1) * 32],
                                                              in1=pt.v[0:32, 0:32], op=ALU.add),
                    pt.r() + Pacc.r(), Pacc.r())
        sb_ = 3
        mm_group(PSB[sb_][:, 0:512], [(onesf.v, Pacc.flat[:, 0:512])], Pacc.r() + onesf.r(), [PSR[sb_]])
        act(lambda e: e.activation(out=rsb.v, in_=PSB[sb_][:, 0:512], func=AF.Ln), [PSR[sb_]], rsb.r())
        act(lambda e: e.activation(out=rsb.v, in_=rsb.v, func=AF.Exp, scale=-1.0), rsb.r(), rsb.r())
        dve(lambda e: e.tensor_tensor(out=oaT.v[:, :, 0:64], in0=PSB[ob][:, 0:512].rearrange("p (h t) -> p h t", t=64),
                                      in1=rsb.v.rearrange("p (h t) -> p h t", t=64), op=ALU.mult),
            [PSR[ob]] + rsb.r(), oaT.r())

    def final_norm_store(N, dst_fn, key):
        rmsnorm_x(N, lambda c: gfs.v[:, c:c + 1], lambda c: xT.v[:, c, 0:N], lambda c: xT.rs(c, 512))
        dma("pool", key, lambda e: e.dma_start(out=dst_fn(), in_=xT.v[:, :, 0:N]), xT.r(), [])

    import os
    DBG = int(os.environ.get("KDBG", "0"))
    for s in range(NSEQ if DBG == 0 else 0):
        for j in range(NT):
            dma("sp", "xload", lambda e, s=s, j=j: e.dma_start(
                out=xT.v, in_=d_xp[s].rearrange("c p t -> p c t")[:, :, j * 512:(j + 1) * 512]), [], xT.r())
            dma("sp", "csload", lambda e, j=j: e.dma_start(
                out=cs.v, in_=d_csp.rearrange("a p t -> p a t")[:, :, j * 512:(j + 1) * 512]), [], cs.r())
            for l in range(DEPTH):
                tile_layer(l, 512, "p", s, j)
            final_norm_store(512, lambda s=s, j=j: o_yp[s].rearrange("c p t -> p c t")[:, :, j * 512:(j + 1) * 512],
                             "o_y")
    dma("sp", "xload", lambda e: e.dma_start(out=xT.v[:, :, 0:64], in_=d_xs.rearrange("c p t -> p c t")), [], xT.r())
    dma("sp", "csload", lambda e: e.dma_start(out=cs.v[:, :, 0:64], in_=d_css.rearrange("a p t -> p a t")), [], cs.r())
    for l in range(DEPTH if DBG == 0 else 0):
        dma("sp", "shalo", lambda e, l=l: [e.dma_start(out=shalo.v[:, q], in_=d_scv[l, q]) for q in range(2)],
            [], shalo.r(), n=2)
        tile_layer(l, 64, "s")
    final_norm_store(64, lambda: o_ys.rearrange("c p t -> p c t"), "o_y")

    import os as _os2
    if _os2.environ.get("KNOSCHED") is None:
        P.schedule()
        if _os2.environ.get("KVERB"):
            print("scheduled makespan us", P.makespan)
    P.finalize()
    sems = {}

    for k in [("eng", e_) for e_ in ("pe", "act", "dve", "pool")] + [("dma", k_) for k_ in P.dma_counts]:
        sems[k] = es.enter_context(nc.semaphore("s_" + "_".join(str(x) for x in k)))

    def semof(k):
        return sems[k]

    with nc.Block() as block:
        @block.sync
        def _(e):
            P.emit("sp", e, semof)

        @block.gpsimd
        def _(e):
            P.emit("pool", e, semof)

        @block.scalar
        def _(e):
            P.emit("act", e, semof)

        @block.vector
        def _(e):
            P.emit("dve", e, semof)

        @block.tensor
        def _(e):
            P.emit("pe", e, semof)
    es.close()
    return nc


def rope_tables(pos):
    half = 32
    inv = (10000.0 ** (-np.arange(half, dtype=np.float32) / half)).astype(np.float32)
    ang = pos.astype(np.float32)[:, None] * inv[None, :]
    cos = np.cos(ang).astype(np.float32).T
    sin = np.sin(ang).astype(np.float32).T
    c64 = np.concatenate([cos, cos], 0)
    s64 = np.concatenate([-sin, sin], 0)
    return np.stack([np.concatenate([c64, c64], 0), np.concatenate([s64, s64], 0)], 0)


def make_consts():
    ident = np.eye(128, dtype=np.float32)
    s_ = np.arange(128)[:, None]
    t_ = np.arange(128)[None, :]
    maskP = ((s_ // 64 == t_ // 64) & (s_ <= t_)).astype(np.float32)
    maskS = ((s_ // 32 == t_ // 32) & (s_ <= t_)).astype(np.float32)
    maskS[64:, :] = 0
    maskS[:, 64:] = 0
    resetP = np.ones((128, 512), np.float32)
    resetP[:, ::64] = 0
    resetS = np.ones((128, 64), np.float32)
    resetS[:, ::32] = 0
    mU = np.zeros((128, 128), np.float32)
    mU[0, 64:] = -30000.0
    mV = np.zeros((128, 512), np.float32)
    mV[0, :64] = 1.0
    return np.concatenate([maskP, maskS, resetP, resetS], 1), np.concatenate([ident, mU, mV], 1)


def perm64(w):
    return np.concatenate([w[..., 32:64], w[..., 0:32]], -1)


def host_prep(cfg, inp, core):
    D = cfg.DEPTH
    f = np.float32
    ps = slice(core * cfg.NSEQ, (core + 1) * cfg.NSEQ)
    ss = slice(core * 2, core * 2 + 2)
    m = {}
    xp = np.asarray(inp["x_prompt"][ps])
    m["xp"] = np.ascontiguousarray(xp.transpose(0, 2, 1).reshape(cfg.NSEQ, 8, 128, cfg.LP))
    xs = np.asarray(inp["x_sample"][ss])
    m["xs"] = np.ascontiguousarray(xs.transpose(2, 0, 1).reshape(8, 128, 64))
    lat = np.asarray(inp["cache_mla_latent"][:, ss])
    m["latp"] = np.ascontiguousarray(lat.transpose(0, 1, 3, 2).reshape(D, 2, 2, 128, cfg.PAST))
    kr = np.asarray(inp["cache_mla_krope"][:, ss]).transpose(0, 1, 3, 2)
    m["krp"] = np.ascontiguousarray(np.concatenate([kr, kr], 2))
    sh = np.asarray(inp["state_hgrn"][:, ss])
    m["sh"] = np.ascontiguousarray(sh.transpose(0, 1, 3, 2, 4))
    scv = np.asarray(inp["state_ffn_conv"][:, ss])
    m["scv"] = np.ascontiguousarray(scv.reshape(D, 2, 2, 22, 128).transpose(0, 1, 4, 3, 2))
    par = np.zeros((128, cfg.NPAR), f)

    def put(name, arr):
        o, n = cfg.po[name]
        par[:, o:o + n] = arr.reshape(128, n)
    put("g1", np.asarray(inp["norm_mix"]).reshape(D, 8, 128).transpose(2, 0, 1))
    put("g2", np.asarray(inp["norm_ffn"]).reshape(D, 8, 128).transpose(2, 0, 1))
    put("gq", np.asarray(inp["q_norm"]).reshape(D, 3, 128).transpose(2, 0, 1))
    put("gkv", np.asarray(inp["kv_norm"]).reshape(D, 2, 128).transpose(2, 0, 1))
    put("gh", np.asarray(inp["hgrn_norm"]).reshape(D, 128).transpose(1, 0))
    put("lbl", np.asarray(inp["lb_logits"]).reshape(D, 4, 128).transpose(2, 1, 0))
    put("cw", np.asarray(inp["conv_w"]).reshape(D, 3, 22, 128).transpose(3, 0, 1, 2))
    put("cb", np.asarray(inp["conv_b"]).reshape(D, 22, 128).transpose(2, 0, 1))
    put("gf", np.asarray(inp["norm_final"]).reshape(8, 128).transpose(1, 0))
    m["par"] = par
    return m


def host_shared(cfg, inp):
    D = cfg.DEPTH
    m = {}
    m["const"], m["const2"] = make_consts()
    m["csp"] = rope_tables(np.arange(cfg.LP))
    m["css"] = np.ascontiguousarray(np.tile(rope_tables(cfg.PAST + np.arange(32)), (1, 1, 2)))
    w_in = np.asarray(inp["w_in"])
    m["w_in"] = w_in
    kr = w_in[:, :, 640:704]
    krp = perm64(kr)
    m["w_inC"] = np.ascontiguousarray(np.concatenate([w_in[:, :, 0:640], kr, kr, krp, krp], 2))
    wuq = np.asarray(inp["w_uq"]).reshape(D, 384, 8, 192)
    nope = wuq[..., 0:128].reshape(D, 384, 1024)
    ropew = wuq[..., 128:192]
    m["w_uqR"] = np.ascontiguousarray(np.concatenate(
        [nope, ropew.reshape(D, 384, 512), perm64(ropew).reshape(D, 384, 512)], 2))
    wukv = np.asarray(inp["w_ukv"]).reshape(D, 256, 8, 256)
    m["w_ukvR"] = np.ascontiguousarray(np.concatenate(
        [wukv[..., 0:128].reshape(D, 256, 1024), wukv[..., 128:256].reshape(D, 256, 1024)], 2))
    m["w_pa"] = np.asarray(inp["w_proj_a"])
    m["w_pb"] = np.asarray(inp["w_proj_b"])
    m["w_o"] = np.asarray(inp["w_out"])
    m["w_up"] = np.asarray(inp["w_up"])
    m["w_dn"] = np.asarray(inp["w_down"])
    return m


def host_gather(cfg, results):
    D, LP, NSEQ = cfg.DEPTH, cfg.LP, cfg.NSEQ
    nb = NCORES * NSEQ
    y_p = np.empty((nb, LP, 1024), np.float32)
    y_s = np.empty((NCORES * 2, 32, 1024), np.float32)
    lat_p = np.empty((D, nb, LP, 256), np.float32)
    kpe_p = np.empty((D, nb, LP, 64), np.float32)
    hs_p = np.empty((D, nb, 4, 128, 128), np.float32)
    cv_p = np.empty((D, nb, 2, 2816), np.float32)
    lat_s = np.empty((D, NCORES * 2, 32, 256), np.float32)
    kpe_s = np.empty((D, NCORES * 2, 32, 64), np.float32)
    hs_s = np.empty((D, NCORES * 2, 4, 128, 128), np.float32)
    cv_s = np.empty((D, NCORES * 2, 2, 2816), np.float32)
    for c, r in enumerate(results):
        ps = slice(c * NSEQ, (c + 1) * NSEQ)
        ss = slice(c * 2, c * 2 + 2)
        y_p[ps] = r["o_yp"].reshape(NSEQ, 1024, LP).transpose(0, 2, 1)
        y_s[ss] = r["o_ys"].reshape(1024, 2, 32).transpose(1, 2, 0)
        lat_p[:, ps] = r["o_latp"].reshape(D, NSEQ, 256, LP).transpose(0, 1, 3, 2)
        kpe_p[:, ps] = r["o_kpep"].transpose(0, 1, 3, 2)
        hs_p[:, ps] = r["o_hsp"].transpose(0, 1, 3, 2, 4)
        cv_p[:, ps] = r["o_cvp"].transpose(0, 1, 4, 3, 2).reshape(D, NSEQ, 2, 2816)
        lat_s[:, ss] = r["o_lats"].reshape(D, 256, 2, 32).transpose(0, 2, 3, 1)
        kpe_s[:, ss] = r["o_kpes"].reshape(D, 64, 2, 32).transpose(0, 2, 3, 1)
        hs_s[:, ss] = r["o_hss"].transpose(0, 1, 3, 2, 4)
        cv_s[:, ss] = r["o_cvs"].transpose(0, 1, 4, 3, 2).reshape(D, 2, 2, 2816)
    return (y_p, y_s, lat_p, kpe_p, hs_p, cv_p, lat_s, kpe_s, hs_s, cv_s)


def run(cfg, inp, ncores=NCORES, trace=False):
    nc = build_program(cfg)
    shared = host_shared(cfg, inp)
    in_maps = []
    for c in range(ncores):
        m = dict(shared)
        m.update(host_prep(cfg, inp, c))
        in_maps.append(m)
    res = run_bass_kernel_spmd(nc, in_maps, core_ids=list(range(ncores)), trace=trace)
    return res


def kernel(**inputs):
    cfg = Cfg()
    res = run(cfg, inputs)
    return host_gather(cfg, res.results)
```
